# Optimizing a Trainium2 kernel written in Bass

```python
import math
import jax, jax.numpy as jnp
from jax import lax
import numpy as np

D_MODEL = 2048
BATCH = 4
SEQ = 2048
DEPTH = 4

HEAD_DIM = 128
DIFF_HEADS = 4
DIFF_QK_DIM = HEAD_DIM // 2
DIFF_V_DIM = HEAD_DIM
MLA_HEADS = 6
MLA_Q_LORA = 512
MLA_KV_LORA = 512
MLA_QK_NOPE = 128
MLA_QK_ROPE = 64
MLA_V_DIM = 128
FOX_HEADS = 6
FOX_HEAD_DIM = 128
D_MIX = DIFF_HEADS * DIFF_V_DIM + MLA_HEADS * MLA_V_DIM + FOX_HEADS * FOX_HEAD_DIM
IN_SIZES = (
    DIFF_HEADS * 2 * DIFF_QK_DIM,
    DIFF_HEADS * 2 * DIFF_QK_DIM,
    DIFF_HEADS * DIFF_V_DIM,
    MLA_Q_LORA,
    MLA_KV_LORA,
    MLA_QK_ROPE,
    FOX_HEADS * FOX_HEAD_DIM,
    FOX_HEADS * FOX_HEAD_DIM,
    FOX_HEADS * FOX_HEAD_DIM,
    FOX_HEADS,
)
IN_WIDTH = 4934
D_FF = 5632
CONV_WIDTH = 3
ROPE_THETA = 500000.0
PARTIAL_ROT_DIM = DIFF_QK_DIM // 4
BLOCK_Q = 128
NORM_EPS = 1e-6
SUBLN_EPS = 1e-5
MAX_POS_OFFSET = 4096

kernel_name = 'hymba_style_diff_mla_fox_convffn_trunk'


def _rmsnorm(x, gain, eps=NORM_EPS):
    x32 = x.astype(jnp.float32)
    y = x32 * lax.rsqrt(jnp.mean(x32 * x32, axis=-1, keepdims=True) + eps)
    return (y * gain.astype(jnp.float32)).astype(x.dtype)


def _rope_tables(positions, rot_dim):
    inv_freq = ROPE_THETA ** (-jnp.arange(0, rot_dim, 2, dtype=jnp.float32) / rot_dim)
    ang = positions.astype(jnp.float32)[..., None] * inv_freq
    return jnp.cos(ang)[:, :, None, :], jnp.sin(ang)[:, :, None, :]


def _apply_rope(x, cos, sin):
    half = cos.shape[-1]
    r = 2 * half
    xr = x[..., :r].astype(jnp.float32)
    x1, x2 = xr[..., :half], xr[..., half:]
    rot = jnp.concatenate([x1 * cos - x2 * sin, x2 * cos + x1 * sin], axis=-1).astype(x.dtype)
    return jnp.concatenate([rot, x[..., r:]], axis=-1)


def _causal_probs(q_blk, k, start, scale, bias=None):
    logits = jnp.einsum('bhqd,bhkd->bhqk', q_blk, k).astype(jnp.float32) * scale
    if bias is not None:
        logits = logits + bias
    qpos = start + jnp.arange(BLOCK_Q)
    kpos = jnp.arange(k.shape[2])
    logits = jnp.where(kpos[None, :] <= qpos[:, None], logits, -jnp.inf)
    return jax.nn.softmax(logits, axis=-1)


def _sweep(block_fn, seq):
    out = lax.map(block_fn, jnp.arange(seq // BLOCK_Q))
    nb, b, h, bq, d = out.shape
    return out.transpose(1, 0, 3, 2, 4).reshape(b, nb * bq, h, d)


def _diff_attention(q, k, v, cos, sin, lam_vecs, out_gain, lambda_init):
    b, s, _ = q.shape
    q = _apply_rope(q.reshape(b, s, 2 * DIFF_HEADS, DIFF_QK_DIM), cos, sin)
    k = _apply_rope(k.reshape(b, s, 2 * DIFF_HEADS, DIFF_QK_DIM), cos, sin)
    q = q.reshape(b, s, DIFF_HEADS, 2, DIFF_QK_DIM).transpose(0, 2, 3, 1, 4)
    k = k.reshape(b, s, DIFF_HEADS, 2, DIFF_QK_DIM).transpose(0, 2, 3, 1, 4)
    v = v.reshape(b, s, DIFF_HEADS, DIFF_V_DIM).transpose(0, 2, 1, 3)
    lv = lam_vecs.astype(jnp.float32)
    lam = jnp.exp(jnp.sum(lv[0] * lv[1])) - jnp.exp(jnp.sum(lv[2] * lv[3])) + lambda_init
    scale = DIFF_QK_DIM ** -0.5

    def block(i):
        start = i * BLOCK_Q
        qb = lax.dynamic_slice_in_dim(q, start, BLOCK_Q, axis=3)
        p1 = _causal_probs(qb[:, :, 0], k[:, :, 0], start, scale)
        p2 = _causal_probs(qb[:, :, 1], k[:, :, 1], start, scale)
        return jnp.einsum('bhqk,bhkd->bhqd', (p1 - lam * p2).astype(v.dtype), v)

    o = _sweep(block, s)
    o = _rmsnorm(o, out_gain, SUBLN_EPS) * (1.0 - lambda_init)
    return o.reshape(b, s, DIFF_HEADS * DIFF_V_DIM)


def _mla_attention(c_q, c_kv, k_rope, cos, sin, q_gain, kv_gain, w_uq, w_ukv):
    b, s, _ = c_q.shape
    q = jnp.einsum('bsr,rf->bsf', _rmsnorm(c_q, q_gain), w_uq)
    q = q.reshape(b, s, MLA_HEADS, MLA_QK_NOPE + MLA_QK_ROPE)
    q = jnp.concatenate([q[..., :MLA_QK_NOPE], _apply_rope(q[..., MLA_QK_NOPE:], cos, sin)], axis=-1)
    kv = jnp.einsum('bsr,rf->bsf', _rmsnorm(c_kv, kv_gain), w_ukv)
    kv = kv.reshape(b, s, MLA_HEADS, MLA_QK_NOPE + MLA_V_DIM)
    k_r = _apply_rope(k_rope[:, :, None, :], cos, sin)
    k = jnp.concatenate([kv[..., :MLA_QK_NOPE],
                         jnp.broadcast_to(k_r, (b, s, MLA_HEADS, MLA_QK_ROPE))], axis=-1)
    v = kv[..., MLA_QK_NOPE:]
    q = q.transpose(0, 2, 1, 3)
    k = k.transpose(0, 2, 1, 3)
    v = v.transpose(0, 2, 1, 3)
    scale = (MLA_QK_NOPE + MLA_QK_ROPE) ** -0.5

    def block(i):
        start = i * BLOCK_Q
        qb = lax.dynamic_slice_in_dim(q, start, BLOCK_Q, axis=2)
        p = _causal_probs(qb, k, start, scale)
        return jnp.einsum('bhqk,bhkd->bhqd', p.astype(v.dtype), v)

    return _sweep(block, s).reshape(b, s, MLA_HEADS * MLA_V_DIM)


def _forgetting_attention(q, k, v, f_logit, f_bias):
    b, s, _ = q.shape
    q = q.reshape(b, s, FOX_HEADS, FOX_HEAD_DIM).transpose(0, 2, 1, 3)
    k = k.reshape(b, s, FOX_HEADS, FOX_HEAD_DIM).transpose(0, 2, 1, 3)
    v = v.reshape(b, s, FOX_HEADS, FOX_HEAD_DIM).transpose(0, 2, 1, 3)
    log_f = jax.nn.log_sigmoid(f_logit.astype(jnp.float32) + f_bias.astype(jnp.float32))
    cum = jnp.cumsum(log_f, axis=1).transpose(0, 2, 1)
    scale = FOX_HEAD_DIM ** -0.5

    def block(i):
        start = i * BLOCK_Q
        qb = lax.dynamic_slice_in_dim(q, start, BLOCK_Q, axis=2)
        cq = lax.dynamic_slice_in_dim(cum, start, BLOCK_Q, axis=2)
        bias = cq[..., :, None] - cum[..., None, :]
        p = _causal_probs(qb, k, start, scale, bias)
        return jnp.einsum('bhqk,bhkd->bhqd', p.astype(v.dtype), v)

    return _sweep(block, s).reshape(b, s, FOX_HEADS * FOX_HEAD_DIM)


def _conv_gated_mlp(h, w_up, conv_w, conv_b, w_down):
    u = jnp.einsum('bsd,df->bsf', h, w_up)
    u = lax.conv_general_dilated(u, conv_w[:, None, :], window_strides=(1,),
                                 padding=[(CONV_WIDTH - 1, 0)],
                                 dimension_numbers=('NWC', 'WIO', 'NWC'),
                                 feature_group_count=u.shape[-1]) + conv_b
    a, g = jnp.split(u, 2, axis=-1)
    return jnp.einsum('bsf,fd->bsd', jax.nn.silu(g) * a, w_down)


def setup_inputs(seed: int = 0) -> dict:
    key = jax.random.key(seed)
    ks = jax.random.split(key, 20)

    def nrm(k, shape, scale):
        return jax.random.normal(k, shape, jnp.float32) * scale

    def gain(k, shape):
        return 1.0 + 0.05 * jax.random.normal(k, shape, jnp.float32)

    x = nrm(ks[0], (BATCH, SEQ, D_MODEL), 1.0)
    offsets = jax.random.randint(ks[1], (BATCH, 1), 0, MAX_POS_OFFSET, dtype=jnp.int32)
    positions = (jnp.arange(SEQ, dtype=jnp.int32)[None, :] + offsets).astype(jnp.int32)
    return {
        'x': x,
        'positions': positions,
        'attn_norm': gain(ks[2], (DEPTH, D_MODEL)),
        'w_in': nrm(ks[3], (DEPTH, D_MODEL, IN_WIDTH), D_MODEL ** -0.5),
        'diff_lambda': nrm(ks[4], (DEPTH, 4, DIFF_QK_DIM), 0.1),
        'diff_out_norm': gain(ks[5], (DEPTH, DIFF_V_DIM)),
        'mla_q_norm': gain(ks[6], (DEPTH, MLA_Q_LORA)),
        'mla_kv_norm': gain(ks[7], (DEPTH, MLA_KV_LORA)),
        'mla_w_uq': nrm(ks[8], (DEPTH, MLA_Q_LORA, MLA_HEADS * (MLA_QK_NOPE + MLA_QK_ROPE)), MLA_Q_LORA ** -0.5),
        'mla_w_ukv': nrm(ks[9], (DEPTH, MLA_KV_LORA, MLA_HEADS * (MLA_QK_NOPE + MLA_V_DIM)), MLA_KV_LORA ** -0.5),
        'fox_forget_bias': jax.random.uniform(ks[10], (DEPTH, FOX_HEADS), jnp.float32, 1.0, 4.0),
        'w_o': nrm(ks[11], (DEPTH, D_MIX, D_MODEL), D_MIX ** -0.5),
        'ffn_norm': gain(ks[12], (DEPTH, D_MODEL)),
        'ffn_w_up': nrm(ks[13], (DEPTH, D_MODEL, 2 * D_FF), D_MODEL ** -0.5),
        'ffn_conv_w': nrm(ks[14], (DEPTH, CONV_WIDTH, 2 * D_FF), CONV_WIDTH ** -0.5),
        'ffn_conv_b': nrm(ks[15], (DEPTH, 2 * D_FF), 0.02),
        'ffn_w_down': nrm(ks[16], (DEPTH, D_FF, D_MODEL), D_FF ** -0.5),
        'final_norm': gain(ks[17], (D_MODEL,)),
    }


def reference(x, positions, attn_norm, w_in, diff_lambda, diff_out_norm, mla_q_norm, mla_kv_norm,
              mla_w_uq, mla_w_ukv, fox_forget_bias, w_o, ffn_norm, ffn_w_up, ffn_conv_w, ffn_conv_b,
              ffn_w_down, final_norm):
    cos_p, sin_p = _rope_tables(positions, PARTIAL_ROT_DIM)
    cos_m, sin_m = _rope_tables(positions, MLA_QK_ROPE)
    split_points = [int(v) for v in np.cumsum(IN_SIZES)[:-1]]
    for l in range(DEPTH):
        lambda_init = 0.8 - 0.6 * math.exp(-0.3 * l)
        h = _rmsnorm(x, attn_norm[l])
        z = jnp.einsum('bsd,de->bse', h, w_in[l])
        (a_q, a_k, a_v, m_cq, m_ckv, m_kr, f_q, f_k, f_v, f_g) = jnp.split(z, split_points, axis=-1)
        o_a = _diff_attention(a_q, a_k, a_v, cos_p, sin_p, diff_lambda[l], diff_out_norm[l], lambda_init)
        o_b = _mla_attention(m_cq, m_ckv, m_kr, cos_m, sin_m, mla_q_norm[l], mla_kv_norm[l],
                             mla_w_uq[l], mla_w_ukv[l])
        o_c = _forgetting_attention(f_q, f_k, f_v, f_g, fox_forget_bias[l])
        mix = jnp.concatenate([o_a, o_b, o_c], axis=-1)
        x = x + jnp.einsum('bsm,md->bsd', mix, w_o[l])
        h = _rmsnorm(x, ffn_norm[l])
        x = x + _conv_gated_mlp(h, ffn_w_up[l], ffn_conv_w[l], ffn_conv_b[l], ffn_w_down[l])
    return _rmsnorm(x, final_norm)
```

```python
import math
import ml_dtypes
from contextlib import ExitStack
import numpy as np
import concourse.bass as bass
import concourse.mybir as mybir

F32 = mybir.dt.float32
BF16 = mybir.dt.bfloat16
I32 = mybir.dt.int32
AF = mybir.ActivationFunctionType
ALU = mybir.AluOpType
AX = mybir.AxisListType

ENGS = ("sync", "scalar", "vector", "gpsimd", "tensor")
EPOCH = 20000


class Buf:
    def __init__(self, prog, name, t):
        self.prog = prog
        self.name = name
        self.t = t
        self.writes = {}
        self.reads = {}
        self.dsem = None
        self.dcount = 0
        self.lock = None

    def __getitem__(self, idx):
        return self.t[idx]


class Prog:
    def __init__(self, nc, stack, n_eng_sems=6):
        self.nc = nc
        self.stack = stack
        self.sem_stack = stack
        self.free_dsems = []
        self.live_dbufs = []
        self.ops = {e: [] for e in ENGS}
        self.semtab = []
        self.eng_sems = {}
        self.eng_epoch = {e: 0 for e in ENGS}
        self.eng_cnt = {e: 0 for e in ENGS}
        self.waited = {e: {} for e in ENGS}
        for e in ENGS:
            self.eng_sems[e] = [self._new_sem(f"s_{e}_{i}") for i in range(n_eng_sems)]
        self.dma_sems = []
        self.nbuf = 0

    def _new_sem(self, name):
        h = self.sem_stack.enter_context(self.nc.semaphore(name))
        self.semtab.append(h)
        return len(self.semtab) - 1

    def _dsem_for(self, owner):
        if owner.dsem is None:
            if self.free_dsems:
                owner.dsem, owner.dcount = self.free_dsems.pop()
            else:
                owner.dsem = self._new_sem(f"d{len(self.semtab)}")
                owner.dcount = 0
            self.live_dbufs.append(owner)

    def sb(self, name, shape, dtype):
        t = self.stack.enter_context(self.nc.sbuf_tensor(name, list(shape), dtype))
        return Buf(self, name, t)

    def ps(self, name, shape, dtype=F32):
        t = self.stack.enter_context(self.nc.psum_tensor(name, list(shape), dtype))
        b = Buf(self, name, t)
        b.lock = Buf(self, name + "_lock", None)
        return b

    def wrap(self, name, t, lock=None):
        b = Buf(self, name, t)
        b.lock = lock
        return b

    def _locks(self, reads, writes):
        ls = []
        for b in list(reads) + list(writes):
            if b.lock is not None and b.lock not in ls:
                ls.append(b.lock)
        return ls

    def _need(self, eng, reads, writes):
        need = {}
        for b in reads:
            for s, v in b.writes.items():
                if need.get(s, 0) < v:
                    need[s] = v
        for b in list(writes) + self._locks(reads, writes):
            for d in (b.writes, b.reads):
                for s, v in d.items():
                    if need.get(s, 0) < v:
                        need[s] = v
        if eng == "tensor":
            own = set(self.eng_sems["tensor"])
            need = {s: v for s, v in need.items() if s not in own}
        out = []
        w = self.waited[eng]
        for s, v in need.items():
            if w.get(s, 0) < v:
                w[s] = v
                out.append((s, v))
        return out

    def _emit_waits(self, eng, waits):
        for s, v in waits:
            h = self.semtab[s]
            self.ops[eng].append(lambda e, h=h, v=v: e.wait_ge(h, v))

    def _next_event(self, eng):
        if self.eng_cnt[eng] >= EPOCH:
            self.eng_epoch[eng] += 1
            self.eng_cnt[eng] = 0
        self.eng_cnt[eng] += 1
        s = self.eng_sems[eng][self.eng_epoch[eng]]
        return s, self.eng_cnt[eng]

    def op(self, eng, fn, reads=(), writes=()):
        waits = self._need(eng, reads, writes)
        self._emit_waits(eng, waits)
        s, v = self._next_event(eng)
        h = self.semtab[s]
        self.ops[eng].append(lambda e, fn=fn, h=h: fn(e).then_inc(h, 1))
        for b in list(writes) + self._locks(reads, writes):
            b.writes = {s: v}
            b.reads = {}
        for b in reads:
            if b in writes:
                continue
            b.reads[s] = max(b.reads.get(s, 0), v)
        return (s, v)

    def dma(self, out_ap, in_ap, reads=(), writes=(), q="sync", **kw):
        waits = self._need(q, reads, writes)
        self._emit_waits(q, waits)
        owner = (list(writes) + list(reads))[0]
        self._dsem_for(owner)
        owner.dcount += 16
        s, v = owner.dsem, owner.dcount
        h = self.semtab[s]
        self.ops[q].append(
            lambda e, o=out_ap, i=in_ap, h=h, kw=kw: e.dma_start(out=o, in_=i, **kw).then_inc(h, 16))
        for b in writes:
            b.writes = {s: v}
            b.reads = {}
        for b in reads:
            if b in writes:
                continue
            b.reads[s] = max(b.reads.get(s, 0), v)
        return (s, v)

    def dma_like(self, q, fn, reads=(), writes=(), inc=16):
        waits = self._need(q, reads, writes)
        self._emit_waits(q, waits)
        owner = (list(writes) + list(reads))[0]
        self._dsem_for(owner)
        owner.dcount += inc
        s, v = owner.dsem, owner.dcount
        h = self.semtab[s]
        self.ops[q].append(lambda e, fn=fn, h=h: fn(e).then_inc(h, inc))
        for b in writes:
            b.writes = {s: v}
            b.reads = {}
        for b in reads:
            if b in writes:
                continue
            b.reads[s] = max(b.reads.get(s, 0), v)
        return (s, v)

    def barrier(self):
        ev = {}
        for e in ENGS:
            if self.eng_cnt[e] > 0:
                ev[self.eng_sems[e][self.eng_epoch[e]]] = self.eng_cnt[e]
        for e in ENGS:
            for s, v in ev.items():
                if self.waited[e].get(s, 0) < v:
                    self.waited[e][s] = v
                    h = self.semtab[s]
                    self.ops[e].append(lambda en, h=h, v=v: en.wait_ge(h, v))

    def drain_all(self):
        ev = {}
        for e in ENGS:
            if self.eng_cnt[e] > 0:
                ev[self.eng_sems[e][self.eng_epoch[e]]] = self.eng_cnt[e]
        for b in self.live_dbufs:
            ev[b.dsem] = max(ev.get(b.dsem, 0), b.dcount)
        for e in ENGS:
            for s, v in ev.items():
                if self.waited[e].get(s, 0) < v:
                    self.waited[e][s] = v
                    h = self.semtab[s]
                    self.ops[e].append(lambda en, h=h, v=v: en.wait_ge(h, v))
        for b in self.live_dbufs:
            self.free_dsems.append((b.dsem, b.dcount))
            b.dsem = None
            b.dcount = 0
            b.writes = {}
            b.reads = {}
        self.live_dbufs = []

    def finish(self, bufs):
        need = {}
        for b in bufs:
            for d in (b.writes, b.reads):
                for s, v in d.items():
                    need[s] = max(need.get(s, 0), v)
        for e in ENGS:
            if e != "sync" and self.eng_cnt[e] > 0:
                s = self.eng_sems[e][self.eng_epoch[e]]
                need[s] = max(need.get(s, 0), self.eng_cnt[e])
        for s, v in need.items():
            h = self.semtab[s]
            self.ops["sync"].append(lambda en, h=h, v=v: en.wait_ge(h, v))

    def emit(self):
        nc = self.nc
        with nc.Block() as block:
            @block.sync
            def _(e):
                for f in self.ops["sync"]:
                    f(e)

            @block.scalar
            def _(e):
                for f in self.ops["scalar"]:
                    f(e)

            @block.vector
            def _(e):
                for f in self.ops["vector"]:
                    f(e)

            @block.gpsimd
            def _(e):
                for f in self.ops["gpsimd"]:
                    f(e)

            @block.tensor
            def _(e):
                for f in self.ops["tensor"]:
                    f(e)


D = 2048
S = 2048
NT = 16
NC1 = 3587
C_Q, C_QS, C_K, C_KS, C_V, C_CQ, C_CKV, C_KR, C_KRS, C_FQ, C_FK, C_FV, C_FG = (
    0, 256, 512, 768, 1024, 1280, 1792, 2304, 2368, 2432, 2816, 3200, 3584)
TWO_PI = 2.0 * math.pi


def body_k1(nc, P, io):
    STOP = 99
    xn, pos, W1, U1, U2 = io["xn"], io["pos"], io["W1"], io["U1"], io["U2"]
    attn_norm, qn, kvn, fb, dl, lc = io["attn_norm"], io["qn"], io["kvn"], io["fb"], io["dl"], io["lc"]
    idf, idb, maskd, rc, mixT = io["idf"], io["idb"], io["mask"], io["rc"], io["mixT"]
    with ExitStack() as st:
        P.stack = st
        HT = P.sb("HT", [128, 16, S], BF16)
        IDF = P.sb("IDF", [128, 128], F32)
        IDB = P.sb("IDB", [128, 128], BF16)
        MASK = P.sb("MASK", [128, 128], BF16)
        RC = P.sb("RC", [128, 8], F32)
        GIN = P.sb("GIN", [128, 16], F32)
        GQ = P.sb("GQ", [128, 4], F32)
        GKV = P.sb("GKV", [128, 4], F32)
        LCB = P.sb("LCB", [128, 4], F32)
        EPS6 = P.sb("EPS6", [128, 1], F32)
        EPS5 = P.sb("EPS5", [128, 1], F32)
        CD = P.sb("CD", [128, S], BF16)
        SD = P.sb("SD", [128, S], BF16)
        CM = P.sb("CM", [128, S], BF16)
        SM = P.sb("SM", [128, S], BF16)
        WB = [P.sb(f"WB{i}", [128, 16, 256], BF16) for i in range(2)]
        STG = [P.sb(f"stg{i}", [128, 512], F32) for i in range(3)]
        PTB = [P.sb(f"PTB{i}", [128, 512], BF16) for i in range(3)]
        ONB = [P.sb(f"ONB{i}", [128, 128], BF16) for i in range(2)]
        MIXH = [P.sb(f"MIXH{i}", [128, S], BF16) for i in range(2)]
        RL = [P.sb(f"RL{i}", [128, 1], F32) for i in range(4)]
        ONESB = P.sb("ONESB", [128, 128], BF16)
        ONESF = P.sb("ONESF", [1, 128], F32)
        _t1 = P.sb("T1_0", [128, 512], F32)
        _t2 = P.sb("T2_0", [128, 512], F32)
        T1 = [_t1, _t1]
        T2 = [_t2, _t2]

        PS = [P.ps(f"PS{i}", [128, 512]) for i in range(2)]
        PO = [P.ps(f"PO{i}", [128, 512]) for i in range(4)]
        PM = P.ps("PM", [128, 512])
        PTR = P.ps("PTR", [128, 1024], BF16)
        PROJ = [PM, PS[0], PS[1]]
        OUT = P.wrap("OUT", mixT)

        cnt = {"stg": 0, "cast": 0, "proj": 0, "wb": 0, "ptb": 0, "onb": 0, "rl": 0, "t": 0}

        def stage():
            b = STG[cnt["stg"] % len(STG)]
            cnt["stg"] += 1
            return b

        def cast(out_ap, in_ap, scale_ap, reads, writes):
            eng = "scalar" if cnt["cast"] % 2 == 0 else "gpsimd"
            cnt["cast"] += 1
            if eng == "scalar":
                if scale_ap is None:
                    P.op("scalar", lambda e: e.activation(out=out_ap, in_=in_ap, func=AF.Copy), reads=reads, writes=writes)
                else:
                    P.op("scalar", lambda e: e.activation(out=out_ap, in_=in_ap, func=AF.Copy, scale=scale_ap),
                         reads=reads, writes=writes)
            else:
                if scale_ap is None:
                    P.op("gpsimd", lambda e: e.tensor_copy(out=out_ap, in_=in_ap), reads=reads, writes=writes)
                else:
                    P.op("gpsimd", lambda e: e.tensor_scalar(out=out_ap, in0=in_ap, scalar1=scale_ap, scalar2=1.0,
                                                              op0=ALU.mult, op1=ALU.mult), reads=reads, writes=writes)

        def next_proj():
            b = PROJ[cnt["proj"] % 3]
            cnt["proj"] += 1
            return b

        def vecT(dst, src_ap, n):
            s = stage()
            pm = next_proj()
            P.dma(s[0:n, 0:128], src_ap.rearrange("(k p) -> k p", p=128), writes=[s])
            P.op("tensor", lambda e: e.transpose(out=pm[:, 0:n], in_=s[0:n, 0:128], identity=IDF[0:n, 0:n]),
                 reads=[s, IDF], writes=[pm])
            P.op("vector", lambda e: e.tensor_copy(out=dst[:, 0:n], in_=pm[:, 0:n]), reads=[pm], writes=[dst])

        P.dma(IDF[:, :], idf, writes=[IDF])
        P.dma(IDB[:, :], idb, writes=[IDB])
        P.dma(MASK[:, :], maskd, writes=[MASK])
        P.dma(RC[:, :], rc, writes=[RC])
        P.dma(LCB[:, :], lc.partition_broadcast(128), writes=[LCB])
        P.op("gpsimd", lambda e: e.memset(EPS6[:, :], 1e-6), writes=[EPS6])
        P.op("gpsimd", lambda e: e.memset(EPS5[:, :], 1e-5), writes=[EPS5])
        P.op("gpsimd", lambda e: e.memset(ONESB[:, :], 1.0), writes=[ONESB])
        P.op("gpsimd", lambda e: e.memset(ONESF[:, :], 1.0), writes=[ONESF])
        vecT(GIN, attn_norm, 16)
        vecT(GQ, qn, 4)
        vecT(GKV, kvn, 4)

        with ExitStack() as stR:
            def sbR(name, shape, dtp):
                return Buf(P, name, stR.enter_context(nc.sbuf_tensor(name, list(shape), dtp)))
            POSI = sbR("POSI", [128, S], I32)
            POSF = sbR("POSF", [128, S], F32)
            Y = sbR("Y", [128, S], F32)
            Y2 = sbR("Y2", [128, S], F32)
            KI = sbR("KI", [128, S], I32)
            KF = sbR("KF", [128, S], F32)
            P.dma(POSI[:, :], pos.partition_broadcast(128), writes=[POSI])
            P.op("vector", lambda e: e.tensor_copy(out=POSF[:, :], in_=POSI[:, :]), reads=[POSI], writes=[POSF])

            def sincos(invf_col, sin_dst, sin_mul_col, cos_dst, cos_mul_col, cos_add_col):
                P.op("vector", lambda e: e.tensor_scalar(out=Y[:, :], in0=POSF[:, :], scalar1=RC[:, invf_col:invf_col + 1],
                                                          scalar2=1.0 / TWO_PI, op0=ALU.mult, op1=ALU.mult),
                     reads=[POSF, RC], writes=[Y])
                for shift, dst, mulc, addc in ((0.0, sin_dst, sin_mul_col, None), (0.25, cos_dst, cos_mul_col, cos_add_col)):
                    P.op("vector", lambda e, shift=shift: e.tensor_scalar(out=Y2[:, :], in0=Y[:, :], scalar1=shift, scalar2=None,
                                                                           op0=ALU.add), reads=[Y], writes=[Y2])
                    P.op("vector", lambda e: e.tensor_copy(out=KI[:, :], in_=Y2[:, :]), reads=[Y2], writes=[KI])
                    P.op("vector", lambda e: e.tensor_copy(out=KF[:, :], in_=KI[:, :]), reads=[KI], writes=[KF])
                    P.op("vector", lambda e: e.tensor_tensor(out=Y2[:, :], in0=Y2[:, :], in1=KF[:, :], op=ALU.subtract),
                         reads=[Y2, KF], writes=[Y2])
                    P.op("vector", lambda e: e.tensor_scalar(out=KF[:, :], in0=Y2[:, :], scalar1=0.5, scalar2=None, op0=ALU.is_gt),
                         reads=[Y2], writes=[KF])
                    P.op("vector", lambda e: e.tensor_tensor(out=Y2[:, :], in0=Y2[:, :], in1=KF[:, :], op=ALU.subtract),
                         reads=[Y2, KF], writes=[Y2])
                    P.op("vector", lambda e: e.tensor_scalar(out=KF[:, :], in0=Y2[:, :], scalar1=-0.5, scalar2=None, op0=ALU.is_lt),
                         reads=[Y2], writes=[KF])
                    P.op("vector", lambda e: e.tensor_tensor(out=Y2[:, :], in0=Y2[:, :], in1=KF[:, :], op=ALU.add),
                         reads=[Y2, KF], writes=[Y2])
                    P.op("scalar", lambda e: e.activation(out=KF[:, :], in_=Y2[:, :], func=AF.Sin, scale=TWO_PI),
                         reads=[Y2], writes=[KF])
                    if addc is None:
                        P.op("vector", lambda e, dst=dst, mulc=mulc: e.tensor_scalar(
                            out=dst[:, :], in0=KF[:, :], scalar1=RC[:, mulc:mulc + 1], scalar2=None, op0=ALU.mult),
                            reads=[KF, RC], writes=[dst])
                    else:
                        P.op("vector", lambda e, dst=dst, mulc=mulc, addc=addc: e.tensor_scalar(
                            out=dst[:, :], in0=KF[:, :], scalar1=RC[:, mulc:mulc + 1], scalar2=RC[:, addc:addc + 1],
                            op0=ALU.mult, op1=ALU.add), reads=[KF, RC], writes=[dst])

            sincos(0, SD, 2, CD, 1, 5)
            sincos(3, SM, 4, CM, 6, 7)
            P.barrier()

        if STOP == 1:
            P.dma(mixT[0:128, :], MIXH[0][:, :], reads=[MIXH[0]], writes=[OUT])
            P.finish([OUT])
            P.emit()
            return nc
        with ExitStack() as stX:
            XT = [Buf(P, f"XT{i}", stX.enter_context(nc.sbuf_tensor(f"XT{i}", [128, D], BF16))) for i in range(2)]
            for t in range(NT):
                xt = XT[t % 2]
                P.dma(xt[:, :], xn[t * 128:(t + 1) * 128, :], writes=[xt])
                for g in range(2):
                    for kk in range(8):
                        k = g * 8 + kk
                        P.op("tensor", lambda e, k=k, kk=kk, xt=xt: e.transpose(
                            out=PTR[:, kk * 128:(kk + 1) * 128], in_=xt[:, k * 128:(k + 1) * 128], identity=IDB[:, :]),
                            reads=[xt, IDB], writes=[PTR])
                    src = PTR[:, :].rearrange("p (k t) -> p k t", t=128)
                    dst = HT[:, g * 8:(g + 1) * 8, t * 128:(t + 1) * 128]
                    if g == 0:
                        P.op("vector", lambda e, src=src, dst=dst: e.tensor_copy(out=dst, in_=src), reads=[PTR], writes=[HT])
                    else:
                        P.op("scalar", lambda e, src=src, dst=dst: e.activation(out=dst, in_=src, func=AF.Copy),
                             reads=[PTR], writes=[HT])
            P.barrier()

        if STOP == 2:
            P.dma(mixT[0:128, :], MIXH[0][:, :], reads=[MIXH[0]], writes=[OUT])
            P.finish([OUT])
            P.emit()
            return nc
        def load_w1(col0, ncols):
            wb = WB[cnt["wb"] % 2]
            cnt["wb"] += 1
            for k in range(16):
                s = stage()
                P.dma(s[:, 0:ncols], W1[k * 128:(k + 1) * 128, col0:col0 + ncols], writes=[s])
                cast(wb[:, k, 0:ncols], s[:, 0:ncols], GIN[:, k:k + 1], reads=[s, GIN], writes=[wb])
            return wb

        def load_u(U, col0, ncols, G):
            wb = WB[cnt["wb"] % 2]
            cnt["wb"] += 1
            for r in range(4):
                s = stage()
                P.dma(s[:, 0:ncols], U[r * 128:(r + 1) * 128, col0:col0 + ncols], writes=[s])
                cast(wb[:, r, 0:ncols], s[:, 0:ncols], G[:, r:r + 1], reads=[s, G], writes=[wb])
            return wb

        def proj_fm(wb, c0, m, src, nk, r, pm):
            for k in range(nk):
                P.op("tensor", lambda e, k=k: e.matmul(pm[0:m, :], lhsT=wb[:, k, c0:c0 + m],
                                                        rhs=src[:, k, r * 512:(r + 1) * 512],
                                                        start=(k == 0), stop=(k == nk - 1)),
                     reads=[wb, src], writes=[pm])

        def proj_tm(wb, c0, n, src, nk, j, pm):
            for k in range(nk):
                P.op("tensor", lambda e, k=k: e.matmul(pm[:, 0:n], lhsT=src[:, k, j * 128:(j + 1) * 128],
                                                        rhs=wb[:, k, c0:c0 + n], start=(k == 0), stop=(k == nk - 1)),
                     reads=[wb, src], writes=[pm])

        def rope_fm(wb, c_main, c_swap, m, src, nk, dst, Ct, St):
            for r in range(4):
                cs = slice(r * 512, (r + 1) * 512)
                i = cnt["t"] % 2
                cnt["t"] += 1
                p1 = next_proj()
                proj_fm(wb, c_main, m, src, nk, r, p1)
                P.op("vector", lambda e, p1=p1, i=i, cs=cs: e.tensor_tensor(out=T1[i][0:m, :], in0=p1[0:m, :], in1=Ct[0:m, cs],
                                                                             op=ALU.mult), reads=[p1, Ct], writes=[T1[i]])
                p2 = next_proj()
                proj_fm(wb, c_swap, m, src, nk, r, p2)
                P.op("vector", lambda e, p2=p2, i=i, cs=cs: e.tensor_tensor(out=T2[i][0:m, :], in0=p2[0:m, :], in1=St[0:m, cs],
                                                                             op=ALU.mult), reads=[p2, St], writes=[T2[i]])
                P.op("gpsimd", lambda e, i=i, cs=cs: e.tensor_tensor(out=dst[0:m, cs], in0=T1[i][0:m, :], in1=T2[i][0:m, :],
                                                                      op=ALU.add), reads=[T1[i], T2[i]], writes=[dst])

        def plain_fm(wb, c0, m, src, nk, dst):
            for r in range(4):
                cs = slice(r * 512, (r + 1) * 512)
                p1 = next_proj()
                proj_fm(wb, c0, m, src, nk, r, p1)
                if r % 2 == 0:
                    P.op("vector", lambda e, p1=p1, cs=cs: e.tensor_copy(out=dst[0:m, cs], in_=p1[0:m, :]), reads=[p1], writes=[dst])
                else:
                    P.op("scalar", lambda e, p1=p1, cs=cs: e.activation(out=dst[0:m, cs], in_=p1[0:m, :], func=AF.Copy),
                         reads=[p1], writes=[dst])

        def v_tm(wb, c0, nheads, src, nk, Vs):
            n = nheads * 128
            for j in range(NT):
                pm = next_proj()
                proj_tm(wb, c0, n, src, nk, j, pm)
                for h in range(nheads):
                    if (j + h) % 2 == 0:
                        P.op("vector", lambda e, pm=pm, h=h, j=j: e.tensor_copy(out=Vs[h][:, j, 0:128], in_=pm[:, h * 128:(h + 1) * 128]),
                             reads=[pm], writes=[Vs[h]])
                    else:
                        P.op("scalar", lambda e, pm=pm, h=h, j=j: e.activation(out=Vs[h][:, j, 0:128], in_=pm[:, h * 128:(h + 1) * 128],
                                                                                func=AF.Copy), reads=[pm], writes=[Vs[h]])

        def new_v(alloc, name):
            v = alloc(name, [128, NT, 136], BF16)
            P.op("gpsimd", lambda e: e.memset(v[:, :, :], 1.0), writes=[v])
            return v

        sidx_box = [0]

        def attention_chunk(c, kparts, qparts, scale, Vp, post, bias=None, extra=None, bias_tiles=None):
            if True:
                for j in range(4 * c + 4):
                    tlo = max(4 * c, j)
                    t0 = tlo * 128
                    n = (4 * c + 4) * 128 - t0
                    sidx = sidx_box[0]
                    sidx_box[0] += 1
                    ps = PS[sidx % 2]
                    ptb = PTB[sidx % 3]
                    nparts = len(kparts)
                    for i in range(nparts):
                        kb, kp0, kp1 = kparts[i]
                        qb, qp0, qp1 = qparts[i]
                        P.op("tensor", lambda e, i=i, kb=kb, kp0=kp0, kp1=kp1, qb=qb, qp0=qp0, qp1=qp1, ps=ps, j=j, t0=t0, n=n: e.matmul(
                            ps[:, 0:n], lhsT=kb[kp0:kp1, j * 128:(j + 1) * 128], rhs=qb[qp0:qp1, t0:t0 + n],
                            start=(i == 0), stop=(i == nparts - 1 and extra is None)), reads=[kb, qb], writes=[ps])
                    if extra is not None:
                        P.op("tensor", lambda e, ps=ps, t0=t0, n=n: e.matmul(
                            ps[:, 0:n], lhsT=ONESF[0:1, 0:128], rhs=extra[0:1, t0:t0 + n], start=False, stop=True),
                            reads=[ONESF, extra], writes=[ps])
                    if bias_tiles is not None:
                        for ti in range(tlo, 4 * c + 4):
                            off = (ti - tlo) * 128
                            P.op("scalar", lambda e, ps=ps, ptb=ptb, off=off, j=j, ti=ti: e.activation(
                                out=ptb[:, off:off + 128], in_=ps[:, off:off + 128], func=AF.Exp, scale=scale,
                                bias=bias_tiles[:, j, ti:ti + 1]), reads=[ps, bias_tiles], writes=[ptb])
                    elif bias is None:
                        P.op("scalar", lambda e, ps=ps, ptb=ptb, n=n: e.activation(out=ptb[:, 0:n], in_=ps[:, 0:n], func=AF.Exp, scale=scale),
                             reads=[ps], writes=[ptb])
                    else:
                        P.op("scalar", lambda e, ps=ps, ptb=ptb, n=n, j=j: e.activation(out=ptb[:, 0:n], in_=ps[:, 0:n], func=AF.Exp,
                                                                                        scale=scale, bias=bias[:, j:j + 1]),
                             reads=[ps, bias], writes=[ptb])
                    if j >= 4 * c:
                        P.op("gpsimd", lambda e, ptb=ptb: e.tensor_tensor(out=ptb[:, 0:128], in0=ptb[:, 0:128], in1=MASK[:, :], op=ALU.mult),
                             reads=[ptb, MASK], writes=[ptb])
                    for ti in range(tlo, 4 * c + 4):
                        po = PO[ti - 4 * c]
                        off = (ti - tlo) * 128
                        P.op("tensor", lambda e, po=po, ptb=ptb, off=off, j=j, ti=ti: e.matmul(
                            po[:, 0:129], lhsT=ptb[:, off:off + 128], rhs=Vp[:, j, 0:129], start=(j == 0), stop=(j == ti)),
                            reads=[ptb, Vp], writes=[po])
                for ti in range(4 * c, 4 * c + 4):
                    post(ti, PO[ti - 4 * c])

        def attention(kparts, qparts, scale, Vp, post, bias=None, extra=None, bias_tiles=None):
            for c in range(4):
                attention_chunk(c, kparts, qparts, scale, Vp, post, bias=bias, extra=extra, bias_tiles=bias_tiles)

        def finish_tile(on_src_fn, ti, mixh, tr_slot):
            onb = ONB[cnt["onb"] % 2]
            cnt["onb"] += 1
            on_src_fn(onb)
            sl = slice(tr_slot * 128, (tr_slot + 1) * 128)
            P.op("tensor", lambda e, onb=onb, sl=sl: e.transpose(out=PTR[:, sl], in_=onb[:, :], identity=IDB[:, :]),
                 reads=[onb, IDB], writes=[PTR])
            P.op("vector", lambda e, sl=sl, ti=ti: e.tensor_copy(out=mixh[:, ti * 128:(ti + 1) * 128], in_=PTR[:, sl]),
                 reads=[PTR], writes=[mixh])

        def std_post(mixh):
            def post(ti, po):
                rl = RL[cnt["rl"] % 4]
                cnt["rl"] += 1
                P.op("vector", lambda e: e.reciprocal(out=rl[:, :], in_=po[:, 128:129]), reads=[po], writes=[rl])

                def w(onb):
                    P.op("vector", lambda e: e.tensor_scalar(out=onb[:, :], in0=po[:, 0:128], scalar1=rl[:, :], scalar2=None,
                                                              op0=ALU.mult), reads=[po, rl], writes=[onb])
                finish_tile(w, ti, mixh, ti % 8)
            return post

        mix_i = [0]

        def store_head(mixh, row0):
            P.dma(mixT[row0:row0 + 128, :], mixh[:, :], reads=[mixh], writes=[OUT])

        with ExitStack() as stD:
            def sbD(name, shape, dtp):
                return Buf(P, name, stD.enter_context(nc.sbuf_tensor(name, list(shape), dtp)))
            DL = sbD("DL", [128, 256], F32)
            DJ = sbD("DJ", [128, 64], F32)
            SL = sbD("SL", [128, 4], F32)
            NEGLAM = sbD("NEGLAM", [128, 1], F32)
            P.dma(DL[:, :], dl.partition_broadcast(128), writes=[DL])
            for i in range(2):
                P.op("vector", lambda e, i=i: e.tensor_tensor(
                    out=DJ[:, :], in0=DL[:, i * 128:i * 128 + 64], in1=DL[:, i * 128 + 64:i * 128 + 128], op=ALU.mult),
                    reads=[DL], writes=[DJ])
                P.op("scalar", lambda e, i=i: e.activation(out=DJ[:, :], in_=DJ[:, :], func=AF.Copy, accum_out=SL[:, i:i + 1]),
                     reads=[DJ], writes=[DJ, SL])
            P.op("scalar", lambda e: e.activation(out=SL[:, 2:4], in_=SL[:, 0:2], func=AF.Exp), reads=[SL], writes=[SL])
            P.op("vector", lambda e: e.tensor_tensor(out=NEGLAM[:, :], in0=SL[:, 3:4], in1=SL[:, 2:3], op=ALU.subtract),
                 reads=[SL], writes=[NEGLAM])
            P.op("vector", lambda e: e.tensor_tensor(out=NEGLAM[:, :], in0=NEGLAM[:, :], in1=LCB[:, 0:1], op=ALU.subtract),
                 reads=[NEGLAM, LCB], writes=[NEGLAM])

            QT = [sbD(f"QTd{h}", [128, S], BF16) for h in range(2)]
            KT = [sbD(f"KTd{h}", [128, S], BF16) for h in range(2)]
            VD = [new_v(sbD, f"VD{h}") for h in range(2)]
            O1N = [sbD(f"O1N{i}", [128, 128], F32) for i in range(4)]
            OD = [sbD(f"OD{i}", [128, 128], F32) for i in range(2)]
            if STOP == 301:
                P.dma(mixT[0:128, :], MIXH[0][:, :], reads=[MIXH[0]], writes=[OUT])
                P.finish([OUT])
                P.emit()
                return nc
            wq = load_w1(C_Q, 256)
            wqs = load_w1(C_QS, 256)
            def rope2(wm, ws, c0, dst):
                for r in range(4):
                    cs = slice(r * 512, (r + 1) * 512)
                    i = cnt["t"] % 2
                    cnt["t"] += 1
                    p1 = next_proj()
                    proj_fm(wm, c0, 128, HT, 16, r, p1)
                    P.op("vector", lambda e, p1=p1, i=i, cs=cs: e.tensor_tensor(out=T1[i][:, :], in0=p1[:, :], in1=CD[:, cs], op=ALU.mult),
                         reads=[p1, CD], writes=[T1[i]])
                    p2 = next_proj()
                    proj_fm(ws, c0, 128, HT, 16, r, p2)
                    P.op("vector", lambda e, p2=p2, i=i, cs=cs: e.tensor_tensor(out=T2[i][:, :], in0=p2[:, :], in1=SD[:, cs], op=ALU.mult),
                         reads=[p2, SD], writes=[T2[i]])
                    P.op("gpsimd", lambda e, i=i, cs=cs: e.tensor_tensor(out=dst[:, cs], in0=T1[i][:, :], in1=T2[i][:, :], op=ALU.add),
                         reads=[T1[i], T2[i]], writes=[dst])
            for h in range(2):
                rope2(wq, wqs, h * 128, QT[h])
            if STOP == 302:
                P.dma(mixT[0:128, :], MIXH[0][:, :], reads=[MIXH[0]], writes=[OUT])
                P.finish([OUT])
                P.emit()
                return nc
            wk = load_w1(C_K, 256)
            wks = load_w1(C_KS, 256)
            for h in range(2):
                rope2(wk, wks, h * 128, KT[h])
            if STOP == 303:
                P.dma(mixT[0:128, :], MIXH[0][:, :], reads=[MIXH[0]], writes=[OUT])
                P.finish([OUT])
                P.emit()
                return nc
            wv = load_w1(C_V, 256)
            v_tm(wv, 0, 2, HT, 16, VD)
            if STOP == 31:
                P.dma(mixT[0:128, :], MIXH[0][:, :], reads=[MIXH[0]], writes=[OUT])
                P.finish([OUT])
                P.emit()
                return nc

            for h in range(2):
                mixh = MIXH[mix_i[0] % 2]
                mix_i[0] += 1

                def post1(ti, po):
                    rl = RL[cnt["rl"] % 4]
                    cnt["rl"] += 1
                    P.op("vector", lambda e: e.reciprocal(out=rl[:, :], in_=po[:, 128:129]), reads=[po], writes=[rl])
                    o1 = O1N[ti % 4]
                    P.op("vector", lambda e: e.tensor_scalar(out=o1[:, :], in0=po[:, 0:128], scalar1=rl[:, :], scalar2=None, op0=ALU.mult),
                         reads=[po, rl], writes=[o1])

                def post2(ti, po, mixh=mixh):
                    rl = RL[cnt["rl"] % 4]
                    cnt["rl"] += 1
                    o1 = O1N[ti % 4]
                    od = OD[ti % 2]
                    P.op("vector", lambda e: e.reciprocal(out=rl[:, :], in_=po[:, 128:129]), reads=[po], writes=[rl])
                    P.op("vector", lambda e: e.tensor_scalar(out=od[:, :], in0=po[:, 0:128], scalar1=rl[:, :], scalar2=None, op0=ALU.mult),
                         reads=[po, rl], writes=[od])
                    P.op("vector", lambda e: e.scalar_tensor_tensor(out=od[:, :], in0=od[:, :], scalar=NEGLAM[:, 0:1], in1=o1[:, :],
                                                                     op0=ALU.mult, op1=ALU.add), reads=[od, NEGLAM, o1], writes=[od])
                    ssq = RL[cnt["rl"] % 4]
                    cnt["rl"] += 1
                    P.op("scalar", lambda e: e.activation(out=o1[:, :], in_=od[:, :], func=AF.Square, accum_out=ssq[:, :]),
                         reads=[od], writes=[o1, ssq])
                    P.op("scalar", lambda e: e.activation(out=ssq[:, :], in_=ssq[:, :], func=AF.Sqrt, scale=1.0 / 128, bias=EPS5[:, :]),
                         reads=[ssq, EPS5], writes=[ssq])
                    P.op("vector", lambda e: e.reciprocal(out=ssq[:, :], in_=ssq[:, :]), reads=[ssq], writes=[ssq])

                    def w(onb):
                        P.op("vector", lambda e: e.tensor_scalar(out=onb[:, :], in0=od[:, :], scalar1=ssq[:, :], scalar2=None, op0=ALU.mult),
                             reads=[od, ssq], writes=[onb])
                    finish_tile(w, ti, mixh, ti % 8)

                for c in range(4):
                    attention_chunk(c, [(KT[h], 0, 64)], [(QT[h], 0, 64)], 0.125, VD[h], post1)
                    if STOP == 32:
                        P.dma(mixT[0:128, :], MIXH[0][:, :], reads=[MIXH[0]], writes=[OUT])
                        P.finish([OUT])
                        P.emit()
                        return nc
                    attention_chunk(c, [(KT[h], 64, 128)], [(QT[h], 64, 128)], 0.125, VD[h], post2)
                store_head(mixh, h * 128)
            P.barrier()

        if STOP == 3:
            P.dma(mixT[0:128, :], MIXH[0][:, :], reads=[MIXH[0]], writes=[OUT])
            P.finish([OUT])
            P.emit()
            return nc
        with ExitStack() as stM:
            def sbM(name, shape, dtp):
                return Buf(P, name, stM.enter_context(nc.sbuf_tensor(name, list(shape), dtp)))
            QN = [sbM(f"QNm{h}", [128, S], BF16) for h in range(3)]
            QR = [sbM(f"QRm{h}", [128, S], BF16) for h in range(3)]
            KN = [sbM(f"KNm{h}", [128, S], BF16) for h in range(3)]
            KR = sbM("KRm", [128, S], BF16)
            VM = [new_v(sbM, f"VM{h}") for h in range(3)]
            RAW = sbM("RAW", [128, 4, 512], F32)
            SQ = sbM("SQ", [128, 4, 512], BF16)
            RSTD = sbM("RSTD", [128, 512], F32)

            def latent(col0, dst):
                wl = [load_w1(col0, 256), load_w1(col0 + 256, 256)]
                for r in range(4):
                    cs = slice(r * 512, (r + 1) * 512)
                    for rcn in range(4):
                        pm = next_proj()
                        proj_fm(wl[rcn // 2], (rcn % 2) * 128, 128, HT, 16, r, pm)
                        P.op("vector", lambda e, pm=pm, rcn=rcn: e.tensor_copy(out=RAW[:, rcn, :], in_=pm[:, :]), reads=[pm], writes=[RAW])
                        P.op("scalar", lambda e, pm=pm, rcn=rcn: e.activation(out=SQ[:, rcn, :], in_=pm[:, :], func=AF.Square),
                             reads=[pm], writes=[SQ])
                    pm = next_proj()
                    for rcn in range(4):
                        P.op("tensor", lambda e, rcn=rcn, pm=pm: e.matmul(pm[:, :], lhsT=ONESB[:, :], rhs=SQ[:, rcn, :],
                                                                         start=(rcn == 0), stop=(rcn == 3)),
                             reads=[ONESB, SQ], writes=[pm])
                    P.op("scalar", lambda e, pm=pm: e.activation(out=RSTD[:, :], in_=pm[:, :], func=AF.Sqrt, scale=1.0 / 512,
                                                                  bias=EPS6[:, :]), reads=[pm, EPS6], writes=[RSTD])
                    P.op("vector", lambda e: e.reciprocal(out=RSTD[:, :], in_=RSTD[:, :]), reads=[RSTD], writes=[RSTD])
                    for rcn in range(4):
                        P.op("gpsimd" if rcn % 2 else "vector", lambda e, rcn=rcn, cs=cs: e.tensor_tensor(
                            out=dst[:, rcn, cs], in0=RAW[:, rcn, :], in1=RSTD[:, :], op=ALU.mult),
                            reads=[RAW, RSTD], writes=[dst])

            with ExitStack() as stM1:
                CQN = Buf(P, "CQN", stM1.enter_context(nc.sbuf_tensor("CQN", [128, 4, S], BF16)))
                latent(C_CQ, CQN)
                for h in range(3):
                    wu = load_u(U1, h * 128, 128, GQ)
                    plain_fm(wu, 0, 128, CQN, 4, QN[h])
                for h in range(3):
                    wu = load_u(U1, 384 + h * 64, 64, GQ)
                    wus = load_u(U1, 576 + h * 64, 64, GQ)
                    for r in range(4):
                        cs = slice(r * 512, (r + 1) * 512)
                        i = cnt["t"] % 2
                        cnt["t"] += 1
                        p1 = next_proj()
                        proj_fm(wu, 0, 64, CQN, 4, r, p1)
                        P.op("vector", lambda e, p1=p1, i=i, cs=cs: e.tensor_tensor(out=T1[i][0:64, :], in0=p1[0:64, :], in1=CM[0:64, cs],
                                                                                     op=ALU.mult), reads=[p1, CM], writes=[T1[i]])
                        p2 = next_proj()
                        proj_fm(wus, 0, 64, CQN, 4, r, p2)
                        P.op("vector", lambda e, p2=p2, i=i, cs=cs: e.tensor_tensor(out=T2[i][0:64, :], in0=p2[0:64, :], in1=SM[0:64, cs],
                                                                                     op=ALU.mult), reads=[p2, SM], writes=[T2[i]])
                        P.op("gpsimd", lambda e, i=i, cs=cs, h=h: e.tensor_tensor(out=QR[h][0:64, cs], in0=T1[i][0:64, :], in1=T2[i][0:64, :],
                                                                                   op=ALU.add), reads=[T1[i], T2[i]], writes=[QR[h]])
                P.barrier()
            with ExitStack() as stM2:
                CKN = Buf(P, "CKN", stM2.enter_context(nc.sbuf_tensor("CKN", [128, 4, S], BF16)))
                latent(C_CKV, CKN)
                for h in range(3):
                    wu = load_u(U2, h * 128, 128, GKV)
                    plain_fm(wu, 0, 128, CKN, 4, KN[h])
                wv1 = load_u(U2, 384, 256, GKV)
                v_tm(wv1, 0, 2, CKN, 4, VM[0:2])
                wv2 = load_u(U2, 640, 128, GKV)
                v_tm(wv2, 0, 1, CKN, 4, VM[2:3])
                P.barrier()
            wkr = load_w1(C_KR, 128)
            for r in range(4):
                cs = slice(r * 512, (r + 1) * 512)
                i = cnt["t"] % 2
                cnt["t"] += 1
                p1 = next_proj()
                proj_fm(wkr, 0, 64, HT, 16, r, p1)
                P.op("vector", lambda e, p1=p1, i=i, cs=cs: e.tensor_tensor(out=T1[i][0:64, :], in0=p1[0:64, :], in1=CM[0:64, cs], op=ALU.mult),
                     reads=[p1, CM], writes=[T1[i]])
                p2 = next_proj()
                proj_fm(wkr, 64, 64, HT, 16, r, p2)
                P.op("vector", lambda e, p2=p2, i=i, cs=cs: e.tensor_tensor(out=T2[i][0:64, :], in0=p2[0:64, :], in1=SM[0:64, cs], op=ALU.mult),
                     reads=[p2, SM], writes=[T2[i]])
                P.op("gpsimd", lambda e, i=i, cs=cs: e.tensor_tensor(out=KR[0:64, cs], in0=T1[i][0:64, :], in1=T2[i][0:64, :], op=ALU.add),
                     reads=[T1[i], T2[i]], writes=[KR])
            for h in range(3):
                mixh = MIXH[mix_i[0] % 2]
                mix_i[0] += 1
                attention([(KN[h], 0, 128), (KR, 0, 64)], [(QN[h], 0, 128), (QR[h], 0, 64)], 192 ** -0.5, VM[h], std_post(mixh))
                store_head(mixh, 256 + h * 128)
            P.barrier()

        if STOP == 4:
            P.dma(mixT[0:128, :], MIXH[0][:, :], reads=[MIXH[0]], writes=[OUT])
            P.finish([OUT])
            P.emit()
            return nc
        with ExitStack() as stF:
            def sbF(name, shape, dtp):
                return Buf(P, name, stF.enter_context(nc.sbuf_tensor(name, list(shape), dtp)))
            QF = [sbF(f"QF{h}", [128, S], BF16) for h in range(3)]
            KF_ = [sbF(f"KF{h}", [128, S], BF16) for h in range(3)]
            VF = [new_v(sbF, f"VF{h}") for h in range(3)]
            NFB = sbF("NFB", [3, 1], F32)
            ONE3 = sbF("ONE3", [3, 1], F32)
            ONEROW = sbF("ONEROW", [3, S], F32)
            GL = sbF("GL", [3, S], F32)
            CL = sbF("CL", [3, S], F32)
            NBALL = sbF("NBALL", [128, NT * 3], F32)
            R1 = sbF("R1", [128, NT * 3], F32)
            NB3 = [sbF(f"NB3_{i}", [128, NT * 3], BF16) for i in range(3)]
            E0 = sbF("E0", [128, 128], BF16)
            CLB = sbF("CLB", [128, NT * 3], F32)
            BI = [sbF(f"BI{h}", [128, NT, NT], F32) for h in range(3)]
            P.dma(NFB[:, :], fb[0:3].rearrange("(p o) -> p o", o=1), writes=[NFB])
            P.op("vector", lambda e: e.tensor_scalar(out=NFB[:, :], in0=NFB[:, :], scalar1=-1.0, scalar2=None, op0=ALU.mult),
                 reads=[NFB], writes=[NFB])
            P.op("gpsimd", lambda e: e.memset(ONEROW[:, :], 1.0), writes=[ONEROW])
            P.op("gpsimd", lambda e: e.memset(ONE3[:, :], 1.0), writes=[ONE3])
            P.op("gpsimd", lambda e: e.memset(E0[:, :], 0.0), writes=[E0])
            P.op("gpsimd", lambda e: e.memset(E0[0:1, :], 1.0), writes=[E0])
            for (c0, dsts) in ((C_FQ, QF), (C_FK, KF_)):
                wa = load_w1(c0, 256)
                wb2 = load_w1(c0 + 256, 128)
                plain_fm(wa, 0, 128, HT, 16, dsts[0])
                plain_fm(wa, 128, 128, HT, 16, dsts[1])
                plain_fm(wb2, 0, 128, HT, 16, dsts[2])
            wv1 = load_w1(C_FV, 256)
            v_tm(wv1, 0, 2, HT, 16, VF[0:2])
            wv2 = load_w1(C_FV + 256, 128)
            v_tm(wv2, 0, 1, HT, 16, VF[2:3])
            wg = load_w1(C_FG, 3)
            for r in range(4):
                cs = slice(r * 512, (r + 1) * 512)
                pm = next_proj()
                proj_fm(wg, 0, 3, HT, 16, r, pm)
                P.op("scalar", lambda e, pm=pm, cs=cs: e.activation(out=GL[0:3, cs], in_=pm[0:3, :], func=AF.Exp, scale=-1.0,
                                                                     bias=NFB[0:3, 0:1]), reads=[pm, NFB], writes=[GL])
            P.op("scalar", lambda e: e.activation(out=GL[0:3, :], in_=GL[0:3, :], func=AF.Ln, bias=ONE3[0:3, 0:1]),
                 reads=[GL, ONE3], writes=[GL])
            P.op("vector", lambda e: e.tensor_tensor_scan(out=CL[0:3, :], data0=ONEROW[0:3, :], data1=GL[0:3, :], initial=0.0,
                                                           op0=ALU.mult, op1=ALU.add), reads=[ONEROW, GL], writes=[CL])
            pm = next_proj()
            for j in range(NT):
                P.op("tensor", lambda e, j=j, pm=pm: e.transpose(out=pm[:, j * 3:(j + 1) * 3], in_=CL[0:3, j * 128:(j + 1) * 128],
                                                                identity=IDF[0:3, 0:3]), reads=[CL, IDF], writes=[pm])
            P.op("vector", lambda e, pm=pm: e.tensor_copy(out=NBALL[:, :], in_=pm[:, 0:NT * 3]), reads=[pm], writes=[NBALL])
            P.op("vector", lambda e: e.tensor_copy(out=NB3[0][:, :], in_=NBALL[:, :]), reads=[NBALL], writes=[NB3[0]])
            P.op("vector", lambda e: e.tensor_tensor(out=R1[:, :], in0=NBALL[:, :], in1=NB3[0][:, :], op=ALU.subtract),
                 reads=[NBALL, NB3[0]], writes=[R1])
            P.op("vector", lambda e: e.tensor_copy(out=NB3[1][:, :], in_=R1[:, :]), reads=[R1], writes=[NB3[1]])
            P.op("vector", lambda e: e.tensor_tensor(out=R1[:, :], in0=R1[:, :], in1=NB3[1][:, :], op=ALU.subtract),
                 reads=[R1, NB3[1]], writes=[R1])
            P.op("vector", lambda e: e.tensor_copy(out=NB3[2][:, :], in_=R1[:, :]), reads=[R1], writes=[NB3[2]])
            pm = next_proj()
            for i in range(3):
                P.op("tensor", lambda e, i=i, pm=pm: e.matmul(pm[:, 0:NT * 3], lhsT=E0[:, :], rhs=NB3[i][:, :],
                                                             start=(i == 0), stop=(i == 2)), reads=[E0, NB3[i]], writes=[pm])
            P.op("vector", lambda e, pm=pm: e.tensor_copy(out=CLB[:, :], in_=pm[:, 0:NT * 3]), reads=[pm], writes=[CLB])
            for h in range(3):
                for j in range(NT):
                    P.op("vector", lambda e, h=h, j=j: e.tensor_scalar(
                        out=BI[h][:, j, :], in0=CLB[:, :].rearrange("p (t h) -> p t h", h=3)[:, :, h],
                        scalar1=NBALL[:, j * 3 + h:j * 3 + h + 1], scalar2=-1.0, op0=ALU.subtract, op1=ALU.mult),
                        reads=[CLB, NBALL], writes=[BI[h]])
            for h in range(3):
                mixh = MIXH[mix_i[0] % 2]
                mix_i[0] += 1
                attention([(KF_[h], 0, 128)], [(QF[h], 0, 128)], 128 ** -0.5, VF[h], std_post(mixh), bias_tiles=BI[h])
                store_head(mixh, 640 + h * 128)
            P.barrier()
        P.drain_all()


D = 2048
DFF = 5632
NTOK = 1024
NH = 2
TT = NTOK // 128
NG = 11
EPS = 1e-6


def body_k2(nc, P, io):
    x_main, x_halo, mix_main, mix_halo = io["x_main"], io["x_halo"], io["mix_main"], io["mix_halo"]
    w_o, w_up, conv_w, conv_b, w_down = io["w_o"], io["w_up"], io["conv_w"], io["conv_b"], io["w_down"]
    ffn_norm, dnorm, fnorm, lc, idf, idb = io["ffn_norm"], io["dnorm"], io["fnorm"], io["lc"], io["idf"], io["idb"]
    x_out, xn_out, fin = io["x_out"], io["xn_out"], io["fin"]
    with ExitStack() as st:
        P.stack = st
        X = [P.sb(f"X{t}", [128, D], F32) for t in range(TT)]
        XH = P.sb("XH", [NH, D], F32)
        IDF = P.sb("IDF", [128, 128], F32)
        IDB = P.sb("IDB", [128, 128], BF16)
        GUP = P.sb("GUP", [128, 16], F32)
        CW = P.sb("CW", [128, 4, 88], F32)
        DN = P.sb("DN", [128, 1], F32)
        EPSB = P.sb("EPSB", [128, 1], F32)
        ss = [P.sb(f"ss{i}", [128, 1], F32) for i in range(2)]
        sd = [P.sb(f"sd{i}", [128, 1], F32) for i in range(2)]
        rs = [P.sb(f"rs{i}", [128, 1], F32) for i in range(2)]
        STG = [P.sb(f"stg{i}", [128, 1024], F32) for i in range(4)]
        PA = P.ps("PA", [128, 1024])
        PG = P.ps("PG", [128, 1024])
        PHALO = P.ps("PHALO", [128, 512])
        ACC = [PA, PG]
        PM = [P.ps(f"PM{i}", [128, 512]) for i in range(2)]
        PT = P.ps("PT", [128, 1024], BF16)
        PH = [P.wrap("PH0", PHALO.t, lock=PHALO.lock), P.wrap("PH1", PT[:, :].bitcast(F32), lock=PT.lock)]

        OUTX = P.wrap("OUTX", x_out)
        OUTN = P.wrap("OUTN", xn_out)
        OUTF = P.wrap("OUTF", fin if fin is not None else x_out)

        stg_i = [0]

        def stage():
            b = STG[stg_i[0] % len(STG)]
            stg_i[0] += 1
            return b

        cast_i = [0]

        def cast(out_ap, in_ap, scale_ap, reads, writes):
            eng = "scalar" if cast_i[0] % 2 == 0 else "gpsimd"
            cast_i[0] += 1
            if eng == "scalar":
                if scale_ap is None:
                    P.op("scalar", lambda e: e.activation(out=out_ap, in_=in_ap, func=AF.Copy), reads=reads, writes=writes)
                else:
                    P.op("scalar", lambda e: e.activation(out=out_ap, in_=in_ap, func=AF.Copy, scale=scale_ap),
                         reads=reads, writes=writes)
            else:
                if scale_ap is None:
                    P.op("gpsimd", lambda e: e.tensor_copy(out=out_ap, in_=in_ap), reads=reads, writes=writes)
                else:
                    P.op("gpsimd", lambda e: e.tensor_scalar(out=out_ap, in0=in_ap, scalar1=scale_ap, scalar2=1.0,
                                                              op0=ALU.mult, op1=ALU.mult), reads=reads, writes=writes)

        P.dma(IDF[:, :], idf, writes=[IDF])
        P.dma(IDB[:, :], idb, writes=[IDB])
        P.op("gpsimd", lambda e: e.memset(EPSB[:, :], EPS), writes=[EPSB])
        s0 = stage()
        P.dma(s0[0:16, 0:128], ffn_norm.rearrange("(k p) -> k p", p=128), writes=[s0])
        P.op("tensor", lambda e: e.transpose(out=PM[0][:, 0:16], in_=s0[0:16, 0:128], identity=IDF[0:16, 0:16]),
             reads=[s0, IDF], writes=[PM[0]])
        P.op("vector", lambda e: e.tensor_copy(out=GUP[:, :], in_=PM[0][:, 0:16]), reads=[PM[0]], writes=[GUP])
        for j in range(4):
            s1 = stage()
            src = conv_w[j:j + 1, :].rearrange("o (c p) -> (o c) p", p=128) if j < 3 else conv_b.rearrange("(c p) -> c p", p=128)
            P.dma(s1[0:88, 0:128], src, writes=[s1])
            pm = PM[(j + 1) % 2]
            P.op("tensor", lambda e, s1=s1, pm=pm: e.transpose(out=pm[:, 0:88], in_=s1[0:88, 0:128], identity=IDF[0:88, 0:88]),
                 reads=[s1, IDF], writes=[pm])
            P.op("vector", lambda e, j=j, pm=pm: e.tensor_copy(out=CW[:, j, :], in_=pm[:, 0:88]), reads=[pm], writes=[CW])
        LCB = P.sb("LCB", [128, 4], F32)
        P.dma(LCB[:, :], lc.partition_broadcast(128), writes=[LCB])
        s2 = stage()
        P.dma(s2[:, 0:1], dnorm.rearrange("(p o) -> p o", o=1), writes=[s2])
        P.op("vector", lambda e: e.tensor_scalar(out=DN[:, :], in0=s2[:, 0:1], scalar1=LCB[:, 1:2], scalar2=None, op0=ALU.mult),
             reads=[s2, LCB], writes=[DN])

        if x_halo is None:
            P.op("gpsimd", lambda e: e.memset(XH[:, :], 0.0), writes=[XH])
        else:
            P.dma(XH[:, :], x_halo, writes=[XH])
        for t in range(TT):
            P.dma(X[t][:, :], x_main[t * 128:(t + 1) * 128, :], writes=[X[t]])

        with ExitStack() as stA:
            MT = []
            for k in range(16):
                t_ = stA.enter_context(nc.sbuf_tensor(f"MT{k}", [128, NH + NTOK], BF16))
                MT.append(Buf(P, f"MT{k}", t_))
            WOB = []
            for i in range(2):
                t_ = stA.enter_context(nc.sbuf_tensor(f"WOB{i}", [128, 16, 512], BF16))
                WOB.append(Buf(P, f"WOB{i}", t_))
            for k in range(16):
                if x_halo is None:
                    P.op("gpsimd", lambda e, k=k: e.memset(MT[k][:, 0:NH], 0.0), writes=[MT[k]])
                else:
                    P.dma(MT[k][:, 0:NH], mix_halo(k), writes=[MT[k]])
                P.dma(MT[k][:, NH:NH + NTOK], mix_main(k), writes=[MT[k]])

            def load_wo(nb):
                wb = WOB[nb % 2]
                for k in range(16):
                    s = stage()
                    P.dma(s[:, 0:512], w_o[k * 128:(k + 1) * 128, nb * 512:(nb + 1) * 512], writes=[s])
                    cast(wb[:, k, :], s[:, 0:512], DN[:, 0:1] if k < 4 else None, reads=[s, DN], writes=[wb])

            load_wo(0)
            pmi = 0
            for nb in range(4):
                if nb + 1 < 4:
                    load_wo(nb + 1)
                wb = WOB[nb % 2]
                cs = slice(nb * 512, (nb + 1) * 512)
                for tt in range(-1, TT):
                    pm = PM[pmi % 2]
                    pmi += 1
                    if tt < 0:
                        np_, c0, c1, xt = NH, 0, NH, XH
                    else:
                        np_, c0, c1, xt = 128, NH + tt * 128, NH + (tt + 1) * 128, X[tt]
                    for k in range(16):
                        P.op("tensor", lambda e, k=k, pm=pm, np_=np_, c0=c0, c1=c1, wb=wb: e.matmul(
                            pm[0:np_, :], lhsT=MT[k][:, c0:c1], rhs=wb[:, k, :], start=(k == 0), stop=(k == 15)),
                            reads=[MT[k], wb], writes=[pm])
                    P.op("vector", lambda e, pm=pm, np_=np_, xt=xt, cs=cs: e.tensor_tensor(
                        out=xt[0:np_, cs], in0=pm[0:np_, :], in1=xt[0:np_, cs], op=ALU.add),
                        reads=[pm, xt], writes=[xt])
            P.barrier()

        def norm_tile(xt, np_, i, sq_out):
            b = i % 2
            P.op("scalar", lambda e: e.activation(out=sq_out[0:np_, :], in_=xt[0:np_, :], func=AF.Square,
                                                    accum_out=ss[b][0:np_, :]), reads=[xt], writes=[sq_out, ss[b]])
            P.op("scalar", lambda e: e.activation(out=sd[b][0:np_, :], in_=ss[b][0:np_, :], func=AF.Sqrt,
                                                    scale=1.0 / D, bias=EPSB[0:np_, :]), reads=[ss[b], EPSB], writes=[sd[b]])
            P.op("vector", lambda e: e.reciprocal(out=rs[b][0:np_, :], in_=sd[b][0:np_, :]), reads=[sd[b]], writes=[rs[b]])
            return rs[b]

        with ExitStack() as stH:
            H2T = Buf(P, "H2T", stH.enter_context(nc.sbuf_tensor("H2T", [128, 16, NH + NTOK], BF16)))
            with ExitStack() as stN:
                xnb = [Buf(P, f"xnb{i}", stN.enter_context(nc.sbuf_tensor(f"xnb{i}", [128, D], BF16))) for i in range(2)]

                def norm_transpose(xt, np_, i, c0):
                    b = i % 2
                    r = norm_tile(xt, np_, i, xnb[b])
                    P.op("vector", lambda e: e.tensor_scalar(out=xnb[b][0:np_, :], in0=xt[0:np_, :], scalar1=r[0:np_, :],
                                                              scalar2=None, op0=ALU.mult), reads=[xt, r], writes=[xnb[b]])
                    for g in range(2):
                        for kk in range(8):
                            k = g * 8 + kk
                            P.op("tensor", lambda e, k=k, kk=kk: e.transpose(
                                out=PT[:, kk * 128: kk * 128 + np_], in_=xnb[b][0:np_, k * 128:(k + 1) * 128],
                                identity=IDB[0:np_, 0:np_]), reads=[xnb[b], IDB], writes=[PT])
                        src = PT[:, :].rearrange("p (k t) -> p k t", t=128)[:, :, 0:np_]
                        dst = H2T[:, g * 8:(g + 1) * 8, c0:c0 + np_]
                        if g == 0:
                            P.op("vector", lambda e, src=src, dst=dst: e.tensor_copy(out=dst, in_=src), reads=[PT], writes=[H2T])
                        else:
                            P.op("scalar", lambda e, src=src, dst=dst: e.activation(out=dst, in_=src, func=AF.Copy),
                                 reads=[PT], writes=[H2T])

                norm_transpose(XH, NH, 0, 0)
                for t in range(TT):
                    norm_transpose(X[t], 128, t + 1, NH + t * 128)
                P.barrier()

            with ExitStack() as stB:
                def sbB(name, shape, dtp):
                    return Buf(P, name, stB.enter_context(nc.sbuf_tensor(name, list(shape), dtp)))
                WUB = [sbB(f"WUB{i}", [128, 16, 512], BF16) for i in range(2)]
                WD = sbB("WD", [128, 4, 2048], BF16)
                ACTT = sbB("ACTT", [128, 4, NTOK], BF16)
                UA = [sbB(f"UA{i}", [128, NTOK], F32) for i in range(4)]
                UG = [sbB(f"UG{i}", [128, NTOK], F32) for i in range(2)]

                def load_up(wb, col0):
                    for k in range(16):
                        s = stage()
                        P.dma(s[:, 0:512], w_up[k * 128:(k + 1) * 128, col0:col0 + 512], writes=[s])
                        cast(wb[:, k, :], s[:, 0:512], GUP[:, k:k + 1], reads=[s, GUP], writes=[wb])

                def load_down(g):
                    for fc in range(4):
                        r0 = (g * 4 + fc) * 128
                        for hh in range(2):
                            s = stage()
                            P.dma(s[:, :], w_down[r0:r0 + 128, hh * 1024:(hh + 1) * 1024], writes=[s])
                            cast(WD[:, fc, hh * 1024:(hh + 1) * 1024], s[:, :], None, reads=[s], writes=[WD])

                def conv(pp, ph, hc, uc, c):
                    w0, w1, w2, bb = CW[:, 0, c:c + 1], CW[:, 1, c:c + 1], CW[:, 2, c:c + 1], CW[:, 3, c:c + 1]
                    P.op("scalar", lambda e: e.activation(out=uc[:, :], in_=pp[:, :], func=AF.Identity, scale=w2, bias=bb),
                         reads=[pp, CW], writes=[uc])
                    P.op("vector", lambda e: e.scalar_tensor_tensor(out=uc[:, 1:NTOK], in0=pp[:, 0:NTOK - 1], scalar=w1,
                                                                     in1=uc[:, 1:NTOK], op0=ALU.mult, op1=ALU.add),
                         reads=[pp, CW, uc], writes=[uc])
                    P.op("vector", lambda e: e.scalar_tensor_tensor(out=uc[:, 2:NTOK], in0=pp[:, 0:NTOK - 2], scalar=w0,
                                                                     in1=uc[:, 2:NTOK], op0=ALU.mult, op1=ALU.add),
                         reads=[pp, CW, uc], writes=[uc])
                    P.op("vector", lambda e: e.scalar_tensor_tensor(out=uc[:, 0:1], in0=ph[:, hc + 1:hc + 2], scalar=w1,
                                                                     in1=uc[:, 0:1], op0=ALU.mult, op1=ALU.add),
                         reads=[ph, CW, uc], writes=[uc])
                    P.op("vector", lambda e: e.scalar_tensor_tensor(out=uc[:, 0:2], in0=ph[:, hc:hc + 2], scalar=w0,
                                                                     in1=uc[:, 0:2], op0=ALU.mult, op1=ALU.add),
                         reads=[ph, CW, uc], writes=[uc])

                def up(wb, fc, pp, ph, hc):
                    for k in range(16):
                        lw = wb[:, k, fc * 128:(fc + 1) * 128]
                        P.op("tensor", lambda e, k=k, lw=lw: e.matmul(ph[:, hc:hc + 2], lhsT=lw, rhs=H2T[:, k, 0:NH],
                                                                       start=(k == 0), stop=(k == 15)),
                             reads=[wb, H2T], writes=[ph])
                        for h in range(2):
                            P.op("tensor", lambda e, k=k, lw=lw, h=h: e.matmul(
                                pp[:, h * 512:(h + 1) * 512], lhsT=lw, rhs=H2T[:, k, NH + h * 512: NH + (h + 1) * 512],
                                start=(k == 0), stop=(k == 15)), reads=[wb, H2T], writes=[pp])

                load_up(WUB[0], 0)
                load_up(WUB[1], DFF)
                load_down(0)
                pmi = 0
                ci = 0
                for g in range(NG):
                    for fc in range(4):
                        pp, ph, hc = ACC[ci % 2], PH[ci % 2], 0
                        ci += 1
                        up(WUB[0], fc, pp, ph, hc)
                        conv(pp, ph, hc, UA[fc], g * 4 + fc)
                    if g + 1 < NG:
                        load_up(WUB[0], (g + 1) * 512)
                    for fc in range(4):
                        pp, ph, hc = ACC[ci % 2], PH[ci % 2], 0
                        ci += 1
                        ug = UG[fc % 2]
                        up(WUB[1], fc, pp, ph, hc)
                        conv(pp, ph, hc, ug, 44 + g * 4 + fc)
                        P.op("scalar", lambda e, ug=ug: e.activation(out=ug[:, :], in_=ug[:, :], func=AF.Silu),
                             reads=[ug], writes=[ug])
                        P.op("gpsimd", lambda e, ug=ug, fc=fc: e.tensor_tensor(out=ACTT[:, fc, :], in0=ug[:, :], in1=UA[fc][:, :],
                                                                                 op=ALU.mult),
                             reads=[ug, UA[fc]], writes=[ACTT])
                    if g + 1 < NG:
                        load_up(WUB[1], DFF + (g + 1) * 512)
                    for tt in range(TT):
                        for nb in range(4):
                            pm = PM[pmi % 2]
                            pmi += 1
                            cs = slice(nb * 512, (nb + 1) * 512)
                            for fc in range(4):
                                P.op("tensor", lambda e, fc=fc, tt=tt, cs=cs, pm=pm: e.matmul(
                                    pm[:, :], lhsT=ACTT[:, fc, tt * 128:(tt + 1) * 128], rhs=WD[:, fc, cs],
                                    start=(fc == 0), stop=(fc == 3)), reads=[ACTT, WD], writes=[pm])
                            P.op("vector", lambda e, tt=tt, cs=cs, pm=pm: e.tensor_tensor(
                                out=X[tt][:, cs], in0=pm[:, :], in1=X[tt][:, cs], op=ALU.add),
                                reads=[pm, X[tt]], writes=[X[tt]])
                    if g + 1 < NG:
                        load_down(g + 1)
                P.barrier()

        with ExitStack() as stC:
            def sbC(name, shape, dtp):
                return Buf(P, name, stC.enter_context(nc.sbuf_tensor(name, list(shape), dtp)))
            FO = [sbC(f"FO{i}", [128, D], F32) for i in range(2)]
            xnc = [sbC(f"xnc{i}", [128, D], BF16) for i in range(2)]
            FG = sbC("FG", [128, D], F32)
            P.dma(FG[:, :], fnorm.partition_broadcast(128), writes=[FG])
            for t in range(TT):
                b = (t + 1) % 2
                P.dma(x_out[t * 128:(t + 1) * 128, :], X[t][:, :], reads=[X[t]], writes=[OUTX])
                r = norm_tile(X[t], 128, t + 1, xnc[b])
                P.op("gpsimd", lambda e, t=t, b=b, r=r: e.tensor_scalar(out=xnc[b][:, :], in0=X[t][:, :], scalar1=r[:, :],
                                                                         scalar2=1.0, op0=ALU.mult, op1=ALU.mult),
                     reads=[X[t], r], writes=[xnc[b]])
                P.dma(xn_out[t * 128:(t + 1) * 128, :], xnc[b][:, :], reads=[xnc[b]], writes=[OUTN])
                if fin is not None:
                    P.op("vector", lambda e, t=t, b=b, r=r: e.scalar_tensor_tensor(out=FO[b][:, :], in0=X[t][:, :], scalar=r[:, :],
                                                                                   in1=FG[:, :], op0=ALU.mult, op1=ALU.mult),
                         reads=[X[t], r, FG], writes=[FO[b]])
                    P.dma(fin[t * 128:(t + 1) * 128, :], FO[b][:, :], reads=[FO[b]], writes=[OUTF])
            P.drain_all()


bf16 = ml_dtypes.bfloat16
bf16 = ml_dtypes.bfloat16

ROPE_THETA = 500000.0

def swap_perm_diff():
    p = np.arange(128)
    for base in (0, 64):
        for i in range(8):
            p[base + i] = base + i + 8
            p[base + i + 8] = base + i
    return p

def swap_perm_rope64():
    p = np.arange(64)
    p[:32] = np.arange(32, 64)
    p[32:] = np.arange(0, 32)
    return p

def rope_consts():
    rc = np.zeros((128, 8), np.float32)
    invd = (ROPE_THETA ** (-np.arange(0, 16, 2, dtype=np.float32) / np.float32(16))).astype(np.float32)
    invm = (ROPE_THETA ** (-np.arange(0, 64, 2, dtype=np.float32) / np.float32(64))).astype(np.float32)
    for p in range(128):
        q = p % 64
        if q < 16:
            rc[p, 0] = invd[q % 8]
            rc[p, 1] = 1.0
            rc[p, 2] = -1.0 if q < 8 else 1.0
        else:
            rc[p, 0] = 0.0
            rc[p, 1] = 0.0
            rc[p, 2] = 0.0
        rc[p, 5] = 1.0 - rc[p, 1]
        rc[p, 3] = invm[q % 32]
        rc[p, 4] = -1.0 if q < 32 else 1.0
        rc[p, 6] = 1.0
        rc[p, 7] = 0.0
    return rc

def pack_k1(l, r, w):
    win = w["w_in"][l]
    offs = np.cumsum([0, 512, 512, 512, 512, 512, 64, 768, 768, 768, 6])
    aq, ak, av, mcq, mckv, mkr, fq, fk, fv, fg = [win[:, offs[i]:offs[i + 1]] for i in range(10)]
    pd = swap_perm_diff()
    pr = swap_perm_rope64()
    cols = []
    q = aq[:, r * 256:(r + 1) * 256]
    k = ak[:, r * 256:(r + 1) * 256]
    def swp(m):
        return np.concatenate([m[:, h * 128:(h + 1) * 128][:, pd] for h in range(2)], 1)
    cols += [q, swp(q), k, swp(k), av[:, r * 256:(r + 1) * 256], mcq, mckv, mkr, mkr[:, pr],
             fq[:, r * 384:(r + 1) * 384], fk[:, r * 384:(r + 1) * 384], fv[:, r * 384:(r + 1) * 384], fg[:, r * 3:(r + 1) * 3]]
    W1 = np.ascontiguousarray(np.concatenate(cols, 1))
    uq = w["mla_w_uq"][l]
    ukv = w["mla_w_ukv"][l]
    hs = [3 * r + i for i in range(3)]
    U1 = np.concatenate([uq[:, h * 192:h * 192 + 128] for h in hs] + [uq[:, h * 192 + 128:h * 192 + 192] for h in hs]
                        + [uq[:, h * 192 + 128:h * 192 + 192][:, pr] for h in hs], 1)
    U2 = np.concatenate([ukv[:, h * 256:h * 256 + 128] for h in hs] + [ukv[:, h * 256 + 128:h * 256 + 256] for h in hs], 1)
    fb = np.zeros(4, np.float32)
    fb[:3] = w["fox_forget_bias"][l][3 * r:3 * r + 3]
    lam_init = 0.8 - 0.6 * math.exp(-0.3 * l)
    return {
        "W1": W1, "U1": np.ascontiguousarray(U1), "U2": np.ascontiguousarray(U2),
        "attn_norm": np.ascontiguousarray(w["attn_norm"][l]), "qn": np.ascontiguousarray(w["mla_q_norm"][l]),
        "kvn": np.ascontiguousarray(w["mla_kv_norm"][l]), "fb": fb,
        "dl": np.ascontiguousarray(w["diff_lambda"][l].reshape(-1)),
        "lc": np.array([lam_init, 1.0 - lam_init, 0, 0], np.float32),
        "idf": np.eye(128, dtype=np.float32), "idb": np.eye(128, dtype=np.float32).astype(bf16),
        "mask": np.triu(np.ones((128, 128), np.float32)).astype(bf16),
        "rc": rope_consts(),
    }


class _NCP:
    def __init__(self, nc, prefix):
        self._nc = nc
        self._p = prefix

    def sbuf_tensor(self, name, *a, **k):
        return self._nc.sbuf_tensor(self._p + name, *a, **k)

    def psum_tensor(self, name, *a, **k):
        return self._nc.psum_tensor(self._p + name, *a, **k)

    def __getattr__(self, n):
        return getattr(self._nc, n)


def body_k0(nc, P, x_ap, xn_ap, ntiles):
    with ExitStack() as st:
        P.stack = st
        xt = [P.sb(f"xt{i}", [128, D], F32) for i in range(2)]
        ot = [P.sb(f"ot{i}", [128, D], BF16) for i in range(2)]
        ss = [P.sb(f"ss{i}", [128, 1], F32) for i in range(2)]
        sd = [P.sb(f"sd{i}", [128, 1], F32) for i in range(2)]
        rs = [P.sb(f"rs{i}", [128, 1], F32) for i in range(2)]
        eps = P.sb("eps", [128, 1], F32)
        P.op("gpsimd", lambda e: e.memset(eps[:, :], 1e-6), writes=[eps])
        outd = P.wrap("outd", xn_ap)
        for i in range(ntiles):
            b = i % 2
            P.dma(xt[b][:, :], x_ap[i * 128:(i + 1) * 128, :], writes=[xt[b]])
            P.op("scalar", lambda e, b=b: e.activation(out=ot[b][:, :], in_=xt[b][:, :], func=AF.Square,
                                                         accum_out=ss[b][:, :]), reads=[xt[b]], writes=[ot[b], ss[b]])
            P.op("scalar", lambda e, b=b: e.activation(out=sd[b][:, :], in_=ss[b][:, :], func=AF.Sqrt,
                                                         scale=1.0 / D, bias=eps[:, :]), reads=[ss[b], eps], writes=[sd[b]])
            P.op("vector", lambda e, b=b: e.reciprocal(out=rs[b][:, :], in_=sd[b][:, :]), reads=[sd[b]], writes=[rs[b]])
            P.op("vector", lambda e, b=b: e.tensor_scalar(out=ot[b][:, :], in0=xt[b][:, :], scalar1=rs[b][:, :],
                                                            scalar2=None, op0=ALU.mult), reads=[xt[b], rs[b]], writes=[ot[b]])
            P.dma(xn_ap[i * 128:(i + 1) * 128, :], ot[b][:, :], reads=[ot[b]], writes=[outd])
        P.drain_all()


def _mix_loc(k):
    if k < 4:
        return k // 2, k % 2
    if k < 10:
        return (k - 4) // 3, 2 + (k - 4) % 3
    return (k - 10) // 3, 5 + (k - 10) % 3


def build_fused(depth=4):
    nc0 = bass.Bass("TRN2", target_bir_lowering=False)
    dt = nc0.dram_tensor

    def ext(name, shape, dtype):
        return dt(name, list(shape), dtype, kind="ExternalInput").ap()

    L = depth
    x = ext("x", [S, D], F32)
    pos = ext("pos", [S], I32)
    W1 = ext("W1", [L, 2, D, NC1], F32)
    U1 = ext("U1", [L, 2, 512, 768], F32)
    U2 = ext("U2", [L, 2, 512, 768], F32)
    attn_norm = ext("attn_norm", [L, D], F32)
    qn = ext("qn", [L, 512], F32)
    kvn = ext("kvn", [L, 512], F32)
    fb = ext("fb", [L, 2, 4], F32)
    dl = ext("dl", [L, 256], F32)
    lc = ext("lc", [L, 4], F32)
    w_o = ext("w_o", [L, D, D], F32)
    w_up = ext("w_up", [L, D, 2 * DFF], F32)
    conv_w = ext("conv_w", [L, 3, 2 * DFF], F32)
    conv_b = ext("conv_b", [L, 2 * DFF], F32)
    w_down = ext("w_down", [L, DFF, D], F32)
    ffn_norm = ext("ffn_norm", [L, D], F32)
    dnorm = ext("dnorm", [L, 128], F32)
    fnorm = ext("fnorm", [D], F32)
    idf = ext("idf", [128, 128], F32)
    idb = ext("idb", [128, 128], BF16)
    mask = ext("mask", [128, 128], BF16)
    rc = ext("rc", [128, 8], F32)
    out = dt("out", [S, D], F32, kind="ExternalOutput").ap()
    XS = [dt(f"xs_scr{i}", [S, D], F32).ap() for i in range(2)]
    XN = [dt(f"xn_scr{i}", [S, D], BF16).ap() for i in range(2)]
    MIX = [dt(f"mix_scr{r}", [1024, S], BF16).ap() for r in range(2)]

    with ExitStack() as st:
        P = Prog(nc0, st)
        cnt = [0]

        def scoped():
            cnt[0] += 1
            P.nc = _NCP(nc0, f"b{cnt[0]}_")
            return P.nc

        body_k0(scoped(), P, x, XN[0], 16)
        for l in range(L):
            x_src = x if l == 0 else XS[l % 2]
            x_dst = XS[(l + 1) % 2]
            xn_src = XN[l % 2]
            xn_dst = XN[(l + 1) % 2]
            for r in range(2):
                body_k1(scoped(), P, {
                    "xn": xn_src, "pos": pos, "W1": W1[l, r], "U1": U1[l, r], "U2": U2[l, r],
                    "attn_norm": attn_norm[l], "qn": qn[l], "kvn": kvn[l], "fb": fb[l, r], "dl": dl[l], "lc": lc[l],
                    "idf": idf, "idb": idb, "mask": mask, "rc": rc, "mixT": MIX[r]})
            for hf in range(2):
                t0 = hf * 1024

                def mix_main(k, t0=t0):
                    r_, c_ = _mix_loc(k)
                    return MIX[r_][c_ * 128:(c_ + 1) * 128, t0:t0 + 1024]

                def mix_halo(k, t0=t0):
                    r_, c_ = _mix_loc(k)
                    return MIX[r_][c_ * 128:(c_ + 1) * 128, t0 - 2:t0]

                body_k2(scoped(), P, {
                    "x_main": x_src[t0:t0 + 1024, :], "x_halo": None if hf == 0 else x_src[t0 - 2:t0, :],
                    "mix_main": mix_main, "mix_halo": mix_halo,
                    "w_o": w_o[l], "w_up": w_up[l], "conv_w": conv_w[l], "conv_b": conv_b[l], "w_down": w_down[l],
                    "ffn_norm": ffn_norm[l], "dnorm": dnorm[l], "fnorm": fnorm, "lc": lc[l], "idf": idf, "idb": idb,
                    "x_out": x_dst[t0:t0 + 1024, :], "xn_out": xn_dst[t0:t0 + 1024, :],
                    "fin": out[t0:t0 + 1024, :] if l == L - 1 else None})
        P.nc = nc0
        P.stack = st
        OUTB = P.wrap("OUTB", out)
        P.finish([OUTB])
        P.emit()
    return nc0


from concourse.bass_utils import run_bass_kernel_spmd

_PROG = {}


def kernel(**inputs):
    w = {k: np.asarray(v) for k, v in inputs.items()}
    x = np.ascontiguousarray(w["x"], dtype=np.float32)
    pos = np.ascontiguousarray(w["positions"]).astype(np.int32)
    L = w["w_in"].shape[0]
    if "f" not in _PROG:
        _PROG["f"] = build_fused(L)
    packs = [[pack_k1(l, r, w) for r in range(2)] for l in range(L)]

    def st2(key):
        return np.ascontiguousarray(np.stack([np.stack([packs[l][r][key] for r in range(2)], 0) for l in range(L)], 0))

    def st1(key):
        return np.ascontiguousarray(np.stack([packs[l][0][key] for l in range(L)], 0))

    shared = {
        "W1": st2("W1"), "U1": st2("U1"), "U2": st2("U2"), "fb": st2("fb"),
        "attn_norm": st1("attn_norm"), "qn": st1("qn"), "kvn": st1("kvn"), "dl": st1("dl"), "lc": st1("lc"),
        "w_o": np.ascontiguousarray(w["w_o"]), "w_up": np.ascontiguousarray(w["ffn_w_up"]),
        "conv_w": np.ascontiguousarray(w["ffn_conv_w"]), "conv_b": np.ascontiguousarray(w["ffn_conv_b"]),
        "w_down": np.ascontiguousarray(w["ffn_w_down"]), "ffn_norm": np.ascontiguousarray(w["ffn_norm"]),
        "dnorm": np.ascontiguousarray(w["diff_out_norm"]), "fnorm": np.ascontiguousarray(w["final_norm"]),
        "idf": packs[0][0]["idf"], "idb": packs[0][0]["idb"], "mask": packs[0][0]["mask"], "rc": packs[0][0]["rc"],
    }
    cores = list(range(8))
    ins = []
    for c in cores:
        d = dict(shared)
        d["x"] = np.ascontiguousarray(x[c // 2])
        d["pos"] = np.ascontiguousarray(pos[c // 2])
        ins.append(d)
    res = run_bass_kernel_spmd(_PROG["f"], ins, core_ids=cores)
    outs = [np.asarray(res.results[2 * b]["out"]) for b in range(4)]
    return np.stack(outs, 0).astype(np.float32)
```

```python
import math
import ml_dtypes
from contextlib import ExitStack
import numpy as np
import concourse.bass as bass
import concourse.mybir as mybir

F32 = mybir.dt.float32
BF16 = mybir.dt.bfloat16
I32 = mybir.dt.int32
AF = mybir.ActivationFunctionType
ALU = mybir.AluOpType
AX = mybir.AxisListType

ENGS = ("sync", "scalar", "vector", "gpsimd", "tensor")
EPOCH = 20000


class Buf:
    def __init__(self, prog, name, t):
        self.prog = prog
        self.name = name
        self.t = t
        self.writes = {}
        self.reads = {}
        self.dsem = None
        self.dcount = 0
        self.lock = None

    def __getitem__(self, idx):
        return self.t[idx]


class Prog:
    def __init__(self, nc, stack, n_eng_sems=6):
        self.nc = nc
        self.stack = stack
        self.sem_stack = stack
        self.free_dsems = []
        self.live_dbufs = []
        self.ops = {e: [] for e in ENGS}
        self.semtab = []
        self.eng_sems = {}
        self.eng_epoch = {e: 0 for e in ENGS}
        self.eng_cnt = {e: 0 for e in ENGS}
        self.waited = {e: {} for e in ENGS}
        for e in ENGS:
            self.eng_sems[e] = [self._new_sem(f"s_{e}_{i}") for i in range(n_eng_sems)]
        self.dma_sems = []
        self.nbuf = 0

    def _new_sem(self, name):
        h = self.sem_stack.enter_context(self.nc.semaphore(name))
        self.semtab.append(h)
        return len(self.semtab) - 1

    def _dsem_for(self, owner):
        if owner.dsem is None:
            if self.free_dsems:
                owner.dsem, owner.dcount = self.free_dsems.pop()
            else:
                owner.dsem = self._new_sem(f"d{len(self.semtab)}")
                owner.dcount = 0
            self.live_dbufs.append(owner)

    def sb(self, name, shape, dtype):
        t = self.stack.enter_context(self.nc.sbuf_tensor(name, list(shape), dtype))
        return Buf(self, name, t)

    def ps(self, name, shape, dtype=F32):
        t = self.stack.enter_context(self.nc.psum_tensor(name, list(shape), dtype))
        b = Buf(self, name, t)
        b.lock = Buf(self, name + "_lock", None)
        return b

    def wrap(self, name, t, lock=None):
        b = Buf(self, name, t)
        b.lock = lock
        return b

    def _locks(self, reads, writes):
        ls = []
        for b in list(reads) + list(writes):
            if b.lock is not None and b.lock not in ls:
                ls.append(b.lock)
        return ls

    def _need(self, eng, reads, writes):
        need = {}
        for b in reads:
            for s, v in b.writes.items():
                if need.get(s, 0) < v:
                    need[s] = v
        for b in list(writes) + self._locks(reads, writes):
            for d in (b.writes, b.reads):
                for s, v in d.items():
                    if need.get(s, 0) < v:
                        need[s] = v
        if eng == "tensor":
            own = set(self.eng_sems["tensor"])
            need = {s: v for s, v in need.items() if s not in own}
        out = []
        w = self.waited[eng]
        for s, v in need.items():
            if w.get(s, 0) < v:
                w[s] = v
                out.append((s, v))
        return out

    def _emit_waits(self, eng, waits):
        for s, v in waits:
            h = self.semtab[s]
            self.ops[eng].append(lambda e, h=h, v=v: e.wait_ge(h, v))

    def _next_event(self, eng):
        if self.eng_cnt[eng] >= EPOCH:
            self.eng_epoch[eng] += 1
            self.eng_cnt[eng] = 0
        self.eng_cnt[eng] += 1
        s = self.eng_sems[eng][self.eng_epoch[eng]]
        return s, self.eng_cnt[eng]

    def op(self, eng, fn, reads=(), writes=()):
        waits = self._need(eng, reads, writes)
        self._emit_waits(eng, waits)
        s, v = self._next_event(eng)
        h = self.semtab[s]
        self.ops[eng].append(lambda e, fn=fn, h=h: fn(e).then_inc(h, 1))
        for b in list(writes) + self._locks(reads, writes):
            b.writes = {s: v}
            b.reads = {}
        for b in reads:
            if b in writes:
                continue
            b.reads[s] = max(b.reads.get(s, 0), v)
        return (s, v)

    def dma(self, out_ap, in_ap, reads=(), writes=(), q="sync", **kw):
        waits = self._need(q, reads, writes)
        self._emit_waits(q, waits)
        owner = (list(writes) + list(reads))[0]
        self._dsem_for(owner)
        owner.dcount += 16
        s, v = owner.dsem, owner.dcount
        h = self.semtab[s]
        self.ops[q].append(
            lambda e, o=out_ap, i=in_ap, h=h, kw=kw: e.dma_start(out=o, in_=i, **kw).then_inc(h, 16))
        for b in writes:
            b.writes = {s: v}
            b.reads = {}
        for b in reads:
            if b in writes:
                continue
            b.reads[s] = max(b.reads.get(s, 0), v)
        return (s, v)

    def dma_like(self, q, fn, reads=(), writes=(), inc=16):
        waits = self._need(q, reads, writes)
        self._emit_waits(q, waits)
        owner = (list(writes) + list(reads))[0]
        self._dsem_for(owner)
        owner.dcount += inc
        s, v = owner.dsem, owner.dcount
        h = self.semtab[s]
        self.ops[q].append(lambda e, fn=fn, h=h: fn(e).then_inc(h, inc))
        for b in writes:
            b.writes = {s: v}
            b.reads = {}
        for b in reads:
            if b in writes:
                continue
            b.reads[s] = max(b.reads.get(s, 0), v)
        return (s, v)

    def barrier(self):
        ev = {}
        for e in ENGS:
            if self.eng_cnt[e] > 0:
                ev[self.eng_sems[e][self.eng_epoch[e]]] = self.eng_cnt[e]
        for e in ENGS:
            for s, v in ev.items():
                if self.waited[e].get(s, 0) < v:
                    self.waited[e][s] = v
                    h = self.semtab[s]
                    self.ops[e].append(lambda en, h=h, v=v: en.wait_ge(h, v))

    def drain_all(self):
        ev = {}
        for e in ENGS:
            if self.eng_cnt[e] > 0:
                ev[self.eng_sems[e][self.eng_epoch[e]]] = self.eng_cnt[e]
        for b in self.live_dbufs:
            ev[b.dsem] = max(ev.get(b.dsem, 0), b.dcount)
        for e in ENGS:
            for s, v in ev.items():
                if self.waited[e].get(s, 0) < v:
                    self.waited[e][s] = v
                    h = self.semtab[s]
                    self.ops[e].append(lambda en, h=h, v=v: en.wait_ge(h, v))
        for b in self.live_dbufs:
            self.free_dsems.append((b.dsem, b.dcount))
            b.dsem = None
            b.dcount = 0
            b.writes = {}
            b.reads = {}
        self.live_dbufs = []

    def finish(self, bufs):
        need = {}
        for b in bufs:
            for d in (b.writes, b.reads):
                for s, v in d.items():
                    need[s] = max(need.get(s, 0), v)
        for e in ENGS:
            if e != "sync" and self.eng_cnt[e] > 0:
                s = self.eng_sems[e][self.eng_epoch[e]]
                need[s] = max(need.get(s, 0), self.eng_cnt[e])
        for s, v in need.items():
            h = self.semtab[s]
            self.ops["sync"].append(lambda en, h=h, v=v: en.wait_ge(h, v))

    def emit(self):
        nc = self.nc
        with nc.Block() as block:
            @block.sync
            def _(e):
                for f in self.ops["sync"]:
                    f(e)

            @block.scalar
            def _(e):
                for f in self.ops["scalar"]:
                    f(e)

            @block.vector
            def _(e):
                for f in self.ops["vector"]:
                    f(e)

            @block.gpsimd
            def _(e):
                for f in self.ops["gpsimd"]:
                    f(e)

            @block.tensor
            def _(e):
                for f in self.ops["tensor"]:
                    f(e)


D = 2048
S = 2048
NT = 16
NC1 = 3587
C_Q, C_QS, C_K, C_KS, C_V, C_CQ, C_CKV, C_KR, C_KRS, C_FQ, C_FK, C_FV, C_FG = (
    0, 256, 512, 768, 1024, 1280, 1792, 2304, 2368, 2432, 2816, 3200, 3584)
TWO_PI = 2.0 * math.pi


def body_k1(nc, P, io):
    STOP = 99
    xn, pos, W1, U1, U2 = io["xn"], io["pos"], io["W1"], io["U1"], io["U2"]
    attn_norm, qn, kvn, fb, dl, lc = io["attn_norm"], io["qn"], io["kvn"], io["fb"], io["dl"], io["lc"]
    idf, idb, maskd, rc, mixT = io["idf"], io["idb"], io["mask"], io["rc"], io["mixT"]
    tabs, tabs_mode = io.get("tabs"), io.get("tabs_mode", "compute")
    with ExitStack() as st:
        P.stack = st
        HT = P.sb("HT", [128, 16, S], BF16)
        IDF = P.sb("IDF", [128, 128], F32)
        IDB = P.sb("IDB", [128, 128], BF16)
        MASK = P.sb("MASK", [128, 128], BF16)
        RC = P.sb("RC", [128, 8], F32)
        GIN = P.sb("GIN", [128, 16], F32)
        GQ = P.sb("GQ", [128, 4], F32)
        GKV = P.sb("GKV", [128, 4], F32)
        LCB = P.sb("LCB", [128, 4], F32)
        EPS6 = P.sb("EPS6", [128, 1], F32)
        EPS5 = P.sb("EPS5", [128, 1], F32)
        CD = P.sb("CD", [128, S], BF16)
        SD = P.sb("SD", [128, S], BF16)
        CM = P.sb("CM", [128, S], BF16)
        SM = P.sb("SM", [128, S], BF16)
        WB = [P.sb(f"WB{i}", [128, 16, 256], BF16) for i in range(4)]
        STG = [P.sb(f"stg{i}", [128, 512], F32) for i in range(3)]
        PTB = [P.sb(f"PTB{i}", [128, 512], BF16) for i in range(3)]
        ONB = [P.sb(f"ONB{i}", [128, 128], BF16) for i in range(2)]
        _mixh = P.sb("MIXH0", [128, S], BF16)
        MIXH = [_mixh, _mixh]
        RL = [P.sb(f"RL{i}", [128, 1], F32) for i in range(4)]
        ONESB = P.sb("ONESB", [128, 128], BF16)
        ONESF = P.sb("ONESF", [1, 128], F32)
        _t1 = P.sb("T1_0", [128, 512], F32)
        _t2 = P.sb("T2_0", [128, 512], F32)
        T1 = [_t1, _t1]
        T2 = [_t2, _t2]

        PS = [P.ps(f"PS{i}", [128, 512]) for i in range(2)]
        PO = [P.ps(f"PO{i}", [128, 512]) for i in range(4)]
        PM = P.ps("PM", [128, 512])
        PTR = P.ps("PTR", [128, 1024], BF16)
        PROJ = [PM, PS[0], PS[1]]
        OUT = P.wrap("OUT", mixT)

        cnt = {"stg": 0, "cast": 0, "proj": 0, "wb": 0, "ptb": 0, "onb": 0, "rl": 0, "t": 0}

        def stage():
            b = STG[cnt["stg"] % len(STG)]
            cnt["stg"] += 1
            return b

        def cast(out_ap, in_ap, scale_ap, reads, writes):
            eng = "scalar" if cnt["cast"] % 2 == 0 else "gpsimd"
            cnt["cast"] += 1
            if eng == "scalar":
                if scale_ap is None:
                    P.op("scalar", lambda e: e.activation(out=out_ap, in_=in_ap, func=AF.Copy), reads=reads, writes=writes)
                else:
                    P.op("scalar", lambda e: e.activation(out=out_ap, in_=in_ap, func=AF.Copy, scale=scale_ap),
                         reads=reads, writes=writes)
            else:
                if scale_ap is None:
                    P.op("gpsimd", lambda e: e.tensor_copy(out=out_ap, in_=in_ap), reads=reads, writes=writes)
                else:
                    P.op("gpsimd", lambda e: e.tensor_scalar(out=out_ap, in0=in_ap, scalar1=scale_ap, scalar2=1.0,
                                                              op0=ALU.mult, op1=ALU.mult), reads=reads, writes=writes)

        def next_proj():
            b = PROJ[cnt["proj"] % 3]
            cnt["proj"] += 1
            return b

        def vecT(dst, src_ap, n):
            s = stage()
            pm = next_proj()
            P.dma(s[0:n, 0:128], src_ap.rearrange("(k p) -> k p", p=128), writes=[s])
            P.op("tensor", lambda e: e.transpose(out=pm[:, 0:n], in_=s[0:n, 0:128], identity=IDF[0:n, 0:n]),
                 reads=[s, IDF], writes=[pm])
            P.op("vector", lambda e: e.tensor_copy(out=dst[:, 0:n], in_=pm[:, 0:n]), reads=[pm], writes=[dst])

        P.dma(IDF[:, :], idf, writes=[IDF])
        P.dma(IDB[:, :], idb, writes=[IDB])
        P.dma(MASK[:, :], maskd, writes=[MASK])
        P.dma(RC[:, :], rc, writes=[RC])
        P.dma(LCB[:, :], lc.partition_broadcast(128), writes=[LCB])
        P.op("gpsimd", lambda e: e.memset(EPS6[:, :], 1e-6), writes=[EPS6])
        P.op("gpsimd", lambda e: e.memset(EPS5[:, :], 1e-5), writes=[EPS5])
        P.op("gpsimd", lambda e: e.memset(ONESB[:, :], 1.0), writes=[ONESB])
        P.op("gpsimd", lambda e: e.memset(ONESF[:, :], 1.0), writes=[ONESF])
        vecT(GIN, attn_norm, 16)
        vecT(GQ, qn, 4)
        vecT(GKV, kvn, 4)

        if tabs_mode == "load":
            for ti_, tb_ in enumerate((CD, SD, CM, SM)):
                P.dma(tb_[:, :], tabs[ti_], writes=[tb_])
        else:
            with ExitStack() as stR:
                def sbR(name, shape, dtp):
                    return Buf(P, name, stR.enter_context(nc.sbuf_tensor(name, list(shape), dtp)))
                POSI = sbR("POSI", [128, S], I32)
                POSF = sbR("POSF", [128, S], F32)
                Y = sbR("Y", [128, S], F32)
                Y2 = sbR("Y2", [128, S], F32)
                KI = sbR("KI", [128, S], I32)
                KF = sbR("KF", [128, S], F32)
                P.dma(POSI[:, :], pos.partition_broadcast(128), writes=[POSI])
                P.op("vector", lambda e: e.tensor_copy(out=POSF[:, :], in_=POSI[:, :]), reads=[POSI], writes=[POSF])

                def sincos(invf_col, sin_dst, sin_mul_col, cos_dst, cos_mul_col, cos_add_col):
                    P.op("vector", lambda e: e.tensor_scalar(out=Y[:, :], in0=POSF[:, :], scalar1=RC[:, invf_col:invf_col + 1],
                                                              scalar2=1.0 / TWO_PI, op0=ALU.mult, op1=ALU.mult),
                         reads=[POSF, RC], writes=[Y])
                    for shift, dst, mulc, addc in ((0.0, sin_dst, sin_mul_col, None), (0.25, cos_dst, cos_mul_col, cos_add_col)):
                        P.op("vector", lambda e, shift=shift: e.tensor_scalar(out=Y2[:, :], in0=Y[:, :], scalar1=shift, scalar2=None,
                                                                               op0=ALU.add), reads=[Y], writes=[Y2])
                        P.op("vector", lambda e: e.tensor_copy(out=KI[:, :], in_=Y2[:, :]), reads=[Y2], writes=[KI])
                        P.op("vector", lambda e: e.tensor_copy(out=KF[:, :], in_=KI[:, :]), reads=[KI], writes=[KF])
                        P.op("vector", lambda e: e.tensor_tensor(out=Y2[:, :], in0=Y2[:, :], in1=KF[:, :], op=ALU.subtract),
                             reads=[Y2, KF], writes=[Y2])
                        P.op("vector", lambda e: e.tensor_scalar(out=KF[:, :], in0=Y2[:, :], scalar1=0.5, scalar2=None, op0=ALU.is_gt),
                             reads=[Y2], writes=[KF])
                        P.op("vector", lambda e: e.tensor_tensor(out=Y2[:, :], in0=Y2[:, :], in1=KF[:, :], op=ALU.subtract),
                             reads=[Y2, KF], writes=[Y2])
                        P.op("vector", lambda e: e.tensor_scalar(out=KF[:, :], in0=Y2[:, :], scalar1=-0.5, scalar2=None, op0=ALU.is_lt),
                             reads=[Y2], writes=[KF])
                        P.op("vector", lambda e: e.tensor_tensor(out=Y2[:, :], in0=Y2[:, :], in1=KF[:, :], op=ALU.add),
                             reads=[Y2, KF], writes=[Y2])
                        P.op("scalar", lambda e: e.activation(out=KF[:, :], in_=Y2[:, :], func=AF.Sin, scale=TWO_PI),
                             reads=[Y2], writes=[KF])
                        if addc is None:
                            P.op("vector", lambda e, dst=dst, mulc=mulc: e.tensor_scalar(
                                out=dst[:, :], in0=KF[:, :], scalar1=RC[:, mulc:mulc + 1], scalar2=None, op0=ALU.mult),
                                reads=[KF, RC], writes=[dst])
                        else:
                            P.op("vector", lambda e, dst=dst, mulc=mulc, addc=addc: e.tensor_scalar(
                                out=dst[:, :], in0=KF[:, :], scalar1=RC[:, mulc:mulc + 1], scalar2=RC[:, addc:addc + 1],
                                op0=ALU.mult, op1=ALU.add), reads=[KF, RC], writes=[dst])

                sincos(0, SD, 2, CD, 1, 5)
                sincos(3, SM, 4, CM, 6, 7)
                if tabs_mode == "compute_store":
                    TABS = P.wrap("TABS", tabs)
                    for ti_, tb_ in enumerate((CD, SD, CM, SM)):
                        P.dma(tabs[ti_], tb_[:, :], reads=[tb_], writes=[TABS])
                P.barrier()

        if STOP == 1:
            P.dma(mixT[0:128, :], MIXH[0][:, :], reads=[MIXH[0]], writes=[OUT])
            P.finish([OUT])
            P.emit()
            return nc
        with ExitStack() as stX:
            XT = [Buf(P, f"XT{i}", stX.enter_context(nc.sbuf_tensor(f"XT{i}", [128, D], BF16))) for i in range(2)]
            for t in range(NT):
                xt = XT[t % 2]
                P.dma(xt[:, :], xn[t * 128:(t + 1) * 128, :], writes=[xt])
                for g in range(2):
                    for kk in range(8):
                        k = g * 8 + kk
                        P.op("tensor", lambda e, k=k, kk=kk, xt=xt: e.transpose(
                            out=PTR[:, kk * 128:(kk + 1) * 128], in_=xt[:, k * 128:(k + 1) * 128], identity=IDB[:, :]),
                            reads=[xt, IDB], writes=[PTR])
                    src = PTR[:, :].rearrange("p (k t) -> p k t", t=128)
                    dst = HT[:, g * 8:(g + 1) * 8, t * 128:(t + 1) * 128]
                    if g == 0:
                        P.op("vector", lambda e, src=src, dst=dst: e.tensor_copy(out=dst, in_=src), reads=[PTR], writes=[HT])
                    else:
                        P.op("scalar", lambda e, src=src, dst=dst: e.activation(out=dst, in_=src, func=AF.Copy),
                             reads=[PTR], writes=[HT])
            P.barrier()

        if STOP == 2:
            P.dma(mixT[0:128, :], MIXH[0][:, :], reads=[MIXH[0]], writes=[OUT])
            P.finish([OUT])
            P.emit()
            return nc
        def load_w1(col0, ncols):
            wb = WB[cnt["wb"] % 4]
            cnt["wb"] += 1
            for k in range(16):
                s = stage()
                P.dma(s[:, 0:ncols], W1[k * 128:(k + 1) * 128, col0:col0 + ncols], writes=[s])
                cast(wb[:, k, 0:ncols], s[:, 0:ncols], GIN[:, k:k + 1], reads=[s, GIN], writes=[wb])
            return wb

        def load_u(U, col0, ncols, G):
            wb = WB[cnt["wb"] % 4]
            cnt["wb"] += 1
            for r in range(4):
                s = stage()
                P.dma(s[:, 0:ncols], U[r * 128:(r + 1) * 128, col0:col0 + ncols], writes=[s])
                cast(wb[:, r, 0:ncols], s[:, 0:ncols], G[:, r:r + 1], reads=[s, G], writes=[wb])
            return wb

        LOADS = [
            lambda: load_w1(C_Q, 256), lambda: load_w1(C_QS, 256), lambda: load_w1(C_K, 256), lambda: load_w1(C_KS, 256),
            lambda: load_w1(C_V, 256),
            lambda: load_w1(C_CQ, 256), lambda: load_w1(C_CQ + 256, 256),
            lambda: load_u(U1, 0, 128, GQ), lambda: load_u(U1, 128, 128, GQ), lambda: load_u(U1, 256, 128, GQ),
            lambda: load_u(U1, 384, 64, GQ), lambda: load_u(U1, 576, 64, GQ),
            lambda: load_u(U1, 448, 64, GQ), lambda: load_u(U1, 640, 64, GQ),
            lambda: load_u(U1, 512, 64, GQ), lambda: load_u(U1, 704, 64, GQ),
            lambda: load_w1(C_CKV, 256), lambda: load_w1(C_CKV + 256, 256),
            lambda: load_u(U2, 0, 128, GKV), lambda: load_u(U2, 128, 128, GKV), lambda: load_u(U2, 256, 128, GKV),
            lambda: load_u(U2, 384, 256, GKV), lambda: load_u(U2, 640, 128, GKV),
            lambda: load_w1(C_KR, 128),
            lambda: load_w1(C_FQ, 256), lambda: load_w1(C_FQ + 256, 128),
            lambda: load_w1(C_FK, 256), lambda: load_w1(C_FK + 256, 128),
            lambda: load_w1(C_FV, 256), lambda: load_w1(C_FV + 256, 128),
            lambda: load_w1(C_FG, 3),
        ]
        issued = []
        consumed = [0]

        def nxt():
            while len(issued) < min(len(LOADS), consumed[0] + 3):
                issued.append(LOADS[len(issued)]())
            wb = issued[consumed[0]]
            consumed[0] += 1
            return wb

        def proj_fm(wb, c0, m, src, nk, r, pm):
            for k in range(nk):
                P.op("tensor", lambda e, k=k: e.matmul(pm[0:m, :], lhsT=wb[:, k, c0:c0 + m],
                                                        rhs=src[:, k, r * 512:(r + 1) * 512],
                                                        start=(k == 0), stop=(k == nk - 1)),
                     reads=[wb, src], writes=[pm])

        def proj_tm(wb, c0, n, src, nk, j, pm):
            for k in range(nk):
                P.op("tensor", lambda e, k=k: e.matmul(pm[:, 0:n], lhsT=src[:, k, j * 128:(j + 1) * 128],
                                                        rhs=wb[:, k, c0:c0 + n], start=(k == 0), stop=(k == nk - 1)),
                     reads=[wb, src], writes=[pm])

        def rope_fm(wb, c_main, c_swap, m, src, nk, dst, Ct, St):
            for r in range(4):
                cs = slice(r * 512, (r + 1) * 512)
                i = cnt["t"] % 2
                cnt["t"] += 1
                p1 = next_proj()
                proj_fm(wb, c_main, m, src, nk, r, p1)
                P.op("vector", lambda e, p1=p1, i=i, cs=cs: e.tensor_tensor(out=T1[i][0:m, :], in0=p1[0:m, :], in1=Ct[0:m, cs],
                                                                             op=ALU.mult), reads=[p1, Ct], writes=[T1[i]])
                p2 = next_proj()
                proj_fm(wb, c_swap, m, src, nk, r, p2)
                P.op("vector", lambda e, p2=p2, i=i, cs=cs: e.tensor_tensor(out=T2[i][0:m, :], in0=p2[0:m, :], in1=St[0:m, cs],
                                                                             op=ALU.mult), reads=[p2, St], writes=[T2[i]])
                P.op("gpsimd", lambda e, i=i, cs=cs: e.tensor_tensor(out=dst[0:m, cs], in0=T1[i][0:m, :], in1=T2[i][0:m, :],
                                                                      op=ALU.add), reads=[T1[i], T2[i]], writes=[dst])

        def plain_fm(wb, c0, m, src, nk, dst):
            for r in range(4):
                cs = slice(r * 512, (r + 1) * 512)
                p1 = next_proj()
                proj_fm(wb, c0, m, src, nk, r, p1)
                if r % 2 == 0:
                    P.op("vector", lambda e, p1=p1, cs=cs: e.tensor_copy(out=dst[0:m, cs], in_=p1[0:m, :]), reads=[p1], writes=[dst])
                else:
                    P.op("scalar", lambda e, p1=p1, cs=cs: e.activation(out=dst[0:m, cs], in_=p1[0:m, :], func=AF.Copy),
                         reads=[p1], writes=[dst])

        def v_tm(wb, c0, nheads, src, nk, Vs):
            n = nheads * 128
            for j in range(NT):
                pm = next_proj()
                proj_tm(wb, c0, n, src, nk, j, pm)
                for h in range(nheads):
                    if (j + h) % 2 == 0:
                        P.op("vector", lambda e, pm=pm, h=h, j=j: e.tensor_copy(out=Vs[h][:, j, 0:128], in_=pm[:, h * 128:(h + 1) * 128]),
                             reads=[pm], writes=[Vs[h]])
                    else:
                        P.op("scalar", lambda e, pm=pm, h=h, j=j: e.activation(out=Vs[h][:, j, 0:128], in_=pm[:, h * 128:(h + 1) * 128],
                                                                                func=AF.Copy), reads=[pm], writes=[Vs[h]])

        def new_v(alloc, name):
            v = alloc(name, [128, NT, 136], BF16)
            P.op("gpsimd", lambda e: e.memset(v[:, :, :], 1.0), writes=[v])
            return v

        sidx_box = [0]

        def attention_chunk(c, kparts, qparts, scale, Vp, post, bias=None, extra=None, bias_tiles=None):
            info = {}

            def qk_exp(j):
                tlo = max(4 * c, j)
                t0 = tlo * 128
                n = (4 * c + 4) * 128 - t0
                sidx = sidx_box[0]
                sidx_box[0] += 1
                ps = PS[sidx % 2]
                ptb = PTB[sidx % 3]
                info[j] = (tlo, ptb)
                nparts = len(kparts)
                for i in range(nparts):
                    kb, kp0, kp1 = kparts[i]
                    qb, qp0, qp1 = qparts[i]
                    P.op("tensor", lambda e, i=i, kb=kb, kp0=kp0, kp1=kp1, qb=qb, qp0=qp0, qp1=qp1: e.matmul(
                        ps[:, 0:n], lhsT=kb[kp0:kp1, j * 128:(j + 1) * 128], rhs=qb[qp0:qp1, t0:t0 + n],
                        start=(i == 0), stop=(i == nparts - 1)), reads=[kb, qb], writes=[ps])
                if bias_tiles is not None:
                    for ti in range(tlo, 4 * c + 4):
                        off = (ti - tlo) * 128
                        P.op("scalar", lambda e, off=off, ti=ti: e.activation(
                            out=ptb[:, off:off + 128], in_=ps[:, off:off + 128], func=AF.Exp, scale=scale,
                            bias=bias_tiles[:, j, ti:ti + 1]), reads=[ps, bias_tiles], writes=[ptb])
                elif bias is None:
                    P.op("scalar", lambda e: e.activation(out=ptb[:, 0:n], in_=ps[:, 0:n], func=AF.Exp, scale=scale),
                         reads=[ps], writes=[ptb])
                else:
                    P.op("scalar", lambda e: e.activation(out=ptb[:, 0:n], in_=ps[:, 0:n], func=AF.Exp, scale=scale,
                                                          bias=bias[:, j:j + 1]), reads=[ps, bias], writes=[ptb])
                if j >= 4 * c:
                    P.op("gpsimd", lambda e: e.tensor_tensor(out=ptb[:, 0:128], in0=ptb[:, 0:128], in1=MASK[:, :], op=ALU.mult),
                         reads=[ptb, MASK], writes=[ptb])

            def pv(j):
                tlo, ptb = info[j]
                for ti in range(tlo, 4 * c + 4):
                    po = PO[ti - 4 * c]
                    off = (ti - tlo) * 128
                    P.op("tensor", lambda e, po=po, off=off, ti=ti: e.matmul(
                        po[:, 0:129], lhsT=ptb[:, off:off + 128], rhs=Vp[:, j, 0:129], start=(j == 0), stop=(j == ti)),
                        reads=[ptb, Vp], writes=[po])

            nj = 4 * c + 4
            qk_exp(0)
            for j in range(nj):
                if j + 1 < nj:
                    qk_exp(j + 1)
                pv(j)
            for ti in range(4 * c, 4 * c + 4):
                post(ti, PO[ti - 4 * c])

        def attention(kparts, qparts, scale, Vp, post, bias=None, extra=None, bias_tiles=None):
            for c in range(4):
                attention_chunk(c, kparts, qparts, scale, Vp, post, bias=bias, extra=extra, bias_tiles=bias_tiles)

        def finish_tile(on_src_fn, ti, mixh, tr_slot):
            onb = ONB[cnt["onb"] % 2]
            cnt["onb"] += 1
            on_src_fn(onb)
            sl = slice(tr_slot * 128, (tr_slot + 1) * 128)
            P.op("tensor", lambda e, onb=onb, sl=sl: e.transpose(out=PTR[:, sl], in_=onb[:, :], identity=IDB[:, :]),
                 reads=[onb, IDB], writes=[PTR])
            P.op("vector", lambda e, sl=sl, ti=ti: e.tensor_copy(out=mixh[:, ti * 128:(ti + 1) * 128], in_=PTR[:, sl]),
                 reads=[PTR], writes=[mixh])

        def std_post(mixh):
            def post(ti, po):
                rl = RL[cnt["rl"] % 4]
                cnt["rl"] += 1
                P.op("vector", lambda e: e.reciprocal(out=rl[:, :], in_=po[:, 128:129]), reads=[po], writes=[rl])

                def w(onb):
                    P.op("vector", lambda e: e.tensor_scalar(out=onb[:, :], in0=po[:, 0:128], scalar1=rl[:, :], scalar2=None,
                                                              op0=ALU.mult), reads=[po, rl], writes=[onb])
                finish_tile(w, ti, mixh, ti % 8)
            return post

        mix_i = [0]

        def store_head(mixh, row0):
            P.dma(mixT[row0:row0 + 128, :], mixh[:, :], reads=[mixh], writes=[OUT])

        with ExitStack() as stD:
            def sbD(name, shape, dtp):
                return Buf(P, name, stD.enter_context(nc.sbuf_tensor(name, list(shape), dtp)))
            DL = sbD("DL", [128, 256], F32)
            DJ = sbD("DJ", [128, 64], F32)
            SL = sbD("SL", [128, 4], F32)
            NEGLAM = sbD("NEGLAM", [128, 1], F32)
            P.dma(DL[:, :], dl.partition_broadcast(128), writes=[DL])
            for i in range(2):
                P.op("vector", lambda e, i=i: e.tensor_tensor(
                    out=DJ[:, :], in0=DL[:, i * 128:i * 128 + 64], in1=DL[:, i * 128 + 64:i * 128 + 128], op=ALU.mult),
                    reads=[DL], writes=[DJ])
                P.op("scalar", lambda e, i=i: e.activation(out=DJ[:, :], in_=DJ[:, :], func=AF.Copy, accum_out=SL[:, i:i + 1]),
                     reads=[DJ], writes=[DJ, SL])
            P.op("scalar", lambda e: e.activation(out=SL[:, 2:4], in_=SL[:, 0:2], func=AF.Exp), reads=[SL], writes=[SL])
            P.op("vector", lambda e: e.tensor_tensor(out=NEGLAM[:, :], in0=SL[:, 3:4], in1=SL[:, 2:3], op=ALU.subtract),
                 reads=[SL], writes=[NEGLAM])
            P.op("vector", lambda e: e.tensor_tensor(out=NEGLAM[:, :], in0=NEGLAM[:, :], in1=LCB[:, 0:1], op=ALU.subtract),
                 reads=[NEGLAM, LCB], writes=[NEGLAM])

            QT = [sbD(f"QTd{h}", [128, S], BF16) for h in range(2)]
            KT = [sbD(f"KTd{h}", [128, S], BF16) for h in range(2)]
            VD = [new_v(sbD, f"VD{h}") for h in range(2)]
            O1N = [sbD(f"O1N{i}", [128, 128], F32) for i in range(4)]
            OD = [sbD(f"OD{i}", [128, 128], F32) for i in range(2)]
            if STOP == 301:
                P.dma(mixT[0:128, :], MIXH[0][:, :], reads=[MIXH[0]], writes=[OUT])
                P.finish([OUT])
                P.emit()
                return nc
            wq = nxt()
            wqs = nxt()
            def rope2(wm, ws, c0, dst):
                for r in range(4):
                    cs = slice(r * 512, (r + 1) * 512)
                    i = cnt["t"] % 2
                    cnt["t"] += 1
                    p1 = next_proj()
                    proj_fm(wm, c0, 128, HT, 16, r, p1)
                    P.op("vector", lambda e, p1=p1, i=i, cs=cs: e.tensor_tensor(out=T1[i][:, :], in0=p1[:, :], in1=CD[:, cs], op=ALU.mult),
                         reads=[p1, CD], writes=[T1[i]])
                    p2 = next_proj()
                    proj_fm(ws, c0, 128, HT, 16, r, p2)
                    P.op("vector", lambda e, p2=p2, i=i, cs=cs: e.tensor_tensor(out=T2[i][:, :], in0=p2[:, :], in1=SD[:, cs], op=ALU.mult),
                         reads=[p2, SD], writes=[T2[i]])
                    P.op("gpsimd", lambda e, i=i, cs=cs: e.tensor_tensor(out=dst[:, cs], in0=T1[i][:, :], in1=T2[i][:, :], op=ALU.add),
                         reads=[T1[i], T2[i]], writes=[dst])
            for h in range(2):
                rope2(wq, wqs, h * 128, QT[h])
            if STOP == 302:
                P.dma(mixT[0:128, :], MIXH[0][:, :], reads=[MIXH[0]], writes=[OUT])
                P.finish([OUT])
                P.emit()
                return nc
            wk = nxt()
            wks = nxt()
            for h in range(2):
                rope2(wk, wks, h * 128, KT[h])
            if STOP == 303:
                P.dma(mixT[0:128, :], MIXH[0][:, :], reads=[MIXH[0]], writes=[OUT])
                P.finish([OUT])
                P.emit()
                return nc
            wv = nxt()
            v_tm(wv, 0, 2, HT, 16, VD)
            if STOP == 31:
                P.dma(mixT[0:128, :], MIXH[0][:, :], reads=[MIXH[0]], writes=[OUT])
                P.finish([OUT])
                P.emit()
                return nc

            for h in range(2):
                mixh = MIXH[mix_i[0] % 2]
                mix_i[0] += 1

                def post1(ti, po):
                    rl = RL[cnt["rl"] % 4]
                    cnt["rl"] += 1
                    P.op("vector", lambda e: e.reciprocal(out=rl[:, :], in_=po[:, 128:129]), reads=[po], writes=[rl])
                    o1 = O1N[ti % 4]
                    P.op("vector", lambda e: e.tensor_scalar(out=o1[:, :], in0=po[:, 0:128], scalar1=rl[:, :], scalar2=None, op0=ALU.mult),
                         reads=[po, rl], writes=[o1])

                def post2(ti, po, mixh=mixh):
                    rl = RL[cnt["rl"] % 4]
                    cnt["rl"] += 1
                    o1 = O1N[ti % 4]
                    od = OD[ti % 2]
                    P.op("vector", lambda e: e.reciprocal(out=rl[:, :], in_=po[:, 128:129]), reads=[po], writes=[rl])
                    P.op("vector", lambda e: e.tensor_scalar(out=od[:, :], in0=po[:, 0:128], scalar1=rl[:, :], scalar2=None, op0=ALU.mult),
                         reads=[po, rl], writes=[od])
                    P.op("vector", lambda e: e.scalar_tensor_tensor(out=od[:, :], in0=od[:, :], scalar=NEGLAM[:, 0:1], in1=o1[:, :],
                                                                     op0=ALU.mult, op1=ALU.add), reads=[od, NEGLAM, o1], writes=[od])
                    ssq = RL[cnt["rl"] % 4]
                    cnt["rl"] += 1
                    P.op("scalar", lambda e: e.activation(out=o1[:, :], in_=od[:, :], func=AF.Square, accum_out=ssq[:, :]),
                         reads=[od], writes=[o1, ssq])
                    P.op("scalar", lambda e: e.activation(out=ssq[:, :], in_=ssq[:, :], func=AF.Sqrt, scale=1.0 / 128, bias=EPS5[:, :]),
                         reads=[ssq, EPS5], writes=[ssq])
                    P.op("vector", lambda e: e.reciprocal(out=ssq[:, :], in_=ssq[:, :]), reads=[ssq], writes=[ssq])

                    def w(onb):
                        P.op("vector", lambda e: e.tensor_scalar(out=onb[:, :], in0=od[:, :], scalar1=ssq[:, :], scalar2=None, op0=ALU.mult),
                             reads=[od, ssq], writes=[onb])
                    finish_tile(w, ti, mixh, ti % 8)

                for c in range(4):
                    attention_chunk(c, [(KT[h], 0, 64)], [(QT[h], 0, 64)], 0.125, VD[h], post1)
                    if STOP == 32:
                        P.dma(mixT[0:128, :], MIXH[0][:, :], reads=[MIXH[0]], writes=[OUT])
                        P.finish([OUT])
                        P.emit()
                        return nc
                    attention_chunk(c, [(KT[h], 64, 128)], [(QT[h], 64, 128)], 0.125, VD[h], post2)
                store_head(mixh, h * 128)
            P.barrier()

        if STOP == 3:
            P.dma(mixT[0:128, :], MIXH[0][:, :], reads=[MIXH[0]], writes=[OUT])
            P.finish([OUT])
            P.emit()
            return nc
        with ExitStack() as stM:
            def sbM(name, shape, dtp):
                return Buf(P, name, stM.enter_context(nc.sbuf_tensor(name, list(shape), dtp)))
            QN = [sbM(f"QNm{h}", [128, S], BF16) for h in range(3)]
            QR = [sbM(f"QRm{h}", [128, S], BF16) for h in range(3)]
            KN = [sbM(f"KNm{h}", [128, S], BF16) for h in range(3)]
            KR = sbM("KRm", [128, S], BF16)
            VM = [new_v(sbM, f"VM{h}") for h in range(3)]
            SQ = [sbM(f"SQ{i}", [128, 512], BF16) for i in range(2)]
            RSTD = sbM("RSTD", [128, 512], F32)

            def latent(col0, dst):
                wl = [nxt(), nxt()]
                for rcn in range(4):
                    for r in range(4):
                        cs = slice(r * 512, (r + 1) * 512)
                        pm = next_proj()
                        sq = SQ[(rcn * 4 + r) % 2]
                        proj_fm(wl[rcn // 2], (rcn % 2) * 128, 128, HT, 16, r, pm)
                        P.op("vector", lambda e, pm=pm, rcn=rcn, cs=cs: e.tensor_copy(out=dst[:, rcn, cs], in_=pm[:, :]), reads=[pm], writes=[dst])
                        P.op("scalar", lambda e, pm=pm, sq=sq: e.activation(out=sq[:, :], in_=pm[:, :], func=AF.Square),
                             reads=[pm], writes=[sq])
                        P.op("tensor", lambda e, rcn=rcn, r=r, sq=sq: e.matmul(PO[r][:, :], lhsT=ONESB[:, :], rhs=sq[:, :],
                                                                             start=(rcn == 0), stop=(rcn == 3)),
                             reads=[ONESB, sq], writes=[PO[r]])
                for r in range(4):
                    cs = slice(r * 512, (r + 1) * 512)
                    P.op("scalar", lambda e, r=r: e.activation(out=RSTD[:, :], in_=PO[r][:, :], func=AF.Sqrt, scale=1.0 / 512,
                                                                bias=EPS6[:, :]), reads=[PO[r], EPS6], writes=[RSTD])
                    P.op("vector", lambda e: e.reciprocal(out=RSTD[:, :], in_=RSTD[:, :]), reads=[RSTD], writes=[RSTD])
                    for rcn in range(4):
                        P.op("gpsimd" if rcn % 2 else "vector", lambda e, rcn=rcn, cs=cs: e.tensor_tensor(
                            out=dst[:, rcn, cs], in0=dst[:, rcn, cs], in1=RSTD[:, :], op=ALU.mult),
                            reads=[dst, RSTD], writes=[dst])

            with ExitStack() as stM1:
                CQN = Buf(P, "CQN", stM1.enter_context(nc.sbuf_tensor("CQN", [128, 4, S], BF16)))
                latent(C_CQ, CQN)
                for h in range(3):
                    wu = nxt()
                    plain_fm(wu, 0, 128, CQN, 4, QN[h])
                for h in range(3):
                    wu = nxt()
                    wus = nxt()
                    for r in range(4):
                        cs = slice(r * 512, (r + 1) * 512)
                        i = cnt["t"] % 2
                        cnt["t"] += 1
                        p1 = next_proj()
                        proj_fm(wu, 0, 64, CQN, 4, r, p1)
                        P.op("vector", lambda e, p1=p1, i=i, cs=cs: e.tensor_tensor(out=T1[i][0:64, :], in0=p1[0:64, :], in1=CM[0:64, cs],
                                                                                     op=ALU.mult), reads=[p1, CM], writes=[T1[i]])
                        p2 = next_proj()
                        proj_fm(wus, 0, 64, CQN, 4, r, p2)
                        P.op("vector", lambda e, p2=p2, i=i, cs=cs: e.tensor_tensor(out=T2[i][0:64, :], in0=p2[0:64, :], in1=SM[0:64, cs],
                                                                                     op=ALU.mult), reads=[p2, SM], writes=[T2[i]])
                        P.op("gpsimd", lambda e, i=i, cs=cs, h=h: e.tensor_tensor(out=QR[h][0:64, cs], in0=T1[i][0:64, :], in1=T2[i][0:64, :],
                                                                                   op=ALU.add), reads=[T1[i], T2[i]], writes=[QR[h]])
                P.barrier()
            with ExitStack() as stM2:
                CKN = Buf(P, "CKN", stM2.enter_context(nc.sbuf_tensor("CKN", [128, 4, S], BF16)))
                latent(C_CKV, CKN)
                for h in range(3):
                    wu = nxt()
                    plain_fm(wu, 0, 128, CKN, 4, KN[h])
                wv1 = nxt()
                v_tm(wv1, 0, 2, CKN, 4, VM[0:2])
                wv2 = nxt()
                v_tm(wv2, 0, 1, CKN, 4, VM[2:3])
                P.barrier()
            wkr = nxt()
            for r in range(4):
                cs = slice(r * 512, (r + 1) * 512)
                i = cnt["t"] % 2
                cnt["t"] += 1
                p1 = next_proj()
                proj_fm(wkr, 0, 64, HT, 16, r, p1)
                P.op("vector", lambda e, p1=p1, i=i, cs=cs: e.tensor_tensor(out=T1[i][0:64, :], in0=p1[0:64, :], in1=CM[0:64, cs], op=ALU.mult),
                     reads=[p1, CM], writes=[T1[i]])
                p2 = next_proj()
                proj_fm(wkr, 64, 64, HT, 16, r, p2)
                P.op("vector", lambda e, p2=p2, i=i, cs=cs: e.tensor_tensor(out=T2[i][0:64, :], in0=p2[0:64, :], in1=SM[0:64, cs], op=ALU.mult),
                     reads=[p2, SM], writes=[T2[i]])
                P.op("gpsimd", lambda e, i=i, cs=cs: e.tensor_tensor(out=KR[0:64, cs], in0=T1[i][0:64, :], in1=T2[i][0:64, :], op=ALU.add),
                     reads=[T1[i], T2[i]], writes=[KR])
            for h in range(3):
                mixh = MIXH[mix_i[0] % 2]
                mix_i[0] += 1
                attention([(KN[h], 0, 128), (KR, 0, 64)], [(QN[h], 0, 128), (QR[h], 0, 64)], 192 ** -0.5, VM[h], std_post(mixh))
                store_head(mixh, 256 + h * 128)
            P.barrier()

        if STOP == 4:
            P.dma(mixT[0:128, :], MIXH[0][:, :], reads=[MIXH[0]], writes=[OUT])
            P.finish([OUT])
            P.emit()
            return nc
        with ExitStack() as stF:
            def sbF(name, shape, dtp):
                return Buf(P, name, stF.enter_context(nc.sbuf_tensor(name, list(shape), dtp)))
            QF = [sbF(f"QF{h}", [128, S], BF16) for h in range(3)]
            KF_ = [sbF(f"KF{h}", [128, S], BF16) for h in range(3)]
            VF = [new_v(sbF, f"VF{h}") for h in range(3)]
            NFB = sbF("NFB", [3, 1], F32)
            ONE3 = sbF("ONE3", [3, 1], F32)
            ONEROW = sbF("ONEROW", [3, S], F32)
            GL = sbF("GL", [3, S], F32)
            CL = sbF("CL", [3, S], F32)
            NBALL = sbF("NBALL", [128, NT * 3], F32)
            R1 = sbF("R1", [128, NT * 3], F32)
            NB3 = [sbF(f"NB3_{i}", [128, NT * 3], BF16) for i in range(3)]
            E0 = sbF("E0", [128, 128], BF16)
            CLB = sbF("CLB", [128, NT * 3], F32)
            BI = [sbF(f"BI{h}", [128, NT, NT], F32) for h in range(3)]
            P.dma(NFB[:, :], fb[0:3].rearrange("(p o) -> p o", o=1), writes=[NFB])
            P.op("vector", lambda e: e.tensor_scalar(out=NFB[:, :], in0=NFB[:, :], scalar1=-1.0, scalar2=None, op0=ALU.mult),
                 reads=[NFB], writes=[NFB])
            P.op("gpsimd", lambda e: e.memset(ONEROW[:, :], 1.0), writes=[ONEROW])
            P.op("gpsimd", lambda e: e.memset(ONE3[:, :], 1.0), writes=[ONE3])
            P.op("gpsimd", lambda e: e.memset(E0[:, :], 0.0), writes=[E0])
            P.op("gpsimd", lambda e: e.memset(E0[0:1, :], 1.0), writes=[E0])
            for (c0, dsts) in ((C_FQ, QF), (C_FK, KF_)):
                wa = nxt()
                wb2 = nxt()
                plain_fm(wa, 0, 128, HT, 16, dsts[0])
                plain_fm(wa, 128, 128, HT, 16, dsts[1])
                plain_fm(wb2, 0, 128, HT, 16, dsts[2])
            wv1 = nxt()
            v_tm(wv1, 0, 2, HT, 16, VF[0:2])
            wv2 = nxt()
            v_tm(wv2, 0, 1, HT, 16, VF[2:3])
            wg = nxt()
            for r in range(4):
                cs = slice(r * 512, (r + 1) * 512)
                pm = next_proj()
                proj_fm(wg, 0, 3, HT, 16, r, pm)
                P.op("scalar", lambda e, pm=pm, cs=cs: e.activation(out=GL[0:3, cs], in_=pm[0:3, :], func=AF.Exp, scale=-1.0,
                                                                     bias=NFB[0:3, 0:1]), reads=[pm, NFB], writes=[GL])
            P.op("scalar", lambda e: e.activation(out=GL[0:3, :], in_=GL[0:3, :], func=AF.Ln, bias=ONE3[0:3, 0:1]),
                 reads=[GL, ONE3], writes=[GL])
            P.op("vector", lambda e: e.tensor_tensor_scan(out=CL[0:3, :], data0=ONEROW[0:3, :], data1=GL[0:3, :], initial=0.0,
                                                           op0=ALU.mult, op1=ALU.add), reads=[ONEROW, GL], writes=[CL])
            pm = next_proj()
            for j in range(NT):
                P.op("tensor", lambda e, j=j, pm=pm: e.transpose(out=pm[:, j * 3:(j + 1) * 3], in_=CL[0:3, j * 128:(j + 1) * 128],
                                                                identity=IDF[0:3, 0:3]), reads=[CL, IDF], writes=[pm])
            P.op("vector", lambda e, pm=pm: e.tensor_copy(out=NBALL[:, :], in_=pm[:, 0:NT * 3]), reads=[pm], writes=[NBALL])
            P.op("vector", lambda e: e.tensor_copy(out=NB3[0][:, :], in_=NBALL[:, :]), reads=[NBALL], writes=[NB3[0]])
            P.op("vector", lambda e: e.tensor_tensor(out=R1[:, :], in0=NBALL[:, :], in1=NB3[0][:, :], op=ALU.subtract),
                 reads=[NBALL, NB3[0]], writes=[R1])
            P.op("vector", lambda e: e.tensor_copy(out=NB3[1][:, :], in_=R1[:, :]), reads=[R1], writes=[NB3[1]])
            P.op("vector", lambda e: e.tensor_tensor(out=R1[:, :], in0=R1[:, :], in1=NB3[1][:, :], op=ALU.subtract),
                 reads=[R1, NB3[1]], writes=[R1])
            P.op("vector", lambda e: e.tensor_copy(out=NB3[2][:, :], in_=R1[:, :]), reads=[R1], writes=[NB3[2]])
            pm = next_proj()
            for i in range(3):
                P.op("tensor", lambda e, i=i, pm=pm: e.matmul(pm[:, 0:NT * 3], lhsT=E0[:, :], rhs=NB3[i][:, :],
                                                             start=(i == 0), stop=(i == 2)), reads=[E0, NB3[i]], writes=[pm])
            P.op("vector", lambda e, pm=pm: e.tensor_copy(out=CLB[:, :], in_=pm[:, 0:NT * 3]), reads=[pm], writes=[CLB])
            for h in range(3):
                for j in range(NT):
                    P.op("vector", lambda e, h=h, j=j: e.tensor_scalar(
                        out=BI[h][:, j, :], in0=CLB[:, :].rearrange("p (t h) -> p t h", h=3)[:, :, h],
                        scalar1=NBALL[:, j * 3 + h:j * 3 + h + 1], scalar2=-1.0, op0=ALU.subtract, op1=ALU.mult),
                        reads=[CLB, NBALL], writes=[BI[h]])
            for h in range(3):
                mixh = MIXH[mix_i[0] % 2]
                mix_i[0] += 1
                attention([(KF_[h], 0, 128)], [(QF[h], 0, 128)], 128 ** -0.5, VF[h], std_post(mixh), bias_tiles=BI[h])
                store_head(mixh, 640 + h * 128)
            P.barrier()
        P.drain_all()


D = 2048
DFF = 5632
NTOK = 1024
NH = 2
TT = NTOK // 128
NG = 11
EPS = 1e-6


def body_k2(nc, P, io):
    x_main, x_halo, mix_main, mix_halo = io["x_main"], io["x_halo"], io["mix_main"], io["mix_halo"]
    w_o, w_up, conv_w, conv_b, w_down = io["w_o"], io["w_up"], io["conv_w"], io["conv_b"], io["w_down"]
    ffn_norm, dnorm, fnorm, lc, idf, idb = io["ffn_norm"], io["dnorm"], io["fnorm"], io["lc"], io["idf"], io["idb"]
    x_out, xn_out, fin = io["x_out"], io["xn_out"], io["fin"]
    with ExitStack() as st:
        P.stack = st
        X = [P.sb(f"X{t}", [128, D], F32) for t in range(TT)]
        XH = P.sb("XH", [NH, D], F32)
        IDF = P.sb("IDF", [128, 128], F32)
        IDB = P.sb("IDB", [128, 128], BF16)
        GUP = P.sb("GUP", [128, 16], F32)
        CW = P.sb("CW", [128, 4, 88], F32)
        DN = P.sb("DN", [128, 1], F32)
        EPSB = P.sb("EPSB", [128, 1], F32)
        ss = [P.sb(f"ss{i}", [128, 1], F32) for i in range(2)]
        sd = [P.sb(f"sd{i}", [128, 1], F32) for i in range(2)]
        rs = [P.sb(f"rs{i}", [128, 1], F32) for i in range(2)]
        STG = [P.sb(f"stg{i}", [128, 1024], F32) for i in range(4)]
        PA = P.ps("PA", [128, 1024])
        PG = P.ps("PG", [128, 1024])
        PHALO = P.ps("PHALO", [128, 512])
        ACC = [PA, PG]
        PM = [P.ps(f"PM{i}", [128, 512]) for i in range(2)]
        PT = P.ps("PT", [128, 1024], BF16)
        PH = [P.wrap("PH0", PHALO.t, lock=PHALO.lock), P.wrap("PH1", PT[:, :].bitcast(F32), lock=PT.lock)]

        OUTX = P.wrap("OUTX", x_out)
        OUTN = P.wrap("OUTN", xn_out)
        OUTF = P.wrap("OUTF", fin if fin is not None else x_out)

        stg_i = [0]

        def stage():
            b = STG[stg_i[0] % len(STG)]
            stg_i[0] += 1
            return b

        cast_i = [0]

        def cast(out_ap, in_ap, scale_ap, reads, writes):
            eng = "scalar" if cast_i[0] % 2 == 0 else "gpsimd"
            cast_i[0] += 1
            if eng == "scalar":
                if scale_ap is None:
                    P.op("scalar", lambda e: e.activation(out=out_ap, in_=in_ap, func=AF.Copy), reads=reads, writes=writes)
                else:
                    P.op("scalar", lambda e: e.activation(out=out_ap, in_=in_ap, func=AF.Copy, scale=scale_ap),
                         reads=reads, writes=writes)
            else:
                if scale_ap is None:
                    P.op("gpsimd", lambda e: e.tensor_copy(out=out_ap, in_=in_ap), reads=reads, writes=writes)
                else:
                    P.op("gpsimd", lambda e: e.tensor_scalar(out=out_ap, in0=in_ap, scalar1=scale_ap, scalar2=1.0,
                                                              op0=ALU.mult, op1=ALU.mult), reads=reads, writes=writes)

        P.dma(IDF[:, :], idf, writes=[IDF])
        P.dma(IDB[:, :], idb, writes=[IDB])
        P.op("gpsimd", lambda e: e.memset(EPSB[:, :], EPS), writes=[EPSB])
        s0 = stage()
        P.dma(s0[0:16, 0:128], ffn_norm.rearrange("(k p) -> k p", p=128), writes=[s0])
        P.op("tensor", lambda e: e.transpose(out=PM[0][:, 0:16], in_=s0[0:16, 0:128], identity=IDF[0:16, 0:16]),
             reads=[s0, IDF], writes=[PM[0]])
        P.op("vector", lambda e: e.tensor_copy(out=GUP[:, :], in_=PM[0][:, 0:16]), reads=[PM[0]], writes=[GUP])
        for j in range(4):
            s1 = stage()
            src = conv_w[j:j + 1, :].rearrange("o (c p) -> (o c) p", p=128) if j < 3 else conv_b.rearrange("(c p) -> c p", p=128)
            P.dma(s1[0:88, 0:128], src, writes=[s1])
            pm = PM[(j + 1) % 2]
            P.op("tensor", lambda e, s1=s1, pm=pm: e.transpose(out=pm[:, 0:88], in_=s1[0:88, 0:128], identity=IDF[0:88, 0:88]),
                 reads=[s1, IDF], writes=[pm])
            P.op("vector", lambda e, j=j, pm=pm: e.tensor_copy(out=CW[:, j, :], in_=pm[:, 0:88]), reads=[pm], writes=[CW])
        LCB = P.sb("LCB", [128, 4], F32)
        P.dma(LCB[:, :], lc.partition_broadcast(128), writes=[LCB])
        s2 = stage()
        P.dma(s2[:, 0:1], dnorm.rearrange("(p o) -> p o", o=1), writes=[s2])
        P.op("vector", lambda e: e.tensor_scalar(out=DN[:, :], in0=s2[:, 0:1], scalar1=LCB[:, 1:2], scalar2=None, op0=ALU.mult),
             reads=[s2, LCB], writes=[DN])

        if x_halo is None:
            P.op("gpsimd", lambda e: e.memset(XH[:, :], 0.0), writes=[XH])
        else:
            P.dma(XH[:, :], x_halo, writes=[XH])
        for t in range(TT):
            P.dma(X[t][:, :], x_main[t * 128:(t + 1) * 128, :], writes=[X[t]])

        with ExitStack() as stA:
            MT = []
            for k in range(16):
                t_ = stA.enter_context(nc.sbuf_tensor(f"MT{k}", [128, NH + NTOK], BF16))
                MT.append(Buf(P, f"MT{k}", t_))
            WOB = []
            for i in range(2):
                t_ = stA.enter_context(nc.sbuf_tensor(f"WOB{i}", [128, 16, 512], BF16))
                WOB.append(Buf(P, f"WOB{i}", t_))
            for k in range(16):
                if x_halo is None:
                    P.op("gpsimd", lambda e, k=k: e.memset(MT[k][:, 0:NH], 0.0), writes=[MT[k]])
                else:
                    P.dma(MT[k][:, 0:NH], mix_halo(k), writes=[MT[k]])
                P.dma(MT[k][:, NH:NH + NTOK], mix_main(k), writes=[MT[k]])

            def load_wo(nb):
                wb = WOB[nb % 2]
                for k in range(16):
                    s = stage()
                    P.dma(s[:, 0:512], w_o[k * 128:(k + 1) * 128, nb * 512:(nb + 1) * 512], writes=[s])
                    cast(wb[:, k, :], s[:, 0:512], DN[:, 0:1] if k < 4 else None, reads=[s, DN], writes=[wb])

            load_wo(0)
            pmi = 0
            for nb in range(4):
                if nb + 1 < 4:
                    load_wo(nb + 1)
                wb = WOB[nb % 2]
                cs = slice(nb * 512, (nb + 1) * 512)
                for tt in range(-1, TT):
                    pm = PM[pmi % 2]
                    pmi += 1
                    if tt < 0:
                        np_, c0, c1, xt = NH, 0, NH, XH
                    else:
                        np_, c0, c1, xt = 128, NH + tt * 128, NH + (tt + 1) * 128, X[tt]
                    for k in range(16):
                        P.op("tensor", lambda e, k=k, pm=pm, np_=np_, c0=c0, c1=c1, wb=wb: e.matmul(
                            pm[0:np_, :], lhsT=MT[k][:, c0:c1], rhs=wb[:, k, :], start=(k == 0), stop=(k == 15)),
                            reads=[MT[k], wb], writes=[pm])
                    P.op("vector", lambda e, pm=pm, np_=np_, xt=xt, cs=cs: e.tensor_tensor(
                        out=xt[0:np_, cs], in0=pm[0:np_, :], in1=xt[0:np_, cs], op=ALU.add),
                        reads=[pm, xt], writes=[xt])
            P.barrier()

        def norm_tile(xt, np_, i, sq_out):
            b = i % 2
            P.op("scalar", lambda e: e.activation(out=sq_out[0:np_, :], in_=xt[0:np_, :], func=AF.Square,
                                                    accum_out=ss[b][0:np_, :]), reads=[xt], writes=[sq_out, ss[b]])
            P.op("scalar", lambda e: e.activation(out=sd[b][0:np_, :], in_=ss[b][0:np_, :], func=AF.Sqrt,
                                                    scale=1.0 / D, bias=EPSB[0:np_, :]), reads=[ss[b], EPSB], writes=[sd[b]])
            P.op("vector", lambda e: e.reciprocal(out=rs[b][0:np_, :], in_=sd[b][0:np_, :]), reads=[sd[b]], writes=[rs[b]])
            return rs[b]

        with ExitStack() as stH:
            H2T = Buf(P, "H2T", stH.enter_context(nc.sbuf_tensor("H2T", [128, 16, NH + NTOK], BF16)))
            with ExitStack() as stN:
                xnb = [Buf(P, f"xnb{i}", stN.enter_context(nc.sbuf_tensor(f"xnb{i}", [128, D], BF16))) for i in range(2)]

                def norm_transpose(xt, np_, i, c0):
                    b = i % 2
                    r = norm_tile(xt, np_, i, xnb[b])
                    P.op("vector", lambda e: e.tensor_scalar(out=xnb[b][0:np_, :], in0=xt[0:np_, :], scalar1=r[0:np_, :],
                                                              scalar2=None, op0=ALU.mult), reads=[xt, r], writes=[xnb[b]])
                    for g in range(2):
                        for kk in range(8):
                            k = g * 8 + kk
                            P.op("tensor", lambda e, k=k, kk=kk: e.transpose(
                                out=PT[:, kk * 128: kk * 128 + np_], in_=xnb[b][0:np_, k * 128:(k + 1) * 128],
                                identity=IDB[0:np_, 0:np_]), reads=[xnb[b], IDB], writes=[PT])
                        src = PT[:, :].rearrange("p (k t) -> p k t", t=128)[:, :, 0:np_]
                        dst = H2T[:, g * 8:(g + 1) * 8, c0:c0 + np_]
                        if g == 0:
                            P.op("vector", lambda e, src=src, dst=dst: e.tensor_copy(out=dst, in_=src), reads=[PT], writes=[H2T])
                        else:
                            P.op("scalar", lambda e, src=src, dst=dst: e.activation(out=dst, in_=src, func=AF.Copy),
                                 reads=[PT], writes=[H2T])

                norm_transpose(XH, NH, 0, 0)
                for t in range(TT):
                    norm_transpose(X[t], 128, t + 1, NH + t * 128)
                P.barrier()

            with ExitStack() as stB:
                def sbB(name, shape, dtp):
                    return Buf(P, name, stB.enter_context(nc.sbuf_tensor(name, list(shape), dtp)))
                WUB = [sbB(f"WUB{i}", [128, 16, 512], BF16) for i in range(2)]
                WD = sbB("WD", [128, 4, 2048], BF16)
                ACTT = sbB("ACTT", [128, 4, NTOK], BF16)
                UA = [sbB(f"UA{i}", [128, NTOK], F32) for i in range(4)]
                UG = [sbB(f"UG{i}", [128, NTOK], F32) for i in range(2)]

                def load_up(wb, col0):
                    for k in range(16):
                        s = stage()
                        P.dma(s[:, 0:512], w_up[k * 128:(k + 1) * 128, col0:col0 + 512], writes=[s])
                        cast(wb[:, k, :], s[:, 0:512], GUP[:, k:k + 1], reads=[s, GUP], writes=[wb])

                def load_down(g):
                    for fc in range(4):
                        r0 = (g * 4 + fc) * 128
                        for hh in range(2):
                            s = stage()
                            P.dma(s[:, :], w_down[r0:r0 + 128, hh * 1024:(hh + 1) * 1024], writes=[s])
                            cast(WD[:, fc, hh * 1024:(hh + 1) * 1024], s[:, :], None, reads=[s], writes=[WD])

                def conv(pp, ph, hc, uc, c):
                    w0, w1, w2, bb = CW[:, 0, c:c + 1], CW[:, 1, c:c + 1], CW[:, 2, c:c + 1], CW[:, 3, c:c + 1]
                    P.op("scalar", lambda e: e.activation(out=uc[:, :], in_=pp[:, :], func=AF.Identity, scale=w2, bias=bb),
                         reads=[pp, CW], writes=[uc])
                    P.op("vector", lambda e: e.scalar_tensor_tensor(out=uc[:, 1:NTOK], in0=pp[:, 0:NTOK - 1], scalar=w1,
                                                                     in1=uc[:, 1:NTOK], op0=ALU.mult, op1=ALU.add),
                         reads=[pp, CW, uc], writes=[uc])
                    P.op("vector", lambda e: e.scalar_tensor_tensor(out=uc[:, 2:NTOK], in0=pp[:, 0:NTOK - 2], scalar=w0,
                                                                     in1=uc[:, 2:NTOK], op0=ALU.mult, op1=ALU.add),
                         reads=[pp, CW, uc], writes=[uc])
                    P.op("vector", lambda e: e.scalar_tensor_tensor(out=uc[:, 0:1], in0=ph[:, hc + 1:hc + 2], scalar=w1,
                                                                     in1=uc[:, 0:1], op0=ALU.mult, op1=ALU.add),
                         reads=[ph, CW, uc], writes=[uc])
                    P.op("vector", lambda e: e.scalar_tensor_tensor(out=uc[:, 0:2], in0=ph[:, hc:hc + 2], scalar=w0,
                                                                     in1=uc[:, 0:2], op0=ALU.mult, op1=ALU.add),
                         reads=[ph, CW, uc], writes=[uc])

                def up(wb, fc, pp, ph, hc):
                    for k in range(16):
                        lw = wb[:, k, fc * 128:(fc + 1) * 128]
                        P.op("tensor", lambda e, k=k, lw=lw: e.matmul(ph[:, hc:hc + 2], lhsT=lw, rhs=H2T[:, k, 0:NH],
                                                                       start=(k == 0), stop=(k == 15)),
                             reads=[wb, H2T], writes=[ph])
                        for h in range(2):
                            P.op("tensor", lambda e, k=k, lw=lw, h=h: e.matmul(
                                pp[:, h * 512:(h + 1) * 512], lhsT=lw, rhs=H2T[:, k, NH + h * 512: NH + (h + 1) * 512],
                                start=(k == 0), stop=(k == 15)), reads=[wb, H2T], writes=[pp])

                load_up(WUB[0], 0)
                load_up(WUB[1], DFF)
                load_down(0)
                pmi = 0
                ci = 0
                for g in range(NG):
                    for fc in range(4):
                        pp, ph, hc = ACC[ci % 2], PH[ci % 2], 0
                        ci += 1
                        up(WUB[0], fc, pp, ph, hc)
                        conv(pp, ph, hc, UA[fc], g * 4 + fc)
                    if g + 1 < NG:
                        load_up(WUB[0], (g + 1) * 512)
                    for fc in range(4):
                        pp, ph, hc = ACC[ci % 2], PH[ci % 2], 0
                        ci += 1
                        ug = UG[fc % 2]
                        up(WUB[1], fc, pp, ph, hc)
                        conv(pp, ph, hc, ug, 44 + g * 4 + fc)
                        P.op("scalar", lambda e, ug=ug: e.activation(out=ug[:, :], in_=ug[:, :], func=AF.Silu),
                             reads=[ug], writes=[ug])
                        P.op("gpsimd", lambda e, ug=ug, fc=fc: e.tensor_tensor(out=ACTT[:, fc, :], in0=ug[:, :], in1=UA[fc][:, :],
                                                                                 op=ALU.mult),
                             reads=[ug, UA[fc]], writes=[ACTT])
                    if g + 1 < NG:
                        load_up(WUB[1], DFF + (g + 1) * 512)
                    for tt in range(TT):
                        for nb in range(4):
                            pm = PM[pmi % 2]
                            pmi += 1
                            cs = slice(nb * 512, (nb + 1) * 512)
                            for fc in range(4):
                                P.op("tensor", lambda e, fc=fc, tt=tt, cs=cs, pm=pm: e.matmul(
                                    pm[:, :], lhsT=ACTT[:, fc, tt * 128:(tt + 1) * 128], rhs=WD[:, fc, cs],
                                    start=(fc == 0), stop=(fc == 3)), reads=[ACTT, WD], writes=[pm])
                            P.op("vector", lambda e, tt=tt, cs=cs, pm=pm: e.tensor_tensor(
                                out=X[tt][:, cs], in0=pm[:, :], in1=X[tt][:, cs], op=ALU.add),
                                reads=[pm, X[tt]], writes=[X[tt]])
                    if g + 1 < NG:
                        load_down(g + 1)
                P.barrier()

        with ExitStack() as stC:
            def sbC(name, shape, dtp):
                return Buf(P, name, stC.enter_context(nc.sbuf_tensor(name, list(shape), dtp)))
            FO = [sbC(f"FO{i}", [128, D], F32) for i in range(2)]
            xnc = [sbC(f"xnc{i}", [128, D], BF16) for i in range(2)]
            FG = sbC("FG", [128, D], F32)
            P.dma(FG[:, :], fnorm.partition_broadcast(128), writes=[FG])
            for t in range(TT):
                b = (t + 1) % 2
                P.dma(x_out[t * 128:(t + 1) * 128, :], X[t][:, :], reads=[X[t]], writes=[OUTX])
                r = norm_tile(X[t], 128, t + 1, xnc[b])
                P.op("gpsimd", lambda e, t=t, b=b, r=r: e.tensor_scalar(out=xnc[b][:, :], in0=X[t][:, :], scalar1=r[:, :],
                                                                         scalar2=1.0, op0=ALU.mult, op1=ALU.mult),
                     reads=[X[t], r], writes=[xnc[b]])
                P.dma(xn_out[t * 128:(t + 1) * 128, :], xnc[b][:, :], reads=[xnc[b]], writes=[OUTN])
                if fin is not None:
                    P.op("vector", lambda e, t=t, b=b, r=r: e.scalar_tensor_tensor(out=FO[b][:, :], in0=X[t][:, :], scalar=r[:, :],
                                                                                   in1=FG[:, :], op0=ALU.mult, op1=ALU.mult),
                         reads=[X[t], r, FG], writes=[FO[b]])
                    P.dma(fin[t * 128:(t + 1) * 128, :], FO[b][:, :], reads=[FO[b]], writes=[OUTF])
            P.drain_all()


bf16 = ml_dtypes.bfloat16
bf16 = ml_dtypes.bfloat16

ROPE_THETA = 500000.0

def swap_perm_diff():
    p = np.arange(128)
    for base in (0, 64):
        for i in range(8):
            p[base + i] = base + i + 8
            p[base + i + 8] = base + i
    return p

def swap_perm_rope64():
    p = np.arange(64)
    p[:32] = np.arange(32, 64)
    p[32:] = np.arange(0, 32)
    return p

def rope_consts():
    rc = np.zeros((128, 8), np.float32)
    invd = (ROPE_THETA ** (-np.arange(0, 16, 2, dtype=np.float32) / np.float32(16))).astype(np.float32)
    invm = (ROPE_THETA ** (-np.arange(0, 64, 2, dtype=np.float32) / np.float32(64))).astype(np.float32)
    for p in range(128):
        q = p % 64
        if q < 16:
            rc[p, 0] = invd[q % 8]
            rc[p, 1] = 1.0
            rc[p, 2] = -1.0 if q < 8 else 1.0
        else:
            rc[p, 0] = 0.0
            rc[p, 1] = 0.0
            rc[p, 2] = 0.0
        rc[p, 5] = 1.0 - rc[p, 1]
        rc[p, 3] = invm[q % 32]
        rc[p, 4] = -1.0 if q < 32 else 1.0
        rc[p, 6] = 1.0
        rc[p, 7] = 0.0
    return rc

def pack_k1(l, r, w):
    win = w["w_in"][l]
    offs = np.cumsum([0, 512, 512, 512, 512, 512, 64, 768, 768, 768, 6])
    aq, ak, av, mcq, mckv, mkr, fq, fk, fv, fg = [win[:, offs[i]:offs[i + 1]] for i in range(10)]
    pd = swap_perm_diff()
    pr = swap_perm_rope64()
    cols = []
    q = aq[:, r * 256:(r + 1) * 256]
    k = ak[:, r * 256:(r + 1) * 256]
    def swp(m):
        return np.concatenate([m[:, h * 128:(h + 1) * 128][:, pd] for h in range(2)], 1)
    cols += [q, swp(q), k, swp(k), av[:, r * 256:(r + 1) * 256], mcq, mckv, mkr, mkr[:, pr],
             fq[:, r * 384:(r + 1) * 384], fk[:, r * 384:(r + 1) * 384], fv[:, r * 384:(r + 1) * 384], fg[:, r * 3:(r + 1) * 3]]
    W1 = np.ascontiguousarray(np.concatenate(cols, 1))
    uq = w["mla_w_uq"][l]
    ukv = w["mla_w_ukv"][l]
    hs = [3 * r + i for i in range(3)]
    U1 = np.concatenate([uq[:, h * 192:h * 192 + 128] for h in hs] + [uq[:, h * 192 + 128:h * 192 + 192] for h in hs]
                        + [uq[:, h * 192 + 128:h * 192 + 192][:, pr] for h in hs], 1)
    U2 = np.concatenate([ukv[:, h * 256:h * 256 + 128] for h in hs] + [ukv[:, h * 256 + 128:h * 256 + 256] for h in hs], 1)
    fb = np.zeros(4, np.float32)
    fb[:3] = w["fox_forget_bias"][l][3 * r:3 * r + 3]
    lam_init = 0.8 - 0.6 * math.exp(-0.3 * l)
    return {
        "W1": W1, "U1": np.ascontiguousarray(U1), "U2": np.ascontiguousarray(U2),
        "attn_norm": np.ascontiguousarray(w["attn_norm"][l]), "qn": np.ascontiguousarray(w["mla_q_norm"][l]),
        "kvn": np.ascontiguousarray(w["mla_kv_norm"][l]), "fb": fb,
        "dl": np.ascontiguousarray(w["diff_lambda"][l].reshape(-1)),
        "lc": np.array([lam_init, 1.0 - lam_init, 0, 0], np.float32),
        "idf": np.eye(128, dtype=np.float32), "idb": np.eye(128, dtype=np.float32).astype(bf16),
        "mask": np.triu(np.ones((128, 128), np.float32)).astype(bf16),
        "rc": rope_consts(),
    }


class _NCP:
    def __init__(self, nc, prefix):
        self._nc = nc
        self._p = prefix

    def sbuf_tensor(self, name, *a, **k):
        return self._nc.sbuf_tensor(self._p + name, *a, **k)

    def psum_tensor(self, name, *a, **k):
        return self._nc.psum_tensor(self._p + name, *a, **k)

    def __getattr__(self, n):
        return getattr(self._nc, n)


def body_k0(nc, P, x_ap, xn_ap, ntiles):
    with ExitStack() as st:
        P.stack = st
        xt = [P.sb(f"xt{i}", [128, D], F32) for i in range(2)]
        ot = [P.sb(f"ot{i}", [128, D], BF16) for i in range(2)]
        ss = [P.sb(f"ss{i}", [128, 1], F32) for i in range(2)]
        sd = [P.sb(f"sd{i}", [128, 1], F32) for i in range(2)]
        rs = [P.sb(f"rs{i}", [128, 1], F32) for i in range(2)]
        eps = P.sb("eps", [128, 1], F32)
        P.op("gpsimd", lambda e: e.memset(eps[:, :], 1e-6), writes=[eps])
        outd = P.wrap("outd", xn_ap)
        for i in range(ntiles):
            b = i % 2
            P.dma(xt[b][:, :], x_ap[i * 128:(i + 1) * 128, :], writes=[xt[b]])
            P.op("scalar", lambda e, b=b: e.activation(out=ot[b][:, :], in_=xt[b][:, :], func=AF.Square,
                                                         accum_out=ss[b][:, :]), reads=[xt[b]], writes=[ot[b], ss[b]])
            P.op("scalar", lambda e, b=b: e.activation(out=sd[b][:, :], in_=ss[b][:, :], func=AF.Sqrt,
                                                         scale=1.0 / D, bias=eps[:, :]), reads=[ss[b], eps], writes=[sd[b]])
            P.op("vector", lambda e, b=b: e.reciprocal(out=rs[b][:, :], in_=sd[b][:, :]), reads=[sd[b]], writes=[rs[b]])
            P.op("vector", lambda e, b=b: e.tensor_scalar(out=ot[b][:, :], in0=xt[b][:, :], scalar1=rs[b][:, :],
                                                            scalar2=None, op0=ALU.mult), reads=[xt[b], rs[b]], writes=[ot[b]])
            P.dma(xn_ap[i * 128:(i + 1) * 128, :], ot[b][:, :], reads=[ot[b]], writes=[outd])
        P.drain_all()


def _mix_loc(k):
    if k < 4:
        return k // 2, k % 2
    if k < 10:
        return (k - 4) // 3, 2 + (k - 4) % 3
    return (k - 10) // 3, 5 + (k - 10) % 3


def build_fused(depth=4):
    nc0 = bass.Bass("TRN2", target_bir_lowering=False)
    dt = nc0.dram_tensor

    def ext(name, shape, dtype):
        return dt(name, list(shape), dtype, kind="ExternalInput").ap()

    L = depth
    x = ext("x", [S, D], F32)
    pos = ext("pos", [S], I32)
    W1 = ext("W1", [L, 2, D, NC1], F32)
    U1 = ext("U1", [L, 2, 512, 768], F32)
    U2 = ext("U2", [L, 2, 512, 768], F32)
    attn_norm = ext("attn_norm", [L, D], F32)
    qn = ext("qn", [L, 512], F32)
    kvn = ext("kvn", [L, 512], F32)
    fb = ext("fb", [L, 2, 4], F32)
    dl = ext("dl", [L, 256], F32)
    lc = ext("lc", [L, 4], F32)
    w_o = ext("w_o", [L, D, D], F32)
    w_up = ext("w_up", [L, D, 2 * DFF], F32)
    conv_w = ext("conv_w", [L, 3, 2 * DFF], F32)
    conv_b = ext("conv_b", [L, 2 * DFF], F32)
    w_down = ext("w_down", [L, DFF, D], F32)
    ffn_norm = ext("ffn_norm", [L, D], F32)
    dnorm = ext("dnorm", [L, 128], F32)
    fnorm = ext("fnorm", [D], F32)
    idf = ext("idf", [128, 128], F32)
    idb = ext("idb", [128, 128], BF16)
    mask = ext("mask", [128, 128], BF16)
    rc = ext("rc", [128, 8], F32)
    out = dt("out", [S, D], F32, kind="ExternalOutput").ap()
    XS = [dt(f"xs_scr{i}", [S, D], F32).ap() for i in range(2)]
    XN = [dt(f"xn_scr{i}", [S, D], BF16).ap() for i in range(2)]
    MIX = [dt(f"mix_scr{r}", [1024, S], BF16).ap() for r in range(2)]
    TABS = dt("rope_tabs", [4, 128, S], BF16).ap()

    with ExitStack() as st:
        P = Prog(nc0, st)
        cnt = [0]

        def scoped():
            cnt[0] += 1
            P.nc = _NCP(nc0, f"b{cnt[0]}_")
            return P.nc

        body_k0(scoped(), P, x, XN[0], 16)
        for l in range(L):
            x_src = x if l == 0 else XS[l % 2]
            x_dst = XS[(l + 1) % 2]
            xn_src = XN[l % 2]
            xn_dst = XN[(l + 1) % 2]
            for r in range(2):
                body_k1(scoped(), P, {
                    "xn": xn_src, "pos": pos, "W1": W1[l, r], "U1": U1[l, r], "U2": U2[l, r],
                    "attn_norm": attn_norm[l], "qn": qn[l], "kvn": kvn[l], "fb": fb[l, r], "dl": dl[l], "lc": lc[l],
                    "idf": idf, "idb": idb, "mask": mask, "rc": rc, "mixT": MIX[r],
                    "tabs": TABS, "tabs_mode": "compute_store" if (l == 0 and r == 0) else "load"})
            for hf in range(2):
                t0 = hf * 1024

                def mix_main(k, t0=t0):
                    r_, c_ = _mix_loc(k)
                    return MIX[r_][c_ * 128:(c_ + 1) * 128, t0:t0 + 1024]

                def mix_halo(k, t0=t0):
                    r_, c_ = _mix_loc(k)
                    return MIX[r_][c_ * 128:(c_ + 1) * 128, t0 - 2:t0]

                body_k2(scoped(), P, {
                    "x_main": x_src[t0:t0 + 1024, :], "x_halo": None if hf == 0 else x_src[t0 - 2:t0, :],
                    "mix_main": mix_main, "mix_halo": mix_halo,
                    "w_o": w_o[l], "w_up": w_up[l], "conv_w": conv_w[l], "conv_b": conv_b[l], "w_down": w_down[l],
                    "ffn_norm": ffn_norm[l], "dnorm": dnorm[l], "fnorm": fnorm, "lc": lc[l], "idf": idf, "idb": idb,
                    "x_out": x_dst[t0:t0 + 1024, :], "xn_out": xn_dst[t0:t0 + 1024, :],
                    "fin": out[t0:t0 + 1024, :] if l == L - 1 else None})
        P.nc = nc0
        P.stack = st
        OUTB = P.wrap("OUTB", out)
        P.finish([OUTB])
        P.emit()
    return nc0


from concourse.bass_utils import run_bass_kernel_spmd

_PROG = {}


def kernel(**inputs):
    w = {k: np.asarray(v) for k, v in inputs.items()}
    x = np.ascontiguousarray(w["x"], dtype=np.float32)
    pos = np.ascontiguousarray(w["positions"]).astype(np.int32)
    L = w["w_in"].shape[0]
    if "f" not in _PROG:
        _PROG["f"] = build_fused(L)
    packs = [[pack_k1(l, r, w) for r in range(2)] for l in range(L)]

    def st2(key):
        return np.ascontiguousarray(np.stack([np.stack([packs[l][r][key] for r in range(2)], 0) for l in range(L)], 0))

    def st1(key):
        return np.ascontiguousarray(np.stack([packs[l][0][key] for l in range(L)], 0))

    shared = {
        "W1": st2("W1"), "U1": st2("U1"), "U2": st2("U2"), "fb": st2("fb"),
        "attn_norm": st1("attn_norm"), "qn": st1("qn"), "kvn": st1("kvn"), "dl": st1("dl"), "lc": st1("lc"),
        "w_o": np.ascontiguousarray(w["w_o"]), "w_up": np.ascontiguousarray(w["ffn_w_up"]),
        "conv_w": np.ascontiguousarray(w["ffn_conv_w"]), "conv_b": np.ascontiguousarray(w["ffn_conv_b"]),
        "w_down": np.ascontiguousarray(w["ffn_w_down"]), "ffn_norm": np.ascontiguousarray(w["ffn_norm"]),
        "dnorm": np.ascontiguousarray(w["diff_out_norm"]), "fnorm": np.ascontiguousarray(w["final_norm"]),
        "idf": packs[0][0]["idf"], "idb": packs[0][0]["idb"], "mask": packs[0][0]["mask"], "rc": packs[0][0]["rc"],
    }
    cores = list(range(8))
    ins = []
    for c in cores:
        d = dict(shared)
        d["x"] = np.ascontiguousarray(x[c // 2])
        d["pos"] = np.ascontiguousarray(pos[c // 2])
        ins.append(d)
    res = run_bass_kernel_spmd(_PROG["f"], ins, core_ids=cores)
    outs = [np.asarray(res.results[2 * b]["out"]) for b in range(4)]
    return np.stack(outs, 0).astype(np.float32)
```

```python
import math
import ml_dtypes
from contextlib import ExitStack
import numpy as np
import concourse.bass as bass
import concourse.mybir as mybir

F32 = mybir.dt.float32
BF16 = mybir.dt.bfloat16
I32 = mybir.dt.int32
AF = mybir.ActivationFunctionType
ALU = mybir.AluOpType
AX = mybir.AxisListType

ENGS = ("sync", "scalar", "vector", "gpsimd", "tensor")
EPOCH = 20000


class Buf:
    def __init__(self, prog, name, t):
        self.prog = prog
        self.name = name
        self.t = t
        self.writes = {}
        self.reads = {}
        self.dsem = None
        self.dcount = 0
        self.lock = None

    def __getitem__(self, idx):
        return self.t[idx]


class Prog:
    def __init__(self, nc, stack, n_eng_sems=6):
        self.nc = nc
        self.stack = stack
        self.sem_stack = stack
        self.free_dsems = []
        self.live_dbufs = []
        self.ops = {e: [] for e in ENGS}
        self.semtab = []
        self.eng_sems = {}
        self.eng_epoch = {e: 0 for e in ENGS}
        self.eng_cnt = {e: 0 for e in ENGS}
        self.waited = {e: {} for e in ENGS}
        for e in ENGS:
            self.eng_sems[e] = [self._new_sem(f"s_{e}_{i}") for i in range(n_eng_sems)]
        self.dma_sems = []
        self.nbuf = 0

    def _new_sem(self, name):
        h = self.sem_stack.enter_context(self.nc.semaphore(name))
        self.semtab.append(h)
        return len(self.semtab) - 1

    def _dsem_for(self, owner):
        if owner.dsem is None:
            if self.free_dsems:
                owner.dsem, owner.dcount = self.free_dsems.pop()
            else:
                owner.dsem = self._new_sem(f"d{len(self.semtab)}")
                owner.dcount = 0
            self.live_dbufs.append(owner)

    def sb(self, name, shape, dtype):
        t = self.stack.enter_context(self.nc.sbuf_tensor(name, list(shape), dtype))
        return Buf(self, name, t)

    def ps(self, name, shape, dtype=F32):
        t = self.stack.enter_context(self.nc.psum_tensor(name, list(shape), dtype))
        b = Buf(self, name, t)
        b.lock = Buf(self, name + "_lock", None)
        return b

    def wrap(self, name, t, lock=None):
        b = Buf(self, name, t)
        b.lock = lock
        return b

    def _locks(self, reads, writes):
        ls = []
        for b in list(reads) + list(writes):
            if b.lock is not None and b.lock not in ls:
                ls.append(b.lock)
        return ls

    def _need(self, eng, reads, writes):
        need = {}
        for b in reads:
            for s, v in b.writes.items():
                if need.get(s, 0) < v:
                    need[s] = v
        for b in list(writes) + self._locks(reads, writes):
            for d in (b.writes, b.reads):
                for s, v in d.items():
                    if need.get(s, 0) < v:
                        need[s] = v
        if eng == "tensor":
            own = set(self.eng_sems["tensor"])
            need = {s: v for s, v in need.items() if s not in own}
        out = []
        w = self.waited[eng]
        for s, v in need.items():
            if w.get(s, 0) < v:
                w[s] = v
                out.append((s, v))
        return out

    def _emit_waits(self, eng, waits):
        for s, v in waits:
            h = self.semtab[s]
            self.ops[eng].append(lambda e, h=h, v=v: e.wait_ge(h, v))

    def _next_event(self, eng):
        if self.eng_cnt[eng] >= EPOCH:
            self.eng_epoch[eng] += 1
            self.eng_cnt[eng] = 0
        self.eng_cnt[eng] += 1
        s = self.eng_sems[eng][self.eng_epoch[eng]]
        return s, self.eng_cnt[eng]

    def op(self, eng, fn, reads=(), writes=()):
        waits = self._need(eng, reads, writes)
        self._emit_waits(eng, waits)
        s, v = self._next_event(eng)
        h = self.semtab[s]
        self.ops[eng].append(lambda e, fn=fn, h=h: fn(e).then_inc(h, 1))
        for b in list(writes) + self._locks(reads, writes):
            b.writes = {s: v}
            b.reads = {}
        for b in reads:
            if b in writes:
                continue
            b.reads[s] = max(b.reads.get(s, 0), v)
        return (s, v)

    def dma(self, out_ap, in_ap, reads=(), writes=(), q="sync", **kw):
        waits = self._need(q, reads, writes)
        self._emit_waits(q, waits)
        owner = (list(writes) + list(reads))[0]
        self._dsem_for(owner)
        owner.dcount += 16
        s, v = owner.dsem, owner.dcount
        h = self.semtab[s]
        self.ops[q].append(
            lambda e, o=out_ap, i=in_ap, h=h, kw=kw: e.dma_start(out=o, in_=i, **kw).then_inc(h, 16))
        for b in writes:
            b.writes = {s: v}
            b.reads = {}
        for b in reads:
            if b in writes:
                continue
            b.reads[s] = max(b.reads.get(s, 0), v)
        return (s, v)

    def dma_like(self, q, fn, reads=(), writes=(), inc=16):
        waits = self._need(q, reads, writes)
        self._emit_waits(q, waits)
        owner = (list(writes) + list(reads))[0]
        self._dsem_for(owner)
        owner.dcount += inc
        s, v = owner.dsem, owner.dcount
        h = self.semtab[s]
        self.ops[q].append(lambda e, fn=fn, h=h: fn(e).then_inc(h, inc))
        for b in writes:
            b.writes = {s: v}
            b.reads = {}
        for b in reads:
            if b in writes:
                continue
            b.reads[s] = max(b.reads.get(s, 0), v)
        return (s, v)

    def barrier(self):
        ev = {}
        for e in ENGS:
            if self.eng_cnt[e] > 0:
                ev[self.eng_sems[e][self.eng_epoch[e]]] = self.eng_cnt[e]
        for e in ENGS:
            for s, v in ev.items():
                if self.waited[e].get(s, 0) < v:
                    self.waited[e][s] = v
                    h = self.semtab[s]
                    self.ops[e].append(lambda en, h=h, v=v: en.wait_ge(h, v))

    def drain_all(self):
        ev = {}
        for e in ENGS:
            if self.eng_cnt[e] > 0:
                ev[self.eng_sems[e][self.eng_epoch[e]]] = self.eng_cnt[e]
        for b in self.live_dbufs:
            ev[b.dsem] = max(ev.get(b.dsem, 0), b.dcount)
        for e in ENGS:
            for s, v in ev.items():
                if self.waited[e].get(s, 0) < v:
                    self.waited[e][s] = v
                    h = self.semtab[s]
                    self.ops[e].append(lambda en, h=h, v=v: en.wait_ge(h, v))
        for b in self.live_dbufs:
            self.free_dsems.append((b.dsem, b.dcount))
            b.dsem = None
            b.dcount = 0
            b.writes = {}
            b.reads = {}
        self.live_dbufs = []

    def finish(self, bufs):
        need = {}
        for b in bufs:
            for d in (b.writes, b.reads):
                for s, v in d.items():
                    need[s] = max(need.get(s, 0), v)
        for e in ENGS:
            if e != "sync" and self.eng_cnt[e] > 0:
                s = self.eng_sems[e][self.eng_epoch[e]]
                need[s] = max(need.get(s, 0), self.eng_cnt[e])
        for s, v in need.items():
            h = self.semtab[s]
            self.ops["sync"].append(lambda en, h=h, v=v: en.wait_ge(h, v))

    def emit(self):
        nc = self.nc
        with nc.Block() as block:
            @block.sync
            def _(e):
                for f in self.ops["sync"]:
                    f(e)

            @block.scalar
            def _(e):
                for f in self.ops["scalar"]:
                    f(e)

            @block.vector
            def _(e):
                for f in self.ops["vector"]:
                    f(e)

            @block.gpsimd
            def _(e):
                for f in self.ops["gpsimd"]:
                    f(e)

            @block.tensor
            def _(e):
                for f in self.ops["tensor"]:
                    f(e)


D = 2048
S = 2048
NT = 16
NC1 = 3587
C_Q, C_QS, C_K, C_KS, C_V, C_CQ, C_CKV, C_KR, C_KRS, C_FQ, C_FK, C_FV, C_FG = (
    0, 256, 512, 768, 1024, 1280, 1792, 2304, 2368, 2432, 2816, 3200, 3584)
TWO_PI = 2.0 * math.pi


def body_k1(nc, P, io):
    STOP = 99
    xn, pos, W1, U1, U2 = io["xn"], io["pos"], io["W1"], io["U1"], io["U2"]
    attn_norm, qn, kvn, fb, dl, lc = io["attn_norm"], io["qn"], io["kvn"], io["fb"], io["dl"], io["lc"]
    idf, idb, maskd, rc, mixT = io["idf"], io["idb"], io["mask"], io["rc"], io["mixT"]
    tabs, tabs_mode = io.get("tabs"), io.get("tabs_mode", "compute")
    with ExitStack() as st:
        P.stack = st
        HT = P.sb("HT", [128, 16, S], BF16)
        IDF = P.sb("IDF", [128, 128], F32)
        IDB = P.sb("IDB", [128, 128], BF16)
        MASK = P.sb("MASK", [128, 128], BF16)
        RC = P.sb("RC", [128, 8], F32)
        GIN = P.sb("GIN", [128, 16], F32)
        GQ = P.sb("GQ", [128, 4], F32)
        GKV = P.sb("GKV", [128, 4], F32)
        LCB = P.sb("LCB", [128, 4], F32)
        EPS6 = P.sb("EPS6", [128, 1], F32)
        EPS5 = P.sb("EPS5", [128, 1], F32)
        CD = P.sb("CD", [128, S], BF16)
        SD = P.sb("SD", [128, S], BF16)
        CM = P.sb("CM", [128, S], BF16)
        SM = P.sb("SM", [128, S], BF16)
        WB = [P.sb(f"WB{i}", [128, 16, 256], BF16) for i in range(4)]
        STG = [P.sb(f"stg{i}", [128, 512], F32) for i in range(3)]
        PTB = [P.sb(f"PTB{i}", [128, 512], BF16) for i in range(3)]
        ONB = [P.sb(f"ONB{i}", [128, 128], BF16) for i in range(2)]
        _mixh = P.sb("MIXH0", [128, S], BF16)
        MIXH = [_mixh, _mixh]
        RL = [P.sb(f"RL{i}", [128, 1], F32) for i in range(4)]
        ONESB = P.sb("ONESB", [128, 128], BF16)
        ONESF = P.sb("ONESF", [1, 128], F32)
        _t1 = P.sb("T1_0", [128, 512], F32)
        _t2 = P.sb("T2_0", [128, 512], F32)
        T1 = [_t1, _t1]
        T2 = [_t2, _t2]

        PS = [P.ps(f"PS{i}", [128, 512]) for i in range(2)]
        PO = [P.ps(f"PO{i}", [128, 512]) for i in range(4)]
        PM = P.ps("PM", [128, 512])
        PTR = P.ps("PTR", [128, 1024], BF16)
        PROJ = [PM, PS[0], PS[1]]
        OUT = P.wrap("OUT", mixT)

        cnt = {"stg": 0, "cast": 0, "proj": 0, "wb": 0, "ptb": 0, "onb": 0, "rl": 0, "t": 0}

        def stage():
            b = STG[cnt["stg"] % len(STG)]
            cnt["stg"] += 1
            return b

        def cast(out_ap, in_ap, scale_ap, reads, writes):
            eng = "scalar" if cnt["cast"] % 2 == 0 else "gpsimd"
            cnt["cast"] += 1
            if eng == "scalar":
                if scale_ap is None:
                    P.op("scalar", lambda e: e.activation(out=out_ap, in_=in_ap, func=AF.Copy), reads=reads, writes=writes)
                else:
                    P.op("scalar", lambda e: e.activation(out=out_ap, in_=in_ap, func=AF.Copy, scale=scale_ap),
                         reads=reads, writes=writes)
            else:
                if scale_ap is None:
                    P.op("gpsimd", lambda e: e.tensor_copy(out=out_ap, in_=in_ap), reads=reads, writes=writes)
                else:
                    P.op("gpsimd", lambda e: e.tensor_scalar(out=out_ap, in0=in_ap, scalar1=scale_ap, scalar2=1.0,
                                                              op0=ALU.mult, op1=ALU.mult), reads=reads, writes=writes)

        def next_proj():
            b = PROJ[cnt["proj"] % 3]
            cnt["proj"] += 1
            return b

        def vecT(dst, src_ap, n):
            s = stage()
            pm = next_proj()
            P.dma(s[0:n, 0:128], src_ap.rearrange("(k p) -> k p", p=128), writes=[s])
            P.op("tensor", lambda e: e.transpose(out=pm[:, 0:n], in_=s[0:n, 0:128], identity=IDF[0:n, 0:n]),
                 reads=[s, IDF], writes=[pm])
            P.op("vector", lambda e: e.tensor_copy(out=dst[:, 0:n], in_=pm[:, 0:n]), reads=[pm], writes=[dst])

        P.dma(IDF[:, :], idf, writes=[IDF])
        P.dma(IDB[:, :], idb, writes=[IDB])
        P.dma(MASK[:, :], maskd, writes=[MASK])
        P.dma(RC[:, :], rc, writes=[RC])
        P.dma(LCB[:, :], lc.partition_broadcast(128), writes=[LCB])
        P.op("gpsimd", lambda e: e.memset(EPS6[:, :], 1e-6), writes=[EPS6])
        P.op("gpsimd", lambda e: e.memset(EPS5[:, :], 1e-5), writes=[EPS5])
        P.op("gpsimd", lambda e: e.memset(ONESB[:, :], 1.0), writes=[ONESB])
        P.op("gpsimd", lambda e: e.memset(ONESF[:, :], 1.0), writes=[ONESF])
        vecT(GIN, attn_norm, 16)
        vecT(GQ, qn, 4)
        vecT(GKV, kvn, 4)

        if tabs_mode == "load":
            for ti_, tb_ in enumerate((CD, SD, CM, SM)):
                P.dma(tb_[:, :], tabs[ti_], writes=[tb_])
        else:
            with ExitStack() as stR:
                def sbR(name, shape, dtp):
                    return Buf(P, name, stR.enter_context(nc.sbuf_tensor(name, list(shape), dtp)))
                POSI = sbR("POSI", [128, S], I32)
                POSF = sbR("POSF", [128, S], F32)
                Y = sbR("Y", [128, S], F32)
                Y2 = sbR("Y2", [128, S], F32)
                KI = sbR("KI", [128, S], I32)
                KF = sbR("KF", [128, S], F32)
                P.dma(POSI[:, :], pos.partition_broadcast(128), writes=[POSI])
                P.op("vector", lambda e: e.tensor_copy(out=POSF[:, :], in_=POSI[:, :]), reads=[POSI], writes=[POSF])

                def sincos(invf_col, sin_dst, sin_mul_col, cos_dst, cos_mul_col, cos_add_col):
                    P.op("vector", lambda e: e.tensor_scalar(out=Y[:, :], in0=POSF[:, :], scalar1=RC[:, invf_col:invf_col + 1],
                                                              scalar2=1.0 / TWO_PI, op0=ALU.mult, op1=ALU.mult),
                         reads=[POSF, RC], writes=[Y])
                    for shift, dst, mulc, addc in ((0.0, sin_dst, sin_mul_col, None), (0.25, cos_dst, cos_mul_col, cos_add_col)):
                        P.op("vector", lambda e, shift=shift: e.tensor_scalar(out=Y2[:, :], in0=Y[:, :], scalar1=shift, scalar2=None,
                                                                               op0=ALU.add), reads=[Y], writes=[Y2])
                        P.op("vector", lambda e: e.tensor_copy(out=KI[:, :], in_=Y2[:, :]), reads=[Y2], writes=[KI])
                        P.op("vector", lambda e: e.tensor_copy(out=KF[:, :], in_=KI[:, :]), reads=[KI], writes=[KF])
                        P.op("vector", lambda e: e.tensor_tensor(out=Y2[:, :], in0=Y2[:, :], in1=KF[:, :], op=ALU.subtract),
                             reads=[Y2, KF], writes=[Y2])
                        P.op("vector", lambda e: e.tensor_scalar(out=KF[:, :], in0=Y2[:, :], scalar1=0.5, scalar2=None, op0=ALU.is_gt),
                             reads=[Y2], writes=[KF])
                        P.op("vector", lambda e: e.tensor_tensor(out=Y2[:, :], in0=Y2[:, :], in1=KF[:, :], op=ALU.subtract),
                             reads=[Y2, KF], writes=[Y2])
                        P.op("vector", lambda e: e.tensor_scalar(out=KF[:, :], in0=Y2[:, :], scalar1=-0.5, scalar2=None, op0=ALU.is_lt),
                             reads=[Y2], writes=[KF])
                        P.op("vector", lambda e: e.tensor_tensor(out=Y2[:, :], in0=Y2[:, :], in1=KF[:, :], op=ALU.add),
                             reads=[Y2, KF], writes=[Y2])
                        P.op("scalar", lambda e: e.activation(out=KF[:, :], in_=Y2[:, :], func=AF.Sin, scale=TWO_PI),
                             reads=[Y2], writes=[KF])
                        if addc is None:
                            P.op("vector", lambda e, dst=dst, mulc=mulc: e.tensor_scalar(
                                out=dst[:, :], in0=KF[:, :], scalar1=RC[:, mulc:mulc + 1], scalar2=None, op0=ALU.mult),
                                reads=[KF, RC], writes=[dst])
                        else:
                            P.op("vector", lambda e, dst=dst, mulc=mulc, addc=addc: e.tensor_scalar(
                                out=dst[:, :], in0=KF[:, :], scalar1=RC[:, mulc:mulc + 1], scalar2=RC[:, addc:addc + 1],
                                op0=ALU.mult, op1=ALU.add), reads=[KF, RC], writes=[dst])

                sincos(0, SD, 2, CD, 1, 5)
                sincos(3, SM, 4, CM, 6, 7)
                if tabs_mode == "compute_store":
                    TABS = P.wrap("TABS", tabs)
                    for ti_, tb_ in enumerate((CD, SD, CM, SM)):
                        P.dma(tabs[ti_], tb_[:, :], reads=[tb_], writes=[TABS])
                P.barrier()

        if STOP == 1:
            P.dma(mixT[0:128, :], MIXH[0][:, :], reads=[MIXH[0]], writes=[OUT])
            P.finish([OUT])
            P.emit()
            return nc
        def load_w1(col0, ncols):
            wb = WB[cnt["wb"] % 4]
            cnt["wb"] += 1
            for k in range(16):
                s = stage()
                P.dma(s[:, 0:ncols], W1[k * 128:(k + 1) * 128, col0:col0 + ncols], writes=[s])
                cast(wb[:, k, 0:ncols], s[:, 0:ncols], GIN[:, k:k + 1], reads=[s, GIN], writes=[wb])
            return wb

        def load_u(U, col0, ncols, G):
            wb = WB[cnt["wb"] % 4]
            cnt["wb"] += 1
            for r in range(4):
                s = stage()
                P.dma(s[:, 0:ncols], U[r * 128:(r + 1) * 128, col0:col0 + ncols], writes=[s])
                cast(wb[:, r, 0:ncols], s[:, 0:ncols], G[:, r:r + 1], reads=[s, G], writes=[wb])
            return wb

        LOADS = [
            lambda: load_w1(C_Q, 256), lambda: load_w1(C_QS, 256), lambda: load_w1(C_K, 256), lambda: load_w1(C_KS, 256),
            lambda: load_w1(C_V, 256),
            lambda: load_w1(C_CQ, 256), lambda: load_w1(C_CQ + 256, 256),
            lambda: load_u(U1, 0, 128, GQ), lambda: load_u(U1, 128, 128, GQ), lambda: load_u(U1, 256, 128, GQ),
            lambda: load_u(U1, 384, 64, GQ), lambda: load_u(U1, 576, 64, GQ),
            lambda: load_u(U1, 448, 64, GQ), lambda: load_u(U1, 640, 64, GQ),
            lambda: load_u(U1, 512, 64, GQ), lambda: load_u(U1, 704, 64, GQ),
            lambda: load_w1(C_CKV, 256), lambda: load_w1(C_CKV + 256, 256),
            lambda: load_u(U2, 0, 128, GKV), lambda: load_u(U2, 128, 128, GKV), lambda: load_u(U2, 256, 128, GKV),
            lambda: load_u(U2, 384, 256, GKV), lambda: load_u(U2, 640, 128, GKV),
            lambda: load_w1(C_KR, 128),
            lambda: load_w1(C_FQ, 256), lambda: load_w1(C_FQ + 256, 128),
            lambda: load_w1(C_FK, 256), lambda: load_w1(C_FK + 256, 128),
            lambda: load_w1(C_FV, 256), lambda: load_w1(C_FV + 256, 128),
            lambda: load_w1(C_FG, 3),
        ]
        issued = []
        consumed = [0]

        def nxt():
            while len(issued) < min(len(LOADS), consumed[0] + 3):
                issued.append(LOADS[len(issued)]())
            wb = issued[consumed[0]]
            consumed[0] += 1
            return wb

        def prefetch(nahead):
            while len(issued) < min(len(LOADS), consumed[0] + nahead):
                issued.append(LOADS[len(issued)]())

        prefetch(3)
        with ExitStack() as stX:
            XT = [Buf(P, f"XT{i}", stX.enter_context(nc.sbuf_tensor(f"XT{i}", [128, D], BF16))) for i in range(2)]
            for t in range(NT):
                xt = XT[t % 2]
                P.dma(xt[:, :], xn[t * 128:(t + 1) * 128, :], writes=[xt])
                for g in range(2):
                    for kk in range(8):
                        k = g * 8 + kk
                        P.op("tensor", lambda e, k=k, kk=kk, xt=xt: e.transpose(
                            out=PTR[:, kk * 128:(kk + 1) * 128], in_=xt[:, k * 128:(k + 1) * 128], identity=IDB[:, :]),
                            reads=[xt, IDB], writes=[PTR])
                    src = PTR[:, :].rearrange("p (k t) -> p k t", t=128)
                    dst = HT[:, g * 8:(g + 1) * 8, t * 128:(t + 1) * 128]
                    if g == 0:
                        P.op("vector", lambda e, src=src, dst=dst: e.tensor_copy(out=dst, in_=src), reads=[PTR], writes=[HT])
                    else:
                        P.op("scalar", lambda e, src=src, dst=dst: e.activation(out=dst, in_=src, func=AF.Copy),
                             reads=[PTR], writes=[HT])
            P.barrier()

        if STOP == 2:
            P.dma(mixT[0:128, :], MIXH[0][:, :], reads=[MIXH[0]], writes=[OUT])
            P.finish([OUT])
            P.emit()
            return nc
        def proj_fm(wb, c0, m, src, nk, r, pm):
            for k in range(nk):
                P.op("tensor", lambda e, k=k: e.matmul(pm[0:m, :], lhsT=wb[:, k, c0:c0 + m],
                                                        rhs=src[:, k, r * 512:(r + 1) * 512],
                                                        start=(k == 0), stop=(k == nk - 1)),
                     reads=[wb, src], writes=[pm])

        def proj_tm(wb, c0, n, src, nk, j, pm):
            for k in range(nk):
                P.op("tensor", lambda e, k=k: e.matmul(pm[:, 0:n], lhsT=src[:, k, j * 128:(j + 1) * 128],
                                                        rhs=wb[:, k, c0:c0 + n], start=(k == 0), stop=(k == nk - 1)),
                     reads=[wb, src], writes=[pm])

        def rope_fm(wb, c_main, c_swap, m, src, nk, dst, Ct, St):
            for r in range(4):
                cs = slice(r * 512, (r + 1) * 512)
                i = cnt["t"] % 2
                cnt["t"] += 1
                p1 = next_proj()
                proj_fm(wb, c_main, m, src, nk, r, p1)
                P.op("vector", lambda e, p1=p1, i=i, cs=cs: e.tensor_tensor(out=T1[i][0:m, :], in0=p1[0:m, :], in1=Ct[0:m, cs],
                                                                             op=ALU.mult), reads=[p1, Ct], writes=[T1[i]])
                p2 = next_proj()
                proj_fm(wb, c_swap, m, src, nk, r, p2)
                P.op("vector", lambda e, p2=p2, i=i, cs=cs: e.tensor_tensor(out=T2[i][0:m, :], in0=p2[0:m, :], in1=St[0:m, cs],
                                                                             op=ALU.mult), reads=[p2, St], writes=[T2[i]])
                P.op("gpsimd", lambda e, i=i, cs=cs: e.tensor_tensor(out=dst[0:m, cs], in0=T1[i][0:m, :], in1=T2[i][0:m, :],
                                                                      op=ALU.add), reads=[T1[i], T2[i]], writes=[dst])

        def plain_fm(wb, c0, m, src, nk, dst):
            for r in range(4):
                cs = slice(r * 512, (r + 1) * 512)
                p1 = next_proj()
                proj_fm(wb, c0, m, src, nk, r, p1)
                if r % 2 == 0:
                    P.op("vector", lambda e, p1=p1, cs=cs: e.tensor_copy(out=dst[0:m, cs], in_=p1[0:m, :]), reads=[p1], writes=[dst])
                else:
                    P.op("scalar", lambda e, p1=p1, cs=cs: e.activation(out=dst[0:m, cs], in_=p1[0:m, :], func=AF.Copy),
                         reads=[p1], writes=[dst])

        def v_tm(wb, c0, nheads, src, nk, Vs):
            n = nheads * 128
            for j in range(NT):
                pm = next_proj()
                proj_tm(wb, c0, n, src, nk, j, pm)
                for h in range(nheads):
                    if (j + h) % 2 == 0:
                        P.op("vector", lambda e, pm=pm, h=h, j=j: e.tensor_copy(out=Vs[h][:, j, 0:128], in_=pm[:, h * 128:(h + 1) * 128]),
                             reads=[pm], writes=[Vs[h]])
                    else:
                        P.op("scalar", lambda e, pm=pm, h=h, j=j: e.activation(out=Vs[h][:, j, 0:128], in_=pm[:, h * 128:(h + 1) * 128],
                                                                                func=AF.Copy), reads=[pm], writes=[Vs[h]])

        def new_v(alloc, name):
            v = alloc(name, [128, NT, 136], BF16)
            P.op("gpsimd", lambda e: e.memset(v[:, :, :], 1.0), writes=[v])
            return v

        sidx_box = [0]

        def attention_chunk(c, kparts, qparts, scale, Vp, post, bias=None, extra=None, bias_tiles=None):
            info = {}

            def qk_exp(j):
                tlo = max(4 * c, j)
                t0 = tlo * 128
                n = (4 * c + 4) * 128 - t0
                sidx = sidx_box[0]
                sidx_box[0] += 1
                ps = PS[sidx % 2]
                ptb = PTB[sidx % 3]
                info[j] = (tlo, ptb)
                nparts = len(kparts)
                for i in range(nparts):
                    kb, kp0, kp1 = kparts[i]
                    qb, qp0, qp1 = qparts[i]
                    P.op("tensor", lambda e, i=i, kb=kb, kp0=kp0, kp1=kp1, qb=qb, qp0=qp0, qp1=qp1: e.matmul(
                        ps[:, 0:n], lhsT=kb[kp0:kp1, j * 128:(j + 1) * 128], rhs=qb[qp0:qp1, t0:t0 + n],
                        start=(i == 0), stop=(i == nparts - 1)), reads=[kb, qb], writes=[ps])
                if bias_tiles is not None:
                    for ti in range(tlo, 4 * c + 4):
                        off = (ti - tlo) * 128
                        P.op("scalar", lambda e, off=off, ti=ti: e.activation(
                            out=ptb[:, off:off + 128], in_=ps[:, off:off + 128], func=AF.Exp, scale=scale,
                            bias=bias_tiles[:, j, ti:ti + 1]), reads=[ps, bias_tiles], writes=[ptb])
                elif bias is None:
                    P.op("scalar", lambda e: e.activation(out=ptb[:, 0:n], in_=ps[:, 0:n], func=AF.Exp, scale=scale),
                         reads=[ps], writes=[ptb])
                else:
                    P.op("scalar", lambda e: e.activation(out=ptb[:, 0:n], in_=ps[:, 0:n], func=AF.Exp, scale=scale,
                                                          bias=bias[:, j:j + 1]), reads=[ps, bias], writes=[ptb])
                if j >= 4 * c:
                    P.op("gpsimd", lambda e: e.tensor_tensor(out=ptb[:, 0:128], in0=ptb[:, 0:128], in1=MASK[:, :], op=ALU.mult),
                         reads=[ptb, MASK], writes=[ptb])

            def pv(j):
                tlo, ptb = info[j]
                for ti in range(tlo, 4 * c + 4):
                    po = PO[ti - 4 * c]
                    off = (ti - tlo) * 128
                    P.op("tensor", lambda e, po=po, off=off, ti=ti: e.matmul(
                        po[:, 0:129], lhsT=ptb[:, off:off + 128], rhs=Vp[:, j, 0:129], start=(j == 0), stop=(j == ti)),
                        reads=[ptb, Vp], writes=[po])

            nj = 4 * c + 4
            qk_exp(0)
            for j in range(nj):
                if j + 1 < nj:
                    qk_exp(j + 1)
                pv(j)
            for ti in range(4 * c, 4 * c + 4):
                post(ti, PO[ti - 4 * c])

        def attention(kparts, qparts, scale, Vp, post, bias=None, extra=None, bias_tiles=None):
            for c in range(4):
                attention_chunk(c, kparts, qparts, scale, Vp, post, bias=bias, extra=extra, bias_tiles=bias_tiles)

        def finish_tile(on_src_fn, ti, mixh, tr_slot):
            onb = ONB[cnt["onb"] % 2]
            cnt["onb"] += 1
            on_src_fn(onb)
            sl = slice(tr_slot * 128, (tr_slot + 1) * 128)
            P.op("tensor", lambda e, onb=onb, sl=sl: e.transpose(out=PTR[:, sl], in_=onb[:, :], identity=IDB[:, :]),
                 reads=[onb, IDB], writes=[PTR])
            P.op("vector", lambda e, sl=sl, ti=ti: e.tensor_copy(out=mixh[:, ti * 128:(ti + 1) * 128], in_=PTR[:, sl]),
                 reads=[PTR], writes=[mixh])

        def std_post(mixh):
            def post(ti, po):
                rl = RL[cnt["rl"] % 4]
                cnt["rl"] += 1
                P.op("vector", lambda e: e.reciprocal(out=rl[:, :], in_=po[:, 128:129]), reads=[po], writes=[rl])

                def w(onb):
                    P.op("vector", lambda e: e.tensor_scalar(out=onb[:, :], in0=po[:, 0:128], scalar1=rl[:, :], scalar2=None,
                                                              op0=ALU.mult), reads=[po, rl], writes=[onb])
                finish_tile(w, ti, mixh, ti % 8)
            return post

        mix_i = [0]

        def store_head(mixh, row0):
            P.dma(mixT[row0:row0 + 128, :], mixh[:, :], reads=[mixh], writes=[OUT])

        with ExitStack() as stD:
            def sbD(name, shape, dtp):
                return Buf(P, name, stD.enter_context(nc.sbuf_tensor(name, list(shape), dtp)))
            DL = sbD("DL", [128, 256], F32)
            DJ = sbD("DJ", [128, 64], F32)
            SL = sbD("SL", [128, 4], F32)
            NEGLAM = sbD("NEGLAM", [128, 1], F32)
            P.dma(DL[:, :], dl.partition_broadcast(128), writes=[DL])
            for i in range(2):
                P.op("vector", lambda e, i=i: e.tensor_tensor(
                    out=DJ[:, :], in0=DL[:, i * 128:i * 128 + 64], in1=DL[:, i * 128 + 64:i * 128 + 128], op=ALU.mult),
                    reads=[DL], writes=[DJ])
                P.op("scalar", lambda e, i=i: e.activation(out=DJ[:, :], in_=DJ[:, :], func=AF.Copy, accum_out=SL[:, i:i + 1]),
                     reads=[DJ], writes=[DJ, SL])
            P.op("scalar", lambda e: e.activation(out=SL[:, 2:4], in_=SL[:, 0:2], func=AF.Exp), reads=[SL], writes=[SL])
            P.op("vector", lambda e: e.tensor_tensor(out=NEGLAM[:, :], in0=SL[:, 3:4], in1=SL[:, 2:3], op=ALU.subtract),
                 reads=[SL], writes=[NEGLAM])
            P.op("vector", lambda e: e.tensor_tensor(out=NEGLAM[:, :], in0=NEGLAM[:, :], in1=LCB[:, 0:1], op=ALU.subtract),
                 reads=[NEGLAM, LCB], writes=[NEGLAM])

            QT = [sbD(f"QTd{h}", [128, S], BF16) for h in range(2)]
            KT = [sbD(f"KTd{h}", [128, S], BF16) for h in range(2)]
            VD = [new_v(sbD, f"VD{h}") for h in range(2)]
            O1N = [sbD(f"O1N{i}", [128, 128], F32) for i in range(4)]
            OD = [sbD(f"OD{i}", [128, 128], F32) for i in range(2)]
            if STOP == 301:
                P.dma(mixT[0:128, :], MIXH[0][:, :], reads=[MIXH[0]], writes=[OUT])
                P.finish([OUT])
                P.emit()
                return nc
            wq = nxt()
            wqs = nxt()
            def rope2(wm, ws, c0, dst):
                for r in range(4):
                    cs = slice(r * 512, (r + 1) * 512)
                    i = cnt["t"] % 2
                    cnt["t"] += 1
                    p1 = next_proj()
                    proj_fm(wm, c0, 128, HT, 16, r, p1)
                    P.op("vector", lambda e, p1=p1, i=i, cs=cs: e.tensor_tensor(out=T1[i][:, :], in0=p1[:, :], in1=CD[:, cs], op=ALU.mult),
                         reads=[p1, CD], writes=[T1[i]])
                    p2 = next_proj()
                    proj_fm(ws, c0, 128, HT, 16, r, p2)
                    P.op("vector", lambda e, p2=p2, i=i, cs=cs: e.tensor_tensor(out=T2[i][:, :], in0=p2[:, :], in1=SD[:, cs], op=ALU.mult),
                         reads=[p2, SD], writes=[T2[i]])
                    P.op("gpsimd", lambda e, i=i, cs=cs: e.tensor_tensor(out=dst[:, cs], in0=T1[i][:, :], in1=T2[i][:, :], op=ALU.add),
                         reads=[T1[i], T2[i]], writes=[dst])
            for h in range(2):
                rope2(wq, wqs, h * 128, QT[h])
            if STOP == 302:
                P.dma(mixT[0:128, :], MIXH[0][:, :], reads=[MIXH[0]], writes=[OUT])
                P.finish([OUT])
                P.emit()
                return nc
            wk = nxt()
            wks = nxt()
            for h in range(2):
                rope2(wk, wks, h * 128, KT[h])
            if STOP == 303:
                P.dma(mixT[0:128, :], MIXH[0][:, :], reads=[MIXH[0]], writes=[OUT])
                P.finish([OUT])
                P.emit()
                return nc
            wv = nxt()
            v_tm(wv, 0, 2, HT, 16, VD)
            if STOP == 31:
                P.dma(mixT[0:128, :], MIXH[0][:, :], reads=[MIXH[0]], writes=[OUT])
                P.finish([OUT])
                P.emit()
                return nc

            for h in range(2):
                mixh = MIXH[mix_i[0] % 2]
                mix_i[0] += 1

                def post1(ti, po):
                    rl = RL[cnt["rl"] % 4]
                    cnt["rl"] += 1
                    P.op("vector", lambda e: e.reciprocal(out=rl[:, :], in_=po[:, 128:129]), reads=[po], writes=[rl])
                    o1 = O1N[ti % 4]
                    P.op("vector", lambda e: e.tensor_scalar(out=o1[:, :], in0=po[:, 0:128], scalar1=rl[:, :], scalar2=None, op0=ALU.mult),
                         reads=[po, rl], writes=[o1])

                def post2(ti, po, mixh=mixh):
                    rl = RL[cnt["rl"] % 4]
                    cnt["rl"] += 1
                    o1 = O1N[ti % 4]
                    od = OD[ti % 2]
                    P.op("vector", lambda e: e.reciprocal(out=rl[:, :], in_=po[:, 128:129]), reads=[po], writes=[rl])
                    P.op("vector", lambda e: e.tensor_scalar(out=od[:, :], in0=po[:, 0:128], scalar1=rl[:, :], scalar2=None, op0=ALU.mult),
                         reads=[po, rl], writes=[od])
                    P.op("vector", lambda e: e.scalar_tensor_tensor(out=od[:, :], in0=od[:, :], scalar=NEGLAM[:, 0:1], in1=o1[:, :],
                                                                     op0=ALU.mult, op1=ALU.add), reads=[od, NEGLAM, o1], writes=[od])
                    ssq = RL[cnt["rl"] % 4]
                    cnt["rl"] += 1
                    P.op("scalar", lambda e: e.activation(out=o1[:, :], in_=od[:, :], func=AF.Square, accum_out=ssq[:, :]),
                         reads=[od], writes=[o1, ssq])
                    P.op("scalar", lambda e: e.activation(out=ssq[:, :], in_=ssq[:, :], func=AF.Sqrt, scale=1.0 / 128, bias=EPS5[:, :]),
                         reads=[ssq, EPS5], writes=[ssq])
                    P.op("vector", lambda e: e.reciprocal(out=ssq[:, :], in_=ssq[:, :]), reads=[ssq], writes=[ssq])

                    def w(onb):
                        P.op("vector", lambda e: e.tensor_scalar(out=onb[:, :], in0=od[:, :], scalar1=ssq[:, :], scalar2=None, op0=ALU.mult),
                             reads=[od, ssq], writes=[onb])
                    finish_tile(w, ti, mixh, ti % 8)

                for c in range(4):
                    attention_chunk(c, [(KT[h], 0, 64)], [(QT[h], 0, 64)], 0.125, VD[h], post1)
                    if STOP == 32:
                        P.dma(mixT[0:128, :], MIXH[0][:, :], reads=[MIXH[0]], writes=[OUT])
                        P.finish([OUT])
                        P.emit()
                        return nc
                    attention_chunk(c, [(KT[h], 64, 128)], [(QT[h], 64, 128)], 0.125, VD[h], post2)
                store_head(mixh, h * 128)
            P.barrier()

        if STOP == 3:
            P.dma(mixT[0:128, :], MIXH[0][:, :], reads=[MIXH[0]], writes=[OUT])
            P.finish([OUT])
            P.emit()
            return nc
        with ExitStack() as stM:
            def sbM(name, shape, dtp):
                return Buf(P, name, stM.enter_context(nc.sbuf_tensor(name, list(shape), dtp)))
            QN = [sbM(f"QNm{h}", [128, S], BF16) for h in range(3)]
            QR = [sbM(f"QRm{h}", [128, S], BF16) for h in range(3)]
            KN = [sbM(f"KNm{h}", [128, S], BF16) for h in range(3)]
            KR = sbM("KRm", [128, S], BF16)
            VM = [new_v(sbM, f"VM{h}") for h in range(3)]
            SQ = [sbM(f"SQ{i}", [128, 512], BF16) for i in range(2)]
            RSTD = sbM("RSTD", [128, 512], F32)

            def latent(col0, dst):
                wl = [nxt(), nxt()]
                for rcn in range(4):
                    for r in range(4):
                        cs = slice(r * 512, (r + 1) * 512)
                        pm = next_proj()
                        sq = SQ[(rcn * 4 + r) % 2]
                        proj_fm(wl[rcn // 2], (rcn % 2) * 128, 128, HT, 16, r, pm)
                        P.op("vector", lambda e, pm=pm, rcn=rcn, cs=cs: e.tensor_copy(out=dst[:, rcn, cs], in_=pm[:, :]), reads=[pm], writes=[dst])
                        P.op("scalar", lambda e, pm=pm, sq=sq: e.activation(out=sq[:, :], in_=pm[:, :], func=AF.Square),
                             reads=[pm], writes=[sq])
                        P.op("tensor", lambda e, rcn=rcn, r=r, sq=sq: e.matmul(PO[r][:, :], lhsT=ONESB[:, :], rhs=sq[:, :],
                                                                             start=(rcn == 0), stop=(rcn == 3)),
                             reads=[ONESB, sq], writes=[PO[r]])
                for r in range(4):
                    cs = slice(r * 512, (r + 1) * 512)
                    P.op("scalar", lambda e, r=r: e.activation(out=RSTD[:, :], in_=PO[r][:, :], func=AF.Sqrt, scale=1.0 / 512,
                                                                bias=EPS6[:, :]), reads=[PO[r], EPS6], writes=[RSTD])
                    P.op("vector", lambda e: e.reciprocal(out=RSTD[:, :], in_=RSTD[:, :]), reads=[RSTD], writes=[RSTD])
                    for rcn in range(4):
                        P.op("gpsimd" if rcn % 2 else "vector", lambda e, rcn=rcn, cs=cs: e.tensor_tensor(
                            out=dst[:, rcn, cs], in0=dst[:, rcn, cs], in1=RSTD[:, :], op=ALU.mult),
                            reads=[dst, RSTD], writes=[dst])

            with ExitStack() as stM1:
                CQN = Buf(P, "CQN", stM1.enter_context(nc.sbuf_tensor("CQN", [128, 4, S], BF16)))
                latent(C_CQ, CQN)
                for h in range(3):
                    wu = nxt()
                    plain_fm(wu, 0, 128, CQN, 4, QN[h])
                for h in range(3):
                    wu = nxt()
                    wus = nxt()
                    for r in range(4):
                        cs = slice(r * 512, (r + 1) * 512)
                        i = cnt["t"] % 2
                        cnt["t"] += 1
                        p1 = next_proj()
                        proj_fm(wu, 0, 64, CQN, 4, r, p1)
                        P.op("vector", lambda e, p1=p1, i=i, cs=cs: e.tensor_tensor(out=T1[i][0:64, :], in0=p1[0:64, :], in1=CM[0:64, cs],
                                                                                     op=ALU.mult), reads=[p1, CM], writes=[T1[i]])
                        p2 = next_proj()
                        proj_fm(wus, 0, 64, CQN, 4, r, p2)
                        P.op("vector", lambda e, p2=p2, i=i, cs=cs: e.tensor_tensor(out=T2[i][0:64, :], in0=p2[0:64, :], in1=SM[0:64, cs],
                                                                                     op=ALU.mult), reads=[p2, SM], writes=[T2[i]])
                        P.op("gpsimd", lambda e, i=i, cs=cs, h=h: e.tensor_tensor(out=QR[h][0:64, cs], in0=T1[i][0:64, :], in1=T2[i][0:64, :],
                                                                                   op=ALU.add), reads=[T1[i], T2[i]], writes=[QR[h]])
                P.barrier()
            with ExitStack() as stM2:
                CKN = Buf(P, "CKN", stM2.enter_context(nc.sbuf_tensor("CKN", [128, 4, S], BF16)))
                latent(C_CKV, CKN)
                for h in range(3):
                    wu = nxt()
                    plain_fm(wu, 0, 128, CKN, 4, KN[h])
                wv1 = nxt()
                v_tm(wv1, 0, 2, CKN, 4, VM[0:2])
                wv2 = nxt()
                v_tm(wv2, 0, 1, CKN, 4, VM[2:3])
                P.barrier()
            wkr = nxt()
            for r in range(4):
                cs = slice(r * 512, (r + 1) * 512)
                i = cnt["t"] % 2
                cnt["t"] += 1
                p1 = next_proj()
                proj_fm(wkr, 0, 64, HT, 16, r, p1)
                P.op("vector", lambda e, p1=p1, i=i, cs=cs: e.tensor_tensor(out=T1[i][0:64, :], in0=p1[0:64, :], in1=CM[0:64, cs], op=ALU.mult),
                     reads=[p1, CM], writes=[T1[i]])
                p2 = next_proj()
                proj_fm(wkr, 64, 64, HT, 16, r, p2)
                P.op("vector", lambda e, p2=p2, i=i, cs=cs: e.tensor_tensor(out=T2[i][0:64, :], in0=p2[0:64, :], in1=SM[0:64, cs], op=ALU.mult),
                     reads=[p2, SM], writes=[T2[i]])
                P.op("gpsimd", lambda e, i=i, cs=cs: e.tensor_tensor(out=KR[0:64, cs], in0=T1[i][0:64, :], in1=T2[i][0:64, :], op=ALU.add),
                     reads=[T1[i], T2[i]], writes=[KR])
            for h in range(3):
                mixh = MIXH[mix_i[0] % 2]
                mix_i[0] += 1
                attention([(KN[h], 0, 128), (KR, 0, 64)], [(QN[h], 0, 128), (QR[h], 0, 64)], 192 ** -0.5, VM[h], std_post(mixh))
                store_head(mixh, 256 + h * 128)
            P.barrier()

        if STOP == 4:
            P.dma(mixT[0:128, :], MIXH[0][:, :], reads=[MIXH[0]], writes=[OUT])
            P.finish([OUT])
            P.emit()
            return nc
        with ExitStack() as stF:
            def sbF(name, shape, dtp):
                return Buf(P, name, stF.enter_context(nc.sbuf_tensor(name, list(shape), dtp)))
            QF = [sbF(f"QF{h}", [128, S], BF16) for h in range(3)]
            KF_ = [sbF(f"KF{h}", [128, S], BF16) for h in range(3)]
            VF = [new_v(sbF, f"VF{h}") for h in range(3)]
            NFB = sbF("NFB", [3, 1], F32)
            ONE3 = sbF("ONE3", [3, 1], F32)
            ONEROW = sbF("ONEROW", [3, S], F32)
            GL = sbF("GL", [3, S], F32)
            CL = sbF("CL", [3, S], F32)
            NBALL = sbF("NBALL", [128, NT * 3], F32)
            R1 = sbF("R1", [128, NT * 3], F32)
            NB3 = [sbF(f"NB3_{i}", [128, NT * 3], BF16) for i in range(3)]
            E0 = sbF("E0", [128, 128], BF16)
            CLB = sbF("CLB", [128, NT * 3], F32)
            BI = [sbF(f"BI{h}", [128, NT, NT], F32) for h in range(3)]
            P.dma(NFB[:, :], fb[0:3].rearrange("(p o) -> p o", o=1), writes=[NFB])
            P.op("vector", lambda e: e.tensor_scalar(out=NFB[:, :], in0=NFB[:, :], scalar1=-1.0, scalar2=None, op0=ALU.mult),
                 reads=[NFB], writes=[NFB])
            P.op("gpsimd", lambda e: e.memset(ONEROW[:, :], 1.0), writes=[ONEROW])
            P.op("gpsimd", lambda e: e.memset(ONE3[:, :], 1.0), writes=[ONE3])
            P.op("gpsimd", lambda e: e.memset(E0[:, :], 0.0), writes=[E0])
            P.op("gpsimd", lambda e: e.memset(E0[0:1, :], 1.0), writes=[E0])
            for (c0, dsts) in ((C_FQ, QF), (C_FK, KF_)):
                wa = nxt()
                wb2 = nxt()
                plain_fm(wa, 0, 128, HT, 16, dsts[0])
                plain_fm(wa, 128, 128, HT, 16, dsts[1])
                plain_fm(wb2, 0, 128, HT, 16, dsts[2])
            wv1 = nxt()
            v_tm(wv1, 0, 2, HT, 16, VF[0:2])
            wv2 = nxt()
            v_tm(wv2, 0, 1, HT, 16, VF[2:3])
            wg = nxt()
            for r in range(4):
                cs = slice(r * 512, (r + 1) * 512)
                pm = next_proj()
                proj_fm(wg, 0, 3, HT, 16, r, pm)
                P.op("scalar", lambda e, pm=pm, cs=cs: e.activation(out=GL[0:3, cs], in_=pm[0:3, :], func=AF.Exp, scale=-1.0,
                                                                     bias=NFB[0:3, 0:1]), reads=[pm, NFB], writes=[GL])
            P.op("scalar", lambda e: e.activation(out=GL[0:3, :], in_=GL[0:3, :], func=AF.Ln, bias=ONE3[0:3, 0:1]),
                 reads=[GL, ONE3], writes=[GL])
            P.op("vector", lambda e: e.tensor_tensor_scan(out=CL[0:3, :], data0=ONEROW[0:3, :], data1=GL[0:3, :], initial=0.0,
                                                           op0=ALU.mult, op1=ALU.add), reads=[ONEROW, GL], writes=[CL])
            pm = next_proj()
            for j in range(NT):
                P.op("tensor", lambda e, j=j, pm=pm: e.transpose(out=pm[:, j * 3:(j + 1) * 3], in_=CL[0:3, j * 128:(j + 1) * 128],
                                                                identity=IDF[0:3, 0:3]), reads=[CL, IDF], writes=[pm])
            P.op("vector", lambda e, pm=pm: e.tensor_copy(out=NBALL[:, :], in_=pm[:, 0:NT * 3]), reads=[pm], writes=[NBALL])
            P.op("vector", lambda e: e.tensor_copy(out=NB3[0][:, :], in_=NBALL[:, :]), reads=[NBALL], writes=[NB3[0]])
            P.op("vector", lambda e: e.tensor_tensor(out=R1[:, :], in0=NBALL[:, :], in1=NB3[0][:, :], op=ALU.subtract),
                 reads=[NBALL, NB3[0]], writes=[R1])
            P.op("vector", lambda e: e.tensor_copy(out=NB3[1][:, :], in_=R1[:, :]), reads=[R1], writes=[NB3[1]])
            P.op("vector", lambda e: e.tensor_tensor(out=R1[:, :], in0=R1[:, :], in1=NB3[1][:, :], op=ALU.subtract),
                 reads=[R1, NB3[1]], writes=[R1])
            P.op("vector", lambda e: e.tensor_copy(out=NB3[2][:, :], in_=R1[:, :]), reads=[R1], writes=[NB3[2]])
            pm = next_proj()
            for i in range(3):
                P.op("tensor", lambda e, i=i, pm=pm: e.matmul(pm[:, 0:NT * 3], lhsT=E0[:, :], rhs=NB3[i][:, :],
                                                             start=(i == 0), stop=(i == 2)), reads=[E0, NB3[i]], writes=[pm])
            P.op("vector", lambda e, pm=pm: e.tensor_copy(out=CLB[:, :], in_=pm[:, 0:NT * 3]), reads=[pm], writes=[CLB])
            for h in range(3):
                for j in range(NT):
                    P.op("vector", lambda e, h=h, j=j: e.tensor_scalar(
                        out=BI[h][:, j, :], in0=CLB[:, :].rearrange("p (t h) -> p t h", h=3)[:, :, h],
                        scalar1=NBALL[:, j * 3 + h:j * 3 + h + 1], scalar2=-1.0, op0=ALU.subtract, op1=ALU.mult),
                        reads=[CLB, NBALL], writes=[BI[h]])
            for h in range(3):
                mixh = MIXH[mix_i[0] % 2]
                mix_i[0] += 1
                attention([(KF_[h], 0, 128)], [(QF[h], 0, 128)], 128 ** -0.5, VF[h], std_post(mixh), bias_tiles=BI[h])
                store_head(mixh, 640 + h * 128)
            P.barrier()
        P.drain_all()


D = 2048
DFF = 5632
NTOK = 1024
NH = 2
TT = NTOK // 128
NG = 11
EPS = 1e-6


def body_k2(nc, P, io):
    x_main, x_halo, mix_main, mix_halo = io["x_main"], io["x_halo"], io["mix_main"], io["mix_halo"]
    w_o, w_up, conv_w, conv_b, w_down = io["w_o"], io["w_up"], io["conv_w"], io["conv_b"], io["w_down"]
    ffn_norm, dnorm, fnorm, lc, idf, idb = io["ffn_norm"], io["dnorm"], io["fnorm"], io["lc"], io["idf"], io["idb"]
    x_out, xn_out, fin = io["x_out"], io["xn_out"], io["fin"]
    with ExitStack() as st:
        P.stack = st
        X = [P.sb(f"X{t}", [128, D], F32) for t in range(TT)]
        XH = P.sb("XH", [NH, D], F32)
        IDF = P.sb("IDF", [128, 128], F32)
        IDB = P.sb("IDB", [128, 128], BF16)
        GUP = P.sb("GUP", [128, 16], F32)
        CW = P.sb("CW", [128, 4, 88], F32)
        DN = P.sb("DN", [128, 1], F32)
        EPSB = P.sb("EPSB", [128, 1], F32)
        ss = [P.sb(f"ss{i}", [128, 1], F32) for i in range(2)]
        sd = [P.sb(f"sd{i}", [128, 1], F32) for i in range(2)]
        rs = [P.sb(f"rs{i}", [128, 1], F32) for i in range(2)]
        STG = [P.sb(f"stg{i}", [128, 1024], F32) for i in range(4)]
        PA = P.ps("PA", [128, 1024])
        PG = P.ps("PG", [128, 1024])
        PHALO = P.ps("PHALO", [128, 512])
        ACC = [PA, PG]
        PM = [P.ps(f"PM{i}", [128, 512]) for i in range(2)]
        PT = P.ps("PT", [128, 1024], BF16)
        PH = [P.wrap("PH0", PHALO.t, lock=PHALO.lock), P.wrap("PH1", PT[:, :].bitcast(F32), lock=PT.lock)]

        OUTX = P.wrap("OUTX", x_out)
        OUTN = P.wrap("OUTN", xn_out)
        OUTF = P.wrap("OUTF", fin if fin is not None else x_out)

        stg_i = [0]

        def stage():
            b = STG[stg_i[0] % len(STG)]
            stg_i[0] += 1
            return b

        cast_i = [0]

        def cast(out_ap, in_ap, scale_ap, reads, writes):
            eng = "scalar" if cast_i[0] % 2 == 0 else "gpsimd"
            cast_i[0] += 1
            if eng == "scalar":
                if scale_ap is None:
                    P.op("scalar", lambda e: e.activation(out=out_ap, in_=in_ap, func=AF.Copy), reads=reads, writes=writes)
                else:
                    P.op("scalar", lambda e: e.activation(out=out_ap, in_=in_ap, func=AF.Copy, scale=scale_ap),
                         reads=reads, writes=writes)
            else:
                if scale_ap is None:
                    P.op("gpsimd", lambda e: e.tensor_copy(out=out_ap, in_=in_ap), reads=reads, writes=writes)
                else:
                    P.op("gpsimd", lambda e: e.tensor_scalar(out=out_ap, in0=in_ap, scalar1=scale_ap, scalar2=1.0,
                                                              op0=ALU.mult, op1=ALU.mult), reads=reads, writes=writes)

        P.dma(IDF[:, :], idf, writes=[IDF])
        P.dma(IDB[:, :], idb, writes=[IDB])
        P.op("gpsimd", lambda e: e.memset(EPSB[:, :], EPS), writes=[EPSB])
        s0 = stage()
        P.dma(s0[0:16, 0:128], ffn_norm.rearrange("(k p) -> k p", p=128), writes=[s0])
        P.op("tensor", lambda e: e.transpose(out=PM[0][:, 0:16], in_=s0[0:16, 0:128], identity=IDF[0:16, 0:16]),
             reads=[s0, IDF], writes=[PM[0]])
        P.op("vector", lambda e: e.tensor_copy(out=GUP[:, :], in_=PM[0][:, 0:16]), reads=[PM[0]], writes=[GUP])
        for j in range(4):
            s1 = stage()
            src = conv_w[j:j + 1, :].rearrange("o (c p) -> (o c) p", p=128) if j < 3 else conv_b.rearrange("(c p) -> c p", p=128)
            P.dma(s1[0:88, 0:128], src, writes=[s1])
            pm = PM[(j + 1) % 2]
            P.op("tensor", lambda e, s1=s1, pm=pm: e.transpose(out=pm[:, 0:88], in_=s1[0:88, 0:128], identity=IDF[0:88, 0:88]),
                 reads=[s1, IDF], writes=[pm])
            P.op("vector", lambda e, j=j, pm=pm: e.tensor_copy(out=CW[:, j, :], in_=pm[:, 0:88]), reads=[pm], writes=[CW])
        LCB = P.sb("LCB", [128, 4], F32)
        P.dma(LCB[:, :], lc.partition_broadcast(128), writes=[LCB])
        s2 = stage()
        P.dma(s2[:, 0:1], dnorm.rearrange("(p o) -> p o", o=1), writes=[s2])
        P.op("vector", lambda e: e.tensor_scalar(out=DN[:, :], in0=s2[:, 0:1], scalar1=LCB[:, 1:2], scalar2=None, op0=ALU.mult),
             reads=[s2, LCB], writes=[DN])

        if x_halo is None:
            P.op("gpsimd", lambda e: e.memset(XH[:, :], 0.0), writes=[XH])
        else:
            P.dma(XH[:, :], x_halo, writes=[XH])
        for t in range(TT):
            P.dma(X[t][:, :], x_main[t * 128:(t + 1) * 128, :], writes=[X[t]])

        WUB = [P.sb(f"WUB{i}", [128, 16, 512], BF16) for i in range(2)]
        WD = P.sb("WD", [128, 4, 2048], BF16)

        def load_up(wb, col0):
            for k in range(16):
                s = stage()
                P.dma(s[:, 0:512], w_up[k * 128:(k + 1) * 128, col0:col0 + 512], writes=[s])
                cast(wb[:, k, :], s[:, 0:512], GUP[:, k:k + 1], reads=[s, GUP], writes=[wb])

        def load_down(g):
            for fc in range(4):
                r0 = (g * 4 + fc) * 128
                for hh in range(2):
                    s = stage()
                    P.dma(s[:, :], w_down[r0:r0 + 128, hh * 1024:(hh + 1) * 1024], writes=[s])
                    cast(WD[:, fc, hh * 1024:(hh + 1) * 1024], s[:, :], None, reads=[s], writes=[WD])


        with ExitStack() as stA:
            MT = []
            for k in range(16):
                t_ = stA.enter_context(nc.sbuf_tensor(f"MT{k}", [128, NH + NTOK], BF16))
                MT.append(Buf(P, f"MT{k}", t_))
            WOB = []
            for i in range(2):
                t_ = stA.enter_context(nc.sbuf_tensor(f"WOB{i}", [128, 16, 512], BF16))
                WOB.append(Buf(P, f"WOB{i}", t_))
            for k in range(16):
                if x_halo is None:
                    P.op("gpsimd", lambda e, k=k: e.memset(MT[k][:, 0:NH], 0.0), writes=[MT[k]])
                else:
                    P.dma(MT[k][:, 0:NH], mix_halo(k), writes=[MT[k]])
                P.dma(MT[k][:, NH:NH + NTOK], mix_main(k), writes=[MT[k]])

            def load_wo(nb):
                wb = WOB[nb % 2]
                for k in range(16):
                    s = stage()
                    P.dma(s[:, 0:512], w_o[k * 128:(k + 1) * 128, nb * 512:(nb + 1) * 512], writes=[s])
                    cast(wb[:, k, :], s[:, 0:512], DN[:, 0:1] if k < 4 else None, reads=[s, DN], writes=[wb])

            load_wo(0)
            pmi = 0
            for nb in range(4):
                if nb + 1 < 4:
                    load_wo(nb + 1)
                if nb == 2:
                    load_up(WUB[0], 0)
                    load_up(WUB[1], DFF)
                    load_down(0)
                wb = WOB[nb % 2]
                cs = slice(nb * 512, (nb + 1) * 512)
                for tt in range(-1, TT):
                    pm = PM[pmi % 2]
                    pmi += 1
                    if tt < 0:
                        np_, c0, c1, xt = NH, 0, NH, XH
                    else:
                        np_, c0, c1, xt = 128, NH + tt * 128, NH + (tt + 1) * 128, X[tt]
                    for k in range(16):
                        P.op("tensor", lambda e, k=k, pm=pm, np_=np_, c0=c0, c1=c1, wb=wb: e.matmul(
                            pm[0:np_, :], lhsT=MT[k][:, c0:c1], rhs=wb[:, k, :], start=(k == 0), stop=(k == 15)),
                            reads=[MT[k], wb], writes=[pm])
                    P.op("vector", lambda e, pm=pm, np_=np_, xt=xt, cs=cs: e.tensor_tensor(
                        out=xt[0:np_, cs], in0=pm[0:np_, :], in1=xt[0:np_, cs], op=ALU.add),
                        reads=[pm, xt], writes=[xt])
            P.barrier()

        def norm_tile(xt, np_, i, sq_out):
            b = i % 2
            P.op("scalar", lambda e: e.activation(out=sq_out[0:np_, :], in_=xt[0:np_, :], func=AF.Square,
                                                    accum_out=ss[b][0:np_, :]), reads=[xt], writes=[sq_out, ss[b]])
            P.op("scalar", lambda e: e.activation(out=sd[b][0:np_, :], in_=ss[b][0:np_, :], func=AF.Sqrt,
                                                    scale=1.0 / D, bias=EPSB[0:np_, :]), reads=[ss[b], EPSB], writes=[sd[b]])
            P.op("vector", lambda e: e.reciprocal(out=rs[b][0:np_, :], in_=sd[b][0:np_, :]), reads=[sd[b]], writes=[rs[b]])
            return rs[b]

        with ExitStack() as stH:
            H2T = Buf(P, "H2T", stH.enter_context(nc.sbuf_tensor("H2T", [128, 16, NH + NTOK], BF16)))
            with ExitStack() as stN:
                xnb = [Buf(P, f"xnb{i}", stN.enter_context(nc.sbuf_tensor(f"xnb{i}", [128, D], BF16))) for i in range(2)]

                def norm_transpose(xt, np_, i, c0):
                    b = i % 2
                    r = norm_tile(xt, np_, i, xnb[b])
                    P.op("vector", lambda e: e.tensor_scalar(out=xnb[b][0:np_, :], in0=xt[0:np_, :], scalar1=r[0:np_, :],
                                                              scalar2=None, op0=ALU.mult), reads=[xt, r], writes=[xnb[b]])
                    for g in range(2):
                        for kk in range(8):
                            k = g * 8 + kk
                            P.op("tensor", lambda e, k=k, kk=kk: e.transpose(
                                out=PT[:, kk * 128: kk * 128 + np_], in_=xnb[b][0:np_, k * 128:(k + 1) * 128],
                                identity=IDB[0:np_, 0:np_]), reads=[xnb[b], IDB], writes=[PT])
                        src = PT[:, :].rearrange("p (k t) -> p k t", t=128)[:, :, 0:np_]
                        dst = H2T[:, g * 8:(g + 1) * 8, c0:c0 + np_]
                        if g == 0:
                            P.op("vector", lambda e, src=src, dst=dst: e.tensor_copy(out=dst, in_=src), reads=[PT], writes=[H2T])
                        else:
                            P.op("scalar", lambda e, src=src, dst=dst: e.activation(out=dst, in_=src, func=AF.Copy),
                                 reads=[PT], writes=[H2T])

                norm_transpose(XH, NH, 0, 0)
                for t in range(TT):
                    norm_transpose(X[t], 128, t + 1, NH + t * 128)
                P.barrier()

            with ExitStack() as stB:
                def sbB(name, shape, dtp):
                    return Buf(P, name, stB.enter_context(nc.sbuf_tensor(name, list(shape), dtp)))
                ACTT = sbB("ACTT", [128, 4, NTOK], BF16)
                UA = [sbB(f"UA{i}", [128, NTOK], F32) for i in range(4)]
                UG = [sbB(f"UG{i}", [128, NTOK], F32) for i in range(2)]

                def conv(pp, ph, hc, uc, c):
                    w0, w1, w2, bb = CW[:, 0, c:c + 1], CW[:, 1, c:c + 1], CW[:, 2, c:c + 1], CW[:, 3, c:c + 1]
                    P.op("scalar", lambda e: e.activation(out=uc[:, :], in_=pp[:, :], func=AF.Identity, scale=w2, bias=bb),
                         reads=[pp, CW], writes=[uc])
                    P.op("vector", lambda e: e.scalar_tensor_tensor(out=uc[:, 1:NTOK], in0=pp[:, 0:NTOK - 1], scalar=w1,
                                                                     in1=uc[:, 1:NTOK], op0=ALU.mult, op1=ALU.add),
                         reads=[pp, CW, uc], writes=[uc])
                    P.op("vector", lambda e: e.scalar_tensor_tensor(out=uc[:, 2:NTOK], in0=pp[:, 0:NTOK - 2], scalar=w0,
                                                                     in1=uc[:, 2:NTOK], op0=ALU.mult, op1=ALU.add),
                         reads=[pp, CW, uc], writes=[uc])
                    P.op("vector", lambda e: e.scalar_tensor_tensor(out=uc[:, 0:1], in0=ph[:, hc + 1:hc + 2], scalar=w1,
                                                                     in1=uc[:, 0:1], op0=ALU.mult, op1=ALU.add),
                         reads=[ph, CW, uc], writes=[uc])
                    P.op("vector", lambda e: e.scalar_tensor_tensor(out=uc[:, 0:2], in0=ph[:, hc:hc + 2], scalar=w0,
                                                                     in1=uc[:, 0:2], op0=ALU.mult, op1=ALU.add),
                         reads=[ph, CW, uc], writes=[uc])

                def up(wb, fc, pp, ph, hc):
                    for k in range(16):
                        lw = wb[:, k, fc * 128:(fc + 1) * 128]
                        P.op("tensor", lambda e, k=k, lw=lw: e.matmul(ph[:, hc:hc + 2], lhsT=lw, rhs=H2T[:, k, 0:NH],
                                                                       start=(k == 0), stop=(k == 15)),
                             reads=[wb, H2T], writes=[ph])
                        for h in range(2):
                            P.op("tensor", lambda e, k=k, lw=lw, h=h: e.matmul(
                                pp[:, h * 512:(h + 1) * 512], lhsT=lw, rhs=H2T[:, k, NH + h * 512: NH + (h + 1) * 512],
                                start=(k == 0), stop=(k == 15)), reads=[wb, H2T], writes=[pp])

                pmi = 0
                ci = 0
                for g in range(NG):
                    for fc in range(4):
                        pp, ph, hc = ACC[ci % 2], PH[ci % 2], 0
                        ci += 1
                        up(WUB[0], fc, pp, ph, hc)
                        conv(pp, ph, hc, UA[fc], g * 4 + fc)
                    if g + 1 < NG:
                        load_up(WUB[0], (g + 1) * 512)
                    for fc in range(4):
                        pp, ph, hc = ACC[ci % 2], PH[ci % 2], 0
                        ci += 1
                        ug = UG[fc % 2]
                        up(WUB[1], fc, pp, ph, hc)
                        conv(pp, ph, hc, ug, 44 + g * 4 + fc)
                        P.op("scalar", lambda e, ug=ug: e.activation(out=ug[:, :], in_=ug[:, :], func=AF.Silu),
                             reads=[ug], writes=[ug])
                        P.op("gpsimd", lambda e, ug=ug, fc=fc: e.tensor_tensor(out=ACTT[:, fc, :], in0=ug[:, :], in1=UA[fc][:, :],
                                                                                 op=ALU.mult),
                             reads=[ug, UA[fc]], writes=[ACTT])
                    if g + 1 < NG:
                        load_up(WUB[1], DFF + (g + 1) * 512)
                    for tt in range(TT):
                        for nb in range(4):
                            pm = PM[pmi % 2]
                            pmi += 1
                            cs = slice(nb * 512, (nb + 1) * 512)
                            for fc in range(4):
                                P.op("tensor", lambda e, fc=fc, tt=tt, cs=cs, pm=pm: e.matmul(
                                    pm[:, :], lhsT=ACTT[:, fc, tt * 128:(tt + 1) * 128], rhs=WD[:, fc, cs],
                                    start=(fc == 0), stop=(fc == 3)), reads=[ACTT, WD], writes=[pm])
                            P.op("vector", lambda e, tt=tt, cs=cs, pm=pm: e.tensor_tensor(
                                out=X[tt][:, cs], in0=pm[:, :], in1=X[tt][:, cs], op=ALU.add),
                                reads=[pm, X[tt]], writes=[X[tt]])
                    if g + 1 < NG:
                        load_down(g + 1)
                P.barrier()

        with ExitStack() as stC:
            def sbC(name, shape, dtp):
                return Buf(P, name, stC.enter_context(nc.sbuf_tensor(name, list(shape), dtp)))
            FO = [sbC(f"FO{i}", [128, D], F32) for i in range(2)]
            xnc = [sbC(f"xnc{i}", [128, D], BF16) for i in range(2)]
            FG = sbC("FG", [128, D], F32)
            P.dma(FG[:, :], fnorm.partition_broadcast(128), writes=[FG])
            for t in range(TT):
                b = (t + 1) % 2
                P.dma(x_out[t * 128:(t + 1) * 128, :], X[t][:, :], reads=[X[t]], writes=[OUTX])
                r = norm_tile(X[t], 128, t + 1, xnc[b])
                P.op("gpsimd", lambda e, t=t, b=b, r=r: e.tensor_scalar(out=xnc[b][:, :], in0=X[t][:, :], scalar1=r[:, :],
                                                                         scalar2=1.0, op0=ALU.mult, op1=ALU.mult),
                     reads=[X[t], r], writes=[xnc[b]])
                P.dma(xn_out[t * 128:(t + 1) * 128, :], xnc[b][:, :], reads=[xnc[b]], writes=[OUTN])
                if fin is not None:
                    P.op("vector", lambda e, t=t, b=b, r=r: e.scalar_tensor_tensor(out=FO[b][:, :], in0=X[t][:, :], scalar=r[:, :],
                                                                                   in1=FG[:, :], op0=ALU.mult, op1=ALU.mult),
                         reads=[X[t], r, FG], writes=[FO[b]])
                    P.dma(fin[t * 128:(t + 1) * 128, :], FO[b][:, :], reads=[FO[b]], writes=[OUTF])
            P.drain_all()


bf16 = ml_dtypes.bfloat16
bf16 = ml_dtypes.bfloat16

ROPE_THETA = 500000.0

def swap_perm_diff():
    p = np.arange(128)
    for base in (0, 64):
        for i in range(8):
            p[base + i] = base + i + 8
            p[base + i + 8] = base + i
    return p

def swap_perm_rope64():
    p = np.arange(64)
    p[:32] = np.arange(32, 64)
    p[32:] = np.arange(0, 32)
    return p

def rope_consts():
    rc = np.zeros((128, 8), np.float32)
    invd = (ROPE_THETA ** (-np.arange(0, 16, 2, dtype=np.float32) / np.float32(16))).astype(np.float32)
    invm = (ROPE_THETA ** (-np.arange(0, 64, 2, dtype=np.float32) / np.float32(64))).astype(np.float32)
    for p in range(128):
        q = p % 64
        if q < 16:
            rc[p, 0] = invd[q % 8]
            rc[p, 1] = 1.0
            rc[p, 2] = -1.0 if q < 8 else 1.0
        else:
            rc[p, 0] = 0.0
            rc[p, 1] = 0.0
            rc[p, 2] = 0.0
        rc[p, 5] = 1.0 - rc[p, 1]
        rc[p, 3] = invm[q % 32]
        rc[p, 4] = -1.0 if q < 32 else 1.0
        rc[p, 6] = 1.0
        rc[p, 7] = 0.0
    return rc

def pack_k1(l, r, w):
    win = w["w_in"][l]
    offs = np.cumsum([0, 512, 512, 512, 512, 512, 64, 768, 768, 768, 6])
    aq, ak, av, mcq, mckv, mkr, fq, fk, fv, fg = [win[:, offs[i]:offs[i + 1]] for i in range(10)]
    pd = swap_perm_diff()
    pr = swap_perm_rope64()
    cols = []
    q = aq[:, r * 256:(r + 1) * 256]
    k = ak[:, r * 256:(r + 1) * 256]
    def swp(m):
        return np.concatenate([m[:, h * 128:(h + 1) * 128][:, pd] for h in range(2)], 1)
    cols += [q, swp(q), k, swp(k), av[:, r * 256:(r + 1) * 256], mcq, mckv, mkr, mkr[:, pr],
             fq[:, r * 384:(r + 1) * 384], fk[:, r * 384:(r + 1) * 384], fv[:, r * 384:(r + 1) * 384], fg[:, r * 3:(r + 1) * 3]]
    W1 = np.ascontiguousarray(np.concatenate(cols, 1))
    uq = w["mla_w_uq"][l]
    ukv = w["mla_w_ukv"][l]
    hs = [3 * r + i for i in range(3)]
    U1 = np.concatenate([uq[:, h * 192:h * 192 + 128] for h in hs] + [uq[:, h * 192 + 128:h * 192 + 192] for h in hs]
                        + [uq[:, h * 192 + 128:h * 192 + 192][:, pr] for h in hs], 1)
    U2 = np.concatenate([ukv[:, h * 256:h * 256 + 128] for h in hs] + [ukv[:, h * 256 + 128:h * 256 + 256] for h in hs], 1)
    fb = np.zeros(4, np.float32)
    fb[:3] = w["fox_forget_bias"][l][3 * r:3 * r + 3]
    lam_init = 0.8 - 0.6 * math.exp(-0.3 * l)
    return {
        "W1": W1, "U1": np.ascontiguousarray(U1), "U2": np.ascontiguousarray(U2),
        "attn_norm": np.ascontiguousarray(w["attn_norm"][l]), "qn": np.ascontiguousarray(w["mla_q_norm"][l]),
        "kvn": np.ascontiguousarray(w["mla_kv_norm"][l]), "fb": fb,
        "dl": np.ascontiguousarray(w["diff_lambda"][l].reshape(-1)),
        "lc": np.array([lam_init, 1.0 - lam_init, 0, 0], np.float32),
        "idf": np.eye(128, dtype=np.float32), "idb": np.eye(128, dtype=np.float32).astype(bf16),
        "mask": np.triu(np.ones((128, 128), np.float32)).astype(bf16),
        "rc": rope_consts(),
    }


class _NCP:
    def __init__(self, nc, prefix):
        self._nc = nc
        self._p = prefix

    def sbuf_tensor(self, name, *a, **k):
        return self._nc.sbuf_tensor(self._p + name, *a, **k)

    def psum_tensor(self, name, *a, **k):
        return self._nc.psum_tensor(self._p + name, *a, **k)

    def __getattr__(self, n):
        return getattr(self._nc, n)


def body_k0(nc, P, x_ap, xn_ap, ntiles):
    with ExitStack() as st:
        P.stack = st
        xt = [P.sb(f"xt{i}", [128, D], F32) for i in range(2)]
        ot = [P.sb(f"ot{i}", [128, D], BF16) for i in range(2)]
        ss = [P.sb(f"ss{i}", [128, 1], F32) for i in range(2)]
        sd = [P.sb(f"sd{i}", [128, 1], F32) for i in range(2)]
        rs = [P.sb(f"rs{i}", [128, 1], F32) for i in range(2)]
        eps = P.sb("eps", [128, 1], F32)
        P.op("gpsimd", lambda e: e.memset(eps[:, :], 1e-6), writes=[eps])
        outd = P.wrap("outd", xn_ap)
        for i in range(ntiles):
            b = i % 2
            P.dma(xt[b][:, :], x_ap[i * 128:(i + 1) * 128, :], writes=[xt[b]])
            P.op("scalar", lambda e, b=b: e.activation(out=ot[b][:, :], in_=xt[b][:, :], func=AF.Square,
                                                         accum_out=ss[b][:, :]), reads=[xt[b]], writes=[ot[b], ss[b]])
            P.op("scalar", lambda e, b=b: e.activation(out=sd[b][:, :], in_=ss[b][:, :], func=AF.Sqrt,
                                                         scale=1.0 / D, bias=eps[:, :]), reads=[ss[b], eps], writes=[sd[b]])
            P.op("vector", lambda e, b=b: e.reciprocal(out=rs[b][:, :], in_=sd[b][:, :]), reads=[sd[b]], writes=[rs[b]])
            P.op("vector", lambda e, b=b: e.tensor_scalar(out=ot[b][:, :], in0=xt[b][:, :], scalar1=rs[b][:, :],
                                                            scalar2=None, op0=ALU.mult), reads=[xt[b], rs[b]], writes=[ot[b]])
            P.dma(xn_ap[i * 128:(i + 1) * 128, :], ot[b][:, :], reads=[ot[b]], writes=[outd])
        P.drain_all()


def _mix_loc(k):
    if k < 4:
        return k // 2, k % 2
    if k < 10:
        return (k - 4) // 3, 2 + (k - 4) % 3
    return (k - 10) // 3, 5 + (k - 10) % 3


def build_fused(depth=4):
    nc0 = bass.Bass("TRN2", target_bir_lowering=False)
    dt = nc0.dram_tensor

    def ext(name, shape, dtype):
        return dt(name, list(shape), dtype, kind="ExternalInput").ap()

    L = depth
    x = ext("x", [S, D], F32)
    pos = ext("pos", [S], I32)
    W1 = ext("W1", [L, 2, D, NC1], F32)
    U1 = ext("U1", [L, 2, 512, 768], F32)
    U2 = ext("U2", [L, 2, 512, 768], F32)
    attn_norm = ext("attn_norm", [L, D], F32)
    qn = ext("qn", [L, 512], F32)
    kvn = ext("kvn", [L, 512], F32)
    fb = ext("fb", [L, 2, 4], F32)
    dl = ext("dl", [L, 256], F32)
    lc = ext("lc", [L, 4], F32)
    w_o = ext("w_o", [L, D, D], F32)
    w_up = ext("w_up", [L, D, 2 * DFF], F32)
    conv_w = ext("conv_w", [L, 3, 2 * DFF], F32)
    conv_b = ext("conv_b", [L, 2 * DFF], F32)
    w_down = ext("w_down", [L, DFF, D], F32)
    ffn_norm = ext("ffn_norm", [L, D], F32)
    dnorm = ext("dnorm", [L, 128], F32)
    fnorm = ext("fnorm", [D], F32)
    idf = ext("idf", [128, 128], F32)
    idb = ext("idb", [128, 128], BF16)
    mask = ext("mask", [128, 128], BF16)
    rc = ext("rc", [128, 8], F32)
    out = dt("out", [S, D], F32, kind="ExternalOutput").ap()
    XS = [dt(f"xs_scr{i}", [S, D], F32).ap() for i in range(2)]
    XN = [dt(f"xn_scr{i}", [S, D], BF16).ap() for i in range(2)]
    MIX = [dt(f"mix_scr{r}", [1024, S], BF16).ap() for r in range(2)]
    TABS = dt("rope_tabs", [4, 128, S], BF16).ap()

    with ExitStack() as st:
        P = Prog(nc0, st)
        cnt = [0]

        def scoped():
            cnt[0] += 1
            P.nc = _NCP(nc0, f"b{cnt[0]}_")
            return P.nc

        body_k0(scoped(), P, x, XN[0], 16)
        for l in range(L):
            x_src = x if l == 0 else XS[l % 2]
            x_dst = XS[(l + 1) % 2]
            xn_src = XN[l % 2]
            xn_dst = XN[(l + 1) % 2]
            for r in range(2):
                body_k1(scoped(), P, {
                    "xn": xn_src, "pos": pos, "W1": W1[l, r], "U1": U1[l, r], "U2": U2[l, r],
                    "attn_norm": attn_norm[l], "qn": qn[l], "kvn": kvn[l], "fb": fb[l, r], "dl": dl[l], "lc": lc[l],
                    "idf": idf, "idb": idb, "mask": mask, "rc": rc, "mixT": MIX[r],
                    "tabs": TABS, "tabs_mode": "compute_store" if (l == 0 and r == 0) else "load"})
            for hf in range(2):
                t0 = hf * 1024

                def mix_main(k, t0=t0):
                    r_, c_ = _mix_loc(k)
                    return MIX[r_][c_ * 128:(c_ + 1) * 128, t0:t0 + 1024]

                def mix_halo(k, t0=t0):
                    r_, c_ = _mix_loc(k)
                    return MIX[r_][c_ * 128:(c_ + 1) * 128, t0 - 2:t0]

                body_k2(scoped(), P, {
                    "x_main": x_src[t0:t0 + 1024, :], "x_halo": None if hf == 0 else x_src[t0 - 2:t0, :],
                    "mix_main": mix_main, "mix_halo": mix_halo,
                    "w_o": w_o[l], "w_up": w_up[l], "conv_w": conv_w[l], "conv_b": conv_b[l], "w_down": w_down[l],
                    "ffn_norm": ffn_norm[l], "dnorm": dnorm[l], "fnorm": fnorm, "lc": lc[l], "idf": idf, "idb": idb,
                    "x_out": x_dst[t0:t0 + 1024, :], "xn_out": xn_dst[t0:t0 + 1024, :],
                    "fin": out[t0:t0 + 1024, :] if l == L - 1 else None})
        P.nc = nc0
        P.stack = st
        OUTB = P.wrap("OUTB", out)
        P.finish([OUTB])
        P.emit()
    return nc0


from concourse.bass_utils import run_bass_kernel_spmd

_PROG = {}


def kernel(**inputs):
    w = {k: np.asarray(v) for k, v in inputs.items()}
    x = np.ascontiguousarray(w["x"], dtype=np.float32)
    pos = np.ascontiguousarray(w["positions"]).astype(np.int32)
    L = w["w_in"].shape[0]
    if "f" not in _PROG:
        _PROG["f"] = build_fused(L)
    packs = [[pack_k1(l, r, w) for r in range(2)] for l in range(L)]

    def st2(key):
        return np.ascontiguousarray(np.stack([np.stack([packs[l][r][key] for r in range(2)], 0) for l in range(L)], 0))

    def st1(key):
        return np.ascontiguousarray(np.stack([packs[l][0][key] for l in range(L)], 0))

    shared = {
        "W1": st2("W1"), "U1": st2("U1"), "U2": st2("U2"), "fb": st2("fb"),
        "attn_norm": st1("attn_norm"), "qn": st1("qn"), "kvn": st1("kvn"), "dl": st1("dl"), "lc": st1("lc"),
        "w_o": np.ascontiguousarray(w["w_o"]), "w_up": np.ascontiguousarray(w["ffn_w_up"]),
        "conv_w": np.ascontiguousarray(w["ffn_conv_w"]), "conv_b": np.ascontiguousarray(w["ffn_conv_b"]),
        "w_down": np.ascontiguousarray(w["ffn_w_down"]), "ffn_norm": np.ascontiguousarray(w["ffn_norm"]),
        "dnorm": np.ascontiguousarray(w["diff_out_norm"]), "fnorm": np.ascontiguousarray(w["final_norm"]),
        "idf": packs[0][0]["idf"], "idb": packs[0][0]["idb"], "mask": packs[0][0]["mask"], "rc": packs[0][0]["rc"],
    }
    cores = list(range(8))
    ins = []
    for c in cores:
        d = dict(shared)
        d["x"] = np.ascontiguousarray(x[c // 2])
        d["pos"] = np.ascontiguousarray(pos[c // 2])
        ins.append(d)
    res = run_bass_kernel_spmd(_PROG["f"], ins, core_ids=cores)
    outs = [np.asarray(res.results[2 * b]["out"]) for b in range(4)]
    return np.stack(outs, 0).astype(np.float32)
```

```python
import math
import ml_dtypes
from contextlib import ExitStack
import numpy as np
import concourse.bass as bass
import concourse.mybir as mybir

F32 = mybir.dt.float32
BF16 = mybir.dt.bfloat16
I32 = mybir.dt.int32
AF = mybir.ActivationFunctionType
ALU = mybir.AluOpType
AX = mybir.AxisListType

ENGS = ("sync", "scalar", "vector", "gpsimd", "tensor")
EPOCH = 20000


class Buf:
    def __init__(self, prog, name, t):
        self.prog = prog
        self.name = name
        self.t = t
        self.writes = {}
        self.reads = {}
        self.dsem = None
        self.dcount = 0
        self.lock = None

    def __getitem__(self, idx):
        return self.t[idx]


class Prog:
    def __init__(self, nc, stack, n_eng_sems=6):
        self.nc = nc
        self.stack = stack
        self.sem_stack = stack
        self.free_dsems = []
        self.live_dbufs = []
        self.ops = {e: [] for e in ENGS}
        self.semtab = []
        self.eng_sems = {}
        self.eng_epoch = {e: 0 for e in ENGS}
        self.eng_cnt = {e: 0 for e in ENGS}
        self.waited = {e: {} for e in ENGS}
        for e in ENGS:
            self.eng_sems[e] = [self._new_sem(f"s_{e}_{i}") for i in range(n_eng_sems)]
        self.dma_sems = []
        self.nbuf = 0

    def _new_sem(self, name):
        h = self.sem_stack.enter_context(self.nc.semaphore(name))
        self.semtab.append(h)
        return len(self.semtab) - 1

    def _dsem_for(self, owner):
        if owner.dsem is None:
            if self.free_dsems:
                owner.dsem, owner.dcount = self.free_dsems.pop()
            else:
                owner.dsem = self._new_sem(f"d{len(self.semtab)}")
                owner.dcount = 0
            self.live_dbufs.append(owner)

    def sb(self, name, shape, dtype):
        t = self.stack.enter_context(self.nc.sbuf_tensor(name, list(shape), dtype))
        return Buf(self, name, t)

    def ps(self, name, shape, dtype=F32):
        t = self.stack.enter_context(self.nc.psum_tensor(name, list(shape), dtype))
        b = Buf(self, name, t)
        b.lock = Buf(self, name + "_lock", None)
        return b

    def wrap(self, name, t, lock=None):
        b = Buf(self, name, t)
        b.lock = lock
        return b

    def _locks(self, reads, writes):
        ls = []
        for b in list(reads) + list(writes):
            if b.lock is not None and b.lock not in ls:
                ls.append(b.lock)
        return ls

    def _need(self, eng, reads, writes):
        need = {}
        for b in reads:
            for s, v in b.writes.items():
                if need.get(s, 0) < v:
                    need[s] = v
        for b in list(writes) + self._locks(reads, writes):
            for d in (b.writes, b.reads):
                for s, v in d.items():
                    if need.get(s, 0) < v:
                        need[s] = v
        if eng == "tensor":
            own = set(self.eng_sems["tensor"])
            need = {s: v for s, v in need.items() if s not in own}
        out = []
        w = self.waited[eng]
        for s, v in need.items():
            if w.get(s, 0) < v:
                w[s] = v
                out.append((s, v))
        return out

    def _emit_waits(self, eng, waits):
        for s, v in waits:
            h = self.semtab[s]
            self.ops[eng].append(lambda e, h=h, v=v: e.wait_ge(h, v))

    def _next_event(self, eng):
        if self.eng_cnt[eng] >= EPOCH:
            self.eng_epoch[eng] += 1
            self.eng_cnt[eng] = 0
        self.eng_cnt[eng] += 1
        s = self.eng_sems[eng][self.eng_epoch[eng]]
        return s, self.eng_cnt[eng]

    def op(self, eng, fn, reads=(), writes=()):
        waits = self._need(eng, reads, writes)
        self._emit_waits(eng, waits)
        s, v = self._next_event(eng)
        h = self.semtab[s]
        self.ops[eng].append(lambda e, fn=fn, h=h: fn(e).then_inc(h, 1))
        for b in list(writes) + self._locks(reads, writes):
            b.writes = {s: v}
            b.reads = {}
        for b in reads:
            if b in writes:
                continue
            b.reads[s] = max(b.reads.get(s, 0), v)
        return (s, v)

    def dma(self, out_ap, in_ap, reads=(), writes=(), q="sync", **kw):
        waits = self._need(q, reads, writes)
        self._emit_waits(q, waits)
        owner = (list(writes) + list(reads))[0]
        self._dsem_for(owner)
        owner.dcount += 16
        s, v = owner.dsem, owner.dcount
        h = self.semtab[s]
        self.ops[q].append(
            lambda e, o=out_ap, i=in_ap, h=h, kw=kw: e.dma_start(out=o, in_=i, **kw).then_inc(h, 16))
        for b in writes:
            b.writes = {s: v}
            b.reads = {}
        for b in reads:
            if b in writes:
                continue
            b.reads[s] = max(b.reads.get(s, 0), v)
        return (s, v)

    def dma_like(self, q, fn, reads=(), writes=(), inc=16):
        waits = self._need(q, reads, writes)
        self._emit_waits(q, waits)
        owner = (list(writes) + list(reads))[0]
        self._dsem_for(owner)
        owner.dcount += inc
        s, v = owner.dsem, owner.dcount
        h = self.semtab[s]
        self.ops[q].append(lambda e, fn=fn, h=h: fn(e).then_inc(h, inc))
        for b in writes:
            b.writes = {s: v}
            b.reads = {}
        for b in reads:
            if b in writes:
                continue
            b.reads[s] = max(b.reads.get(s, 0), v)
        return (s, v)

    def barrier(self):
        ev = {}
        for e in ENGS:
            if self.eng_cnt[e] > 0:
                ev[self.eng_sems[e][self.eng_epoch[e]]] = self.eng_cnt[e]
        for e in ENGS:
            for s, v in ev.items():
                if self.waited[e].get(s, 0) < v:
                    self.waited[e][s] = v
                    h = self.semtab[s]
                    self.ops[e].append(lambda en, h=h, v=v: en.wait_ge(h, v))

    def drain_all(self):
        ev = {}
        for e in ENGS:
            if self.eng_cnt[e] > 0:
                ev[self.eng_sems[e][self.eng_epoch[e]]] = self.eng_cnt[e]
        for b in self.live_dbufs:
            ev[b.dsem] = max(ev.get(b.dsem, 0), b.dcount)
        for e in ENGS:
            for s, v in ev.items():
                if self.waited[e].get(s, 0) < v:
                    self.waited[e][s] = v
                    h = self.semtab[s]
                    self.ops[e].append(lambda en, h=h, v=v: en.wait_ge(h, v))
        for b in self.live_dbufs:
            self.free_dsems.append((b.dsem, b.dcount))
            b.dsem = None
            b.dcount = 0
            b.writes = {}
            b.reads = {}
        self.live_dbufs = []

    def finish(self, bufs):
        need = {}
        for b in bufs:
            for d in (b.writes, b.reads):
                for s, v in d.items():
                    need[s] = max(need.get(s, 0), v)
        for e in ENGS:
            if e != "sync" and self.eng_cnt[e] > 0:
                s = self.eng_sems[e][self.eng_epoch[e]]
                need[s] = max(need.get(s, 0), self.eng_cnt[e])
        for s, v in need.items():
            h = self.semtab[s]
            self.ops["sync"].append(lambda en, h=h, v=v: en.wait_ge(h, v))

    def emit(self):
        nc = self.nc
        with nc.Block() as block:
            @block.sync
            def _(e):
                for f in self.ops["sync"]:
                    f(e)

            @block.scalar
            def _(e):
                for f in self.ops["scalar"]:
                    f(e)

            @block.vector
            def _(e):
                for f in self.ops["vector"]:
                    f(e)

            @block.gpsimd
            def _(e):
                for f in self.ops["gpsimd"]:
                    f(e)

            @block.tensor
            def _(e):
                for f in self.ops["tensor"]:
                    f(e)


D = 2048
S = 2048
NT = 16
NC1 = 3587
C_Q, C_QS, C_K, C_KS, C_V, C_CQ, C_CKV, C_KR, C_KRS, C_FQ, C_FK, C_FV, C_FG = (
    0, 256, 512, 768, 1024, 1280, 1792, 2304, 2368, 2432, 2816, 3200, 3584)
TWO_PI = 2.0 * math.pi


def body_k1(nc, P, io):
    STOP = 99
    xn, pos, W1, U1, U2 = io["xn"], io["pos"], io["W1"], io["U1"], io["U2"]
    attn_norm, qn, kvn, fb, dl, lc = io["attn_norm"], io["qn"], io["kvn"], io["fb"], io["dl"], io["lc"]
    idf, idb, maskd, rc, mixT = io["idf"], io["idb"], io["mask"], io["rc"], io["mixT"]
    tabs, tabs_mode = io.get("tabs"), io.get("tabs_mode", "compute")
    with ExitStack() as st:
        P.stack = st
        HT = P.sb("HT", [128, 16, S], BF16)
        IDF = P.sb("IDF", [128, 128], F32)
        IDB = P.sb("IDB", [128, 128], BF16)
        MASK = P.sb("MASK", [128, 128], BF16)
        RC = P.sb("RC", [128, 8], F32)
        GIN = P.sb("GIN", [128, 16], F32)
        GQ = P.sb("GQ", [128, 4], F32)
        GKV = P.sb("GKV", [128, 4], F32)
        LCB = P.sb("LCB", [128, 4], F32)
        EPS6 = P.sb("EPS6", [128, 1], F32)
        EPS5 = P.sb("EPS5", [128, 1], F32)
        CD = P.sb("CD", [128, S], BF16)
        SD = P.sb("SD", [128, S], BF16)
        CM = P.sb("CM", [128, S], BF16)
        SM = P.sb("SM", [128, S], BF16)
        WB = [P.sb(f"WB{i}", [128, 16, 256], BF16) for i in range(4)]
        STG = [P.sb(f"stg{i}", [128, 512], F32) for i in range(3)]
        PTB = [P.sb(f"PTB{i}", [128, 512], BF16) for i in range(3)]
        ONB = [P.sb(f"ONB{i}", [128, 128], BF16) for i in range(4)]
        _mixh = P.sb("MIXH0", [128, S], BF16)
        MIXH = [_mixh, _mixh]
        RL = [P.sb(f"RL{i}", [128, 1], F32) for i in range(8)]
        ONESB = P.sb("ONESB", [128, 128], BF16)
        ONESF = P.sb("ONESF", [1, 128], F32)
        _t1 = P.sb("T1_0", [128, 512], F32)
        _t2 = P.sb("T2_0", [128, 512], F32)
        T1 = [_t1, _t1]
        T2 = [_t2, _t2]

        PS = [P.ps(f"PS{i}", [128, 512]) for i in range(2)]
        PO = [P.ps(f"PO{i}", [128, 512]) for i in range(4)]
        PM = P.ps("PM", [128, 512])
        PTR = P.ps("PTR", [128, 1024], BF16)
        PROJ = [PM, PS[0], PS[1]]
        OUT = P.wrap("OUT", mixT)

        cnt = {"stg": 0, "cast": 0, "proj": 0, "wb": 0, "ptb": 0, "onb": 0, "rl": 0, "t": 0}

        def stage():
            b = STG[cnt["stg"] % len(STG)]
            cnt["stg"] += 1
            return b

        def cast(out_ap, in_ap, scale_ap, reads, writes):
            eng = "scalar" if cnt["cast"] % 2 == 0 else "gpsimd"
            cnt["cast"] += 1
            if eng == "scalar":
                if scale_ap is None:
                    P.op("scalar", lambda e: e.activation(out=out_ap, in_=in_ap, func=AF.Copy), reads=reads, writes=writes)
                else:
                    P.op("scalar", lambda e: e.activation(out=out_ap, in_=in_ap, func=AF.Copy, scale=scale_ap),
                         reads=reads, writes=writes)
            else:
                if scale_ap is None:
                    P.op("gpsimd", lambda e: e.tensor_copy(out=out_ap, in_=in_ap), reads=reads, writes=writes)
                else:
                    P.op("gpsimd", lambda e: e.tensor_scalar(out=out_ap, in0=in_ap, scalar1=scale_ap, scalar2=1.0,
                                                              op0=ALU.mult, op1=ALU.mult), reads=reads, writes=writes)

        def next_proj():
            b = PROJ[cnt["proj"] % 3]
            cnt["proj"] += 1
            return b

        def vecT(dst, src_ap, n):
            s = stage()
            pm = next_proj()
            P.dma(s[0:n, 0:128], src_ap.rearrange("(k p) -> k p", p=128), writes=[s])
            P.op("tensor", lambda e: e.transpose(out=pm[:, 0:n], in_=s[0:n, 0:128], identity=IDF[0:n, 0:n]),
                 reads=[s, IDF], writes=[pm])
            P.op("vector", lambda e: e.tensor_copy(out=dst[:, 0:n], in_=pm[:, 0:n]), reads=[pm], writes=[dst])

        P.dma(IDF[:, :], idf, writes=[IDF])
        P.dma(IDB[:, :], idb, writes=[IDB])
        P.dma(MASK[:, :], maskd, writes=[MASK])
        P.dma(RC[:, :], rc, writes=[RC])
        P.dma(LCB[:, :], lc.partition_broadcast(128), writes=[LCB])
        P.op("gpsimd", lambda e: e.memset(EPS6[:, :], 1e-6), writes=[EPS6])
        P.op("gpsimd", lambda e: e.memset(EPS5[:, :], 1e-5), writes=[EPS5])
        P.op("gpsimd", lambda e: e.memset(ONESB[:, :], 1.0), writes=[ONESB])
        P.op("gpsimd", lambda e: e.memset(ONESF[:, :], 1.0), writes=[ONESF])
        vecT(GIN, attn_norm, 16)
        vecT(GQ, qn, 4)
        vecT(GKV, kvn, 4)

        if tabs_mode == "load":
            for ti_, tb_ in enumerate((CD, SD, CM, SM)):
                P.dma(tb_[:, :], tabs[ti_], writes=[tb_])
        else:
            with ExitStack() as stR:
                def sbR(name, shape, dtp):
                    return Buf(P, name, stR.enter_context(nc.sbuf_tensor(name, list(shape), dtp)))
                POSI = sbR("POSI", [128, S], I32)
                POSF = sbR("POSF", [128, S], F32)
                Y = sbR("Y", [128, S], F32)
                Y2 = sbR("Y2", [128, S], F32)
                KI = sbR("KI", [128, S], I32)
                KF = sbR("KF", [128, S], F32)
                P.dma(POSI[:, :], pos.partition_broadcast(128), writes=[POSI])
                P.op("vector", lambda e: e.tensor_copy(out=POSF[:, :], in_=POSI[:, :]), reads=[POSI], writes=[POSF])

                def sincos(invf_col, sin_dst, sin_mul_col, cos_dst, cos_mul_col, cos_add_col):
                    P.op("vector", lambda e: e.tensor_scalar(out=Y[:, :], in0=POSF[:, :], scalar1=RC[:, invf_col:invf_col + 1],
                                                              scalar2=1.0 / TWO_PI, op0=ALU.mult, op1=ALU.mult),
                         reads=[POSF, RC], writes=[Y])
                    for shift, dst, mulc, addc in ((0.0, sin_dst, sin_mul_col, None), (0.25, cos_dst, cos_mul_col, cos_add_col)):
                        P.op("vector", lambda e, shift=shift: e.tensor_scalar(out=Y2[:, :], in0=Y[:, :], scalar1=shift, scalar2=None,
                                                                               op0=ALU.add), reads=[Y], writes=[Y2])
                        P.op("vector", lambda e: e.tensor_copy(out=KI[:, :], in_=Y2[:, :]), reads=[Y2], writes=[KI])
                        P.op("vector", lambda e: e.tensor_copy(out=KF[:, :], in_=KI[:, :]), reads=[KI], writes=[KF])
                        P.op("vector", lambda e: e.tensor_tensor(out=Y2[:, :], in0=Y2[:, :], in1=KF[:, :], op=ALU.subtract),
                             reads=[Y2, KF], writes=[Y2])
                        P.op("vector", lambda e: e.tensor_scalar(out=KF[:, :], in0=Y2[:, :], scalar1=0.5, scalar2=None, op0=ALU.is_gt),
                             reads=[Y2], writes=[KF])
                        P.op("vector", lambda e: e.tensor_tensor(out=Y2[:, :], in0=Y2[:, :], in1=KF[:, :], op=ALU.subtract),
                             reads=[Y2, KF], writes=[Y2])
                        P.op("vector", lambda e: e.tensor_scalar(out=KF[:, :], in0=Y2[:, :], scalar1=-0.5, scalar2=None, op0=ALU.is_lt),
                             reads=[Y2], writes=[KF])
                        P.op("vector", lambda e: e.tensor_tensor(out=Y2[:, :], in0=Y2[:, :], in1=KF[:, :], op=ALU.add),
                             reads=[Y2, KF], writes=[Y2])
                        P.op("scalar", lambda e: e.activation(out=KF[:, :], in_=Y2[:, :], func=AF.Sin, scale=TWO_PI),
                             reads=[Y2], writes=[KF])
                        if addc is None:
                            P.op("vector", lambda e, dst=dst, mulc=mulc: e.tensor_scalar(
                                out=dst[:, :], in0=KF[:, :], scalar1=RC[:, mulc:mulc + 1], scalar2=None, op0=ALU.mult),
                                reads=[KF, RC], writes=[dst])
                        else:
                            P.op("vector", lambda e, dst=dst, mulc=mulc, addc=addc: e.tensor_scalar(
                                out=dst[:, :], in0=KF[:, :], scalar1=RC[:, mulc:mulc + 1], scalar2=RC[:, addc:addc + 1],
                                op0=ALU.mult, op1=ALU.add), reads=[KF, RC], writes=[dst])

                sincos(0, SD, 2, CD, 1, 5)
                sincos(3, SM, 4, CM, 6, 7)
                if tabs_mode == "compute_store":
                    TABS = P.wrap("TABS", tabs)
                    for ti_, tb_ in enumerate((CD, SD, CM, SM)):
                        P.dma(tabs[ti_], tb_[:, :], reads=[tb_], writes=[TABS])
                P.barrier()

        if STOP == 1:
            P.dma(mixT[0:128, :], MIXH[0][:, :], reads=[MIXH[0]], writes=[OUT])
            P.finish([OUT])
            P.emit()
            return nc
        def load_w1(col0, ncols):
            wb = WB[cnt["wb"] % 4]
            cnt["wb"] += 1
            for k in range(16):
                s = stage()
                P.dma(s[:, 0:ncols], W1[k * 128:(k + 1) * 128, col0:col0 + ncols], writes=[s])
                cast(wb[:, k, 0:ncols], s[:, 0:ncols], GIN[:, k:k + 1], reads=[s, GIN], writes=[wb])
            return wb

        def load_u(U, col0, ncols, G):
            wb = WB[cnt["wb"] % 4]
            cnt["wb"] += 1
            for r in range(4):
                s = stage()
                P.dma(s[:, 0:ncols], U[r * 128:(r + 1) * 128, col0:col0 + ncols], writes=[s])
                cast(wb[:, r, 0:ncols], s[:, 0:ncols], G[:, r:r + 1], reads=[s, G], writes=[wb])
            return wb

        LOADS = [
            lambda: load_w1(C_Q, 256), lambda: load_w1(C_QS, 256), lambda: load_w1(C_K, 256), lambda: load_w1(C_KS, 256),
            lambda: load_w1(C_V, 256),
            lambda: load_w1(C_CQ, 256), lambda: load_w1(C_CQ + 256, 256),
            lambda: load_u(U1, 0, 128, GQ), lambda: load_u(U1, 128, 128, GQ), lambda: load_u(U1, 256, 128, GQ),
            lambda: load_u(U1, 384, 64, GQ), lambda: load_u(U1, 576, 64, GQ),
            lambda: load_u(U1, 448, 64, GQ), lambda: load_u(U1, 640, 64, GQ),
            lambda: load_u(U1, 512, 64, GQ), lambda: load_u(U1, 704, 64, GQ),
            lambda: load_w1(C_CKV, 256), lambda: load_w1(C_CKV + 256, 256),
            lambda: load_u(U2, 0, 128, GKV), lambda: load_u(U2, 128, 128, GKV), lambda: load_u(U2, 256, 128, GKV),
            lambda: load_u(U2, 384, 256, GKV), lambda: load_u(U2, 640, 128, GKV),
            lambda: load_w1(C_KR, 128),
            lambda: load_w1(C_FQ, 256), lambda: load_w1(C_FQ + 256, 128),
            lambda: load_w1(C_FK, 256), lambda: load_w1(C_FK + 256, 128),
            lambda: load_w1(C_FV, 256), lambda: load_w1(C_FV + 256, 128),
            lambda: load_w1(C_FG, 3),
        ]
        issued = []
        consumed = [0]

        def nxt():
            while len(issued) < min(len(LOADS), consumed[0] + 3):
                issued.append(LOADS[len(issued)]())
            wb = issued[consumed[0]]
            consumed[0] += 1
            return wb

        def prefetch(nahead):
            while len(issued) < min(len(LOADS), consumed[0] + nahead):
                issued.append(LOADS[len(issued)]())

        prefetch(3)
        with ExitStack() as stX:
            XT = [Buf(P, f"XT{i}", stX.enter_context(nc.sbuf_tensor(f"XT{i}", [128, D], BF16))) for i in range(2)]
            for t in range(NT):
                xt = XT[t % 2]
                P.dma(xt[:, :], xn[t * 128:(t + 1) * 128, :], writes=[xt])
                for g in range(2):
                    for kk in range(8):
                        k = g * 8 + kk
                        P.op("tensor", lambda e, k=k, kk=kk, xt=xt: e.transpose(
                            out=PTR[:, kk * 128:(kk + 1) * 128], in_=xt[:, k * 128:(k + 1) * 128], identity=IDB[:, :]),
                            reads=[xt, IDB], writes=[PTR])
                    src = PTR[:, :].rearrange("p (k t) -> p k t", t=128)
                    dst = HT[:, g * 8:(g + 1) * 8, t * 128:(t + 1) * 128]
                    if g == 0:
                        P.op("vector", lambda e, src=src, dst=dst: e.tensor_copy(out=dst, in_=src), reads=[PTR], writes=[HT])
                    else:
                        P.op("scalar", lambda e, src=src, dst=dst: e.activation(out=dst, in_=src, func=AF.Copy),
                             reads=[PTR], writes=[HT])
            P.barrier()

        if STOP == 2:
            P.dma(mixT[0:128, :], MIXH[0][:, :], reads=[MIXH[0]], writes=[OUT])
            P.finish([OUT])
            P.emit()
            return nc
        def proj_fm(wb, c0, m, src, nk, r, pm):
            for k in range(nk):
                P.op("tensor", lambda e, k=k: e.matmul(pm[0:m, :], lhsT=wb[:, k, c0:c0 + m],
                                                        rhs=src[:, k, r * 512:(r + 1) * 512],
                                                        start=(k == 0), stop=(k == nk - 1)),
                     reads=[wb, src], writes=[pm])

        def proj_tm(wb, c0, n, src, nk, j, pm):
            for k in range(nk):
                P.op("tensor", lambda e, k=k: e.matmul(pm[:, 0:n], lhsT=src[:, k, j * 128:(j + 1) * 128],
                                                        rhs=wb[:, k, c0:c0 + n], start=(k == 0), stop=(k == nk - 1)),
                     reads=[wb, src], writes=[pm])

        def rope_fm(wb, c_main, c_swap, m, src, nk, dst, Ct, St):
            for r in range(4):
                cs = slice(r * 512, (r + 1) * 512)
                i = cnt["t"] % 2
                cnt["t"] += 1
                p1 = next_proj()
                proj_fm(wb, c_main, m, src, nk, r, p1)
                P.op("vector", lambda e, p1=p1, i=i, cs=cs: e.tensor_tensor(out=T1[i][0:m, :], in0=p1[0:m, :], in1=Ct[0:m, cs],
                                                                             op=ALU.mult), reads=[p1, Ct], writes=[T1[i]])
                p2 = next_proj()
                proj_fm(wb, c_swap, m, src, nk, r, p2)
                P.op("vector", lambda e, p2=p2, i=i, cs=cs: e.tensor_tensor(out=T2[i][0:m, :], in0=p2[0:m, :], in1=St[0:m, cs],
                                                                             op=ALU.mult), reads=[p2, St], writes=[T2[i]])
                P.op("gpsimd", lambda e, i=i, cs=cs: e.tensor_tensor(out=dst[0:m, cs], in0=T1[i][0:m, :], in1=T2[i][0:m, :],
                                                                      op=ALU.add), reads=[T1[i], T2[i]], writes=[dst])

        def plain_fm(wb, c0, m, src, nk, dst):
            for r in range(4):
                cs = slice(r * 512, (r + 1) * 512)
                p1 = next_proj()
                proj_fm(wb, c0, m, src, nk, r, p1)
                if r % 2 == 0:
                    P.op("vector", lambda e, p1=p1, cs=cs: e.tensor_copy(out=dst[0:m, cs], in_=p1[0:m, :]), reads=[p1], writes=[dst])
                else:
                    P.op("scalar", lambda e, p1=p1, cs=cs: e.activation(out=dst[0:m, cs], in_=p1[0:m, :], func=AF.Copy),
                         reads=[p1], writes=[dst])

        def v_tm(wb, c0, nheads, src, nk, Vs):
            n = nheads * 128
            for j in range(NT):
                pm = next_proj()
                proj_tm(wb, c0, n, src, nk, j, pm)
                for h in range(nheads):
                    if (j + h) % 2 == 0:
                        P.op("vector", lambda e, pm=pm, h=h, j=j: e.tensor_copy(out=Vs[h][:, j, 0:128], in_=pm[:, h * 128:(h + 1) * 128]),
                             reads=[pm], writes=[Vs[h]])
                    else:
                        P.op("scalar", lambda e, pm=pm, h=h, j=j: e.activation(out=Vs[h][:, j, 0:128], in_=pm[:, h * 128:(h + 1) * 128],
                                                                                func=AF.Copy), reads=[pm], writes=[Vs[h]])

        def new_v(alloc, name):
            v = alloc(name, [128, NT, 136], BF16)
            P.op("gpsimd", lambda e: e.memset(v[:, :, :], 1.0), writes=[v])
            return v

        sidx_box = [0]

        def attention_chunk(c, kparts, qparts, scale, Vp, post, bias=None, extra=None, bias_tiles=None):
            info = {}

            def qk_exp(j):
                tlo = max(4 * c, j)
                t0 = tlo * 128
                n = (4 * c + 4) * 128 - t0
                sidx = sidx_box[0]
                sidx_box[0] += 1
                ps = PS[sidx % 2]
                ptb = PTB[sidx % 3]
                info[j] = (tlo, ptb)
                nparts = len(kparts)
                for i in range(nparts):
                    kb, kp0, kp1 = kparts[i]
                    qb, qp0, qp1 = qparts[i]
                    P.op("tensor", lambda e, i=i, kb=kb, kp0=kp0, kp1=kp1, qb=qb, qp0=qp0, qp1=qp1: e.matmul(
                        ps[:, 0:n], lhsT=kb[kp0:kp1, j * 128:(j + 1) * 128], rhs=qb[qp0:qp1, t0:t0 + n],
                        start=(i == 0), stop=(i == nparts - 1)), reads=[kb, qb], writes=[ps])
                if bias_tiles is not None:
                    for ti in range(tlo, 4 * c + 4):
                        off = (ti - tlo) * 128
                        P.op("scalar", lambda e, off=off, ti=ti: e.activation(
                            out=ptb[:, off:off + 128], in_=ps[:, off:off + 128], func=AF.Exp, scale=scale,
                            bias=bias_tiles[:, j, ti:ti + 1]), reads=[ps, bias_tiles], writes=[ptb])
                elif bias is None:
                    P.op("scalar", lambda e: e.activation(out=ptb[:, 0:n], in_=ps[:, 0:n], func=AF.Exp, scale=scale),
                         reads=[ps], writes=[ptb])
                else:
                    P.op("scalar", lambda e: e.activation(out=ptb[:, 0:n], in_=ps[:, 0:n], func=AF.Exp, scale=scale,
                                                          bias=bias[:, j:j + 1]), reads=[ps, bias], writes=[ptb])
                if j >= 4 * c:
                    P.op("gpsimd", lambda e: e.tensor_tensor(out=ptb[:, 0:128], in0=ptb[:, 0:128], in1=MASK[:, :], op=ALU.mult),
                         reads=[ptb, MASK], writes=[ptb])

            def pv(j):
                tlo, ptb = info[j]
                for ti in range(tlo, 4 * c + 4):
                    po = PO[ti - 4 * c]
                    off = (ti - tlo) * 128
                    P.op("tensor", lambda e, po=po, off=off, ti=ti: e.matmul(
                        po[:, 0:129], lhsT=ptb[:, off:off + 128], rhs=Vp[:, j, 0:129], start=(j == 0), stop=(j == ti)),
                        reads=[ptb, Vp], writes=[po])

            nj = 4 * c + 4
            qk_exp(0)
            for j in range(nj):
                if j + 1 < nj:
                    qk_exp(j + 1)
                pv(j)
            conts = [post(ti, PO[ti - 4 * c]) for ti in range(4 * c, 4 * c + 4)]
            for k_ in conts:
                if k_ is not None:
                    k_()

        def attention(kparts, qparts, scale, Vp, post, bias=None, extra=None, bias_tiles=None):
            for c in range(4):
                attention_chunk(c, kparts, qparts, scale, Vp, post, bias=bias, extra=extra, bias_tiles=bias_tiles)

        def finish_tile(on_src_fn, ti, mixh, tr_slot):
            onb = ONB[cnt["onb"] % 4]
            cnt["onb"] += 1
            on_src_fn(onb)
            sl = slice(tr_slot * 128, (tr_slot + 1) * 128)

            def cont():
                P.op("tensor", lambda e: e.transpose(out=PTR[:, sl], in_=onb[:, :], identity=IDB[:, :]),
                     reads=[onb, IDB], writes=[PTR])
                P.op("vector", lambda e: e.tensor_copy(out=mixh[:, ti * 128:(ti + 1) * 128], in_=PTR[:, sl]),
                     reads=[PTR], writes=[mixh])
            return cont

        def std_post(mixh):
            def post(ti, po):
                rl = RL[cnt["rl"] % 8]
                cnt["rl"] += 1
                P.op("vector", lambda e: e.reciprocal(out=rl[:, :], in_=po[:, 128:129]), reads=[po], writes=[rl])

                def w(onb):
                    P.op("vector", lambda e: e.tensor_scalar(out=onb[:, :], in0=po[:, 0:128], scalar1=rl[:, :], scalar2=None,
                                                              op0=ALU.mult), reads=[po, rl], writes=[onb])
                return finish_tile(w, ti, mixh, ti % 8)
            return post

        mix_i = [0]

        def store_head(mixh, row0):
            P.dma(mixT[row0:row0 + 128, :], mixh[:, :], reads=[mixh], writes=[OUT])

        with ExitStack() as stD:
            def sbD(name, shape, dtp):
                return Buf(P, name, stD.enter_context(nc.sbuf_tensor(name, list(shape), dtp)))
            DL = sbD("DL", [128, 256], F32)
            DJ = sbD("DJ", [128, 64], F32)
            SL = sbD("SL", [128, 4], F32)
            NEGLAM = sbD("NEGLAM", [128, 1], F32)
            P.dma(DL[:, :], dl.partition_broadcast(128), writes=[DL])
            for i in range(2):
                P.op("vector", lambda e, i=i: e.tensor_tensor(
                    out=DJ[:, :], in0=DL[:, i * 128:i * 128 + 64], in1=DL[:, i * 128 + 64:i * 128 + 128], op=ALU.mult),
                    reads=[DL], writes=[DJ])
                P.op("scalar", lambda e, i=i: e.activation(out=DJ[:, :], in_=DJ[:, :], func=AF.Copy, accum_out=SL[:, i:i + 1]),
                     reads=[DJ], writes=[DJ, SL])
            P.op("scalar", lambda e: e.activation(out=SL[:, 2:4], in_=SL[:, 0:2], func=AF.Exp), reads=[SL], writes=[SL])
            P.op("vector", lambda e: e.tensor_tensor(out=NEGLAM[:, :], in0=SL[:, 3:4], in1=SL[:, 2:3], op=ALU.subtract),
                 reads=[SL], writes=[NEGLAM])
            P.op("vector", lambda e: e.tensor_tensor(out=NEGLAM[:, :], in0=NEGLAM[:, :], in1=LCB[:, 0:1], op=ALU.subtract),
                 reads=[NEGLAM, LCB], writes=[NEGLAM])

            QT = [sbD(f"QTd{h}", [128, S], BF16) for h in range(2)]
            KT = [sbD(f"KTd{h}", [128, S], BF16) for h in range(2)]
            VD = [new_v(sbD, f"VD{h}") for h in range(2)]
            O1N = [sbD(f"O1N{i}", [128, 128], F32) for i in range(4)]
            OD = [sbD(f"OD{i}", [128, 128], F32) for i in range(4)]
            if STOP == 301:
                P.dma(mixT[0:128, :], MIXH[0][:, :], reads=[MIXH[0]], writes=[OUT])
                P.finish([OUT])
                P.emit()
                return nc
            wq = nxt()
            wqs = nxt()
            def rope2(wm, ws, c0, dst):
                for r in range(4):
                    cs = slice(r * 512, (r + 1) * 512)
                    i = cnt["t"] % 2
                    cnt["t"] += 1
                    p1 = next_proj()
                    proj_fm(wm, c0, 128, HT, 16, r, p1)
                    P.op("vector", lambda e, p1=p1, i=i, cs=cs: e.tensor_tensor(out=T1[i][:, :], in0=p1[:, :], in1=CD[:, cs], op=ALU.mult),
                         reads=[p1, CD], writes=[T1[i]])
                    p2 = next_proj()
                    proj_fm(ws, c0, 128, HT, 16, r, p2)
                    P.op("vector", lambda e, p2=p2, i=i, cs=cs: e.tensor_tensor(out=T2[i][:, :], in0=p2[:, :], in1=SD[:, cs], op=ALU.mult),
                         reads=[p2, SD], writes=[T2[i]])
                    P.op("gpsimd", lambda e, i=i, cs=cs: e.tensor_tensor(out=dst[:, cs], in0=T1[i][:, :], in1=T2[i][:, :], op=ALU.add),
                         reads=[T1[i], T2[i]], writes=[dst])
            for h in range(2):
                rope2(wq, wqs, h * 128, QT[h])
            if STOP == 302:
                P.dma(mixT[0:128, :], MIXH[0][:, :], reads=[MIXH[0]], writes=[OUT])
                P.finish([OUT])
                P.emit()
                return nc
            wk = nxt()
            wks = nxt()
            for h in range(2):
                rope2(wk, wks, h * 128, KT[h])
            if STOP == 303:
                P.dma(mixT[0:128, :], MIXH[0][:, :], reads=[MIXH[0]], writes=[OUT])
                P.finish([OUT])
                P.emit()
                return nc
            wv = nxt()
            v_tm(wv, 0, 2, HT, 16, VD)
            if STOP == 31:
                P.dma(mixT[0:128, :], MIXH[0][:, :], reads=[MIXH[0]], writes=[OUT])
                P.finish([OUT])
                P.emit()
                return nc

            for h in range(2):
                mixh = MIXH[mix_i[0] % 2]
                mix_i[0] += 1

                def post1(ti, po):
                    rl = RL[cnt["rl"] % 8]
                    cnt["rl"] += 1
                    P.op("vector", lambda e: e.reciprocal(out=rl[:, :], in_=po[:, 128:129]), reads=[po], writes=[rl])
                    o1 = O1N[ti % 4]
                    P.op("vector", lambda e: e.tensor_scalar(out=o1[:, :], in0=po[:, 0:128], scalar1=rl[:, :], scalar2=None, op0=ALU.mult),
                         reads=[po, rl], writes=[o1])

                def post2(ti, po, mixh=mixh):
                    rl = RL[cnt["rl"] % 8]
                    cnt["rl"] += 1
                    ssq = RL[cnt["rl"] % 8]
                    cnt["rl"] += 1
                    o1 = O1N[ti % 4]
                    od = OD[ti % 4]
                    P.op("vector", lambda e: e.reciprocal(out=rl[:, :], in_=po[:, 128:129]), reads=[po], writes=[rl])
                    P.op("vector", lambda e: e.tensor_scalar(out=od[:, :], in0=po[:, 0:128], scalar1=rl[:, :], scalar2=None, op0=ALU.mult),
                         reads=[po, rl], writes=[od])

                    def cont():
                        P.op("vector", lambda e: e.scalar_tensor_tensor(out=od[:, :], in0=od[:, :], scalar=NEGLAM[:, 0:1], in1=o1[:, :],
                                                                         op0=ALU.mult, op1=ALU.add), reads=[od, NEGLAM, o1], writes=[od])
                        P.op("scalar", lambda e: e.activation(out=o1[:, :], in_=od[:, :], func=AF.Square, accum_out=ssq[:, :]),
                             reads=[od], writes=[o1, ssq])
                        P.op("scalar", lambda e: e.activation(out=ssq[:, :], in_=ssq[:, :], func=AF.Sqrt, scale=1.0 / 128, bias=EPS5[:, :]),
                             reads=[ssq, EPS5], writes=[ssq])
                        P.op("vector", lambda e: e.reciprocal(out=ssq[:, :], in_=ssq[:, :]), reads=[ssq], writes=[ssq])

                        def w(onb):
                            P.op("vector", lambda e: e.tensor_scalar(out=onb[:, :], in0=od[:, :], scalar1=ssq[:, :], scalar2=None, op0=ALU.mult),
                                 reads=[od, ssq], writes=[onb])
                        k2_ = finish_tile(w, ti, mixh, ti % 8)
                        k2_()
                    return cont

                for c in range(4):
                    attention_chunk(c, [(KT[h], 0, 64)], [(QT[h], 0, 64)], 0.125, VD[h], post1)
                    if STOP == 32:
                        P.dma(mixT[0:128, :], MIXH[0][:, :], reads=[MIXH[0]], writes=[OUT])
                        P.finish([OUT])
                        P.emit()
                        return nc
                    attention_chunk(c, [(KT[h], 64, 128)], [(QT[h], 64, 128)], 0.125, VD[h], post2)
                store_head(mixh, h * 128)
            P.barrier()

        if STOP == 3:
            P.dma(mixT[0:128, :], MIXH[0][:, :], reads=[MIXH[0]], writes=[OUT])
            P.finish([OUT])
            P.emit()
            return nc
        with ExitStack() as stM:
            def sbM(name, shape, dtp):
                return Buf(P, name, stM.enter_context(nc.sbuf_tensor(name, list(shape), dtp)))
            QN = [sbM(f"QNm{h}", [128, S], BF16) for h in range(3)]
            QR = [sbM(f"QRm{h}", [128, S], BF16) for h in range(3)]
            KN = [sbM(f"KNm{h}", [128, S], BF16) for h in range(3)]
            KR = sbM("KRm", [128, S], BF16)
            VM = [new_v(sbM, f"VM{h}") for h in range(3)]
            SQ = [sbM(f"SQ{i}", [128, 512], BF16) for i in range(3)]
            RSTD = sbM("RSTD", [128, 512], F32)

            def latent(col0, dst):
                wl = [nxt(), nxt()]
                pend = []
                for rcn in range(4):
                    for r in range(4):
                        cs = slice(r * 512, (r + 1) * 512)
                        pm = next_proj()
                        sq = SQ[(rcn * 4 + r) % 3]
                        proj_fm(wl[rcn // 2], (rcn % 2) * 128, 128, HT, 16, r, pm)
                        P.op("vector", lambda e, pm=pm, rcn=rcn, cs=cs: e.tensor_copy(out=dst[:, rcn, cs], in_=pm[:, :]), reads=[pm], writes=[dst])
                        P.op("scalar", lambda e, pm=pm, sq=sq: e.activation(out=sq[:, :], in_=pm[:, :], func=AF.Square),
                             reads=[pm], writes=[sq])
                        if pend:
                            pend.pop()()
                        pend.append(lambda rcn=rcn, r=r, sq=sq: P.op("tensor", lambda e: e.matmul(
                            PO[r][:, :], lhsT=ONESB[:, :], rhs=sq[:, :], start=(rcn == 0), stop=(rcn == 3)),
                            reads=[ONESB, sq], writes=[PO[r]]))
                pend.pop()()
                for r in range(4):
                    cs = slice(r * 512, (r + 1) * 512)
                    P.op("scalar", lambda e, r=r: e.activation(out=RSTD[:, :], in_=PO[r][:, :], func=AF.Sqrt, scale=1.0 / 512,
                                                                bias=EPS6[:, :]), reads=[PO[r], EPS6], writes=[RSTD])
                    P.op("vector", lambda e: e.reciprocal(out=RSTD[:, :], in_=RSTD[:, :]), reads=[RSTD], writes=[RSTD])
                    for rcn in range(4):
                        P.op("gpsimd" if rcn % 2 else "vector", lambda e, rcn=rcn, cs=cs: e.tensor_tensor(
                            out=dst[:, rcn, cs], in0=dst[:, rcn, cs], in1=RSTD[:, :], op=ALU.mult),
                            reads=[dst, RSTD], writes=[dst])

            with ExitStack() as stM1:
                CQN = Buf(P, "CQN", stM1.enter_context(nc.sbuf_tensor("CQN", [128, 4, S], BF16)))
                latent(C_CQ, CQN)
                for h in range(3):
                    wu = nxt()
                    plain_fm(wu, 0, 128, CQN, 4, QN[h])
                for h in range(3):
                    wu = nxt()
                    wus = nxt()
                    for r in range(4):
                        cs = slice(r * 512, (r + 1) * 512)
                        i = cnt["t"] % 2
                        cnt["t"] += 1
                        p1 = next_proj()
                        proj_fm(wu, 0, 64, CQN, 4, r, p1)
                        P.op("vector", lambda e, p1=p1, i=i, cs=cs: e.tensor_tensor(out=T1[i][0:64, :], in0=p1[0:64, :], in1=CM[0:64, cs],
                                                                                     op=ALU.mult), reads=[p1, CM], writes=[T1[i]])
                        p2 = next_proj()
                        proj_fm(wus, 0, 64, CQN, 4, r, p2)
                        P.op("vector", lambda e, p2=p2, i=i, cs=cs: e.tensor_tensor(out=T2[i][0:64, :], in0=p2[0:64, :], in1=SM[0:64, cs],
                                                                                     op=ALU.mult), reads=[p2, SM], writes=[T2[i]])
                        P.op("gpsimd", lambda e, i=i, cs=cs, h=h: e.tensor_tensor(out=QR[h][0:64, cs], in0=T1[i][0:64, :], in1=T2[i][0:64, :],
                                                                                   op=ALU.add), reads=[T1[i], T2[i]], writes=[QR[h]])
                P.barrier()
            with ExitStack() as stM2:
                CKN = Buf(P, "CKN", stM2.enter_context(nc.sbuf_tensor("CKN", [128, 4, S], BF16)))
                latent(C_CKV, CKN)
                for h in range(3):
                    wu = nxt()
                    plain_fm(wu, 0, 128, CKN, 4, KN[h])
                wv1 = nxt()
                v_tm(wv1, 0, 2, CKN, 4, VM[0:2])
                wv2 = nxt()
                v_tm(wv2, 0, 1, CKN, 4, VM[2:3])
                P.barrier()
            wkr = nxt()
            for r in range(4):
                cs = slice(r * 512, (r + 1) * 512)
                i = cnt["t"] % 2
                cnt["t"] += 1
                p1 = next_proj()
                proj_fm(wkr, 0, 64, HT, 16, r, p1)
                P.op("vector", lambda e, p1=p1, i=i, cs=cs: e.tensor_tensor(out=T1[i][0:64, :], in0=p1[0:64, :], in1=CM[0:64, cs], op=ALU.mult),
                     reads=[p1, CM], writes=[T1[i]])
                p2 = next_proj()
                proj_fm(wkr, 64, 64, HT, 16, r, p2)
                P.op("vector", lambda e, p2=p2, i=i, cs=cs: e.tensor_tensor(out=T2[i][0:64, :], in0=p2[0:64, :], in1=SM[0:64, cs], op=ALU.mult),
                     reads=[p2, SM], writes=[T2[i]])
                P.op("gpsimd", lambda e, i=i, cs=cs: e.tensor_tensor(out=KR[0:64, cs], in0=T1[i][0:64, :], in1=T2[i][0:64, :], op=ALU.add),
                     reads=[T1[i], T2[i]], writes=[KR])
            for h in range(3):
                mixh = MIXH[mix_i[0] % 2]
                mix_i[0] += 1
                attention([(KN[h], 0, 128), (KR, 0, 64)], [(QN[h], 0, 128), (QR[h], 0, 64)], 192 ** -0.5, VM[h], std_post(mixh))
                store_head(mixh, 256 + h * 128)
            P.barrier()

        if STOP == 4:
            P.dma(mixT[0:128, :], MIXH[0][:, :], reads=[MIXH[0]], writes=[OUT])
            P.finish([OUT])
            P.emit()
            return nc
        with ExitStack() as stF:
            def sbF(name, shape, dtp):
                return Buf(P, name, stF.enter_context(nc.sbuf_tensor(name, list(shape), dtp)))
            QF = [sbF(f"QF{h}", [128, S], BF16) for h in range(3)]
            KF_ = [sbF(f"KF{h}", [128, S], BF16) for h in range(3)]
            VF = [new_v(sbF, f"VF{h}") for h in range(3)]
            NFB = sbF("NFB", [3, 1], F32)
            ONE3 = sbF("ONE3", [3, 1], F32)
            ONEROW = sbF("ONEROW", [3, S], F32)
            GL = sbF("GL", [3, S], F32)
            CL = sbF("CL", [3, S], F32)
            NBALL = sbF("NBALL", [128, NT * 3], F32)
            R1 = sbF("R1", [128, NT * 3], F32)
            NB3 = [sbF(f"NB3_{i}", [128, NT * 3], BF16) for i in range(3)]
            E0 = sbF("E0", [128, 128], BF16)
            CLB = sbF("CLB", [128, NT * 3], F32)
            BI = [sbF(f"BI{h}", [128, NT, NT], F32) for h in range(3)]
            P.dma(NFB[:, :], fb[0:3].rearrange("(p o) -> p o", o=1), writes=[NFB])
            P.op("vector", lambda e: e.tensor_scalar(out=NFB[:, :], in0=NFB[:, :], scalar1=-1.0, scalar2=None, op0=ALU.mult),
                 reads=[NFB], writes=[NFB])
            P.op("gpsimd", lambda e: e.memset(ONEROW[:, :], 1.0), writes=[ONEROW])
            P.op("gpsimd", lambda e: e.memset(ONE3[:, :], 1.0), writes=[ONE3])
            P.op("gpsimd", lambda e: e.memset(E0[:, :], 0.0), writes=[E0])
            P.op("gpsimd", lambda e: e.memset(E0[0:1, :], 1.0), writes=[E0])
            for (c0, dsts) in ((C_FQ, QF), (C_FK, KF_)):
                wa = nxt()
                wb2 = nxt()
                plain_fm(wa, 0, 128, HT, 16, dsts[0])
                plain_fm(wa, 128, 128, HT, 16, dsts[1])
                plain_fm(wb2, 0, 128, HT, 16, dsts[2])
            wv1 = nxt()
            v_tm(wv1, 0, 2, HT, 16, VF[0:2])
            wv2 = nxt()
            v_tm(wv2, 0, 1, HT, 16, VF[2:3])
            wg = nxt()
            for r in range(4):
                cs = slice(r * 512, (r + 1) * 512)
                pm = next_proj()
                proj_fm(wg, 0, 3, HT, 16, r, pm)
                P.op("scalar", lambda e, pm=pm, cs=cs: e.activation(out=GL[0:3, cs], in_=pm[0:3, :], func=AF.Exp, scale=-1.0,
                                                                     bias=NFB[0:3, 0:1]), reads=[pm, NFB], writes=[GL])
            P.op("scalar", lambda e: e.activation(out=GL[0:3, :], in_=GL[0:3, :], func=AF.Ln, bias=ONE3[0:3, 0:1]),
                 reads=[GL, ONE3], writes=[GL])
            P.op("vector", lambda e: e.tensor_tensor_scan(out=CL[0:3, :], data0=ONEROW[0:3, :], data1=GL[0:3, :], initial=0.0,
                                                           op0=ALU.mult, op1=ALU.add), reads=[ONEROW, GL], writes=[CL])
            pm = next_proj()
            for j in range(NT):
                P.op("tensor", lambda e, j=j, pm=pm: e.transpose(out=pm[:, j * 3:(j + 1) * 3], in_=CL[0:3, j * 128:(j + 1) * 128],
                                                                identity=IDF[0:3, 0:3]), reads=[CL, IDF], writes=[pm])
            P.op("vector", lambda e, pm=pm: e.tensor_copy(out=NBALL[:, :], in_=pm[:, 0:NT * 3]), reads=[pm], writes=[NBALL])
            P.op("vector", lambda e: e.tensor_copy(out=NB3[0][:, :], in_=NBALL[:, :]), reads=[NBALL], writes=[NB3[0]])
            P.op("vector", lambda e: e.tensor_tensor(out=R1[:, :], in0=NBALL[:, :], in1=NB3[0][:, :], op=ALU.subtract),
                 reads=[NBALL, NB3[0]], writes=[R1])
            P.op("vector", lambda e: e.tensor_copy(out=NB3[1][:, :], in_=R1[:, :]), reads=[R1], writes=[NB3[1]])
            P.op("vector", lambda e: e.tensor_tensor(out=R1[:, :], in0=R1[:, :], in1=NB3[1][:, :], op=ALU.subtract),
                 reads=[R1, NB3[1]], writes=[R1])
            P.op("vector", lambda e: e.tensor_copy(out=NB3[2][:, :], in_=R1[:, :]), reads=[R1], writes=[NB3[2]])
            pm = next_proj()
            for i in range(3):
                P.op("tensor", lambda e, i=i, pm=pm: e.matmul(pm[:, 0:NT * 3], lhsT=E0[:, :], rhs=NB3[i][:, :],
                                                             start=(i == 0), stop=(i == 2)), reads=[E0, NB3[i]], writes=[pm])
            P.op("vector", lambda e, pm=pm: e.tensor_copy(out=CLB[:, :], in_=pm[:, 0:NT * 3]), reads=[pm], writes=[CLB])
            for h in range(3):
                for j in range(NT):
                    P.op("vector", lambda e, h=h, j=j: e.tensor_scalar(
                        out=BI[h][:, j, :], in0=CLB[:, :].rearrange("p (t h) -> p t h", h=3)[:, :, h],
                        scalar1=NBALL[:, j * 3 + h:j * 3 + h + 1], scalar2=-1.0, op0=ALU.subtract, op1=ALU.mult),
                        reads=[CLB, NBALL], writes=[BI[h]])
            for h in range(3):
                mixh = MIXH[mix_i[0] % 2]
                mix_i[0] += 1
                attention([(KF_[h], 0, 128)], [(QF[h], 0, 128)], 128 ** -0.5, VF[h], std_post(mixh), bias_tiles=BI[h])
                store_head(mixh, 640 + h * 128)
            P.barrier()
        P.drain_all()


D = 2048
DFF = 5632
NTOK = 1024
NH = 2
TT = NTOK // 128
NG = 11
EPS = 1e-6


def body_k2(nc, P, io):
    x_main, x_halo, mix_main, mix_halo = io["x_main"], io["x_halo"], io["mix_main"], io["mix_halo"]
    w_o, w_up, conv_w, conv_b, w_down = io["w_o"], io["w_up"], io["conv_w"], io["conv_b"], io["w_down"]
    ffn_norm, dnorm, fnorm, lc, idf, idb = io["ffn_norm"], io["dnorm"], io["fnorm"], io["lc"], io["idf"], io["idb"]
    x_out, xn_out, fin = io["x_out"], io["xn_out"], io["fin"]
    with ExitStack() as st:
        P.stack = st
        X = [P.sb(f"X{t}", [128, D], F32) for t in range(TT)]
        XH = P.sb("XH", [NH, D], F32)
        IDF = P.sb("IDF", [128, 128], F32)
        IDB = P.sb("IDB", [128, 128], BF16)
        GUP = P.sb("GUP", [128, 16], F32)
        CW = P.sb("CW", [128, 4, 88], F32)
        DN = P.sb("DN", [128, 1], F32)
        EPSB = P.sb("EPSB", [128, 1], F32)
        ss = [P.sb(f"ss{i}", [128, 1], F32) for i in range(2)]
        sd = [P.sb(f"sd{i}", [128, 1], F32) for i in range(2)]
        rs = [P.sb(f"rs{i}", [128, 1], F32) for i in range(2)]
        STG = [P.sb(f"stg{i}", [128, 1024], F32) for i in range(4)]
        PA = P.ps("PA", [128, 1024])
        PG = P.ps("PG", [128, 1024])
        PHALO = P.ps("PHALO", [128, 512])
        ACC = [PA, PG]
        PM = [P.ps(f"PM{i}", [128, 512]) for i in range(2)]
        PT = P.ps("PT", [128, 1024], BF16)
        PH = [P.wrap("PH0", PHALO.t, lock=PHALO.lock), P.wrap("PH1", PT[:, :].bitcast(F32), lock=PT.lock)]

        OUTX = P.wrap("OUTX", x_out)
        OUTN = P.wrap("OUTN", xn_out)
        OUTF = P.wrap("OUTF", fin if fin is not None else x_out)

        stg_i = [0]

        def stage():
            b = STG[stg_i[0] % len(STG)]
            stg_i[0] += 1
            return b

        cast_i = [0]

        def cast(out_ap, in_ap, scale_ap, reads, writes):
            eng = "scalar" if cast_i[0] % 2 == 0 else "gpsimd"
            cast_i[0] += 1
            if eng == "scalar":
                if scale_ap is None:
                    P.op("scalar", lambda e: e.activation(out=out_ap, in_=in_ap, func=AF.Copy), reads=reads, writes=writes)
                else:
                    P.op("scalar", lambda e: e.activation(out=out_ap, in_=in_ap, func=AF.Copy, scale=scale_ap),
                         reads=reads, writes=writes)
            else:
                if scale_ap is None:
                    P.op("gpsimd", lambda e: e.tensor_copy(out=out_ap, in_=in_ap), reads=reads, writes=writes)
                else:
                    P.op("gpsimd", lambda e: e.tensor_scalar(out=out_ap, in0=in_ap, scalar1=scale_ap, scalar2=1.0,
                                                              op0=ALU.mult, op1=ALU.mult), reads=reads, writes=writes)

        P.dma(IDF[:, :], idf, writes=[IDF])
        P.dma(IDB[:, :], idb, writes=[IDB])
        P.op("gpsimd", lambda e: e.memset(EPSB[:, :], EPS), writes=[EPSB])
        s0 = stage()
        P.dma(s0[0:16, 0:128], ffn_norm.rearrange("(k p) -> k p", p=128), writes=[s0])
        P.op("tensor", lambda e: e.transpose(out=PM[0][:, 0:16], in_=s0[0:16, 0:128], identity=IDF[0:16, 0:16]),
             reads=[s0, IDF], writes=[PM[0]])
        P.op("vector", lambda e: e.tensor_copy(out=GUP[:, :], in_=PM[0][:, 0:16]), reads=[PM[0]], writes=[GUP])
        for j in range(4):
            s1 = stage()
            src = conv_w[j:j + 1, :].rearrange("o (c p) -> (o c) p", p=128) if j < 3 else conv_b.rearrange("(c p) -> c p", p=128)
            P.dma(s1[0:88, 0:128], src, writes=[s1])
            pm = PM[(j + 1) % 2]
            P.op("tensor", lambda e, s1=s1, pm=pm: e.transpose(out=pm[:, 0:88], in_=s1[0:88, 0:128], identity=IDF[0:88, 0:88]),
                 reads=[s1, IDF], writes=[pm])
            P.op("vector", lambda e, j=j, pm=pm: e.tensor_copy(out=CW[:, j, :], in_=pm[:, 0:88]), reads=[pm], writes=[CW])
        LCB = P.sb("LCB", [128, 4], F32)
        P.dma(LCB[:, :], lc.partition_broadcast(128), writes=[LCB])
        s2 = stage()
        P.dma(s2[:, 0:1], dnorm.rearrange("(p o) -> p o", o=1), writes=[s2])
        P.op("vector", lambda e: e.tensor_scalar(out=DN[:, :], in0=s2[:, 0:1], scalar1=LCB[:, 1:2], scalar2=None, op0=ALU.mult),
             reads=[s2, LCB], writes=[DN])

        if x_halo is None:
            P.op("gpsimd", lambda e: e.memset(XH[:, :], 0.0), writes=[XH])
        else:
            P.dma(XH[:, :], x_halo, writes=[XH])
        for t in range(TT):
            P.dma(X[t][:, :], x_main[t * 128:(t + 1) * 128, :], writes=[X[t]])

        WUB = [P.sb(f"WUB{i}", [128, 16, 512], BF16) for i in range(2)]
        WD = P.sb("WD", [128, 4, 2048], BF16)

        def load_up(wb, col0):
            for k in range(16):
                s = stage()
                P.dma(s[:, 0:512], w_up[k * 128:(k + 1) * 128, col0:col0 + 512], writes=[s])
                cast(wb[:, k, :], s[:, 0:512], GUP[:, k:k + 1], reads=[s, GUP], writes=[wb])

        def load_down(g):
            for fc in range(4):
                r0 = (g * 4 + fc) * 128
                for hh in range(2):
                    s = stage()
                    P.dma(s[:, :], w_down[r0:r0 + 128, hh * 1024:(hh + 1) * 1024], writes=[s])
                    cast(WD[:, fc, hh * 1024:(hh + 1) * 1024], s[:, :], None, reads=[s], writes=[WD])


        with ExitStack() as stA:
            MT = []
            for k in range(16):
                t_ = stA.enter_context(nc.sbuf_tensor(f"MT{k}", [128, NH + NTOK], BF16))
                MT.append(Buf(P, f"MT{k}", t_))
            WOB = []
            for i in range(2):
                t_ = stA.enter_context(nc.sbuf_tensor(f"WOB{i}", [128, 16, 512], BF16))
                WOB.append(Buf(P, f"WOB{i}", t_))
            for k in range(16):
                if x_halo is None:
                    P.op("gpsimd", lambda e, k=k: e.memset(MT[k][:, 0:NH], 0.0), writes=[MT[k]])
                else:
                    P.dma(MT[k][:, 0:NH], mix_halo(k), writes=[MT[k]])
                P.dma(MT[k][:, NH:NH + NTOK], mix_main(k), writes=[MT[k]])

            def load_wo(nb):
                wb = WOB[nb % 2]
                for k in range(16):
                    s = stage()
                    P.dma(s[:, 0:512], w_o[k * 128:(k + 1) * 128, nb * 512:(nb + 1) * 512], writes=[s])
                    cast(wb[:, k, :], s[:, 0:512], DN[:, 0:1] if k < 4 else None, reads=[s, DN], writes=[wb])

            load_wo(0)
            pmi = 0
            for nb in range(4):
                if nb + 1 < 4:
                    load_wo(nb + 1)
                if nb == 2:
                    load_up(WUB[0], 0)
                    load_up(WUB[1], DFF)
                    load_down(0)
                wb = WOB[nb % 2]
                cs = slice(nb * 512, (nb + 1) * 512)
                for tt in range(-1, TT):
                    pm = PM[pmi % 2]
                    pmi += 1
                    if tt < 0:
                        np_, c0, c1, xt = NH, 0, NH, XH
                    else:
                        np_, c0, c1, xt = 128, NH + tt * 128, NH + (tt + 1) * 128, X[tt]
                    for k in range(16):
                        P.op("tensor", lambda e, k=k, pm=pm, np_=np_, c0=c0, c1=c1, wb=wb: e.matmul(
                            pm[0:np_, :], lhsT=MT[k][:, c0:c1], rhs=wb[:, k, :], start=(k == 0), stop=(k == 15)),
                            reads=[MT[k], wb], writes=[pm])
                    P.op("vector", lambda e, pm=pm, np_=np_, xt=xt, cs=cs: e.tensor_tensor(
                        out=xt[0:np_, cs], in0=pm[0:np_, :], in1=xt[0:np_, cs], op=ALU.add),
                        reads=[pm, xt], writes=[xt])
            P.barrier()

        def norm_tile(xt, np_, i, sq_out):
            b = i % 2
            P.op("scalar", lambda e: e.activation(out=sq_out[0:np_, :], in_=xt[0:np_, :], func=AF.Square,
                                                    accum_out=ss[b][0:np_, :]), reads=[xt], writes=[sq_out, ss[b]])
            P.op("scalar", lambda e: e.activation(out=sd[b][0:np_, :], in_=ss[b][0:np_, :], func=AF.Sqrt,
                                                    scale=1.0 / D, bias=EPSB[0:np_, :]), reads=[ss[b], EPSB], writes=[sd[b]])
            P.op("vector", lambda e: e.reciprocal(out=rs[b][0:np_, :], in_=sd[b][0:np_, :]), reads=[sd[b]], writes=[rs[b]])
            return rs[b]

        with ExitStack() as stH:
            H2T = Buf(P, "H2T", stH.enter_context(nc.sbuf_tensor("H2T", [128, 16, NH + NTOK], BF16)))
            with ExitStack() as stN:
                xnb = [Buf(P, f"xnb{i}", stN.enter_context(nc.sbuf_tensor(f"xnb{i}", [128, D], BF16))) for i in range(2)]

                def norm_transpose(xt, np_, i, c0):
                    b = i % 2
                    r = norm_tile(xt, np_, i, xnb[b])
                    P.op("vector", lambda e: e.tensor_scalar(out=xnb[b][0:np_, :], in0=xt[0:np_, :], scalar1=r[0:np_, :],
                                                              scalar2=None, op0=ALU.mult), reads=[xt, r], writes=[xnb[b]])
                    for g in range(2):
                        for kk in range(8):
                            k = g * 8 + kk
                            P.op("tensor", lambda e, k=k, kk=kk: e.transpose(
                                out=PT[:, kk * 128: kk * 128 + np_], in_=xnb[b][0:np_, k * 128:(k + 1) * 128],
                                identity=IDB[0:np_, 0:np_]), reads=[xnb[b], IDB], writes=[PT])
                        src = PT[:, :].rearrange("p (k t) -> p k t", t=128)[:, :, 0:np_]
                        dst = H2T[:, g * 8:(g + 1) * 8, c0:c0 + np_]
                        if g == 0:
                            P.op("vector", lambda e, src=src, dst=dst: e.tensor_copy(out=dst, in_=src), reads=[PT], writes=[H2T])
                        else:
                            P.op("scalar", lambda e, src=src, dst=dst: e.activation(out=dst, in_=src, func=AF.Copy),
                                 reads=[PT], writes=[H2T])

                norm_transpose(XH, NH, 0, 0)
                for t in range(TT):
                    norm_transpose(X[t], 128, t + 1, NH + t * 128)
                P.barrier()

            with ExitStack() as stB:
                def sbB(name, shape, dtp):
                    return Buf(P, name, stB.enter_context(nc.sbuf_tensor(name, list(shape), dtp)))
                ACTT = sbB("ACTT", [128, 4, NTOK], BF16)
                UA = [sbB(f"UA{i}", [128, NTOK], F32) for i in range(4)]
                UG = [sbB(f"UG{i}", [128, NTOK], F32) for i in range(2)]

                def conv(pp, ph, hc, uc, c):
                    w0, w1, w2, bb = CW[:, 0, c:c + 1], CW[:, 1, c:c + 1], CW[:, 2, c:c + 1], CW[:, 3, c:c + 1]
                    P.op("scalar", lambda e: e.activation(out=uc[:, :], in_=pp[:, :], func=AF.Identity, scale=w2, bias=bb),
                         reads=[pp, CW], writes=[uc])
                    P.op("vector", lambda e: e.scalar_tensor_tensor(out=uc[:, 1:NTOK], in0=pp[:, 0:NTOK - 1], scalar=w1,
                                                                     in1=uc[:, 1:NTOK], op0=ALU.mult, op1=ALU.add),
                         reads=[pp, CW, uc], writes=[uc])
                    P.op("vector", lambda e: e.scalar_tensor_tensor(out=uc[:, 2:NTOK], in0=pp[:, 0:NTOK - 2], scalar=w0,
                                                                     in1=uc[:, 2:NTOK], op0=ALU.mult, op1=ALU.add),
                         reads=[pp, CW, uc], writes=[uc])
                    P.op("vector", lambda e: e.scalar_tensor_tensor(out=uc[:, 0:1], in0=ph[:, hc + 1:hc + 2], scalar=w1,
                                                                     in1=uc[:, 0:1], op0=ALU.mult, op1=ALU.add),
                         reads=[ph, CW, uc], writes=[uc])
                    P.op("vector", lambda e: e.scalar_tensor_tensor(out=uc[:, 0:2], in0=ph[:, hc:hc + 2], scalar=w0,
                                                                     in1=uc[:, 0:2], op0=ALU.mult, op1=ALU.add),
                         reads=[ph, CW, uc], writes=[uc])

                def up(wb, fc, pp, ph, hc):
                    for k in range(16):
                        lw = wb[:, k, fc * 128:(fc + 1) * 128]
                        P.op("tensor", lambda e, k=k, lw=lw: e.matmul(ph[:, hc:hc + 2], lhsT=lw, rhs=H2T[:, k, 0:NH],
                                                                       start=(k == 0), stop=(k == 15)),
                             reads=[wb, H2T], writes=[ph])
                        for h in range(2):
                            P.op("tensor", lambda e, k=k, lw=lw, h=h: e.matmul(
                                pp[:, h * 512:(h + 1) * 512], lhsT=lw, rhs=H2T[:, k, NH + h * 512: NH + (h + 1) * 512],
                                start=(k == 0), stop=(k == 15)), reads=[wb, H2T], writes=[pp])

                pmi = 0
                ci = 0
                for g in range(NG):
                    for fc in range(4):
                        pp, ph, hc = ACC[ci % 2], PH[ci % 2], 0
                        ci += 1
                        up(WUB[0], fc, pp, ph, hc)
                        conv(pp, ph, hc, UA[fc], g * 4 + fc)
                    if g + 1 < NG:
                        load_up(WUB[0], (g + 1) * 512)
                    for fc in range(4):
                        pp, ph, hc = ACC[ci % 2], PH[ci % 2], 0
                        ci += 1
                        ug = UG[fc % 2]
                        up(WUB[1], fc, pp, ph, hc)
                        conv(pp, ph, hc, ug, 44 + g * 4 + fc)
                        P.op("scalar", lambda e, ug=ug: e.activation(out=ug[:, :], in_=ug[:, :], func=AF.Silu),
                             reads=[ug], writes=[ug])
                        P.op("gpsimd", lambda e, ug=ug, fc=fc: e.tensor_tensor(out=ACTT[:, fc, :], in0=ug[:, :], in1=UA[fc][:, :],
                                                                                 op=ALU.mult),
                             reads=[ug, UA[fc]], writes=[ACTT])
                    if g + 1 < NG:
                        load_up(WUB[1], DFF + (g + 1) * 512)
                    for tt in range(TT):
                        for nb in range(4):
                            pm = PM[pmi % 2]
                            pmi += 1
                            cs = slice(nb * 512, (nb + 1) * 512)
                            for fc in range(4):
                                P.op("tensor", lambda e, fc=fc, tt=tt, cs=cs, pm=pm: e.matmul(
                                    pm[:, :], lhsT=ACTT[:, fc, tt * 128:(tt + 1) * 128], rhs=WD[:, fc, cs],
                                    start=(fc == 0), stop=(fc == 3)), reads=[ACTT, WD], writes=[pm])
                            P.op("vector", lambda e, tt=tt, cs=cs, pm=pm: e.tensor_tensor(
                                out=X[tt][:, cs], in0=pm[:, :], in1=X[tt][:, cs], op=ALU.add),
                                reads=[pm, X[tt]], writes=[X[tt]])
                    if g + 1 < NG:
                        load_down(g + 1)
                P.barrier()

        with ExitStack() as stC:
            def sbC(name, shape, dtp):
                return Buf(P, name, stC.enter_context(nc.sbuf_tensor(name, list(shape), dtp)))
            FO = [sbC(f"FO{i}", [128, D], F32) for i in range(2)]
            xnc = [sbC(f"xnc{i}", [128, D], BF16) for i in range(2)]
            FG = sbC("FG", [128, D], F32)
            P.dma(FG[:, :], fnorm.partition_broadcast(128), writes=[FG])
            for t in range(TT):
                b = (t + 1) % 2
                P.dma(x_out[t * 128:(t + 1) * 128, :], X[t][:, :], reads=[X[t]], writes=[OUTX])
                r = norm_tile(X[t], 128, t + 1, xnc[b])
                P.op("gpsimd", lambda e, t=t, b=b, r=r: e.tensor_scalar(out=xnc[b][:, :], in0=X[t][:, :], scalar1=r[:, :],
                                                                         scalar2=1.0, op0=ALU.mult, op1=ALU.mult),
                     reads=[X[t], r], writes=[xnc[b]])
                P.dma(xn_out[t * 128:(t + 1) * 128, :], xnc[b][:, :], reads=[xnc[b]], writes=[OUTN])
                if fin is not None:
                    P.op("vector", lambda e, t=t, b=b, r=r: e.scalar_tensor_tensor(out=FO[b][:, :], in0=X[t][:, :], scalar=r[:, :],
                                                                                   in1=FG[:, :], op0=ALU.mult, op1=ALU.mult),
                         reads=[X[t], r, FG], writes=[FO[b]])
                    P.dma(fin[t * 128:(t + 1) * 128, :], FO[b][:, :], reads=[FO[b]], writes=[OUTF])
            P.drain_all()


bf16 = ml_dtypes.bfloat16
bf16 = ml_dtypes.bfloat16

ROPE_THETA = 500000.0

def swap_perm_diff():
    p = np.arange(128)
    for base in (0, 64):
        for i in range(8):
            p[base + i] = base + i + 8
            p[base + i + 8] = base + i
    return p

def swap_perm_rope64():
    p = np.arange(64)
    p[:32] = np.arange(32, 64)
    p[32:] = np.arange(0, 32)
    return p

def rope_consts():
    rc = np.zeros((128, 8), np.float32)
    invd = (ROPE_THETA ** (-np.arange(0, 16, 2, dtype=np.float32) / np.float32(16))).astype(np.float32)
    invm = (ROPE_THETA ** (-np.arange(0, 64, 2, dtype=np.float32) / np.float32(64))).astype(np.float32)
    for p in range(128):
        q = p % 64
        if q < 16:
            rc[p, 0] = invd[q % 8]
            rc[p, 1] = 1.0
            rc[p, 2] = -1.0 if q < 8 else 1.0
        else:
            rc[p, 0] = 0.0
            rc[p, 1] = 0.0
            rc[p, 2] = 0.0
        rc[p, 5] = 1.0 - rc[p, 1]
        rc[p, 3] = invm[q % 32]
        rc[p, 4] = -1.0 if q < 32 else 1.0
        rc[p, 6] = 1.0
        rc[p, 7] = 0.0
    return rc

def pack_k1(l, r, w):
    win = w["w_in"][l]
    offs = np.cumsum([0, 512, 512, 512, 512, 512, 64, 768, 768, 768, 6])
    aq, ak, av, mcq, mckv, mkr, fq, fk, fv, fg = [win[:, offs[i]:offs[i + 1]] for i in range(10)]
    pd = swap_perm_diff()
    pr = swap_perm_rope64()
    cols = []
    q = aq[:, r * 256:(r + 1) * 256]
    k = ak[:, r * 256:(r + 1) * 256]
    def swp(m):
        return np.concatenate([m[:, h * 128:(h + 1) * 128][:, pd] for h in range(2)], 1)
    cols += [q, swp(q), k, swp(k), av[:, r * 256:(r + 1) * 256], mcq, mckv, mkr, mkr[:, pr],
             fq[:, r * 384:(r + 1) * 384], fk[:, r * 384:(r + 1) * 384], fv[:, r * 384:(r + 1) * 384], fg[:, r * 3:(r + 1) * 3]]
    W1 = np.ascontiguousarray(np.concatenate(cols, 1))
    uq = w["mla_w_uq"][l]
    ukv = w["mla_w_ukv"][l]
    hs = [3 * r + i for i in range(3)]
    U1 = np.concatenate([uq[:, h * 192:h * 192 + 128] for h in hs] + [uq[:, h * 192 + 128:h * 192 + 192] for h in hs]
                        + [uq[:, h * 192 + 128:h * 192 + 192][:, pr] for h in hs], 1)
    U2 = np.concatenate([ukv[:, h * 256:h * 256 + 128] for h in hs] + [ukv[:, h * 256 + 128:h * 256 + 256] for h in hs], 1)
    fb = np.zeros(4, np.float32)
    fb[:3] = w["fox_forget_bias"][l][3 * r:3 * r + 3]
    lam_init = 0.8 - 0.6 * math.exp(-0.3 * l)
    return {
        "W1": W1, "U1": np.ascontiguousarray(U1), "U2": np.ascontiguousarray(U2),
        "attn_norm": np.ascontiguousarray(w["attn_norm"][l]), "qn": np.ascontiguousarray(w["mla_q_norm"][l]),
        "kvn": np.ascontiguousarray(w["mla_kv_norm"][l]), "fb": fb,
        "dl": np.ascontiguousarray(w["diff_lambda"][l].reshape(-1)),
        "lc": np.array([lam_init, 1.0 - lam_init, 0, 0], np.float32),
        "idf": np.eye(128, dtype=np.float32), "idb": np.eye(128, dtype=np.float32).astype(bf16),
        "mask": np.triu(np.ones((128, 128), np.float32)).astype(bf16),
        "rc": rope_consts(),
    }


class _NCP:
    def __init__(self, nc, prefix):
        self._nc = nc
        self._p = prefix

    def sbuf_tensor(self, name, *a, **k):
        return self._nc.sbuf_tensor(self._p + name, *a, **k)

    def psum_tensor(self, name, *a, **k):
        return self._nc.psum_tensor(self._p + name, *a, **k)

    def __getattr__(self, n):
        return getattr(self._nc, n)


def body_k0(nc, P, x_ap, xn_ap, ntiles):
    with ExitStack() as st:
        P.stack = st
        xt = [P.sb(f"xt{i}", [128, D], F32) for i in range(2)]
        ot = [P.sb(f"ot{i}", [128, D], BF16) for i in range(2)]
        ss = [P.sb(f"ss{i}", [128, 1], F32) for i in range(2)]
        sd = [P.sb(f"sd{i}", [128, 1], F32) for i in range(2)]
        rs = [P.sb(f"rs{i}", [128, 1], F32) for i in range(2)]
        eps = P.sb("eps", [128, 1], F32)
        P.op("gpsimd", lambda e: e.memset(eps[:, :], 1e-6), writes=[eps])
        outd = P.wrap("outd", xn_ap)
        for i in range(ntiles):
            b = i % 2
            P.dma(xt[b][:, :], x_ap[i * 128:(i + 1) * 128, :], writes=[xt[b]])
            P.op("scalar", lambda e, b=b: e.activation(out=ot[b][:, :], in_=xt[b][:, :], func=AF.Square,
                                                         accum_out=ss[b][:, :]), reads=[xt[b]], writes=[ot[b], ss[b]])
            P.op("scalar", lambda e, b=b: e.activation(out=sd[b][:, :], in_=ss[b][:, :], func=AF.Sqrt,
                                                         scale=1.0 / D, bias=eps[:, :]), reads=[ss[b], eps], writes=[sd[b]])
            P.op("vector", lambda e, b=b: e.reciprocal(out=rs[b][:, :], in_=sd[b][:, :]), reads=[sd[b]], writes=[rs[b]])
            P.op("vector", lambda e, b=b: e.tensor_scalar(out=ot[b][:, :], in0=xt[b][:, :], scalar1=rs[b][:, :],
                                                            scalar2=None, op0=ALU.mult), reads=[xt[b], rs[b]], writes=[ot[b]])
            P.dma(xn_ap[i * 128:(i + 1) * 128, :], ot[b][:, :], reads=[ot[b]], writes=[outd])
        P.drain_all()


def _mix_loc(k):
    if k < 4:
        return k // 2, k % 2
    if k < 10:
        return (k - 4) // 3, 2 + (k - 4) % 3
    return (k - 10) // 3, 5 + (k - 10) % 3


def build_fused(depth=4):
    nc0 = bass.Bass("TRN2", target_bir_lowering=False)
    dt = nc0.dram_tensor

    def ext(name, shape, dtype):
        return dt(name, list(shape), dtype, kind="ExternalInput").ap()

    L = depth
    x = ext("x", [S, D], F32)
    pos = ext("pos", [S], I32)
    W1 = ext("W1", [L, 2, D, NC1], F32)
    U1 = ext("U1", [L, 2, 512, 768], F32)
    U2 = ext("U2", [L, 2, 512, 768], F32)
    attn_norm = ext("attn_norm", [L, D], F32)
    qn = ext("qn", [L, 512], F32)
    kvn = ext("kvn", [L, 512], F32)
    fb = ext("fb", [L, 2, 4], F32)
    dl = ext("dl", [L, 256], F32)
    lc = ext("lc", [L, 4], F32)
    w_o = ext("w_o", [L, D, D], F32)
    w_up = ext("w_up", [L, D, 2 * DFF], F32)
    conv_w = ext("conv_w", [L, 3, 2 * DFF], F32)
    conv_b = ext("conv_b", [L, 2 * DFF], F32)
    w_down = ext("w_down", [L, DFF, D], F32)
    ffn_norm = ext("ffn_norm", [L, D], F32)
    dnorm = ext("dnorm", [L, 128], F32)
    fnorm = ext("fnorm", [D], F32)
    idf = ext("idf", [128, 128], F32)
    idb = ext("idb", [128, 128], BF16)
    mask = ext("mask", [128, 128], BF16)
    rc = ext("rc", [128, 8], F32)
    out = dt("out", [S, D], F32, kind="ExternalOutput").ap()
    XS = [dt(f"xs_scr{i}", [S, D], F32).ap() for i in range(2)]
    XN = [dt(f"xn_scr{i}", [S, D], BF16).ap() for i in range(2)]
    MIX = [dt(f"mix_scr{r}", [1024, S], BF16).ap() for r in range(2)]
    TABS = dt("rope_tabs", [4, 128, S], BF16).ap()

    with ExitStack() as st:
        P = Prog(nc0, st)
        cnt = [0]

        def scoped():
            cnt[0] += 1
            P.nc = _NCP(nc0, f"b{cnt[0]}_")
            return P.nc

        body_k0(scoped(), P, x, XN[0], 16)
        for l in range(L):
            x_src = x if l == 0 else XS[l % 2]
            x_dst = XS[(l + 1) % 2]
            xn_src = XN[l % 2]
            xn_dst = XN[(l + 1) % 2]
            for r in range(2):
                body_k1(scoped(), P, {
                    "xn": xn_src, "pos": pos, "W1": W1[l, r], "U1": U1[l, r], "U2": U2[l, r],
                    "attn_norm": attn_norm[l], "qn": qn[l], "kvn": kvn[l], "fb": fb[l, r], "dl": dl[l], "lc": lc[l],
                    "idf": idf, "idb": idb, "mask": mask, "rc": rc, "mixT": MIX[r],
                    "tabs": TABS, "tabs_mode": "compute_store" if (l == 0 and r == 0) else "load"})
            for hf in range(2):
                t0 = hf * 1024

                def mix_main(k, t0=t0):
                    r_, c_ = _mix_loc(k)
                    return MIX[r_][c_ * 128:(c_ + 1) * 128, t0:t0 + 1024]

                def mix_halo(k, t0=t0):
                    r_, c_ = _mix_loc(k)
                    return MIX[r_][c_ * 128:(c_ + 1) * 128, t0 - 2:t0]

                body_k2(scoped(), P, {
                    "x_main": x_src[t0:t0 + 1024, :], "x_halo": None if hf == 0 else x_src[t0 - 2:t0, :],
                    "mix_main": mix_main, "mix_halo": mix_halo,
                    "w_o": w_o[l], "w_up": w_up[l], "conv_w": conv_w[l], "conv_b": conv_b[l], "w_down": w_down[l],
                    "ffn_norm": ffn_norm[l], "dnorm": dnorm[l], "fnorm": fnorm, "lc": lc[l], "idf": idf, "idb": idb,
                    "x_out": x_dst[t0:t0 + 1024, :], "xn_out": xn_dst[t0:t0 + 1024, :],
                    "fin": out[t0:t0 + 1024, :] if l == L - 1 else None})
        P.nc = nc0
        P.stack = st
        OUTB = P.wrap("OUTB", out)
        P.finish([OUTB])
        P.emit()
    return nc0


from concourse.bass_utils import run_bass_kernel_spmd

_PROG = {}


def kernel(**inputs):
    w = {k: np.asarray(v) for k, v in inputs.items()}
    x = np.ascontiguousarray(w["x"], dtype=np.float32)
    pos = np.ascontiguousarray(w["positions"]).astype(np.int32)
    L = w["w_in"].shape[0]
    if "f" not in _PROG:
        _PROG["f"] = build_fused(L)
    packs = [[pack_k1(l, r, w) for r in range(2)] for l in range(L)]

    def st2(key):
        return np.ascontiguousarray(np.stack([np.stack([packs[l][r][key] for r in range(2)], 0) for l in range(L)], 0))

    def st1(key):
        return np.ascontiguousarray(np.stack([packs[l][0][key] for l in range(L)], 0))

    shared = {
        "W1": st2("W1"), "U1": st2("U1"), "U2": st2("U2"), "fb": st2("fb"),
        "attn_norm": st1("attn_norm"), "qn": st1("qn"), "kvn": st1("kvn"), "dl": st1("dl"), "lc": st1("lc"),
        "w_o": np.ascontiguousarray(w["w_o"]), "w_up": np.ascontiguousarray(w["ffn_w_up"]),
        "conv_w": np.ascontiguousarray(w["ffn_conv_w"]), "conv_b": np.ascontiguousarray(w["ffn_conv_b"]),
        "w_down": np.ascontiguousarray(w["ffn_w_down"]), "ffn_norm": np.ascontiguousarray(w["ffn_norm"]),
        "dnorm": np.ascontiguousarray(w["diff_out_norm"]), "fnorm": np.ascontiguousarray(w["final_norm"]),
        "idf": packs[0][0]["idf"], "idb": packs[0][0]["idb"], "mask": packs[0][0]["mask"], "rc": packs[0][0]["rc"],
    }
    cores = list(range(8))
    ins = []
    for c in cores:
        d = dict(shared)
        d["x"] = np.ascontiguousarray(x[c // 2])
        d["pos"] = np.ascontiguousarray(pos[c // 2])
        ins.append(d)
    res = run_bass_kernel_spmd(_PROG["f"], ins, core_ids=cores)
    outs = [np.asarray(res.results[2 * b]["out"]) for b in range(4)]
    return np.stack(outs, 0).astype(np.float32)
```

```python
import math
import ml_dtypes
from contextlib import ExitStack
import numpy as np
import concourse.bass as bass
import concourse.mybir as mybir

F32 = mybir.dt.float32
BF16 = mybir.dt.bfloat16
I32 = mybir.dt.int32
AF = mybir.ActivationFunctionType
ALU = mybir.AluOpType
AX = mybir.AxisListType

ENGS = ("sync", "scalar", "vector", "gpsimd", "tensor")
EPOCH = 20000


class Buf:
    def __init__(self, prog, name, t):
        self.prog = prog
        self.name = name
        self.t = t
        self.writes = {}
        self.reads = {}
        self.dsem = None
        self.dcount = 0
        self.lock = None

    def __getitem__(self, idx):
        return self.t[idx]


class Prog:
    def __init__(self, nc, stack, n_eng_sems=6):
        self.nc = nc
        self.stack = stack
        self.sem_stack = stack
        self.free_dsems = []
        self.live_dbufs = []
        self.ops = {e: [] for e in ENGS}
        self.semtab = []
        self.eng_sems = {}
        self.eng_epoch = {e: 0 for e in ENGS}
        self.eng_cnt = {e: 0 for e in ENGS}
        self.waited = {e: {} for e in ENGS}
        for e in ENGS:
            self.eng_sems[e] = [self._new_sem(f"s_{e}_{i}") for i in range(n_eng_sems)]
        self.dma_sems = []
        self.nbuf = 0

    def _new_sem(self, name):
        h = self.sem_stack.enter_context(self.nc.semaphore(name))
        self.semtab.append(h)
        return len(self.semtab) - 1

    def _dsem_for(self, owner):
        if owner.dsem is None:
            if self.free_dsems:
                owner.dsem, owner.dcount = self.free_dsems.pop()
            else:
                owner.dsem = self._new_sem(f"d{len(self.semtab)}")
                owner.dcount = 0
            self.live_dbufs.append(owner)

    def sb(self, name, shape, dtype):
        t = self.stack.enter_context(self.nc.sbuf_tensor(name, list(shape), dtype))
        return Buf(self, name, t)

    def ps(self, name, shape, dtype=F32):
        t = self.stack.enter_context(self.nc.psum_tensor(name, list(shape), dtype))
        b = Buf(self, name, t)
        b.lock = Buf(self, name + "_lock", None)
        return b

    def wrap(self, name, t, lock=None):
        b = Buf(self, name, t)
        b.lock = lock
        return b

    def _locks(self, reads, writes):
        ls = []
        for b in list(reads) + list(writes):
            if b.lock is not None and b.lock not in ls:
                ls.append(b.lock)
        return ls

    def _need(self, eng, reads, writes):
        need = {}
        for b in reads:
            for s, v in b.writes.items():
                if need.get(s, 0) < v:
                    need[s] = v
        for b in list(writes) + self._locks(reads, writes):
            for d in (b.writes, b.reads):
                for s, v in d.items():
                    if need.get(s, 0) < v:
                        need[s] = v
        if eng == "tensor":
            own = set(self.eng_sems["tensor"])
            need = {s: v for s, v in need.items() if s not in own}
        out = []
        w = self.waited[eng]
        for s, v in need.items():
            if w.get(s, 0) < v:
                w[s] = v
                out.append((s, v))
        return out

    def _emit_waits(self, eng, waits):
        for s, v in waits:
            h = self.semtab[s]
            self.ops[eng].append(lambda e, h=h, v=v: e.wait_ge(h, v))

    def _next_event(self, eng):
        if self.eng_cnt[eng] >= EPOCH:
            self.eng_epoch[eng] += 1
            self.eng_cnt[eng] = 0
        self.eng_cnt[eng] += 1
        s = self.eng_sems[eng][self.eng_epoch[eng]]
        return s, self.eng_cnt[eng]

    def op(self, eng, fn, reads=(), writes=()):
        waits = self._need(eng, reads, writes)
        self._emit_waits(eng, waits)
        s, v = self._next_event(eng)
        h = self.semtab[s]
        self.ops[eng].append(lambda e, fn=fn, h=h: fn(e).then_inc(h, 1))
        for b in list(writes) + self._locks(reads, writes):
            b.writes = {s: v}
            b.reads = {}
        for b in reads:
            if b in writes:
                continue
            b.reads[s] = max(b.reads.get(s, 0), v)
        return (s, v)

    def dma(self, out_ap, in_ap, reads=(), writes=(), q="sync", **kw):
        waits = self._need(q, reads, writes)
        self._emit_waits(q, waits)
        owner = (list(writes) + list(reads))[0]
        self._dsem_for(owner)
        owner.dcount += 16
        s, v = owner.dsem, owner.dcount
        h = self.semtab[s]
        self.ops[q].append(
            lambda e, o=out_ap, i=in_ap, h=h, kw=kw: e.dma_start(out=o, in_=i, **kw).then_inc(h, 16))
        for b in writes:
            b.writes = {s: v}
            b.reads = {}
        for b in reads:
            if b in writes:
                continue
            b.reads[s] = max(b.reads.get(s, 0), v)
        return (s, v)

    def dma_like(self, q, fn, reads=(), writes=(), inc=16):
        waits = self._need(q, reads, writes)
        self._emit_waits(q, waits)
        owner = (list(writes) + list(reads))[0]
        self._dsem_for(owner)
        owner.dcount += inc
        s, v = owner.dsem, owner.dcount
        h = self.semtab[s]
        self.ops[q].append(lambda e, fn=fn, h=h: fn(e).then_inc(h, inc))
        for b in writes:
            b.writes = {s: v}
            b.reads = {}
        for b in reads:
            if b in writes:
                continue
            b.reads[s] = max(b.reads.get(s, 0), v)
        return (s, v)

    def barrier(self):
        ev = {}
        for e in ENGS:
            if self.eng_cnt[e] > 0:
                ev[self.eng_sems[e][self.eng_epoch[e]]] = self.eng_cnt[e]
        for e in ENGS:
            for s, v in ev.items():
                if self.waited[e].get(s, 0) < v:
                    self.waited[e][s] = v
                    h = self.semtab[s]
                    self.ops[e].append(lambda en, h=h, v=v: en.wait_ge(h, v))

    def drain_all(self):
        ev = {}
        for e in ENGS:
            if self.eng_cnt[e] > 0:
                ev[self.eng_sems[e][self.eng_epoch[e]]] = self.eng_cnt[e]
        for b in self.live_dbufs:
            ev[b.dsem] = max(ev.get(b.dsem, 0), b.dcount)
        for e in ENGS:
            for s, v in ev.items():
                if self.waited[e].get(s, 0) < v:
                    self.waited[e][s] = v
                    h = self.semtab[s]
                    self.ops[e].append(lambda en, h=h, v=v: en.wait_ge(h, v))
        for b in self.live_dbufs:
            self.free_dsems.append((b.dsem, b.dcount))
            b.dsem = None
            b.dcount = 0
            b.writes = {}
            b.reads = {}
        self.live_dbufs = []

    def finish(self, bufs):
        need = {}
        for b in bufs:
            for d in (b.writes, b.reads):
                for s, v in d.items():
                    need[s] = max(need.get(s, 0), v)
        for e in ENGS:
            if e != "sync" and self.eng_cnt[e] > 0:
                s = self.eng_sems[e][self.eng_epoch[e]]
                need[s] = max(need.get(s, 0), self.eng_cnt[e])
        for s, v in need.items():
            h = self.semtab[s]
            self.ops["sync"].append(lambda en, h=h, v=v: en.wait_ge(h, v))

    def emit(self):
        nc = self.nc
        with nc.Block() as block:
            @block.sync
            def _(e):
                for f in self.ops["sync"]:
                    f(e)

            @block.scalar
            def _(e):
                for f in self.ops["scalar"]:
                    f(e)

            @block.vector
            def _(e):
                for f in self.ops["vector"]:
                    f(e)

            @block.gpsimd
            def _(e):
                for f in self.ops["gpsimd"]:
                    f(e)

            @block.tensor
            def _(e):
                for f in self.ops["tensor"]:
                    f(e)


D = 2048
S = 2048
NT = 16
NC1 = 3587
C_Q, C_QS, C_K, C_KS, C_V, C_CQ, C_CKV, C_KR, C_KRS, C_FQ, C_FK, C_FV, C_FG = (
    0, 256, 512, 768, 1024, 1280, 1792, 2304, 2368, 2432, 2816, 3200, 3584)
TWO_PI = 2.0 * math.pi


def body_k1(nc, P, io):
    STOP = 99
    xn, pos, W1, U1, U2 = io["xn"], io["pos"], io["W1"], io["U1"], io["U2"]
    attn_norm, qn, kvn, fb, dl, lc = io["attn_norm"], io["qn"], io["kvn"], io["fb"], io["dl"], io["lc"]
    idf, idb, maskd, rc, mixT = io["idf"], io["idb"], io["mask"], io["rc"], io["mixT"]
    tabs, tabs_mode = io.get("tabs"), io.get("tabs_mode", "compute")
    with ExitStack() as st:
        P.stack = st
        HT = P.sb("HT", [128, 16, S], BF16)
        IDF = P.sb("IDF", [128, 128], F32)
        IDB = P.sb("IDB", [128, 128], BF16)
        MASK = P.sb("MASK", [128, 128], BF16)
        RC = P.sb("RC", [128, 8], F32)
        GIN = P.sb("GIN", [128, 16], F32)
        GQ = P.sb("GQ", [128, 4], F32)
        GKV = P.sb("GKV", [128, 4], F32)
        LCB = P.sb("LCB", [128, 4], F32)
        EPS6 = P.sb("EPS6", [128, 1], F32)
        EPS5 = P.sb("EPS5", [128, 1], F32)
        CD = P.sb("CD", [128, S], BF16)
        SD = P.sb("SD", [128, S], BF16)
        CM = P.sb("CM", [128, S], BF16)
        SM = P.sb("SM", [128, S], BF16)
        WB = [P.sb(f"WB{i}", [128, 16, 256], BF16) for i in range(4)]
        STG = [P.sb(f"stg{i}", [128, 512], F32) for i in range(3)]
        PTB = [P.sb(f"PTB{i}", [128, 512], BF16) for i in range(3)]
        ONB = [P.sb(f"ONB{i}", [128, 128], BF16) for i in range(4)]
        _mixh = P.sb("MIXH0", [128, S], BF16)
        MIXH = [_mixh, _mixh]
        RL = [P.sb(f"RL{i}", [128, 1], F32) for i in range(8)]
        ONESB = P.sb("ONESB", [128, 128], BF16)
        ONESF = P.sb("ONESF", [1, 128], F32)
        _t1 = P.sb("T1_0", [128, 512], F32)
        _t2 = P.sb("T2_0", [128, 512], F32)
        T1 = [_t1, _t1]
        T2 = [_t2, _t2]

        PS = [P.ps(f"PS{i}", [128, 512]) for i in range(2)]
        PO = [P.ps(f"PO{i}", [128, 512]) for i in range(4)]
        PM = P.ps("PM", [128, 512])
        PTR = P.ps("PTR", [128, 1024], BF16)
        PROJ = [PM, PS[0], PS[1]]
        OUT = P.wrap("OUT", mixT)

        cnt = {"stg": 0, "cast": 0, "proj": 0, "wb": 0, "ptb": 0, "onb": 0, "rl": 0, "t": 0}

        def stage():
            b = STG[cnt["stg"] % len(STG)]
            cnt["stg"] += 1
            return b

        def cast(out_ap, in_ap, scale_ap, reads, writes):
            eng = "scalar" if cnt["cast"] % 2 == 0 else "gpsimd"
            cnt["cast"] += 1
            if eng == "scalar":
                if scale_ap is None:
                    P.op("scalar", lambda e: e.activation(out=out_ap, in_=in_ap, func=AF.Copy), reads=reads, writes=writes)
                else:
                    P.op("scalar", lambda e: e.activation(out=out_ap, in_=in_ap, func=AF.Copy, scale=scale_ap),
                         reads=reads, writes=writes)
            else:
                if scale_ap is None:
                    P.op("gpsimd", lambda e: e.tensor_copy(out=out_ap, in_=in_ap), reads=reads, writes=writes)
                else:
                    P.op("gpsimd", lambda e: e.tensor_scalar(out=out_ap, in0=in_ap, scalar1=scale_ap, scalar2=1.0,
                                                              op0=ALU.mult, op1=ALU.mult), reads=reads, writes=writes)

        def next_proj():
            b = PROJ[cnt["proj"] % 3]
            cnt["proj"] += 1
            return b

        def vecT(dst, src_ap, n):
            s = stage()
            pm = next_proj()
            P.dma(s[0:n, 0:128], src_ap.rearrange("(k p) -> k p", p=128), writes=[s])
            P.op("tensor", lambda e: e.transpose(out=pm[:, 0:n], in_=s[0:n, 0:128], identity=IDF[0:n, 0:n]),
                 reads=[s, IDF], writes=[pm])
            P.op("vector", lambda e: e.tensor_copy(out=dst[:, 0:n], in_=pm[:, 0:n]), reads=[pm], writes=[dst])

        P.dma(IDF[:, :], idf, writes=[IDF])
        P.dma(IDB[:, :], idb, writes=[IDB])
        P.dma(MASK[:, :], maskd, writes=[MASK])
        P.dma(RC[:, :], rc, writes=[RC])
        P.dma(LCB[:, :], lc.partition_broadcast(128), writes=[LCB])
        P.op("gpsimd", lambda e: e.memset(EPS6[:, :], 1e-6), writes=[EPS6])
        P.op("gpsimd", lambda e: e.memset(EPS5[:, :], 1e-5), writes=[EPS5])
        P.op("gpsimd", lambda e: e.memset(ONESB[:, :], 1.0), writes=[ONESB])
        P.op("gpsimd", lambda e: e.memset(ONESF[:, :], 1.0), writes=[ONESF])
        vecT(GIN, attn_norm, 16)
        vecT(GQ, qn, 4)
        vecT(GKV, kvn, 4)

        if tabs_mode == "load":
            for ti_, tb_ in enumerate((CD, SD, CM, SM)):
                P.dma(tb_[:, :], tabs[ti_], writes=[tb_])
        else:
            with ExitStack() as stR:
                def sbR(name, shape, dtp):
                    return Buf(P, name, stR.enter_context(nc.sbuf_tensor(name, list(shape), dtp)))
                POSI = sbR("POSI", [128, S], I32)
                POSF = sbR("POSF", [128, S], F32)
                Y = sbR("Y", [128, S], F32)
                Y2 = sbR("Y2", [128, S], F32)
                KI = sbR("KI", [128, S], I32)
                KF = sbR("KF", [128, S], F32)
                P.dma(POSI[:, :], pos.partition_broadcast(128), writes=[POSI])
                P.op("vector", lambda e: e.tensor_copy(out=POSF[:, :], in_=POSI[:, :]), reads=[POSI], writes=[POSF])

                def sincos(invf_col, sin_dst, sin_mul_col, cos_dst, cos_mul_col, cos_add_col):
                    P.op("vector", lambda e: e.tensor_scalar(out=Y[:, :], in0=POSF[:, :], scalar1=RC[:, invf_col:invf_col + 1],
                                                              scalar2=1.0 / TWO_PI, op0=ALU.mult, op1=ALU.mult),
                         reads=[POSF, RC], writes=[Y])
                    for shift, dst, mulc, addc in ((0.0, sin_dst, sin_mul_col, None), (0.25, cos_dst, cos_mul_col, cos_add_col)):
                        P.op("vector", lambda e, shift=shift: e.tensor_scalar(out=Y2[:, :], in0=Y[:, :], scalar1=shift, scalar2=None,
                                                                               op0=ALU.add), reads=[Y], writes=[Y2])
                        P.op("vector", lambda e: e.tensor_copy(out=KI[:, :], in_=Y2[:, :]), reads=[Y2], writes=[KI])
                        P.op("vector", lambda e: e.tensor_copy(out=KF[:, :], in_=KI[:, :]), reads=[KI], writes=[KF])
                        P.op("vector", lambda e: e.tensor_tensor(out=Y2[:, :], in0=Y2[:, :], in1=KF[:, :], op=ALU.subtract),
                             reads=[Y2, KF], writes=[Y2])
                        P.op("vector", lambda e: e.tensor_scalar(out=KF[:, :], in0=Y2[:, :], scalar1=0.5, scalar2=None, op0=ALU.is_gt),
                             reads=[Y2], writes=[KF])
                        P.op("vector", lambda e: e.tensor_tensor(out=Y2[:, :], in0=Y2[:, :], in1=KF[:, :], op=ALU.subtract),
                             reads=[Y2, KF], writes=[Y2])
                        P.op("vector", lambda e: e.tensor_scalar(out=KF[:, :], in0=Y2[:, :], scalar1=-0.5, scalar2=None, op0=ALU.is_lt),
                             reads=[Y2], writes=[KF])
                        P.op("vector", lambda e: e.tensor_tensor(out=Y2[:, :], in0=Y2[:, :], in1=KF[:, :], op=ALU.add),
                             reads=[Y2, KF], writes=[Y2])
                        P.op("scalar", lambda e: e.activation(out=KF[:, :], in_=Y2[:, :], func=AF.Sin, scale=TWO_PI),
                             reads=[Y2], writes=[KF])
                        if addc is None:
                            P.op("vector", lambda e, dst=dst, mulc=mulc: e.tensor_scalar(
                                out=dst[:, :], in0=KF[:, :], scalar1=RC[:, mulc:mulc + 1], scalar2=None, op0=ALU.mult),
                                reads=[KF, RC], writes=[dst])
                        else:
                            P.op("vector", lambda e, dst=dst, mulc=mulc, addc=addc: e.tensor_scalar(
                                out=dst[:, :], in0=KF[:, :], scalar1=RC[:, mulc:mulc + 1], scalar2=RC[:, addc:addc + 1],
                                op0=ALU.mult, op1=ALU.add), reads=[KF, RC], writes=[dst])

                sincos(0, SD, 2, CD, 1, 5)
                sincos(3, SM, 4, CM, 6, 7)
                if tabs_mode == "compute_store":
                    TABS = P.wrap("TABS", tabs)
                    for ti_, tb_ in enumerate((CD, SD, CM, SM)):
                        P.dma(tabs[ti_], tb_[:, :], reads=[tb_], writes=[TABS])
                P.barrier()

        if STOP == 1:
            P.dma(mixT[0:128, :], MIXH[0][:, :], reads=[MIXH[0]], writes=[OUT])
            P.finish([OUT])
            P.emit()
            return nc
        def load_w1(col0, ncols):
            wb = WB[cnt["wb"] % 4]
            cnt["wb"] += 1
            for k in range(16):
                s = stage()
                P.dma(s[:, 0:ncols], W1[k * 128:(k + 1) * 128, col0:col0 + ncols], writes=[s])
                cast(wb[:, k, 0:ncols], s[:, 0:ncols], GIN[:, k:k + 1], reads=[s, GIN], writes=[wb])
            return wb

        def load_u(U, col0, ncols, G):
            wb = WB[cnt["wb"] % 4]
            cnt["wb"] += 1
            for r in range(4):
                s = stage()
                P.dma(s[:, 0:ncols], U[r * 128:(r + 1) * 128, col0:col0 + ncols], writes=[s])
                cast(wb[:, r, 0:ncols], s[:, 0:ncols], G[:, r:r + 1], reads=[s, G], writes=[wb])
            return wb

        LOADS = [
            lambda: load_w1(C_Q, 256), lambda: load_w1(C_QS, 256), lambda: load_w1(C_K, 256), lambda: load_w1(C_KS, 256),
            lambda: load_w1(C_V, 256),
            lambda: load_w1(C_CQ, 256), lambda: load_w1(C_CQ + 256, 256),
            lambda: load_u(U1, 0, 128, GQ), lambda: load_u(U1, 128, 128, GQ), lambda: load_u(U1, 256, 128, GQ),
            lambda: load_u(U1, 384, 64, GQ), lambda: load_u(U1, 576, 64, GQ),
            lambda: load_u(U1, 448, 64, GQ), lambda: load_u(U1, 640, 64, GQ),
            lambda: load_u(U1, 512, 64, GQ), lambda: load_u(U1, 704, 64, GQ),
            lambda: load_w1(C_CKV, 256), lambda: load_w1(C_CKV + 256, 256),
            lambda: load_u(U2, 0, 128, GKV), lambda: load_u(U2, 128, 128, GKV), lambda: load_u(U2, 256, 128, GKV),
            lambda: load_u(U2, 384, 256, GKV), lambda: load_u(U2, 640, 128, GKV),
            lambda: load_w1(C_KR, 128),
            lambda: load_w1(C_FQ, 256), lambda: load_w1(C_FQ + 256, 128),
            lambda: load_w1(C_FK, 256), lambda: load_w1(C_FK + 256, 128),
            lambda: load_w1(C_FV, 256), lambda: load_w1(C_FV + 256, 128),
            lambda: load_w1(C_FG, 3),
        ]
        issued = []
        consumed = [0]

        def nxt():
            while len(issued) < min(len(LOADS), consumed[0] + 3):
                issued.append(LOADS[len(issued)]())
            wb = issued[consumed[0]]
            consumed[0] += 1
            return wb

        def prefetch(nahead):
            while len(issued) < min(len(LOADS), consumed[0] + nahead):
                issued.append(LOADS[len(issued)]())

        prefetch(3)
        with ExitStack() as stX:
            XT = [Buf(P, f"XT{i}", stX.enter_context(nc.sbuf_tensor(f"XT{i}", [128, D], BF16))) for i in range(2)]
            for t in range(NT):
                xt = XT[t % 2]
                P.dma(xt[:, :], xn[t * 128:(t + 1) * 128, :], writes=[xt])
                for g in range(2):
                    for kk in range(8):
                        k = g * 8 + kk
                        P.op("tensor", lambda e, k=k, kk=kk, xt=xt: e.transpose(
                            out=PTR[:, kk * 128:(kk + 1) * 128], in_=xt[:, k * 128:(k + 1) * 128], identity=IDB[:, :]),
                            reads=[xt, IDB], writes=[PTR])
                    src = PTR[:, :].rearrange("p (k t) -> p k t", t=128)
                    dst = HT[:, g * 8:(g + 1) * 8, t * 128:(t + 1) * 128]
                    if g == 0:
                        P.op("vector", lambda e, src=src, dst=dst: e.tensor_copy(out=dst, in_=src), reads=[PTR], writes=[HT])
                    else:
                        P.op("scalar", lambda e, src=src, dst=dst: e.activation(out=dst, in_=src, func=AF.Copy),
                             reads=[PTR], writes=[HT])
            P.barrier()

        if STOP == 2:
            P.dma(mixT[0:128, :], MIXH[0][:, :], reads=[MIXH[0]], writes=[OUT])
            P.finish([OUT])
            P.emit()
            return nc
        def proj_fm(wb, c0, m, src, nk, r, pm):
            for k in range(nk):
                P.op("tensor", lambda e, k=k: e.matmul(pm[0:m, :], lhsT=wb[:, k, c0:c0 + m],
                                                        rhs=src[:, k, r * 512:(r + 1) * 512],
                                                        start=(k == 0), stop=(k == nk - 1)),
                     reads=[wb, src], writes=[pm])

        def proj_tm(wb, c0, n, src, nk, j, pm):
            for k in range(nk):
                P.op("tensor", lambda e, k=k: e.matmul(pm[:, 0:n], lhsT=src[:, k, j * 128:(j + 1) * 128],
                                                        rhs=wb[:, k, c0:c0 + n], start=(k == 0), stop=(k == nk - 1)),
                     reads=[wb, src], writes=[pm])

        def rope_fm(wb, c_main, c_swap, m, src, nk, dst, Ct, St):
            for r in range(4):
                cs = slice(r * 512, (r + 1) * 512)
                i = cnt["t"] % 2
                cnt["t"] += 1
                p1 = next_proj()
                proj_fm(wb, c_main, m, src, nk, r, p1)
                P.op("vector", lambda e, p1=p1, i=i, cs=cs: e.tensor_tensor(out=T1[i][0:m, :], in0=p1[0:m, :], in1=Ct[0:m, cs],
                                                                             op=ALU.mult), reads=[p1, Ct], writes=[T1[i]])
                p2 = next_proj()
                proj_fm(wb, c_swap, m, src, nk, r, p2)
                P.op("vector", lambda e, p2=p2, i=i, cs=cs: e.tensor_tensor(out=T2[i][0:m, :], in0=p2[0:m, :], in1=St[0:m, cs],
                                                                             op=ALU.mult), reads=[p2, St], writes=[T2[i]])
                P.op("gpsimd", lambda e, i=i, cs=cs: e.tensor_tensor(out=dst[0:m, cs], in0=T1[i][0:m, :], in1=T2[i][0:m, :],
                                                                      op=ALU.add), reads=[T1[i], T2[i]], writes=[dst])

        def plain_fm(wb, c0, m, src, nk, dst):
            for r in range(4):
                cs = slice(r * 512, (r + 1) * 512)
                p1 = next_proj()
                proj_fm(wb, c0, m, src, nk, r, p1)
                if r % 2 == 0:
                    P.op("vector", lambda e, p1=p1, cs=cs: e.tensor_copy(out=dst[0:m, cs], in_=p1[0:m, :]), reads=[p1], writes=[dst])
                else:
                    P.op("scalar", lambda e, p1=p1, cs=cs: e.activation(out=dst[0:m, cs], in_=p1[0:m, :], func=AF.Copy),
                         reads=[p1], writes=[dst])

        def v_tm(wb, c0, nheads, src, nk, Vs):
            n = nheads * 128
            for j in range(NT):
                pm = next_proj()
                proj_tm(wb, c0, n, src, nk, j, pm)
                for h in range(nheads):
                    if (j + h) % 2 == 0:
                        P.op("vector", lambda e, pm=pm, h=h, j=j: e.tensor_copy(out=Vs[h][:, j, 0:128], in_=pm[:, h * 128:(h + 1) * 128]),
                             reads=[pm], writes=[Vs[h]])
                    else:
                        P.op("scalar", lambda e, pm=pm, h=h, j=j: e.activation(out=Vs[h][:, j, 0:128], in_=pm[:, h * 128:(h + 1) * 128],
                                                                                func=AF.Copy), reads=[pm], writes=[Vs[h]])

        def new_v(alloc, name):
            v = alloc(name, [128, NT, 136], BF16)
            P.op("gpsimd", lambda e: e.memset(v[:, :, :], 1.0), writes=[v])
            return v

        sidx_box = [0]

        def attention_chunk(c, kparts, qparts, scale, Vp, post, bias=None, extra=None, bias_tiles=None):
            info = {}

            def qk_exp(j):
                tlo = max(4 * c, j)
                t0 = tlo * 128
                n = (4 * c + 4) * 128 - t0
                sidx = sidx_box[0]
                sidx_box[0] += 1
                ps = PS[sidx % 2]
                ptb = PTB[sidx % 3]
                info[j] = (tlo, ptb)
                nparts = len(kparts)
                for i in range(nparts):
                    kb, kp0, kp1 = kparts[i]
                    qb, qp0, qp1 = qparts[i]
                    P.op("tensor", lambda e, i=i, kb=kb, kp0=kp0, kp1=kp1, qb=qb, qp0=qp0, qp1=qp1: e.matmul(
                        ps[:, 0:n], lhsT=kb[kp0:kp1, j * 128:(j + 1) * 128], rhs=qb[qp0:qp1, t0:t0 + n],
                        start=(i == 0), stop=(i == nparts - 1)), reads=[kb, qb], writes=[ps])
                if bias_tiles is not None:
                    for ti in range(tlo, 4 * c + 4):
                        off = (ti - tlo) * 128
                        P.op("scalar", lambda e, off=off, ti=ti: e.activation(
                            out=ptb[:, off:off + 128], in_=ps[:, off:off + 128], func=AF.Exp, scale=scale,
                            bias=bias_tiles[:, j, ti:ti + 1]), reads=[ps, bias_tiles], writes=[ptb])
                elif bias is None:
                    P.op("scalar", lambda e: e.activation(out=ptb[:, 0:n], in_=ps[:, 0:n], func=AF.Exp, scale=scale),
                         reads=[ps], writes=[ptb])
                else:
                    P.op("scalar", lambda e: e.activation(out=ptb[:, 0:n], in_=ps[:, 0:n], func=AF.Exp, scale=scale,
                                                          bias=bias[:, j:j + 1]), reads=[ps, bias], writes=[ptb])
                if j >= 4 * c:
                    P.op("gpsimd", lambda e: e.tensor_tensor(out=ptb[:, 0:128], in0=ptb[:, 0:128], in1=MASK[:, :], op=ALU.mult),
                         reads=[ptb, MASK], writes=[ptb])

            def pv(j):
                tlo, ptb = info[j]
                for ti in range(tlo, 4 * c + 4):
                    po = PO[ti - 4 * c]
                    off = (ti - tlo) * 128
                    P.op("tensor", lambda e, po=po, off=off, ti=ti: e.matmul(
                        po[:, 0:129], lhsT=ptb[:, off:off + 128], rhs=Vp[:, j, 0:129], start=(j == 0), stop=(j == ti)),
                        reads=[ptb, Vp], writes=[po])

            nj = 4 * c + 4
            qk_exp(0)
            for j in range(nj):
                if j + 1 < nj:
                    qk_exp(j + 1)
                pv(j)
            conts = [post(ti, PO[ti - 4 * c]) for ti in range(4 * c, 4 * c + 4)]
            for k_ in conts:
                if k_ is not None:
                    k_()

        def attention(kparts, qparts, scale, Vp, post, bias=None, extra=None, bias_tiles=None):
            for c in range(4):
                attention_chunk(c, kparts, qparts, scale, Vp, post, bias=bias, extra=extra, bias_tiles=bias_tiles)

        def finish_tile(on_src_fn, ti, mixh, tr_slot):
            onb = ONB[cnt["onb"] % 4]
            cnt["onb"] += 1
            on_src_fn(onb)
            sl = slice(tr_slot * 128, (tr_slot + 1) * 128)

            def cont():
                P.op("tensor", lambda e: e.transpose(out=PTR[:, sl], in_=onb[:, :], identity=IDB[:, :]),
                     reads=[onb, IDB], writes=[PTR])
                P.op("vector", lambda e: e.tensor_copy(out=mixh[:, ti * 128:(ti + 1) * 128], in_=PTR[:, sl]),
                     reads=[PTR], writes=[mixh])
            return cont

        def std_post(mixh):
            def post(ti, po):
                rl = RL[cnt["rl"] % 8]
                cnt["rl"] += 1
                P.op("vector", lambda e: e.reciprocal(out=rl[:, :], in_=po[:, 128:129]), reads=[po], writes=[rl])

                def w(onb):
                    P.op("vector", lambda e: e.tensor_scalar(out=onb[:, :], in0=po[:, 0:128], scalar1=rl[:, :], scalar2=None,
                                                              op0=ALU.mult), reads=[po, rl], writes=[onb])
                return finish_tile(w, ti, mixh, ti % 8)
            return post

        mix_i = [0]

        def store_head(mixh, row0):
            P.dma(mixT[row0:row0 + 128, :], mixh[:, :], reads=[mixh], writes=[OUT])

        with ExitStack() as stD:
            def sbD(name, shape, dtp):
                return Buf(P, name, stD.enter_context(nc.sbuf_tensor(name, list(shape), dtp)))
            DL = sbD("DL", [128, 256], F32)
            DJ = sbD("DJ", [128, 64], F32)
            SL = sbD("SL", [128, 4], F32)
            NEGLAM = sbD("NEGLAM", [128, 1], F32)
            P.dma(DL[:, :], dl.partition_broadcast(128), writes=[DL])
            for i in range(2):
                P.op("vector", lambda e, i=i: e.tensor_tensor(
                    out=DJ[:, :], in0=DL[:, i * 128:i * 128 + 64], in1=DL[:, i * 128 + 64:i * 128 + 128], op=ALU.mult),
                    reads=[DL], writes=[DJ])
                P.op("scalar", lambda e, i=i: e.activation(out=DJ[:, :], in_=DJ[:, :], func=AF.Copy, accum_out=SL[:, i:i + 1]),
                     reads=[DJ], writes=[DJ, SL])
            P.op("scalar", lambda e: e.activation(out=SL[:, 2:4], in_=SL[:, 0:2], func=AF.Exp), reads=[SL], writes=[SL])
            P.op("vector", lambda e: e.tensor_tensor(out=NEGLAM[:, :], in0=SL[:, 3:4], in1=SL[:, 2:3], op=ALU.subtract),
                 reads=[SL], writes=[NEGLAM])
            P.op("vector", lambda e: e.tensor_tensor(out=NEGLAM[:, :], in0=NEGLAM[:, :], in1=LCB[:, 0:1], op=ALU.subtract),
                 reads=[NEGLAM, LCB], writes=[NEGLAM])

            QT = [sbD(f"QTd{h}", [128, S], BF16) for h in range(2)]
            KT = [sbD(f"KTd{h}", [128, S], BF16) for h in range(2)]
            VD = [new_v(sbD, f"VD{h}") for h in range(2)]
            O1N = [sbD(f"O1N{i}", [128, 128], F32) for i in range(4)]
            OD = [sbD(f"OD{i}", [128, 128], F32) for i in range(4)]
            if STOP == 301:
                P.dma(mixT[0:128, :], MIXH[0][:, :], reads=[MIXH[0]], writes=[OUT])
                P.finish([OUT])
                P.emit()
                return nc
            wq = nxt()
            wqs = nxt()
            def rope2(wm, ws, c0, dst):
                for r in range(4):
                    cs = slice(r * 512, (r + 1) * 512)
                    i = cnt["t"] % 2
                    cnt["t"] += 1
                    p1 = next_proj()
                    proj_fm(wm, c0, 128, HT, 16, r, p1)
                    P.op("vector", lambda e, p1=p1, i=i, cs=cs: e.tensor_tensor(out=T1[i][:, :], in0=p1[:, :], in1=CD[:, cs], op=ALU.mult),
                         reads=[p1, CD], writes=[T1[i]])
                    p2 = next_proj()
                    proj_fm(ws, c0, 128, HT, 16, r, p2)
                    P.op("vector", lambda e, p2=p2, i=i, cs=cs: e.tensor_tensor(out=T2[i][:, :], in0=p2[:, :], in1=SD[:, cs], op=ALU.mult),
                         reads=[p2, SD], writes=[T2[i]])
                    P.op("gpsimd", lambda e, i=i, cs=cs: e.tensor_tensor(out=dst[:, cs], in0=T1[i][:, :], in1=T2[i][:, :], op=ALU.add),
                         reads=[T1[i], T2[i]], writes=[dst])
            for h in range(2):
                rope2(wq, wqs, h * 128, QT[h])
            if STOP == 302:
                P.dma(mixT[0:128, :], MIXH[0][:, :], reads=[MIXH[0]], writes=[OUT])
                P.finish([OUT])
                P.emit()
                return nc
            wk = nxt()
            wks = nxt()
            for h in range(2):
                rope2(wk, wks, h * 128, KT[h])
            if STOP == 303:
                P.dma(mixT[0:128, :], MIXH[0][:, :], reads=[MIXH[0]], writes=[OUT])
                P.finish([OUT])
                P.emit()
                return nc
            wv = nxt()
            v_tm(wv, 0, 2, HT, 16, VD)
            if STOP == 31:
                P.dma(mixT[0:128, :], MIXH[0][:, :], reads=[MIXH[0]], writes=[OUT])
                P.finish([OUT])
                P.emit()
                return nc

            for h in range(2):
                mixh = MIXH[mix_i[0] % 2]
                mix_i[0] += 1

                def post1(ti, po):
                    rl = RL[cnt["rl"] % 8]
                    cnt["rl"] += 1
                    P.op("vector", lambda e: e.reciprocal(out=rl[:, :], in_=po[:, 128:129]), reads=[po], writes=[rl])
                    o1 = O1N[ti % 4]
                    P.op("vector", lambda e: e.tensor_scalar(out=o1[:, :], in0=po[:, 0:128], scalar1=rl[:, :], scalar2=None, op0=ALU.mult),
                         reads=[po, rl], writes=[o1])

                def post2(ti, po, mixh=mixh):
                    rl = RL[cnt["rl"] % 8]
                    cnt["rl"] += 1
                    ssq = RL[cnt["rl"] % 8]
                    cnt["rl"] += 1
                    o1 = O1N[ti % 4]
                    od = OD[ti % 4]
                    P.op("vector", lambda e: e.reciprocal(out=rl[:, :], in_=po[:, 128:129]), reads=[po], writes=[rl])
                    P.op("vector", lambda e: e.tensor_scalar(out=od[:, :], in0=po[:, 0:128], scalar1=rl[:, :], scalar2=None, op0=ALU.mult),
                         reads=[po, rl], writes=[od])

                    def cont():
                        P.op("vector", lambda e: e.scalar_tensor_tensor(out=od[:, :], in0=od[:, :], scalar=NEGLAM[:, 0:1], in1=o1[:, :],
                                                                         op0=ALU.mult, op1=ALU.add), reads=[od, NEGLAM, o1], writes=[od])
                        P.op("scalar", lambda e: e.activation(out=o1[:, :], in_=od[:, :], func=AF.Square, accum_out=ssq[:, :]),
                             reads=[od], writes=[o1, ssq])
                        P.op("scalar", lambda e: e.activation(out=ssq[:, :], in_=ssq[:, :], func=AF.Sqrt, scale=1.0 / 128, bias=EPS5[:, :]),
                             reads=[ssq, EPS5], writes=[ssq])
                        P.op("vector", lambda e: e.reciprocal(out=ssq[:, :], in_=ssq[:, :]), reads=[ssq], writes=[ssq])

                        def w(onb):
                            P.op("vector", lambda e: e.tensor_scalar(out=onb[:, :], in0=od[:, :], scalar1=ssq[:, :], scalar2=None, op0=ALU.mult),
                                 reads=[od, ssq], writes=[onb])
                        k2_ = finish_tile(w, ti, mixh, ti % 8)
                        k2_()
                    return cont

                for c in range(4):
                    attention_chunk(c, [(KT[h], 0, 64)], [(QT[h], 0, 64)], 0.125, VD[h], post1)
                    if STOP == 32:
                        P.dma(mixT[0:128, :], MIXH[0][:, :], reads=[MIXH[0]], writes=[OUT])
                        P.finish([OUT])
                        P.emit()
                        return nc
                    attention_chunk(c, [(KT[h], 64, 128)], [(QT[h], 64, 128)], 0.125, VD[h], post2)
                store_head(mixh, h * 128)
            P.barrier()

        if STOP == 3:
            P.dma(mixT[0:128, :], MIXH[0][:, :], reads=[MIXH[0]], writes=[OUT])
            P.finish([OUT])
            P.emit()
            return nc
        with ExitStack() as stM:
            def sbM(name, shape, dtp):
                return Buf(P, name, stM.enter_context(nc.sbuf_tensor(name, list(shape), dtp)))
            QN = [sbM(f"QNm{h}", [128, S], BF16) for h in range(3)]
            QR = [sbM(f"QRm{h}", [128, S], BF16) for h in range(3)]
            KN = [sbM(f"KNm{h}", [128, S], BF16) for h in range(3)]
            KR = sbM("KRm", [128, S], BF16)
            VM = [new_v(sbM, f"VM{h}") for h in range(3)]
            SQ = [sbM(f"SQ{i}", [128, 512], BF16) for i in range(3)]
            RSTD = sbM("RSTD", [128, 512], F32)

            def latent(col0, dst):
                wl = [nxt(), nxt()]
                pend = []
                for rcn in range(4):
                    for r in range(4):
                        cs = slice(r * 512, (r + 1) * 512)
                        pm = next_proj()
                        sq = SQ[(rcn * 4 + r) % 3]
                        proj_fm(wl[rcn // 2], (rcn % 2) * 128, 128, HT, 16, r, pm)
                        P.op("vector", lambda e, pm=pm, rcn=rcn, cs=cs: e.tensor_copy(out=dst[:, rcn, cs], in_=pm[:, :]), reads=[pm], writes=[dst])
                        P.op("scalar", lambda e, pm=pm, sq=sq: e.activation(out=sq[:, :], in_=pm[:, :], func=AF.Square),
                             reads=[pm], writes=[sq])
                        if pend:
                            pend.pop()()
                        pend.append(lambda rcn=rcn, r=r, sq=sq: P.op("tensor", lambda e: e.matmul(
                            PO[r][:, :], lhsT=ONESB[:, :], rhs=sq[:, :], start=(rcn == 0), stop=(rcn == 3)),
                            reads=[ONESB, sq], writes=[PO[r]]))
                pend.pop()()
                for r in range(4):
                    cs = slice(r * 512, (r + 1) * 512)
                    P.op("scalar", lambda e, r=r: e.activation(out=RSTD[:, :], in_=PO[r][:, :], func=AF.Sqrt, scale=1.0 / 512,
                                                                bias=EPS6[:, :]), reads=[PO[r], EPS6], writes=[RSTD])
                    P.op("vector", lambda e: e.reciprocal(out=RSTD[:, :], in_=RSTD[:, :]), reads=[RSTD], writes=[RSTD])
                    for rcn in range(4):
                        P.op("gpsimd" if rcn % 2 else "vector", lambda e, rcn=rcn, cs=cs: e.tensor_tensor(
                            out=dst[:, rcn, cs], in0=dst[:, rcn, cs], in1=RSTD[:, :], op=ALU.mult),
                            reads=[dst, RSTD], writes=[dst])

            with ExitStack() as stM1:
                CQN = Buf(P, "CQN", stM1.enter_context(nc.sbuf_tensor("CQN", [128, 4, S], BF16)))
                latent(C_CQ, CQN)
                for h in range(3):
                    wu = nxt()
                    plain_fm(wu, 0, 128, CQN, 4, QN[h])
                for h in range(3):
                    wu = nxt()
                    wus = nxt()
                    for r in range(4):
                        cs = slice(r * 512, (r + 1) * 512)
                        i = cnt["t"] % 2
                        cnt["t"] += 1
                        p1 = next_proj()
                        proj_fm(wu, 0, 64, CQN, 4, r, p1)
                        P.op("vector", lambda e, p1=p1, i=i, cs=cs: e.tensor_tensor(out=T1[i][0:64, :], in0=p1[0:64, :], in1=CM[0:64, cs],
                                                                                     op=ALU.mult), reads=[p1, CM], writes=[T1[i]])
                        p2 = next_proj()
                        proj_fm(wus, 0, 64, CQN, 4, r, p2)
                        P.op("vector", lambda e, p2=p2, i=i, cs=cs: e.tensor_tensor(out=T2[i][0:64, :], in0=p2[0:64, :], in1=SM[0:64, cs],
                                                                                     op=ALU.mult), reads=[p2, SM], writes=[T2[i]])
                        P.op("gpsimd", lambda e, i=i, cs=cs, h=h: e.tensor_tensor(out=QR[h][0:64, cs], in0=T1[i][0:64, :], in1=T2[i][0:64, :],
                                                                                   op=ALU.add), reads=[T1[i], T2[i]], writes=[QR[h]])
                P.barrier()
            with ExitStack() as stM2:
                CKN = Buf(P, "CKN", stM2.enter_context(nc.sbuf_tensor("CKN", [128, 4, S], BF16)))
                latent(C_CKV, CKN)
                for h in range(3):
                    wu = nxt()
                    plain_fm(wu, 0, 128, CKN, 4, KN[h])
                wv1 = nxt()
                v_tm(wv1, 0, 2, CKN, 4, VM[0:2])
                wv2 = nxt()
                v_tm(wv2, 0, 1, CKN, 4, VM[2:3])
                P.barrier()
            wkr = nxt()
            for r in range(4):
                cs = slice(r * 512, (r + 1) * 512)
                i = cnt["t"] % 2
                cnt["t"] += 1
                p1 = next_proj()
                proj_fm(wkr, 0, 64, HT, 16, r, p1)
                P.op("vector", lambda e, p1=p1, i=i, cs=cs: e.tensor_tensor(out=T1[i][0:64, :], in0=p1[0:64, :], in1=CM[0:64, cs], op=ALU.mult),
                     reads=[p1, CM], writes=[T1[i]])
                p2 = next_proj()
                proj_fm(wkr, 64, 64, HT, 16, r, p2)
                P.op("vector", lambda e, p2=p2, i=i, cs=cs: e.tensor_tensor(out=T2[i][0:64, :], in0=p2[0:64, :], in1=SM[0:64, cs], op=ALU.mult),
                     reads=[p2, SM], writes=[T2[i]])
                P.op("gpsimd", lambda e, i=i, cs=cs: e.tensor_tensor(out=KR[0:64, cs], in0=T1[i][0:64, :], in1=T2[i][0:64, :], op=ALU.add),
                     reads=[T1[i], T2[i]], writes=[KR])
            for h in range(3):
                mixh = MIXH[mix_i[0] % 2]
                mix_i[0] += 1
                attention([(KN[h], 0, 128), (KR, 0, 64)], [(QN[h], 0, 128), (QR[h], 0, 64)], 192 ** -0.5, VM[h], std_post(mixh))
                store_head(mixh, 256 + h * 128)
            P.barrier()

        if STOP == 4:
            P.dma(mixT[0:128, :], MIXH[0][:, :], reads=[MIXH[0]], writes=[OUT])
            P.finish([OUT])
            P.emit()
            return nc
        with ExitStack() as stF:
            def sbF(name, shape, dtp):
                return Buf(P, name, stF.enter_context(nc.sbuf_tensor(name, list(shape), dtp)))
            QF = [sbF(f"QF{h}", [128, S], BF16) for h in range(3)]
            KF_ = [sbF(f"KF{h}", [128, S], BF16) for h in range(3)]
            VF = [new_v(sbF, f"VF{h}") for h in range(3)]
            NFB = sbF("NFB", [3, 1], F32)
            ONE3 = sbF("ONE3", [3, 1], F32)
            ONEROW = sbF("ONEROW", [3, S], F32)
            GL = sbF("GL", [3, S], F32)
            CL = sbF("CL", [3, S], F32)
            NBALL = sbF("NBALL", [128, NT * 3], F32)
            R1 = sbF("R1", [128, NT * 3], F32)
            NB3 = [sbF(f"NB3_{i}", [128, NT * 3], BF16) for i in range(3)]
            E0 = sbF("E0", [128, 128], BF16)
            CLB = sbF("CLB", [128, NT * 3], F32)
            BI = [sbF(f"BI{h}", [128, NT, NT], F32) for h in range(3)]
            P.dma(NFB[:, :], fb[0:3].rearrange("(p o) -> p o", o=1), writes=[NFB])
            P.op("vector", lambda e: e.tensor_scalar(out=NFB[:, :], in0=NFB[:, :], scalar1=-1.0, scalar2=None, op0=ALU.mult),
                 reads=[NFB], writes=[NFB])
            P.op("gpsimd", lambda e: e.memset(ONEROW[:, :], 1.0), writes=[ONEROW])
            P.op("gpsimd", lambda e: e.memset(ONE3[:, :], 1.0), writes=[ONE3])
            P.op("gpsimd", lambda e: e.memset(E0[:, :], 0.0), writes=[E0])
            P.op("gpsimd", lambda e: e.memset(E0[0:1, :], 1.0), writes=[E0])
            for (c0, dsts) in ((C_FQ, QF), (C_FK, KF_)):
                wa = nxt()
                wb2 = nxt()
                plain_fm(wa, 0, 128, HT, 16, dsts[0])
                plain_fm(wa, 128, 128, HT, 16, dsts[1])
                plain_fm(wb2, 0, 128, HT, 16, dsts[2])
            wv1 = nxt()
            v_tm(wv1, 0, 2, HT, 16, VF[0:2])
            wv2 = nxt()
            v_tm(wv2, 0, 1, HT, 16, VF[2:3])
            wg = nxt()
            for r in range(4):
                cs = slice(r * 512, (r + 1) * 512)
                pm = next_proj()
                proj_fm(wg, 0, 3, HT, 16, r, pm)
                P.op("scalar", lambda e, pm=pm, cs=cs: e.activation(out=GL[0:3, cs], in_=pm[0:3, :], func=AF.Exp, scale=-1.0,
                                                                     bias=NFB[0:3, 0:1]), reads=[pm, NFB], writes=[GL])
            P.op("scalar", lambda e: e.activation(out=GL[0:3, :], in_=GL[0:3, :], func=AF.Ln, bias=ONE3[0:3, 0:1]),
                 reads=[GL, ONE3], writes=[GL])
            P.op("vector", lambda e: e.tensor_tensor_scan(out=CL[0:3, :], data0=ONEROW[0:3, :], data1=GL[0:3, :], initial=0.0,
                                                           op0=ALU.mult, op1=ALU.add), reads=[ONEROW, GL], writes=[CL])
            pm = next_proj()
            for j in range(NT):
                P.op("tensor", lambda e, j=j, pm=pm: e.transpose(out=pm[:, j * 3:(j + 1) * 3], in_=CL[0:3, j * 128:(j + 1) * 128],
                                                                identity=IDF[0:3, 0:3]), reads=[CL, IDF], writes=[pm])
            P.op("vector", lambda e, pm=pm: e.tensor_copy(out=NBALL[:, :], in_=pm[:, 0:NT * 3]), reads=[pm], writes=[NBALL])
            P.op("vector", lambda e: e.tensor_copy(out=NB3[0][:, :], in_=NBALL[:, :]), reads=[NBALL], writes=[NB3[0]])
            P.op("vector", lambda e: e.tensor_tensor(out=R1[:, :], in0=NBALL[:, :], in1=NB3[0][:, :], op=ALU.subtract),
                 reads=[NBALL, NB3[0]], writes=[R1])
            P.op("vector", lambda e: e.tensor_copy(out=NB3[1][:, :], in_=R1[:, :]), reads=[R1], writes=[NB3[1]])
            P.op("vector", lambda e: e.tensor_tensor(out=R1[:, :], in0=R1[:, :], in1=NB3[1][:, :], op=ALU.subtract),
                 reads=[R1, NB3[1]], writes=[R1])
            P.op("vector", lambda e: e.tensor_copy(out=NB3[2][:, :], in_=R1[:, :]), reads=[R1], writes=[NB3[2]])
            pm = next_proj()
            for i in range(3):
                P.op("tensor", lambda e, i=i, pm=pm: e.matmul(pm[:, 0:NT * 3], lhsT=E0[:, :], rhs=NB3[i][:, :],
                                                             start=(i == 0), stop=(i == 2)), reads=[E0, NB3[i]], writes=[pm])
            P.op("vector", lambda e, pm=pm: e.tensor_copy(out=CLB[:, :], in_=pm[:, 0:NT * 3]), reads=[pm], writes=[CLB])
            for h in range(3):
                for j in range(NT):
                    P.op("vector", lambda e, h=h, j=j: e.tensor_scalar(
                        out=BI[h][:, j, :], in0=CLB[:, :].rearrange("p (t h) -> p t h", h=3)[:, :, h],
                        scalar1=NBALL[:, j * 3 + h:j * 3 + h + 1], scalar2=-1.0, op0=ALU.subtract, op1=ALU.mult),
                        reads=[CLB, NBALL], writes=[BI[h]])
            for h in range(3):
                mixh = MIXH[mix_i[0] % 2]
                mix_i[0] += 1
                attention([(KF_[h], 0, 128)], [(QF[h], 0, 128)], 128 ** -0.5, VF[h], std_post(mixh), bias_tiles=BI[h])
                store_head(mixh, 640 + h * 128)
            P.barrier()
        P.drain_all()


D = 2048
DFF = 5632
NTOK = 1024
NH = 2
TT = NTOK // 128
NG = 11
EPS = 1e-6


def body_k2(nc, P, io):
    x_main, x_halo, mix_main, mix_halo = io["x_main"], io["x_halo"], io["mix_main"], io["mix_halo"]
    w_o, w_up, conv_w, conv_b, w_down = io["w_o"], io["w_up"], io["conv_w"], io["conv_b"], io["w_down"]
    ffn_norm, dnorm, fnorm, lc, idf, idb = io["ffn_norm"], io["dnorm"], io["fnorm"], io["lc"], io["idf"], io["idb"]
    x_out, xn_out, fin = io["x_out"], io["xn_out"], io["fin"]
    HALO = x_halo is not None
    with ExitStack() as st:
        P.stack = st
        X = [P.sb(f"X{t}", [128, D], F32) for t in range(TT)]
        XH = P.sb("XH", [NH, D], F32)
        IDF = P.sb("IDF", [128, 128], F32)
        IDB = P.sb("IDB", [128, 128], BF16)
        GUP = P.sb("GUP", [128, 16], F32)
        CW = P.sb("CW", [128, 4, 88], F32)
        DN = P.sb("DN", [128, 1], F32)
        EPSB = P.sb("EPSB", [128, 1], F32)
        ss = [P.sb(f"ss{i}", [128, 1], F32) for i in range(2)]
        sd = [P.sb(f"sd{i}", [128, 1], F32) for i in range(2)]
        rs = [P.sb(f"rs{i}", [128, 1], F32) for i in range(2)]
        STG = [P.sb(f"stg{i}", [128, 1024], F32) for i in range(4)]
        PA = P.ps("PA", [128, 1024])
        PG = P.ps("PG", [128, 1024])
        PHALO = P.ps("PHALO", [128, 512])
        ACC = [PA, PG]
        PM = [P.ps(f"PM{i}", [128, 512]) for i in range(2)]
        PT = P.ps("PT", [128, 1024], BF16)
        PH = [P.wrap("PH0", PHALO.t, lock=PHALO.lock), P.wrap("PH1", PT[:, :].bitcast(F32), lock=PT.lock)]

        OUTX = P.wrap("OUTX", x_out)
        OUTN = P.wrap("OUTN", xn_out)
        OUTF = P.wrap("OUTF", fin if fin is not None else x_out)

        stg_i = [0]

        def stage():
            b = STG[stg_i[0] % len(STG)]
            stg_i[0] += 1
            return b

        cast_i = [0]

        def cast(out_ap, in_ap, scale_ap, reads, writes):
            eng = "scalar" if cast_i[0] % 2 == 0 else "gpsimd"
            cast_i[0] += 1
            if eng == "scalar":
                if scale_ap is None:
                    P.op("scalar", lambda e: e.activation(out=out_ap, in_=in_ap, func=AF.Copy), reads=reads, writes=writes)
                else:
                    P.op("scalar", lambda e: e.activation(out=out_ap, in_=in_ap, func=AF.Copy, scale=scale_ap),
                         reads=reads, writes=writes)
            else:
                if scale_ap is None:
                    P.op("gpsimd", lambda e: e.tensor_copy(out=out_ap, in_=in_ap), reads=reads, writes=writes)
                else:
                    P.op("gpsimd", lambda e: e.tensor_scalar(out=out_ap, in0=in_ap, scalar1=scale_ap, scalar2=1.0,
                                                              op0=ALU.mult, op1=ALU.mult), reads=reads, writes=writes)

        P.dma(IDF[:, :], idf, writes=[IDF])
        P.dma(IDB[:, :], idb, writes=[IDB])
        P.op("gpsimd", lambda e: e.memset(EPSB[:, :], EPS), writes=[EPSB])
        s0 = stage()
        P.dma(s0[0:16, 0:128], ffn_norm.rearrange("(k p) -> k p", p=128), writes=[s0])
        P.op("tensor", lambda e: e.transpose(out=PM[0][:, 0:16], in_=s0[0:16, 0:128], identity=IDF[0:16, 0:16]),
             reads=[s0, IDF], writes=[PM[0]])
        P.op("vector", lambda e: e.tensor_copy(out=GUP[:, :], in_=PM[0][:, 0:16]), reads=[PM[0]], writes=[GUP])
        for j in range(4):
            s1 = stage()
            src = conv_w[j:j + 1, :].rearrange("o (c p) -> (o c) p", p=128) if j < 3 else conv_b.rearrange("(c p) -> c p", p=128)
            P.dma(s1[0:88, 0:128], src, writes=[s1])
            pm = PM[(j + 1) % 2]
            P.op("tensor", lambda e, s1=s1, pm=pm: e.transpose(out=pm[:, 0:88], in_=s1[0:88, 0:128], identity=IDF[0:88, 0:88]),
                 reads=[s1, IDF], writes=[pm])
            P.op("vector", lambda e, j=j, pm=pm: e.tensor_copy(out=CW[:, j, :], in_=pm[:, 0:88]), reads=[pm], writes=[CW])
        LCB = P.sb("LCB", [128, 4], F32)
        P.dma(LCB[:, :], lc.partition_broadcast(128), writes=[LCB])
        s2 = stage()
        P.dma(s2[:, 0:1], dnorm.rearrange("(p o) -> p o", o=1), writes=[s2])
        P.op("vector", lambda e: e.tensor_scalar(out=DN[:, :], in0=s2[:, 0:1], scalar1=LCB[:, 1:2], scalar2=None, op0=ALU.mult),
             reads=[s2, LCB], writes=[DN])

        if x_halo is None:
            P.op("gpsimd", lambda e: e.memset(XH[:, :], 0.0), writes=[XH])
        else:
            P.dma(XH[:, :], x_halo, writes=[XH])
        for t in range(TT):
            P.dma(X[t][:, :], x_main[t * 128:(t + 1) * 128, :], writes=[X[t]])

        WUB = [P.sb(f"WUB{i}", [128, 16, 512], BF16) for i in range(2)]
        WD = P.sb("WD", [128, 4, 2048], BF16)

        def load_up(wb, col0):
            for k in range(16):
                s = stage()
                P.dma(s[:, 0:512], w_up[k * 128:(k + 1) * 128, col0:col0 + 512], writes=[s])
                cast(wb[:, k, :], s[:, 0:512], GUP[:, k:k + 1], reads=[s, GUP], writes=[wb])

        def load_down(g):
            for fc in range(4):
                r0 = (g * 4 + fc) * 128
                for hh in range(2):
                    s = stage()
                    P.dma(s[:, :], w_down[r0:r0 + 128, hh * 1024:(hh + 1) * 1024], writes=[s])
                    cast(WD[:, fc, hh * 1024:(hh + 1) * 1024], s[:, :], None, reads=[s], writes=[WD])


        with ExitStack() as stA:
            MT = []
            for k in range(16):
                t_ = stA.enter_context(nc.sbuf_tensor(f"MT{k}", [128, NH + NTOK], BF16))
                MT.append(Buf(P, f"MT{k}", t_))
            WOB = []
            for i in range(2):
                t_ = stA.enter_context(nc.sbuf_tensor(f"WOB{i}", [128, 16, 512], BF16))
                WOB.append(Buf(P, f"WOB{i}", t_))
            for k in range(16):
                if x_halo is None:
                    P.op("gpsimd", lambda e, k=k: e.memset(MT[k][:, 0:NH], 0.0), writes=[MT[k]])
                else:
                    P.dma(MT[k][:, 0:NH], mix_halo(k), writes=[MT[k]])
                P.dma(MT[k][:, NH:NH + NTOK], mix_main(k), writes=[MT[k]])

            def load_wo(nb):
                wb = WOB[nb % 2]
                for k in range(16):
                    s = stage()
                    P.dma(s[:, 0:512], w_o[k * 128:(k + 1) * 128, nb * 512:(nb + 1) * 512], writes=[s])
                    cast(wb[:, k, :], s[:, 0:512], DN[:, 0:1] if k < 4 else None, reads=[s, DN], writes=[wb])

            load_wo(0)
            pmi = 0
            for nb in range(4):
                if nb + 1 < 4:
                    load_wo(nb + 1)
                if nb == 2:
                    load_up(WUB[0], 0)
                    load_up(WUB[1], DFF)
                    load_down(0)
                wb = WOB[nb % 2]
                cs = slice(nb * 512, (nb + 1) * 512)
                for tt in range(-1 if HALO else 0, TT):
                    pm = PM[pmi % 2]
                    pmi += 1
                    if tt < 0:
                        np_, c0, c1, xt = NH, 0, NH, XH
                    else:
                        np_, c0, c1, xt = 128, NH + tt * 128, NH + (tt + 1) * 128, X[tt]
                    for k in range(16):
                        P.op("tensor", lambda e, k=k, pm=pm, np_=np_, c0=c0, c1=c1, wb=wb: e.matmul(
                            pm[0:np_, :], lhsT=MT[k][:, c0:c1], rhs=wb[:, k, :], start=(k == 0), stop=(k == 15)),
                            reads=[MT[k], wb], writes=[pm])
                    P.op("vector", lambda e, pm=pm, np_=np_, xt=xt, cs=cs: e.tensor_tensor(
                        out=xt[0:np_, cs], in0=pm[0:np_, :], in1=xt[0:np_, cs], op=ALU.add),
                        reads=[pm, xt], writes=[xt])
            P.barrier()

        def norm_tile(xt, np_, i, sq_out):
            b = i % 2
            P.op("scalar", lambda e: e.activation(out=sq_out[0:np_, :], in_=xt[0:np_, :], func=AF.Square,
                                                    accum_out=ss[b][0:np_, :]), reads=[xt], writes=[sq_out, ss[b]])
            P.op("scalar", lambda e: e.activation(out=sd[b][0:np_, :], in_=ss[b][0:np_, :], func=AF.Sqrt,
                                                    scale=1.0 / D, bias=EPSB[0:np_, :]), reads=[ss[b], EPSB], writes=[sd[b]])
            P.op("vector", lambda e: e.reciprocal(out=rs[b][0:np_, :], in_=sd[b][0:np_, :]), reads=[sd[b]], writes=[rs[b]])
            return rs[b]

        with ExitStack() as stH:
            H2T = Buf(P, "H2T", stH.enter_context(nc.sbuf_tensor("H2T", [128, 16, NH + NTOK], BF16)))
            with ExitStack() as stN:
                xnb = [Buf(P, f"xnb{i}", stN.enter_context(nc.sbuf_tensor(f"xnb{i}", [128, D], BF16))) for i in range(2)]

                def norm_transpose(xt, np_, i, c0):
                    b = i % 2
                    r = norm_tile(xt, np_, i, xnb[b])
                    P.op("vector", lambda e: e.tensor_scalar(out=xnb[b][0:np_, :], in0=xt[0:np_, :], scalar1=r[0:np_, :],
                                                              scalar2=None, op0=ALU.mult), reads=[xt, r], writes=[xnb[b]])
                    for g in range(2):
                        for kk in range(8):
                            k = g * 8 + kk
                            P.op("tensor", lambda e, k=k, kk=kk: e.transpose(
                                out=PT[:, kk * 128: kk * 128 + np_], in_=xnb[b][0:np_, k * 128:(k + 1) * 128],
                                identity=IDB[0:np_, 0:np_]), reads=[xnb[b], IDB], writes=[PT])
                        src = PT[:, :].rearrange("p (k t) -> p k t", t=128)[:, :, 0:np_]
                        dst = H2T[:, g * 8:(g + 1) * 8, c0:c0 + np_]
                        if g == 0:
                            P.op("vector", lambda e, src=src, dst=dst: e.tensor_copy(out=dst, in_=src), reads=[PT], writes=[H2T])
                        else:
                            P.op("scalar", lambda e, src=src, dst=dst: e.activation(out=dst, in_=src, func=AF.Copy),
                                 reads=[PT], writes=[H2T])

                if HALO:
                    norm_transpose(XH, NH, 0, 0)
                for t in range(TT):
                    norm_transpose(X[t], 128, t + 1, NH + t * 128)
                P.barrier()

            with ExitStack() as stB:
                def sbB(name, shape, dtp):
                    return Buf(P, name, stB.enter_context(nc.sbuf_tensor(name, list(shape), dtp)))
                ACTT = sbB("ACTT", [128, 4, NTOK], BF16)
                UA = [sbB(f"UA{i}", [128, NTOK], F32) for i in range(4)]
                UG = [sbB(f"UG{i}", [128, NTOK], F32) for i in range(2)]

                def conv(pp, ph, hc, uc, c):
                    w0, w1, w2, bb = CW[:, 0, c:c + 1], CW[:, 1, c:c + 1], CW[:, 2, c:c + 1], CW[:, 3, c:c + 1]
                    P.op("scalar", lambda e: e.activation(out=uc[:, :], in_=pp[:, :], func=AF.Identity, scale=w2, bias=bb),
                         reads=[pp, CW], writes=[uc])
                    P.op("vector", lambda e: e.scalar_tensor_tensor(out=uc[:, 1:NTOK], in0=pp[:, 0:NTOK - 1], scalar=w1,
                                                                     in1=uc[:, 1:NTOK], op0=ALU.mult, op1=ALU.add),
                         reads=[pp, CW, uc], writes=[uc])
                    P.op("vector", lambda e: e.scalar_tensor_tensor(out=uc[:, 2:NTOK], in0=pp[:, 0:NTOK - 2], scalar=w0,
                                                                     in1=uc[:, 2:NTOK], op0=ALU.mult, op1=ALU.add),
                         reads=[pp, CW, uc], writes=[uc])
                    if HALO:
                        P.op("vector", lambda e: e.scalar_tensor_tensor(out=uc[:, 0:1], in0=ph[:, hc + 1:hc + 2], scalar=w1,
                                                                         in1=uc[:, 0:1], op0=ALU.mult, op1=ALU.add),
                             reads=[ph, CW, uc], writes=[uc])
                        P.op("vector", lambda e: e.scalar_tensor_tensor(out=uc[:, 0:2], in0=ph[:, hc:hc + 2], scalar=w0,
                                                                         in1=uc[:, 0:2], op0=ALU.mult, op1=ALU.add),
                             reads=[ph, CW, uc], writes=[uc])

                def up(wb, fc, pp, ph, hc):
                    for k in range(16):
                        lw = wb[:, k, fc * 128:(fc + 1) * 128]
                        if HALO:
                            P.op("tensor", lambda e, k=k, lw=lw: e.matmul(ph[:, hc:hc + 2], lhsT=lw, rhs=H2T[:, k, 0:NH],
                                                                           start=(k == 0), stop=(k == 15)),
                                 reads=[wb, H2T], writes=[ph])
                        for h in range(2):
                            P.op("tensor", lambda e, k=k, lw=lw, h=h: e.matmul(
                                pp[:, h * 512:(h + 1) * 512], lhsT=lw, rhs=H2T[:, k, NH + h * 512: NH + (h + 1) * 512],
                                start=(k == 0), stop=(k == 15)), reads=[wb, H2T], writes=[pp])

                pmi = 0
                ci = 0
                for g in range(NG):
                    for fc in range(4):
                        pp, ph, hc = ACC[ci % 2], PH[ci % 2], 0
                        ci += 1
                        up(WUB[0], fc, pp, ph, hc)
                        conv(pp, ph, hc, UA[fc], g * 4 + fc)
                    if g + 1 < NG:
                        load_up(WUB[0], (g + 1) * 512)
                    for fc in range(4):
                        pp, ph, hc = ACC[ci % 2], PH[ci % 2], 0
                        ci += 1
                        ug = UG[fc % 2]
                        up(WUB[1], fc, pp, ph, hc)
                        conv(pp, ph, hc, ug, 44 + g * 4 + fc)
                        P.op("scalar", lambda e, ug=ug: e.activation(out=ug[:, :], in_=ug[:, :], func=AF.Silu),
                             reads=[ug], writes=[ug])
                        P.op("gpsimd", lambda e, ug=ug, fc=fc: e.tensor_tensor(out=ACTT[:, fc, :], in0=ug[:, :], in1=UA[fc][:, :],
                                                                                 op=ALU.mult),
                             reads=[ug, UA[fc]], writes=[ACTT])
                    if g + 1 < NG:
                        load_up(WUB[1], DFF + (g + 1) * 512)
                    for tt in range(TT):
                        for nb in range(4):
                            pm = PM[pmi % 2]
                            pmi += 1
                            cs = slice(nb * 512, (nb + 1) * 512)
                            for fc in range(4):
                                P.op("tensor", lambda e, fc=fc, tt=tt, cs=cs, pm=pm: e.matmul(
                                    pm[:, :], lhsT=ACTT[:, fc, tt * 128:(tt + 1) * 128], rhs=WD[:, fc, cs],
                                    start=(fc == 0), stop=(fc == 3)), reads=[ACTT, WD], writes=[pm])
                            P.op("vector", lambda e, tt=tt, cs=cs, pm=pm: e.tensor_tensor(
                                out=X[tt][:, cs], in0=pm[:, :], in1=X[tt][:, cs], op=ALU.add),
                                reads=[pm, X[tt]], writes=[X[tt]])
                    if g + 1 < NG:
                        load_down(g + 1)
                P.barrier()

        with ExitStack() as stC:
            def sbC(name, shape, dtp):
                return Buf(P, name, stC.enter_context(nc.sbuf_tensor(name, list(shape), dtp)))
            FO = [sbC(f"FO{i}", [128, D], F32) for i in range(2)]
            xnc = [sbC(f"xnc{i}", [128, D], BF16) for i in range(2)]
            FG = sbC("FG", [128, D], F32)
            P.dma(FG[:, :], fnorm.partition_broadcast(128), writes=[FG])
            for t in range(TT):
                b = (t + 1) % 2
                P.dma(x_out[t * 128:(t + 1) * 128, :], X[t][:, :], reads=[X[t]], writes=[OUTX])
                r = norm_tile(X[t], 128, t + 1, xnc[b])
                P.op("gpsimd", lambda e, t=t, b=b, r=r: e.tensor_scalar(out=xnc[b][:, :], in0=X[t][:, :], scalar1=r[:, :],
                                                                         scalar2=1.0, op0=ALU.mult, op1=ALU.mult),
                     reads=[X[t], r], writes=[xnc[b]])
                P.dma(xn_out[t * 128:(t + 1) * 128, :], xnc[b][:, :], reads=[xnc[b]], writes=[OUTN])
                if fin is not None:
                    P.op("vector", lambda e, t=t, b=b, r=r: e.scalar_tensor_tensor(out=FO[b][:, :], in0=X[t][:, :], scalar=r[:, :],
                                                                                   in1=FG[:, :], op0=ALU.mult, op1=ALU.mult),
                         reads=[X[t], r, FG], writes=[FO[b]])
                    P.dma(fin[t * 128:(t + 1) * 128, :], FO[b][:, :], reads=[FO[b]], writes=[OUTF])
            P.drain_all()


bf16 = ml_dtypes.bfloat16
bf16 = ml_dtypes.bfloat16

ROPE_THETA = 500000.0

def swap_perm_diff():
    p = np.arange(128)
    for base in (0, 64):
        for i in range(8):
            p[base + i] = base + i + 8
            p[base + i + 8] = base + i
    return p

def swap_perm_rope64():
    p = np.arange(64)
    p[:32] = np.arange(32, 64)
    p[32:] = np.arange(0, 32)
    return p

def rope_consts():
    rc = np.zeros((128, 8), np.float32)
    invd = (ROPE_THETA ** (-np.arange(0, 16, 2, dtype=np.float32) / np.float32(16))).astype(np.float32)
    invm = (ROPE_THETA ** (-np.arange(0, 64, 2, dtype=np.float32) / np.float32(64))).astype(np.float32)
    for p in range(128):
        q = p % 64
        if q < 16:
            rc[p, 0] = invd[q % 8]
            rc[p, 1] = 1.0
            rc[p, 2] = -1.0 if q < 8 else 1.0
        else:
            rc[p, 0] = 0.0
            rc[p, 1] = 0.0
            rc[p, 2] = 0.0
        rc[p, 5] = 1.0 - rc[p, 1]
        rc[p, 3] = invm[q % 32]
        rc[p, 4] = -1.0 if q < 32 else 1.0
        rc[p, 6] = 1.0
        rc[p, 7] = 0.0
    return rc

def pack_k1(l, r, w):
    win = w["w_in"][l]
    offs = np.cumsum([0, 512, 512, 512, 512, 512, 64, 768, 768, 768, 6])
    aq, ak, av, mcq, mckv, mkr, fq, fk, fv, fg = [win[:, offs[i]:offs[i + 1]] for i in range(10)]
    pd = swap_perm_diff()
    pr = swap_perm_rope64()
    cols = []
    q = aq[:, r * 256:(r + 1) * 256]
    k = ak[:, r * 256:(r + 1) * 256]
    def swp(m):
        return np.concatenate([m[:, h * 128:(h + 1) * 128][:, pd] for h in range(2)], 1)
    cols += [q, swp(q), k, swp(k), av[:, r * 256:(r + 1) * 256], mcq, mckv, mkr, mkr[:, pr],
             fq[:, r * 384:(r + 1) * 384], fk[:, r * 384:(r + 1) * 384], fv[:, r * 384:(r + 1) * 384], fg[:, r * 3:(r + 1) * 3]]
    W1 = np.ascontiguousarray(np.concatenate(cols, 1))
    uq = w["mla_w_uq"][l]
    ukv = w["mla_w_ukv"][l]
    hs = [3 * r + i for i in range(3)]
    U1 = np.concatenate([uq[:, h * 192:h * 192 + 128] for h in hs] + [uq[:, h * 192 + 128:h * 192 + 192] for h in hs]
                        + [uq[:, h * 192 + 128:h * 192 + 192][:, pr] for h in hs], 1)
    U2 = np.concatenate([ukv[:, h * 256:h * 256 + 128] for h in hs] + [ukv[:, h * 256 + 128:h * 256 + 256] for h in hs], 1)
    fb = np.zeros(4, np.float32)
    fb[:3] = w["fox_forget_bias"][l][3 * r:3 * r + 3]
    lam_init = 0.8 - 0.6 * math.exp(-0.3 * l)
    return {
        "W1": W1, "U1": np.ascontiguousarray(U1), "U2": np.ascontiguousarray(U2),
        "attn_norm": np.ascontiguousarray(w["attn_norm"][l]), "qn": np.ascontiguousarray(w["mla_q_norm"][l]),
        "kvn": np.ascontiguousarray(w["mla_kv_norm"][l]), "fb": fb,
        "dl": np.ascontiguousarray(w["diff_lambda"][l].reshape(-1)),
        "lc": np.array([lam_init, 1.0 - lam_init, 0, 0], np.float32),
        "idf": np.eye(128, dtype=np.float32), "idb": np.eye(128, dtype=np.float32).astype(bf16),
        "mask": np.triu(np.ones((128, 128), np.float32)).astype(bf16),
        "rc": rope_consts(),
    }


class _NCP:
    def __init__(self, nc, prefix):
        self._nc = nc
        self._p = prefix

    def sbuf_tensor(self, name, *a, **k):
        return self._nc.sbuf_tensor(self._p + name, *a, **k)

    def psum_tensor(self, name, *a, **k):
        return self._nc.psum_tensor(self._p + name, *a, **k)

    def __getattr__(self, n):
        return getattr(self._nc, n)


def body_k0(nc, P, x_ap, xn_ap, ntiles):
    with ExitStack() as st:
        P.stack = st
        xt = [P.sb(f"xt{i}", [128, D], F32) for i in range(2)]
        ot = [P.sb(f"ot{i}", [128, D], BF16) for i in range(2)]
        ss = [P.sb(f"ss{i}", [128, 1], F32) for i in range(2)]
        sd = [P.sb(f"sd{i}", [128, 1], F32) for i in range(2)]
        rs = [P.sb(f"rs{i}", [128, 1], F32) for i in range(2)]
        eps = P.sb("eps", [128, 1], F32)
        P.op("gpsimd", lambda e: e.memset(eps[:, :], 1e-6), writes=[eps])
        outd = P.wrap("outd", xn_ap)
        for i in range(ntiles):
            b = i % 2
            P.dma(xt[b][:, :], x_ap[i * 128:(i + 1) * 128, :], writes=[xt[b]])
            P.op("scalar", lambda e, b=b: e.activation(out=ot[b][:, :], in_=xt[b][:, :], func=AF.Square,
                                                         accum_out=ss[b][:, :]), reads=[xt[b]], writes=[ot[b], ss[b]])
            P.op("scalar", lambda e, b=b: e.activation(out=sd[b][:, :], in_=ss[b][:, :], func=AF.Sqrt,
                                                         scale=1.0 / D, bias=eps[:, :]), reads=[ss[b], eps], writes=[sd[b]])
            P.op("vector", lambda e, b=b: e.reciprocal(out=rs[b][:, :], in_=sd[b][:, :]), reads=[sd[b]], writes=[rs[b]])
            P.op("vector", lambda e, b=b: e.tensor_scalar(out=ot[b][:, :], in0=xt[b][:, :], scalar1=rs[b][:, :],
                                                            scalar2=None, op0=ALU.mult), reads=[xt[b], rs[b]], writes=[ot[b]])
            P.dma(xn_ap[i * 128:(i + 1) * 128, :], ot[b][:, :], reads=[ot[b]], writes=[outd])
        P.drain_all()


def _mix_loc(k):
    if k < 4:
        return k // 2, k % 2
    if k < 10:
        return (k - 4) // 3, 2 + (k - 4) % 3
    return (k - 10) // 3, 5 + (k - 10) % 3


def build_fused(depth=4):
    nc0 = bass.Bass("TRN2", target_bir_lowering=False)
    dt = nc0.dram_tensor

    def ext(name, shape, dtype):
        return dt(name, list(shape), dtype, kind="ExternalInput").ap()

    L = depth
    x = ext("x", [S, D], F32)
    pos = ext("pos", [S], I32)
    W1 = ext("W1", [L, 2, D, NC1], F32)
    U1 = ext("U1", [L, 2, 512, 768], F32)
    U2 = ext("U2", [L, 2, 512, 768], F32)
    attn_norm = ext("attn_norm", [L, D], F32)
    qn = ext("qn", [L, 512], F32)
    kvn = ext("kvn", [L, 512], F32)
    fb = ext("fb", [L, 2, 4], F32)
    dl = ext("dl", [L, 256], F32)
    lc = ext("lc", [L, 4], F32)
    w_o = ext("w_o", [L, D, D], F32)
    w_up = ext("w_up", [L, D, 2 * DFF], F32)
    conv_w = ext("conv_w", [L, 3, 2 * DFF], F32)
    conv_b = ext("conv_b", [L, 2 * DFF], F32)
    w_down = ext("w_down", [L, DFF, D], F32)
    ffn_norm = ext("ffn_norm", [L, D], F32)
    dnorm = ext("dnorm", [L, 128], F32)
    fnorm = ext("fnorm", [D], F32)
    idf = ext("idf", [128, 128], F32)
    idb = ext("idb", [128, 128], BF16)
    mask = ext("mask", [128, 128], BF16)
    rc = ext("rc", [128, 8], F32)
    out = dt("out", [S, D], F32, kind="ExternalOutput").ap()
    XS = [dt(f"xs_scr{i}", [S, D], F32).ap() for i in range(2)]
    XN = [dt(f"xn_scr{i}", [S, D], BF16).ap() for i in range(2)]
    MIX = [dt(f"mix_scr{r}", [1024, S], BF16).ap() for r in range(2)]
    TABS = dt("rope_tabs", [4, 128, S], BF16).ap()

    with ExitStack() as st:
        P = Prog(nc0, st)
        cnt = [0]

        def scoped():
            cnt[0] += 1
            P.nc = _NCP(nc0, f"b{cnt[0]}_")
            return P.nc

        body_k0(scoped(), P, x, XN[0], 16)
        for l in range(L):
            x_src = x if l == 0 else XS[l % 2]
            x_dst = XS[(l + 1) % 2]
            xn_src = XN[l % 2]
            xn_dst = XN[(l + 1) % 2]
            for r in range(2):
                body_k1(scoped(), P, {
                    "xn": xn_src, "pos": pos, "W1": W1[l, r], "U1": U1[l, r], "U2": U2[l, r],
                    "attn_norm": attn_norm[l], "qn": qn[l], "kvn": kvn[l], "fb": fb[l, r], "dl": dl[l], "lc": lc[l],
                    "idf": idf, "idb": idb, "mask": mask, "rc": rc, "mixT": MIX[r],
                    "tabs": TABS, "tabs_mode": "compute_store" if (l == 0 and r == 0) else "load"})
            for hf in range(2):
                t0 = hf * 1024

                def mix_main(k, t0=t0):
                    r_, c_ = _mix_loc(k)
                    return MIX[r_][c_ * 128:(c_ + 1) * 128, t0:t0 + 1024]

                def mix_halo(k, t0=t0):
                    r_, c_ = _mix_loc(k)
                    return MIX[r_][c_ * 128:(c_ + 1) * 128, t0 - 2:t0]

                body_k2(scoped(), P, {
                    "x_main": x_src[t0:t0 + 1024, :], "x_halo": None if hf == 0 else x_src[t0 - 2:t0, :],
                    "mix_main": mix_main, "mix_halo": mix_halo,
                    "w_o": w_o[l], "w_up": w_up[l], "conv_w": conv_w[l], "conv_b": conv_b[l], "w_down": w_down[l],
                    "ffn_norm": ffn_norm[l], "dnorm": dnorm[l], "fnorm": fnorm, "lc": lc[l], "idf": idf, "idb": idb,
                    "x_out": x_dst[t0:t0 + 1024, :], "xn_out": xn_dst[t0:t0 + 1024, :],
                    "fin": out[t0:t0 + 1024, :] if l == L - 1 else None})
        P.nc = nc0
        P.stack = st
        OUTB = P.wrap("OUTB", out)
        P.finish([OUTB])
        P.emit()
    return nc0


from concourse.bass_utils import run_bass_kernel_spmd

_PROG = {}


def kernel(**inputs):
    w = {k: np.asarray(v) for k, v in inputs.items()}
    x = np.ascontiguousarray(w["x"], dtype=np.float32)
    pos = np.ascontiguousarray(w["positions"]).astype(np.int32)
    L = w["w_in"].shape[0]
    if "f" not in _PROG:
        _PROG["f"] = build_fused(L)
    packs = [[pack_k1(l, r, w) for r in range(2)] for l in range(L)]

    def st2(key):
        return np.ascontiguousarray(np.stack([np.stack([packs[l][r][key] for r in range(2)], 0) for l in range(L)], 0))

    def st1(key):
        return np.ascontiguousarray(np.stack([packs[l][0][key] for l in range(L)], 0))

    shared = {
        "W1": st2("W1"), "U1": st2("U1"), "U2": st2("U2"), "fb": st2("fb"),
        "attn_norm": st1("attn_norm"), "qn": st1("qn"), "kvn": st1("kvn"), "dl": st1("dl"), "lc": st1("lc"),
        "w_o": np.ascontiguousarray(w["w_o"]), "w_up": np.ascontiguousarray(w["ffn_w_up"]),
        "conv_w": np.ascontiguousarray(w["ffn_conv_w"]), "conv_b": np.ascontiguousarray(w["ffn_conv_b"]),
        "w_down": np.ascontiguousarray(w["ffn_w_down"]), "ffn_norm": np.ascontiguousarray(w["ffn_norm"]),
        "dnorm": np.ascontiguousarray(w["diff_out_norm"]), "fnorm": np.ascontiguousarray(w["final_norm"]),
        "idf": packs[0][0]["idf"], "idb": packs[0][0]["idb"], "mask": packs[0][0]["mask"], "rc": packs[0][0]["rc"],
    }
    cores = list(range(8))
    ins = []
    for c in cores:
        d = dict(shared)
        d["x"] = np.ascontiguousarray(x[c // 2])
        d["pos"] = np.ascontiguousarray(pos[c // 2])
        ins.append(d)
    res = run_bass_kernel_spmd(_PROG["f"], ins, core_ids=cores)
    outs = [np.asarray(res.results[2 * b]["out"]) for b in range(4)]
    return np.stack(outs, 0).astype(np.float32)
```
